# Optimizing a Trainium2 kernel written in Bass

```python
import jax
import jax.numpy as jnp
from jax import lax
import numpy as np

D_MODEL = 1024
BATCH = 2
SEQ = 16384
DEPTH = 2

CTX_LEN = 256
GRID_W = 64
HG_HEADS = 4
HG_DK = 128
HG_DV = 128
HG_WIDTH = HG_HEADS * HG_DK
HG_CHUNK = 64
RG_WIDTH = 512
RG_BLOCKS = 8
RG_BLOCK = RG_WIDTH // RG_BLOCKS
RG_CONV = 4
RG_C = 8.0
MIX_WIDTH = HG_WIDTH + RG_WIDTH
IN_COLS = 5 * HG_WIDTH + 2 * RG_WIDTH
D_FF = -(-8 * D_MODEL // (3 * 256)) * 256
N_MOD = 6
EPS = 1e-6

kernel_name = 'hybrid_hgrn2_rglru_prefix_dit_block'


def rms_norm(x, w):
    xf = x.astype(jnp.float32)
    y = xf * lax.rsqrt(jnp.mean(xf * xf, axis=-1, keepdims=True) + EPS)
    return (y * w.astype(jnp.float32)).astype(x.dtype)


def split_heads(a):
    b, t, _ = a.shape
    return a.reshape(b, t, HG_HEADS, -1).transpose(0, 2, 1, 3)


def merge_heads(a):
    b, h, t, d = a.shape
    return a.transpose(0, 2, 1, 3).reshape(b, t, h * d)


def flip_time(a, reverse, axis):
    return jnp.flip(a, axis=axis) if reverse else a


def hgrn2_gates(z, lb):
    lb = lb[None, :, None, :]
    log_f = jnp.log(lb + (1.0 - lb) * jax.nn.sigmoid(z))
    k = (1.0 - lb) * jax.nn.sigmoid(-z)
    return log_f, k


def hgrn2_chunk_scan(q, k, v, log_f, s0):
    b, h, t, _ = q.shape
    n = t // HG_CHUNK

    def chunks(a):
        return jnp.moveaxis(a.reshape(b, h, n, HG_CHUNK, a.shape[-1]), 2, 0)

    lower_tri = jnp.tril(jnp.ones((HG_CHUNK, HG_CHUNK), dtype=bool))[:, :, None]

    def step(s, inp):
        qc, kc, vc, lc = inp
        cum = jnp.cumsum(lc, axis=2)
        diff = cum[:, :, :, None, :] - cum[:, :, None, :, :]
        decay = jnp.exp(jnp.where(lower_tri, diff, -jnp.inf))
        scores = jnp.einsum('bhtk,bhtsk,bhsk->bhts', qc, decay, kc)
        o = (jnp.einsum('bhts,bhsv->bhtv', scores, vc)
             + jnp.einsum('bhtk,bhkv->bhtv', qc * jnp.exp(cum), s))
        total = cum[:, :, -1:, :]
        s_new = (jnp.exp(total[:, :, 0, :, None]) * s
                 + jnp.einsum('bhsk,bhsv->bhkv', kc * jnp.exp(total - cum), vc))
        return s_new, o

    s_fin, o = lax.scan(step, s0, (chunks(q), chunks(k), chunks(v), chunks(log_f)))
    return jnp.moveaxis(o, 0, 2).reshape(b, h, t, -1), s_fin


def hgrn2_mixer(p_ctx, p_lat, lb, g_norm_w, with_ctx_out):
    def parts(p):
        q, zf, zb, v, g = jnp.split(p[..., :5 * HG_WIDTH].astype(jnp.float32), 5, axis=-1)
        return jax.nn.silu(split_heads(q)), (split_heads(zf), split_heads(zb)), split_heads(v), g

    q_c, z_c, v_c, g_c = parts(p_ctx)
    q_l, z_l, v_l, g_l = parts(p_lat)
    s0 = jnp.zeros((q_c.shape[0], HG_HEADS, HG_DK, HG_DV), jnp.float32)
    outs_c, outs_l = [], []
    for d in range(2):
        rev = d == 1
        lb_d = lb[d].astype(jnp.float32).reshape(HG_HEADS, HG_DK)
        lf_c, k_c = hgrn2_gates(z_c[d], lb_d)
        lf_l, k_l = hgrn2_gates(z_l[d], lb_d)
        o_c, s_ctx = hgrn2_chunk_scan(*(flip_time(a, rev, 2) for a in (q_c, k_c, v_c, lf_c)), s0)
        o_l, _ = hgrn2_chunk_scan(*(flip_time(a, rev, 2) for a in (q_l, k_l, v_l, lf_l)), s_ctx)
        outs_c.append(flip_time(o_c, rev, 2))
        outs_l.append(flip_time(o_l, rev, 2))

    def readout(o, g):
        return merge_heads(rms_norm(o, g_norm_w)) * jax.nn.silu(g)

    out_l = readout(outs_l[0] + outs_l[1], g_l).astype(p_lat.dtype)
    out_c = readout(outs_c[0] + outs_c[1], g_c).astype(p_ctx.dtype) if with_ctx_out else None
    return out_c, out_l


def depthwise_conv(u, w, bias):
    pad_l = RG_CONV // 2
    out = lax.conv_general_dilated(
        u, w[:, None, :], window_strides=(1,), padding=[(pad_l, RG_CONV - 1 - pad_l)],
        dimension_numbers=('NWC', 'WIO', 'NWC'), feature_group_count=u.shape[-1])
    return out + bias


def rglru_coeffs(u, w_a, b_a, w_x, b_x, lam):
    ub = u.reshape(*u.shape[:-1], RG_BLOCKS, RG_BLOCK)
    r = jax.nn.sigmoid(jnp.einsum('btni,nij->btnj', ub, w_a).reshape(u.shape) + b_a)
    i = jax.nn.sigmoid(jnp.einsum('btni,nij->btnj', ub, w_x).reshape(u.shape) + b_x)
    log_a = RG_C * r * jax.nn.log_sigmoid(lam)
    return jnp.exp(log_a), jnp.sqrt(-jnp.expm1(2.0 * log_a)) * (i * u)


def linear_recurrence(a, bx, h0):
    def combine(e1, e2):
        return e1[0] * e2[0], e2[0] * e1[1] + e2[1]
    a_cum, h = lax.associative_scan(combine, (a, bx), axis=1)
    return h + a_cum * h0[:, None, :]


def rglru_mixer(p_ctx, p_lat, conv_w, conv_b, w_a, b_a, w_x, b_x, lam, with_ctx_out):
    f32 = jnp.float32
    conv_w, conv_b = conv_w.astype(f32), conv_b.astype(f32)
    u_c, gate_c = jnp.split(p_ctx[..., 5 * HG_WIDTH:].astype(f32), 2, axis=-1)
    u_l, gate_l = jnp.split(p_lat[..., 5 * HG_WIDTH:].astype(f32), 2, axis=-1)
    b, n_lat, ch = u_l.shape
    rows = n_lat // GRID_W
    u_c = depthwise_conv(u_c, conv_w, conv_b)
    u_l = depthwise_conv(u_l.reshape(b * rows, GRID_W, ch), conv_w, conv_b).reshape(b, n_lat, ch)
    h0 = jnp.zeros((b, ch), f32)
    hs_c, hs_l = [], []
    for d in range(2):
        rev = d == 1
        prm = (w_a[d].astype(f32), b_a[d].astype(f32), w_x[d].astype(f32), b_x[d].astype(f32), lam[d].astype(f32))
        a_c, x_c = rglru_coeffs(flip_time(u_c, rev, 1), *prm)
        a_l, x_l = rglru_coeffs(flip_time(u_l, rev, 1), *prm)
        h_c = linear_recurrence(a_c, x_c, h0)
        h_l = linear_recurrence(a_l, x_l, h_c[:, -1])
        hs_c.append(flip_time(h_c, rev, 1))
        hs_l.append(flip_time(h_l, rev, 1))
    out_l = ((hs_l[0] + hs_l[1]) * jax.nn.gelu(gate_l)).astype(p_lat.dtype)
    out_c = ((hs_c[0] + hs_c[1]) * jax.nn.gelu(gate_c)).astype(p_ctx.dtype) if with_ctx_out else None
    return out_c, out_l


def swiglu(h, w_gate, w_up, w_down):
    return (jax.nn.silu(h @ w_gate) * (h @ w_up)) @ w_down


def setup_inputs(seed: int = 0) -> dict:
    key = jax.random.key(seed)
    ks = jax.random.split(key, 24)

    def nrm(k, shape, scale):
        return scale * jax.random.normal(k, shape, jnp.float32)

    a0 = jax.random.uniform(ks[16], (DEPTH, 2, RG_WIDTH), jnp.float32, 0.9, 0.999)
    s = a0 ** (1.0 / RG_C)
    return {
        'x': nrm(ks[0], (BATCH, SEQ, D_MODEL), 1.0),
        'c': nrm(ks[1], (BATCH, D_MODEL), 1.0),
        'ctx': nrm(ks[2], (BATCH, CTX_LEN, D_MODEL), 1.0),
        'c_ctx': nrm(ks[3], (D_MODEL,), 1.0),
        'w_mod': nrm(ks[4], (DEPTH, D_MODEL, N_MOD * D_MODEL), 0.5 * D_MODEL ** -0.5),
        'b_mod': nrm(ks[5], (DEPTH, N_MOD * D_MODEL), 0.02),
        'norm_g': 1.0 + nrm(ks[6], (DEPTH, 4, D_MODEL), 0.02),
        'w_in': nrm(ks[7], (DEPTH, D_MODEL, IN_COLS), D_MODEL ** -0.5),
        'hg_lb_logits': nrm(ks[8], (DEPTH, 2, HG_WIDTH), 1.0),
        'hg_gnorm': 1.0 + nrm(ks[9], (DEPTH, HG_DV), 0.02),
        'rg_conv_w': nrm(ks[10], (DEPTH, RG_CONV, RG_WIDTH), RG_CONV ** -0.5),
        'rg_conv_b': nrm(ks[11], (DEPTH, RG_WIDTH), 0.02),
        'rg_w_a': nrm(ks[12], (DEPTH, 2, RG_BLOCKS, RG_BLOCK, RG_BLOCK), RG_BLOCK ** -0.5),
        'rg_b_a': nrm(ks[13], (DEPTH, 2, RG_WIDTH), 0.02),
        'rg_w_x': nrm(ks[14], (DEPTH, 2, RG_BLOCKS, RG_BLOCK, RG_BLOCK), RG_BLOCK ** -0.5),
        'rg_b_x': nrm(ks[15], (DEPTH, 2, RG_WIDTH), 0.02),
        'rg_lambda': jnp.log(s) - jnp.log1p(-s),
        'w_out': nrm(ks[17], (DEPTH, MIX_WIDTH, D_MODEL), MIX_WIDTH ** -0.5),
        'w_ffn_gate': nrm(ks[18], (DEPTH, D_MODEL, D_FF), D_MODEL ** -0.5),
        'w_ffn_up': nrm(ks[19], (DEPTH, D_MODEL, D_FF), D_MODEL ** -0.5),
        'w_ffn_down': nrm(ks[20], (DEPTH, D_FF, D_MODEL), D_FF ** -0.5),
    }


def reference(x, c, ctx, c_ctx, w_mod, b_mod, norm_g, w_in, hg_lb_logits, hg_gnorm,
              rg_conv_w, rg_conv_b, rg_w_a, rg_b_a, rg_w_x, rg_b_x, rg_lambda,
              w_out, w_ffn_gate, w_ffn_up, w_ffn_down):
    b = x.shape[0]
    lb_all = jnp.cumsum(jax.nn.softmax(hg_lb_logits.astype(jnp.float32), axis=0), axis=0)
    lb_all = lb_all - lb_all[0]
    x_lat, x_ctx = x, ctx
    for layer in range(DEPTH):
        with_ctx = layer < DEPTH - 1
        m_l = (jax.nn.silu(c) @ w_mod[layer] + b_mod[layer]).reshape(b, N_MOD, 1, D_MODEL)
        m_c = (jax.nn.silu(c_ctx) @ w_mod[layer] + b_mod[layer]).reshape(N_MOD, 1, 1, D_MODEL)
        g = norm_g[layer]
        h_l = rms_norm(x_lat, g[0]) * (1.0 + m_l[:, 1]) + m_l[:, 0]
        h_c = rms_norm(x_ctx, g[0]) * (1.0 + m_c[1]) + m_c[0]
        p_l = h_l @ w_in[layer]
        p_c = h_c @ w_in[layer]
        hg_c, hg_l = hgrn2_mixer(p_c, p_l, lb_all[layer], hg_gnorm[layer], with_ctx)
        rg_c, rg_l = rglru_mixer(p_c, p_l, rg_conv_w[layer], rg_conv_b[layer], rg_w_a[layer], rg_b_a[layer],
                                 rg_w_x[layer], rg_b_x[layer], rg_lambda[layer], with_ctx)
        o_l = jnp.concatenate([hg_l, rg_l], axis=-1) @ w_out[layer]
        x_lat = x_lat + m_l[:, 2] * rms_norm(o_l, g[1])
        f_l = rms_norm(x_lat, g[2]) * (1.0 + m_l[:, 4]) + m_l[:, 3]
        x_lat = x_lat + m_l[:, 5] * rms_norm(swiglu(f_l, w_ffn_gate[layer], w_ffn_up[layer], w_ffn_down[layer]), g[3])
        if with_ctx:
            o_c = jnp.concatenate([hg_c, rg_c], axis=-1) @ w_out[layer]
            x_ctx = x_ctx + m_c[2] * rms_norm(o_c, g[1])
            f_c = rms_norm(x_ctx, g[2]) * (1.0 + m_c[4]) + m_c[3]
            x_ctx = x_ctx + m_c[5] * rms_norm(swiglu(f_c, w_ffn_gate[layer], w_ffn_up[layer], w_ffn_down[layer]), g[3])
    return x_lat
```

```python
import numpy as np
from contextlib import ExitStack
import concourse.bass as bass
import concourse.mybir as mybir
from concourse.bass_utils import run_bass_kernel_spmd

F32 = mybir.dt.float32
BF16 = mybir.dt.bfloat16
AF = mybir.ActivationFunctionType
ALU = mybir.AluOpType
AX = mybir.AxisListType

MODE = "whole2"
CHAIN = (MODE == "whole2")
D = 1024
KD = 8
NLAT = 16384 if CHAIN else 4096
NCTX = 256
NT = NLAT + NCTX
HGW = 512
RGW = 512
INC = 3584
DFF = 2816
NFF = 22
DEPTH = 2
EPS = 1e-6
CH = 64

DEBUG = False
STAGE = 99
USE_CC = True


class Buf:
    def __init__(self, prog, t, name):
        self.prog = prog
        self.t = t
        self.name = name
        self.last_w = None
        self.readers = []
        self.dma_sem = None
        self.dma_cnt = 0
        self.rd_sem = None
        self.rd_cnt = 0

    def __getitem__(self, idx):
        return self.t[idx]


class Prog:
    ENGS = ("pe", "act", "dve", "pool", "sp")

    def __init__(self, nc):
        self.nc = nc
        self.es = ExitStack()
        self.ops = {e: [] for e in self.ENGS}
        self.cnt = {e: 0 for e in self.ENGS}
        self.sems = {}
        for e in self.ENGS:
            self.sems[e] = self.es.enter_context(nc.semaphore("prog_" + e))
        self.known = {e: {} for e in self.ENGS}
        self.final_waits = []
        self._dma_sem_vals = {}
        self.sem_pool = []
        self.cur_bufs = []
        self.nsem = 0

    def sbuf(self, name, shape, dtype, es=None):
        self.nuniq = getattr(self, "nuniq", 0) + 1
        t = (es or self.es).enter_context(self.nc.sbuf_tensor("sb%d_%s" % (self.nuniq, name), list(shape), dtype))
        b = Buf(self, t, name)
        if es is not None:
            self.cur_bufs.append(b)
        return b

    def end_phase(self):
        self.barrier()
        for b in self.cur_bufs:
            if b.dma_sem is not None:
                self.sem_pool.append((b.dma_sem, b.dma_cnt))
                b.dma_sem = None
            if b.rd_sem is not None:
                self.sem_pool.append((b.rd_sem, b.rd_cnt))
                b.rd_sem = None
        self.cur_bufs = []

    def psum(self, name, shape, dtype):
        t = self.es.enter_context(self.nc.psum_tensor(name, list(shape), dtype))
        return Buf(self, t, name)

    def dram(self, name, shape, dtype):
        t = self.nc.dram_tensor(name, list(shape), dtype)
        return Buf(self, t, name)

    def _new_sem(self, name):
        if self.sem_pool:
            return self.sem_pool.pop()
        self.nsem += 1
        return (self.es.enter_context(self.nc.semaphore("s%d" % self.nsem)), 0)

    def _deps(self, eng, reads, writes):
        need = {}

        def add(ev):
            if ev is None:
                return
            key, val, src = ev
            if src == eng and eng == "pe":
                return
            if key not in need or need[key][0] < val:
                need[key] = (val, src)

        for b in reads:
            add(b.last_w)
        for b in writes:
            add(b.last_w)
            for r in b.readers:
                if r[2] == eng:
                    continue
                add(r)
        waits = []
        kn = self.known[eng]
        for key, (val, src) in need.items():
            if kn.get(key, 0) >= val:
                continue
            kn[key] = val
            waits.append((key, val))
        return waits

    def _semobj(self, key):
        if isinstance(key, str):
            return self.sems[key]
        return key

    def _mark(self, ev, reads, writes):
        for b in writes:
            b.last_w = ev
            b.readers = []
        for b in reads:
            if b in writes:
                continue
            b.readers.append(ev)
            if len(b.readers) > 48:
                latest = {}
                for r in b.readers:
                    k = r[0] if isinstance(r[0], str) else id(r[0])
                    if k not in latest or latest[k][1] < r[1]:
                        latest[k] = r
                b.readers = list(latest.values())

    def op(self, eng, fn, reads=(), writes=()):
        reads = [b for b in reads if b is not None]
        writes = [b for b in writes if b is not None]
        waits = self._deps(eng, reads, writes)
        self.cnt[eng] += 1
        ev = (eng, self.cnt[eng], eng)
        self.ops[eng].append(("op", waits, fn))
        self._mark(ev, reads, writes)
        return ev

    def dma(self, q, out_ap, in_ap, reads=(), writes=(), **kw):
        reads = [b for b in reads if b is not None]
        writes = [b for b in writes if b is not None]
        waits = self._deps(q, reads, writes)
        owner = writes[0] if writes else reads[0]
        if writes:
            if owner.dma_sem is None:
                owner.dma_sem, owner.dma_cnt = self._new_sem("dw_" + owner.name)
            owner.dma_cnt += 16
            sem, val = owner.dma_sem, owner.dma_cnt
        else:
            if owner.rd_sem is None:
                owner.rd_sem, owner.rd_cnt = self._new_sem("dr_" + owner.name)
            owner.rd_cnt += 16
            sem, val = owner.rd_sem, owner.rd_cnt
        ev = (sem, val, "dma")
        self._dma_sem_vals[sem] = val
        self.ops[q].append(("dma", waits, (out_ap, in_ap, sem, kw)))
        self._mark(ev, reads, writes)
        return ev

    def custom(self, q, fn, reads=(), writes=(), inc=16):
        reads = [b for b in reads if b is not None]
        writes = [b for b in writes if b is not None]
        waits = self._deps(q, reads, writes)
        owner = writes[0]
        if owner.dma_sem is None:
            owner.dma_sem, owner.dma_cnt = self._new_sem("dw_" + owner.name)
        owner.dma_cnt += inc
        sem, val = owner.dma_sem, owner.dma_cnt
        ev = (sem, val, "dma")
        self._dma_sem_vals[sem] = val
        self.ops[q].append(("custom", waits, (fn, sem, inc)))
        self._mark(ev, reads, writes)
        return ev

    def barrier(self):
        evs = [(e, self.cnt[e]) for e in self.ENGS if self.cnt[e] > 0]
        evs += list(self._dma_sem_vals.items())
        for e in self.ENGS:
            waits = []
            kn = self.known[e]
            for key, val in evs:
                if kn.get(key, 0) >= val:
                    continue
                kn[key] = val
                waits.append((key, val))
            if waits:
                self.ops[e].append(("wait", waits, None))

    def emit(self):
        nc = self.nc
        engmap = {"pe": "tensor", "act": "scalar", "dve": "vector", "pool": "gpsimd", "sp": "sync"}
        with nc.Block() as block:
            for e in self.ENGS:
                ops = self.ops[e]
                if not ops:
                    continue
                semself = self.sems[e]

                def body(engine, ops=ops, semself=semself):
                    for kind, waits, payload in ops:
                        for key, val in waits:
                            engine.wait_ge(self._semobj(key), val)
                        if kind == "op":
                            payload(engine).then_inc(semself, 1)
                        elif kind == "dma":
                            out_ap, in_ap, sem, kw = payload
                            engine.dma_start(out=out_ap, in_=in_ap, **kw).then_inc(sem, 16)
                        elif kind == "custom":
                            fn, sem, inc = payload
                            fn(engine).then_inc(sem, inc)

                getattr(block, engmap[e])(body)
        self.es.close()


class K:
    def __init__(self):
        nc = bass.Bass("TRN2", target_bir_lowering=False)
        self.nc = nc
        self.P = Prog(nc)
        self.ins = {}
        self.outs = {}
        self.psn = 0

    def inp(self, name, shape, dtype=F32):
        t = self.nc.dram_tensor(name, list(shape), dtype, kind="ExternalInput")
        self.ins[name] = t
        return t

    def outp(self, name, shape, dtype=F32):
        t = self.nc.dram_tensor(name, list(shape), dtype, kind="ExternalOutput")
        b = Buf(self.P, t, name)
        self.outs[name] = b
        return b

    def ACT(self, out, in_, func, R, W, bias=None, scale=None, accum=None):
        kw = {}
        if bias is not None:
            kw["bias"] = bias
        if scale is not None:
            kw["scale"] = scale
        if accum is not None:
            kw["accum_out"] = accum
        return self.P.op("act", lambda e: e.activation(out=out, in_=in_, func=func, **kw), R, W)

    def TS(self, eng, out, in0, s1, s2, op0, op1, R, W):
        if op1 is None:
            return self.P.op(eng, lambda e: e.tensor_scalar(out=out, in0=in0, scalar1=s1, scalar2=None, op0=op0), R, W)
        return self.P.op(eng, lambda e: e.tensor_scalar(out=out, in0=in0, scalar1=s1, scalar2=s2, op0=op0, op1=op1), R, W)

    def TT(self, eng, out, in0, in1, op, R, W):
        return self.P.op(eng, lambda e: e.tensor_tensor(out=out, in0=in0, in1=in1, op=op), R, W)

    def STT(self, out, in0, scalar, in1, op0, op1, R, W):
        return self.P.op("dve", lambda e: e.scalar_tensor_tensor(out=out, in0=in0, scalar=scalar, in1=in1, op0=op0, op1=op1), R, W)

    def MM(self, out, lhsT, rhs, start, stop, R, W):
        return self.P.op("pe", lambda e: e.matmul(out, lhsT=lhsT, rhs=rhs, start=start, stop=stop), R, W)

    def TR(self, out, in_, ident, R, W):
        return self.P.op("pe", lambda e: e.transpose(out, in_, ident), R, W)

    def CP(self, eng, out, in_, R, W):
        if eng == "act":
            return self.P.op("act", lambda e: e.copy(out=out, in_=in_), R, W)
        return self.P.op(eng, lambda e: e.tensor_copy(out=out, in_=in_), R, W)

    def SCAN(self, out, d0, d1, init, R, W):
        return self.P.op("dve", lambda e: e.tensor_tensor_scan(out=out, data0=d0, data1=d1, initial=init, op0=ALU.mult, op1=ALU.add), R, W)

    def MEMSET(self, eng, ap, val, W):
        return self.P.op(eng, lambda e: e.memset(ap, val), [], W)

    def LD(self, out, in_, W, R=(), q="sp"):
        return self.P.dma(q, out, in_, reads=list(R), writes=list(W))

    def ST(self, out, in_, R, W=(), q="sp"):
        return self.P.dma(q, out, in_, reads=list(R), writes=list(W))

    def ps(self):
        b = self.psb[self.psn % 8]
        self.psn += 1
        return b


def build(nlayers=DEPTH):
    k = K()
    nc, P = k.nc, k.P
    x_in = k.inp("x", [NLAT, D])
    ctx_in = k.inp("ctx", [NCTX, D])
    cvec = k.inp("cvec", [128, KD, 2])
    w_mod = k.inp("w_mod", [DEPTH, D, 6 * D])
    bmod_fm = k.inp("bmod_fm", [DEPTH, 128, 48])
    bmod_row = k.inp("bmod_row", [DEPTH, 1, 6 * D])
    normg_fm = k.inp("normg_fm", [DEPTH, 128, 32])
    normg_row = k.inp("normg_row", [DEPTH, 4, D])
    w_in = k.inp("w_in", [DEPTH, D, INC])
    lb_fm = k.inp("lb_fm", [128, DEPTH, 2, 4])
    gn_fm = k.inp("gn_fm", [DEPTH, 128, 1])
    convw_fm = k.inp("convw_fm", [DEPTH, 128, 4, 4])
    convb_fm = k.inp("convb_fm", [DEPTH, 128, 4])
    ba_fm = k.inp("ba_fm", [DEPTH, 128, 2, 4])
    bx_fm = k.inp("bx_fm", [DEPTH, 128, 2, 4])
    lam_fm = k.inp("lam_fm", [DEPTH, 128, 2, 4])
    rg_w_a = k.inp("rg_w_a", [DEPTH, 2, 8, 64, 64])
    rg_w_x = k.inp("rg_w_x", [DEPTH, 2, 8, 64, 64])
    w_out = k.inp("w_out", [DEPTH, D, D])
    w_gate = k.inp("w_ffn_gate", [DEPTH, D, DFF])
    w_up = k.inp("w_ffn_up", [DEPTH, D, DFF])
    w_down = k.inp("w_ffn_down", [DEPTH, DFF, D])
    flags_in = k.inp("flags", [128, 16])
    ident_in = k.inp("ident", [128, 128])
    masks_in = k.inp("masks", [2, 64, 512])
    out_t = k.outp("out", [NLAT, D])

    s_qT = P.dram("s_qT", [4, 128, NT], BF16)
    s_fT = P.dram("s_fT", [2, 4, 128, NT], F32)
    s_v = P.dram("s_v", [NT, HGW], BF16)
    s_g = P.dram("s_g", [NT, HGW], BF16)
    s_u = P.dram("s_u", [4, 128, NT], BF16)
    s_gate = P.dram("s_gate", [4, 128, NT], BF16)
    s_hsum = P.dram("s_hsum", [4, 128, NT], F32)
    s_hsumf = P.dram("s_hsumf", [4, 128, NT], F32)
    s_acum = P.dram("s_acum", [2, 4, 128, NT], F32)
    s_osum = P.dram("s_osum", [NT, HGW], F32)
    s_of = P.dram("s_of", [NT, HGW], F32)
    s_qseg = P.dram("s_qseg", [2, 4, 128, NT], BF16)
    s_xres = P.dram("s_xres", [NT, D], F32)
    s_xmid = P.dram("s_xmid", [NT, D], F32)
    XW = 1024 + 8 + 8 + 8
    s_xsrc = P.dram("s_xsrc", [128, XW], F32)
    s_xdst = P.dram("s_xdst", [8 * 128, XW], F32)

    dbg = {}
    if DEBUG:
        dbg["qT"] = k.outp("d_qT", [4, 128, NT], BF16)
        dbg["fT"] = k.outp("d_fT", [2, 4, 128, NT], F32)
        dbg["v"] = k.outp("d_v", [NT, HGW], BF16)
        dbg["g"] = k.outp("d_g", [NT, HGW], BF16)
        dbg["u"] = k.outp("d_u", [4, 128, NT], BF16)
        dbg["gate"] = k.outp("d_gate", [4, 128, NT], BF16)
        dbg["hsum"] = k.outp("d_hsum", [4, 128, NT], F32)
        dbg["acum"] = k.outp("d_acum", [2, 4, 128, NT], F32)
        dbg["osum"] = k.outp("d_osum", [NT, HGW], F32)
        dbg["qseg"] = k.outp("d_qseg", [2, 4, 128, NT], BF16)
        dbg["xmid"] = k.outp("d_xmid", [NT, D], F32)
        dbg["xres"] = k.outp("d_xres", [NT, D], F32)
        dbg["xdst"] = k.outp("d_xdst", [8 * 128, XW], F32)
        dbg["misc"] = k.outp("d_misc", [128, 256], F32)
        dbg["sst"] = k.outp("d_sst", [128, 1024], F32)
        dbg["sctx"] = k.outp("d_sctx", [128, 1024], F32)
        dbg["sin"] = k.outp("d_sin", [128, 1024], F32)

    k.psb = [P.psum("psb%d" % i, [128, 512], F32) for i in range(8)]

    ident_f = P.sbuf("ident_f", [128, 128], F32)
    ident = P.sbuf("ident", [128, 128], BF16)
    masks_f = P.sbuf("masks_f", [64, 2, 512], F32)
    maski = P.sbuf("maski", [64, 2, 512], mybir.dt.int32)
    ones_row = P.sbuf("ones_row", [1, 128], F32)
    ones_t = P.sbuf("ones_t", [128, 512], F32)
    zeros_t = P.sbuf("zeros_t", [128, 512], F32)
    flags = P.sbuf("flags", [128, 16], F32)
    cv_f = P.sbuf("cv_f", [128, KD, 2], F32)
    scbf = P.sbuf("scbf", [128, KD, 2], BF16)
    modfm = P.sbuf("modfm", [128, 48, 2], F32)
    bmodfm = P.sbuf("bmodfm", [128, 48], F32)
    gfm = P.sbuf("gfm", [128, 32], F32)
    scsh = P.sbuf("scsh", [128, 4, KD, 2], F32)
    grow = [[P.sbuf("grow%d%d" % (a, j), [128, D], F32) for j in range(2)] for a in range(2)]
    lbt = P.sbuf("lbt", [128, DEPTH, 2, 4], F32)
    lbv = P.sbuf("lbv", [128, 2, 4], F32)
    omlv = P.sbuf("omlv", [128, 2, 4], F32)
    gnv = P.sbuf("gnv", [128, 1], F32)
    cwv = P.sbuf("cwv", [128, 4, 4], F32)
    cbv = P.sbuf("cbv", [128, 4], F32)
    bav = P.sbuf("bav", [128, 2, 4], F32)
    bxv = P.sbuf("bxv", [128, 2, 4], F32)
    c1v = P.sbuf("c1v", [128, 2, 4], F32)
    wbd = P.sbuf("wbd", [128, 2, 2, 4, 128], BF16)
    Sst = P.sbuf("Sst", [128, 2, 4, 128], F32)
    dtot = P.sbuf("dtot", [128, 2, 4], F32)
    hctx = P.sbuf("hctx", [128, 2, 4], F32)
    hfin = P.sbuf("hfin", [128, 2, 4], F32)
    atot = P.sbuf("atot", [128, 2, 4], F32)
    hin = P.sbuf("hin", [128, 2, 4], F32)
    small = P.sbuf("small", [128, 64], F32)

    k.LD(ident_f[:], ident_in.ap(), [ident_f])
    k.CP("dve", ident[:], ident_f[:], [ident_f], [ident])
    k.LD(masks_f[:], masks_in.ap().rearrange("a s t -> s a t"), [masks_f])
    k.CP("dve", maski[:], masks_f[:], [masks_f], [maski])
    k.MEMSET("pool", ones_row[:], 1.0, [ones_row])
    k.MEMSET("pool", ones_t[:], 1.0, [ones_t])
    k.MEMSET("pool", zeros_t[:], 0.0, [zeros_t])
    k.LD(flags[:], flags_in.ap(), [flags])
    k.LD(cv_f[:], cvec.ap(), [cv_f])
    k.ACT(scbf[:], cv_f[:], AF.Silu, [cv_f], [scbf])
    k.LD(lbt[:], lb_fm.ap(), [lbt])

    units = [(0, NCTX, True)] + [(NCTX + 512 * i, 512, False) for i in range(NLAT // 512)]

    def x_src(L, tok0, T):
        if L == 0:
            if tok0 < NCTX:
                return ctx_in.ap()[tok0:tok0 + T, :], None
            return x_in.ap()[tok0 - NCTX:tok0 - NCTX + T, :], None
        return s_xres.t[tok0:tok0 + T, :], s_xres

    def norm_bufs(es, tag, ntmax):
        return (P.sbuf("ssq_" + tag, [128, 4], F32, es), P.sbuf("rstd_" + tag, [128, 4], F32, es),
                P.sbuf("junk_" + tag, [128, D], BF16, es), P.sbuf("xn_" + tag, [128, ntmax, D], BF16, es))

    def norm_mod(nb, xt, nt, a, j, hT):
        T = nt * 128
        ssq, rstd, junk, xn = nb
        for jj in range(nt):
            k.ACT(junk[:], xt[:, jj, :], AF.Square, [xt], [junk, ssq], accum=ssq[:, jj:jj + 1])
        k.TS("dve", rstd[:, 0:nt], ssq[:, 0:nt], 1.0 / D, EPS, ALU.mult, ALU.add, [ssq], [rstd])
        k.ACT(rstd[:, 0:nt], rstd[:, 0:nt], AF.Sqrt, [rstd], [rstd])
        P.op("dve", lambda e: e.reciprocal(out=rstd[:, 0:nt], in_=rstd[:, 0:nt]), [rstd], [rstd])
        for jj in range(nt):
            k.TS("pool", xn[:, jj, :], xt[:, jj, :], rstd[:, jj:jj + 1], None, ALU.mult, None, [xt, rstd], [xn])
        for kc in range(KD):
            pb = k.ps()
            pst = pb.t[:, :].bitcast(BF16)
            for jj in range(nt):
                k.TR(pst[:, jj * 128:(jj + 1) * 128], xn[:, jj, kc * 128:(kc + 1) * 128], ident[:], [xn, ident], [pb])
            k.TS("dve", hT[:, kc, 0:T], pst[:, 0:T], scsh[:, 2 * a, kc, j:j + 1], scsh[:, 2 * a + 1, kc, j:j + 1],
                 ALU.mult, ALU.add, [pb, scsh], [hT])

    for L in range(nlayers):
        last = (L == DEPTH - 1)
        es0 = ExitStack()
        wblk = P.sbuf("wblk", [128, KD, D], BF16, es0)
        rowt = P.sbuf("rowt", [1, 512], F32, es0)
        brow = P.sbuf("brow", [1, 6 * D], F32, es0)
        g1row = P.sbuf("g1row", [1, D], F32, es0)
        g3row = P.sbuf("g3row", [1, D], F32, es0)
        lamt = P.sbuf("lamt", [128, 2, 4], F32, es0)
        wstage = P.sbuf("wstage", [128, 2, 2, 4, 128], F32, es0)
        k.LD(bmodfm[:], bmod_fm.ap()[L], [bmodfm])
        k.LD(gfm[:], normg_fm.ap()[L], [gfm])
        k.LD(brow[:], bmod_row.ap()[L], [brow])
        k.LD(g1row[:], normg_row.ap()[L, 1:2, :], [g1row])
        k.LD(g3row[:], normg_row.ap()[L, 3:4, :], [g3row])
        k.LD(gnv[:], gn_fm.ap()[L], [gnv])
        k.LD(cwv[:], convw_fm.ap()[L], [cwv])
        k.LD(cbv[:], convb_fm.ap()[L], [cbv])
        k.LD(bav[:], ba_fm.ap()[L], [bav])
        k.LD(bxv[:], bx_fm.ap()[L], [bxv])
        k.LD(lamt[:], lam_fm.ap()[L], [lamt])
        k.ACT(c1v[:], lamt[:], AF.Exp, [lamt], [c1v], scale=-1.0)
        k.ACT(c1v[:], c1v[:], AF.Ln, [c1v], [c1v], bias=1.0)
        k.TS("dve", c1v[:], c1v[:], -8.0, None, ALU.mult, None, [c1v], [c1v])
        if L == 0:
            k.MEMSET("pool", lbv[:], 0.0, [lbv])
        else:
            k.TT("dve", lbv[:], lbt[:, 1], lbt[:, 0], ALU.subtract, [lbt], [lbv])
            k.ACT(lbv[:], lbv[:], AF.Sigmoid, [lbv], [lbv])
        k.TS("dve", omlv[:], lbv[:], -1.0, 1.0, ALU.mult, ALU.add, [lbv], [omlv])
        k.MEMSET("pool", wstage[:], 0.0, [wstage])
        for gi, wsrc in enumerate((rg_w_a, rg_w_x)):
            for dr in range(2):
                for half in range(2):
                    src = wsrc.ap()[L, dr].rearrange("(ct h) i j -> h i ct j", h=2)[half]
                    k.LD(wstage[half * 64:(half + 1) * 64, gi, dr, :, half * 64:(half + 1) * 64], src, [wstage])
        k.CP("dve", wbd[:], wstage[:], [wstage], [wbd])
        for n in range(6):
            k.LD(wblk[:], w_mod.ap()[L, :, n * D:(n + 1) * D].rearrange("(kc p) c -> p kc c", p=128), [wblk], q="pool")
            pb = k.ps()
            for kd in range(KD):
                for kc in range(KD):
                    k.MM(pb.t[:, kd * 2:kd * 2 + 2], wblk[:, kc, kd * 128:(kd + 1) * 128], scbf[:, kc, :],
                         kc == 0, kc == KD - 1, [wblk, scbf], [pb])
            k.TT("dve", modfm[:, n * 8:(n + 1) * 8, :], pb.t[:, 0:16].rearrange("p (a b) -> p a b", b=2),
                 bmodfm[:, n * 8:(n + 1) * 8].unsqueeze(2).to_broadcast([128, 8, 2]), ALU.add, [pb, bmodfm], [modfm])
            if n in (2, 5):
                a = 0 if n == 2 else 1
                grow_g = g1row if n == 2 else g3row
                for j in range(2):
                    for half in range(2):
                        pb2 = k.ps()
                        for kc in range(KD):
                            k.MM(pb2.t[0:1, 0:512], scbf[:, kc, j:j + 1], wblk[:, kc, half * 512:(half + 1) * 512],
                                 kc == 0, kc == KD - 1, [wblk, scbf], [pb2])
                        k.TT("dve", rowt[:], pb2.t[0:1, 0:512], brow[0:1, n * D + half * 512:n * D + (half + 1) * 512],
                             ALU.add, [pb2, brow], [rowt])
                        k.TT("dve", rowt[:], rowt[:], grow_g[0:1, half * 512:(half + 1) * 512], ALU.mult,
                             [rowt, grow_g], [rowt])
                        pb3 = k.ps()
                        k.MM(pb3.t[:, 0:512], ones_row[0:1, :], rowt[0:1, :], True, True, [ones_row, rowt], [pb3])
                        k.CP("act", grow[a][j][:, half * 512:(half + 1) * 512], pb3.t[:, 0:512], [pb3], [grow[a][j]])
        for a, (nsc, nsh, gi) in enumerate(((1, 0, 0), (4, 3, 2))):
            k.TS("dve", scsh[:, 2 * a], modfm[:, nsc * 8:(nsc + 1) * 8, :], 1.0, None, ALU.add, None, [modfm], [scsh])
            k.TT("dve", scsh[:, 2 * a], scsh[:, 2 * a], gfm[:, gi * 8:(gi + 1) * 8].unsqueeze(2).to_broadcast([128, 8, 2]),
                 ALU.mult, [scsh, gfm], [scsh])
            k.CP("dve", scsh[:, 2 * a + 1], modfm[:, nsh * 8:(nsh + 1) * 8, :], [modfm], [scsh])
        P.end_phase()
        es0.close()
        if STAGE < 1:
            break

        es1 = ExitStack()
        winb = P.sbuf("winb", [128, KD, INC], BF16, es1)
        for kc in range(KD):
            k.LD(winb[:, kc, :], w_in.ap()[L, kc * 128:(kc + 1) * 128, :], [winb], q="pool")
        xts = [P.sbuf("xt%d" % i, [128, 4, D], F32, es1) for i in range(1)]
        hTs = [P.sbuf("hT%d" % i, [128, KD, 512], BF16, es1) for i in range(2)]
        qs = P.sbuf("qs", [128, 4, 512], BF16, es1)
        sg = P.sbuf("sg", [128, 512], F32, es1)
        ft = P.sbuf("ft", [128, 2, 4, 512], F32, es1)
        uf = P.sbuf("uf", [128, 512], F32, es1)
        uc = P.sbuf("uc", [128, 512], F32, es1)
        ub = P.sbuf("ub", [128, 4, 512], BF16, es1)
        gb = P.sbuf("gb", [128, 4, 512], BF16, es1)
        vt = P.sbuf("vt", [64, 8, HGW], BF16, es1)
        gt = P.sbuf("gt", [64, 8, HGW], BF16, es1)
        nb1 = norm_bufs(es1, "p1", 4)
        for ui, (tok0, T, isctx) in enumerate(units):
            nt = T // 128
            nch = T // CH
            j = 1 if isctx else 0
            xt = xts[0]
            hT = hTs[ui % 2]
            src, srcbuf = x_src(L, tok0, T)
            k.LD(xt[:, 0:nt, :], src.rearrange("(j p) d -> p j d", p=128), [xt], R=[srcbuf])
            norm_mod(nb1, xt, nt, 0, j, hT)
            for ct in range(20):
                if ct < 12:
                    c0 = ct * 128
                else:
                    c0 = 5 * HGW + (ct - 12) * 128
                pb = k.ps()
                for kc in range(KD):
                    k.MM(pb.t[:, 0:T], winb[:, kc, c0:c0 + 128], hT[:, kc, 0:T], kc == 0, kc == KD - 1, [winb, hT], [pb])
                if ct < 4:
                    k.ACT(qs[:, ct, 0:T], pb.t[:, 0:T], AF.Silu, [pb], [qs])
                elif ct < 12:
                    dr, h = (ct - 4) // 4, (ct - 4) % 4
                    k.ACT(sg[:, 0:T], pb.t[:, 0:T], AF.Sigmoid, [pb], [sg])
                    k.TS("dve", ft[:, dr, h, 0:T], sg[:, 0:T], omlv[:, dr, h:h + 1], lbv[:, dr, h:h + 1], ALU.mult, ALU.add,
                         [sg, omlv, lbv], [ft])
                elif ct < 16:
                    c = ct - 12
                    k.CP("act", uf[:, 0:T], pb.t[:, 0:T], [pb], [uf])
                    RW = T if isctx else 64
                    ufv = uf[:, 0:T].rearrange("p (r w) -> p r w", w=RW)
                    ucv = uc[:, 0:T].rearrange("p (r w) -> p r w", w=RW)
                    k.TS("dve", uc[:, 0:T], uf[:, 0:T], cwv[:, c, 2:3], cbv[:, c:c + 1], ALU.mult, ALU.add, [uf, cwv, cbv], [uc])
                    for tap in (0, 1, 3):
                        s = tap - 2
                        if s < 0:
                            o_sl = ucv[:, :, -s:RW]
                            i_sl = ufv[:, :, 0:RW + s]
                        else:
                            o_sl = ucv[:, :, 0:RW - s]
                            i_sl = ufv[:, :, s:RW]
                        k.STT(o_sl, i_sl, cwv[:, c, tap:tap + 1], o_sl, ALU.mult, ALU.add, [uf, uc, cwv], [uc])
                    k.CP("pool", ub[:, c, 0:T], uc[:, 0:T], [uc], [ub])
                else:
                    c = ct - 16
                    k.ACT(gb[:, c, 0:T], pb.t[:, 0:T], AF.Gelu_apprx_tanh, [pb], [gb])
            for cc in range(nch):
                for which in range(2):
                    c0 = (3 + which) * HGW
                    pb = k.ps()
                    for kc in range(KD):
                        k.MM(pb.t[0:CH, 0:512], hT[:, kc, cc * CH:(cc + 1) * CH], winb[:, kc, c0:c0 + 512],
                             kc == 0, kc == KD - 1, [winb, hT], [pb])
                    if which == 0:
                        k.CP("act", vt[:, cc, :], pb.t[0:CH, 0:512], [pb], [vt])
                    else:
                        k.ACT(gt[:, cc, :], pb.t[0:CH, 0:512], AF.Silu, [pb], [gt])
            k.ST(s_qT.t[:, :, tok0:tok0 + T].rearrange("h p t -> p h t"), qs[:, :, 0:T], [qs], [s_qT])
            for dr in range(2):
                k.ST(s_fT.t[dr, :, :, tok0:tok0 + T].rearrange("h p t -> p h t"), ft[:, dr, :, 0:T], [ft], [s_fT])
            k.ST(s_u.t[:, :, tok0:tok0 + T].rearrange("h p t -> p h t"), ub[:, :, 0:T], [ub], [s_u])
            k.ST(s_gate.t[:, :, tok0:tok0 + T].rearrange("h p t -> p h t"), gb[:, :, 0:T], [gb], [s_gate])
            k.ST(s_v.t[tok0:tok0 + T, :].rearrange("(c p) n -> p c n", p=CH), vt[:, 0:nch, :], [vt], [s_v])
            k.ST(s_g.t[tok0:tok0 + T, :].rearrange("(c p) n -> p c n", p=CH), gt[:, 0:nch, :], [gt], [s_g])
        P.end_phase()
        es1.close()
        if STAGE < 2:
            break


        esm = ExitStack()
        Sst = P.sbuf("Sst", [128, 2, 4, 128], F32, esm)
        Sctx = P.sbuf("Sctx", [128, 2, 4, 128], F32, esm)
        Sin = P.sbuf("Sin", [128, 2, 4, 128], F32, esm)
        Sinb = P.sbuf("Sinb", [128, 2, 4, 128], BF16, esm)
        es2 = ExitStack()
        ut = P.sbuf("ut", [128, 4, 512], BF16, es2)
        rt = P.sbuf("rt", [128, 4, 512], F32, es2)
        it = P.sbuf("it", [128, 4, 512], F32, es2)
        at = P.sbuf("at", [128, 4, 512], F32, es2)
        a2t = P.sbuf("a2t", [128, 4, 512], F32, es2)
        bxt = P.sbuf("bxt", [128, 4, 512], F32, es2)
        hl = P.sbuf("hl", [128, 4, 512], F32, es2)
        ac = P.sbuf("ac", [128, 4, 512], F32, es2)
        hs = P.sbuf("hs", [128, 4, 512], F32, es2)
        car_h = P.sbuf("car_h", [128, 4], F32, es2)
        car_a = P.sbuf("car_a", [128, 4], F32, es2)
        for dr in range(2):
            order = [units[0]] + (units[1:] if dr == 0 else units[1:][::-1])
            for ui, (tok0, T, isctx) in enumerate(order):
                fresh = (ui == 0) if CHAIN else (ui <= 1)
                k.LD(ut[:, :, 0:T], s_u.t[:, :, tok0:tok0 + T].rearrange("c p t -> p c t"), [ut], R=[s_u])
                if dr == 1:
                    k.LD(hs[:, :, 0:T], s_hsumf.t[:, :, tok0:tok0 + T].rearrange("c p t -> p c t"), [hs], R=[s_hsumf])
                pbs = []
                for c in range(4):
                    for gi in range(2):
                        pb = k.ps()
                        k.MM(pb.t[:, 0:T], wbd[:, gi, dr, c, :], ut[:, c, 0:T], True, True, [wbd, ut], [pb])
                        pbs.append(pb)
                for c in range(4):
                    k.ACT(rt[:, c, 0:T], pbs[2 * c].t[:, 0:T], AF.Sigmoid, [pbs[2 * c], bav], [rt], bias=bav[:, dr, c:c + 1])
                    k.ACT(it[:, c, 0:T], pbs[2 * c + 1].t[:, 0:T], AF.Sigmoid, [pbs[2 * c + 1], bxv], [it], bias=bxv[:, dr, c:c + 1])
                for c in range(4):
                    k.ACT(at[:, c, 0:T], rt[:, c, 0:T], AF.Exp, [rt, c1v], [at], scale=c1v[:, dr, c:c + 1])
                k.TT("pool", a2t[:, :, 0:T], at[:, :, 0:T], at[:, :, 0:T], ALU.mult, [at], [a2t])
                k.ACT(a2t[:, :, 0:T], a2t[:, :, 0:T], AF.Sqrt, [a2t], [a2t], scale=-1.0, bias=1.0)
                k.TT("pool", it[:, :, 0:T], it[:, :, 0:T], ut[:, :, 0:T], ALU.mult, [it, ut], [it])
                k.TT("dve", bxt[:, :, 0:T], it[:, :, 0:T], a2t[:, :, 0:T], ALU.mult, [it, a2t], [bxt])
                for c in range(4):
                    ih = 0.0 if fresh else car_h[:, c:c + 1]
                    ia = 1.0 if fresh else car_a[:, c:c + 1]
                    if dr == 0:
                        k.SCAN(hl[:, c, 0:T], at[:, c, 0:T], bxt[:, c, 0:T], ih, [at, bxt, car_h], [hl])
                        if not CHAIN:
                            k.SCAN(ac[:, c, 0:T], at[:, c, 0:T], zeros_t[:, 0:T], ia, [at, zeros_t, car_a], [ac])
                        lastcol = slice(T - 1, T)
                    else:
                        k.SCAN(hl[:, c, T - 1::-1] if False else hl[:, c, 0:T][:, ::-1], at[:, c, 0:T][:, ::-1], bxt[:, c, 0:T][:, ::-1], ih,
                               [at, bxt, car_h], [hl])
                        if not CHAIN:
                            k.SCAN(ac[:, c, 0:T][:, ::-1], at[:, c, 0:T][:, ::-1], zeros_t[:, 0:T], ia, [at, zeros_t, car_a], [ac])
                        lastcol = slice(0, 1)
                    if isctx:
                        k.CP("pool", hctx[:, dr, c:c + 1], hl[:, c, lastcol], [hl], [hctx])
                    if CHAIN or not isctx:
                        k.CP("pool", car_h[:, c:c + 1], hl[:, c, lastcol], [hl], [car_h])
                        if not CHAIN:
                            k.CP("pool", car_a[:, c:c + 1], ac[:, c, lastcol], [ac], [car_a])
                if dr == 0:
                    k.ST(s_hsumf.t[:, :, tok0:tok0 + T].rearrange("c p t -> p c t"), hl[:, :, 0:T], [hl], [s_hsumf])
                else:
                    k.TT("pool", hl[:, :, 0:T], hl[:, :, 0:T], hs[:, :, 0:T], ALU.add, [hl, hs], [hl])
                    k.ST(s_hsum.t[:, :, tok0:tok0 + T].rearrange("c p t -> p c t"), hl[:, :, 0:T], [hl], [s_hsum])
                if not CHAIN:
                    k.ST(s_acum.t[dr, :, :, tok0:tok0 + T].rearrange("c p t -> p c t"), ac[:, :, 0:T], [ac], [s_acum])
            if not CHAIN:
                k.CP("pool", hfin[:, dr, :], car_h[:], [car_h], [hfin])
                k.CP("pool", atot[:, dr, :], car_a[:], [car_a], [atot])
        P.end_phase()
        es2.close()
        if STAGE < 3:
            break

        es3 = ExitStack()
        fTt = P.sbuf("fTt", [128, 4, 512], F32, es3)
        qTt = P.sbuf("qTt", [128, 4, 512], BF16, es3)
        vtt = P.sbuf("vtt", [64, 8, HGW], BF16, es3)
        lnf = P.sbuf("lnf", [128, 4, 512], F32, es3)
        kTt = P.sbuf("kTt", [128, 4, 512], F32, es3)
        Ct = P.sbuf("Ct", [128, 4, 512], F32, es3)
        crel = P.sbuf("crel", [128, 4, 512], F32, es3)
        crel2 = P.sbuf("crel2", [128, 4, 512], F32, es3)
        e1 = P.sbuf("e1", [128, 4, 512], F32, es3)
        e2 = P.sbuf("e2", [128, 4, 512], F32, es3)
        e3 = P.sbuf("e3", [128, 4, 512], F32, es3)
        qtl = P.sbuf("qtl", [128, 4, 512], BF16, es3)
        ktl = P.sbuf("ktl", [128, 4, 512], BF16, es3)
        khl = P.sbuf("khl", [128, 4, 512], BF16, es3)
        qsg = P.sbuf("qsg", [128, 4, 512], BF16, es3)
        khT = P.sbuf("khT", [64, 4, 8, 128], BF16, es3)
        scTs = [P.sbuf("scT%d" % i, [64, 4, 512], BF16, es3) for i in range(2)]
        for i in range(2):
            k.MEMSET("pool", scTs[i][:], 0.0, [scTs[i]])
        ot = P.sbuf("ot", [64, 8, HGW], F32, es3)
        oft = P.sbuf("oft", [64, 8, HGW], F32, es3)
        Spb = [P.sbuf("Spb%d" % i, [128, 4, 128], BF16, es3) for i in range(2)]
        carC = P.sbuf("carC", [128, 4], F32, es3)
        cprev = P.sbuf("cprev", [128, 4, 8], F32, es3)
        dif = P.sbuf("dif", [128, 4, 2, 8], F32, es3)
        BD = P.sbuf("BD", [128, 4, 2, 8], F32, es3)
        for dr in range(2):
            order = [units[0]] + (units[1:] if dr == 0 else units[1:][::-1])
            scT = scTs[dr]
            k.MEMSET("pool", Sst[:, dr], 0.0, [Sst])
            for ui, (tok0, T, isctx) in enumerate(order):
                fresh = (ui == 0) if CHAIN else (ui <= 1)
                nch = T // CH
                k.LD(fTt[:, :, 0:T], s_fT.t[dr, :, :, tok0:tok0 + T].rearrange("h p t -> p h t"), [fTt], R=[s_fT])
                k.LD(qTt[:, :, 0:T], s_qT.t[:, :, tok0:tok0 + T].rearrange("h p t -> p h t"), [qTt], R=[s_qT])
                k.LD(vtt[:, 0:nch, :], s_v.t[tok0:tok0 + T, :].rearrange("(c p) n -> p c n", p=CH), [vtt], R=[s_v])
                if dr == 1:
                    k.LD(oft[:, 0:nch, :], s_of.t[tok0:tok0 + T, :].rearrange("(c p) n -> p c n", p=CH), [oft], R=[s_of])
                k.ACT(lnf[:, :, 0:T], fTt[:, :, 0:T], AF.Ln, [fTt], [lnf])
                k.TS("pool", kTt[:, :, 0:T], fTt[:, :, 0:T], -1.0, 1.0, ALU.mult, ALU.add, [fTt], [kTt])
                C4 = Ct[:, :, 0:T].rearrange("p h (n w) -> p h n w", w=CH)
                if dr == 0:
                    if fresh:
                        k.MEMSET("pool", cprev[:, :, 0:1], 0.0, [cprev])
                    else:
                        k.CP("pool", cprev[:, :, 0:1], carC[:].unsqueeze(2), [carC], [cprev])
                else:
                    if fresh:
                        k.MEMSET("pool", cprev[:, :, nch - 1:nch], 0.0, [cprev])
                    else:
                        k.CP("pool", cprev[:, :, nch - 1:nch], carC[:].unsqueeze(2), [carC], [cprev])
                for h in range(4):
                    init = 0.0 if fresh else carC[:, h:h + 1]
                    if dr == 0:
                        k.SCAN(Ct[:, h, 0:T], ones_t[:, 0:T], lnf[:, h, 0:T], init, [ones_t, lnf, carC], [Ct])
                    else:
                        k.SCAN(Ct[:, h, 0:T][:, ::-1], ones_t[:, 0:T], lnf[:, h, 0:T][:, ::-1], init, [ones_t, lnf, carC], [Ct])
                if dr == 0:
                    k.CP("pool", carC[:].unsqueeze(2), Ct[:, :, T - 1:T], [Ct], [carC])
                    Aanc = C4[:, :, :, 31]
                    Cend = C4[:, :, :, 63]
                    if nch > 1:
                        k.CP("pool", cprev[:, :, 1:nch], C4[:, :, 0:nch - 1, 63], [Ct], [cprev])
                else:
                    k.CP("pool", carC[:].unsqueeze(2), Ct[:, :, 0:1], [Ct], [carC])
                    Aanc = C4[:, :, :, 32]
                    Cend = C4[:, :, :, 0]
                    if nch > 1:
                        k.CP("pool", cprev[:, :, 0:nch - 1], C4[:, :, 1:nch, 0], [Ct], [cprev])
                k.TT("dve", dif[:, :, 0, 0:nch], Aanc, cprev[:, :, 0:nch], ALU.subtract, [Ct, cprev], [dif])
                k.TT("dve", dif[:, :, 1, 0:nch], Cend, cprev[:, :, 0:nch], ALU.subtract, [Ct, cprev], [dif])
                k.ACT(BD[:, :, :, 0:nch], dif[:, :, :, 0:nch], AF.Exp, [dif], [BD])
                cr4 = crel[:, :, 0:T].rearrange("p h (n w) -> p h n w", w=CH)
                cr24 = crel2[:, :, 0:T].rearrange("p h (n w) -> p h n w", w=CH)
                k.TT("dve", cr4, C4, Aanc.unsqueeze(3).to_broadcast([128, 4, nch, CH]), ALU.subtract, [Ct], [crel])
                k.TT("dve", cr24, C4, Cend.unsqueeze(3).to_broadcast([128, 4, nch, CH]), ALU.subtract, [Ct], [crel2])
                k.ACT(e1[:, :, 0:T], crel[:, :, 0:T], AF.Exp, [crel], [e1])
                k.ACT(e2[:, :, 0:T], crel[:, :, 0:T], AF.Exp, [crel], [e2], scale=-1.0)
                k.ACT(e3[:, :, 0:T], crel2[:, :, 0:T], AF.Exp, [crel2], [e3], scale=-1.0)
                k.TT("pool", qtl[:, :, 0:T], qTt[:, :, 0:T], e1[:, :, 0:T], ALU.mult, [qTt, e1], [qtl])
                k.TT("dve", ktl[:, :, 0:T], kTt[:, :, 0:T], e2[:, :, 0:T], ALU.mult, [kTt, e2], [ktl])
                k.TT("pool", khl[:, :, 0:T], kTt[:, :, 0:T], e3[:, :, 0:T], ALU.mult, [kTt, e3], [khl])
                if not isctx and not CHAIN:
                    k.ACT(e1[:, :, 0:T], Ct[:, :, 0:T], AF.Exp, [Ct, qtl], [e1])
                    k.TT("pool", qsg[:, :, 0:T], qTt[:, :, 0:T], e1[:, :, 0:T], ALU.mult, [qTt, e1], [qsg])
                    k.ST(s_qseg.t[dr, :, :, tok0:tok0 + T].rearrange("h p t -> p h t"), qsg[:, :, 0:T], [qsg], [s_qseg])
                for h in range(4):
                    pbT = k.ps()
                    pT = pbT.t[:, :].bitcast(BF16)
                    for n in range(nch):
                        k.TR(pT[0:CH, n * 128:(n + 1) * 128], khl[:, h, n * CH:(n + 1) * CH], ident[:], [khl, ident], [pbT])
                    k.CP("act", khT[:, h, 0:nch, :], pT[0:CH, 0:nch * 128].rearrange("p (n k) -> p n k", k=128), [pbT], [khT])
                    pbS = k.ps()
                    for n in range(nch):
                        k.MM(pbS.t[0:CH, n * CH:(n + 1) * CH], ktl[:, h, n * CH:(n + 1) * CH], qtl[:, h, n * CH:(n + 1) * CH],
                             True, True, [ktl, qtl], [pbS])
                    P.op("dve", (lambda sc_, m_, p_: (lambda e: e.copy_predicated(sc_, m_, p_)))(scT[:, h, 0:T], maski[:, dr, 0:T], pbS.t[0:CH, 0:T]),
                         [pbS, maski], [scT])
                chunks = list(range(nch)) if dr == 0 else list(range(nch))[::-1]
                for ci, n in enumerate(chunks):
                    sp = Spb[ci % 2]
                    for h in range(4):
                        k.TS("pool", sp[:, h, :], Sst[:, dr, h, :], BD[:, h, 0, n:n + 1], None, ALU.mult, None, [Sst, BD], [sp])
                    po = k.ps()
                    pk = k.ps()
                    for h in range(4):
                        hs_ = slice(h * 128, (h + 1) * 128)
                        k.MM(po.t[0:CH, hs_], scT[:, h, n * CH:(n + 1) * CH], vtt[:, n, hs_], True, False, [scT, vtt], [po])
                        k.MM(po.t[0:CH, hs_], qtl[:, h, n * CH:(n + 1) * CH], sp[:, h, :], False, True, [qtl, sp], [po])
                        k.MM(pk.t[:, hs_], khT[:, h, n, :], vtt[:, n, hs_], True, True, [khT, vtt], [pk])
                    for h in range(4):
                        hs_ = slice(h * 128, (h + 1) * 128)
                        k.STT(Sst[:, dr, h, :], Sst[:, dr, h, :], BD[:, h, 1, n:n + 1], pk.t[:, hs_], ALU.mult, ALU.add,
                              [Sst, BD, pk], [Sst])
                    if dr == 0:
                        k.CP("act", ot[:, n, :], po.t[0:CH, 0:512], [po], [ot])
                    else:
                        k.TT("dve", ot[:, n, :], po.t[0:CH, 0:512], oft[:, n, :], ALU.add, [po, oft], [ot])
                dst = s_of if dr == 0 else s_osum
                k.ST(dst.t[tok0:tok0 + T, :].rearrange("(c p) n -> p c n", p=CH), ot[:, 0:nch, :], [ot], [dst])
                if isctx and not CHAIN:
                    k.CP("pool", Sctx[:, dr], Sst[:, dr], [Sst], [Sctx])
                    k.MEMSET("pool", Sst[:, dr], 0.0, [Sst])
            if not CHAIN:
                k.ACT(dtot[:, dr, :], carC[:], AF.Exp, [carC], [dtot])
        P.end_phase()
        es3.close()
        if STAGE < 4:
            break


        if not CHAIN:
            esx = ExitStack()
            xs = P.sbuf("xs", [128, XW], F32, esx)
            xg = P.sbuf("xg", [128, 8, XW], F32, esx)
            dm1 = P.sbuf("dm1", [128, 8, 16], F32, esx)
            tS = P.sbuf("tS", [128, 128], F32, esx)
            tH = P.sbuf("tH", [128, 4], F32, esx)
            k.CP("pool", xs[:, 0:1024], Sst[:].rearrange("p a b c -> p (a b c)"), [Sst], [xs])
            k.CP("pool", xs[:, 1024:1032], dtot[:].rearrange("p a b -> p (a b)"), [dtot], [xs])
            k.CP("pool", xs[:, 1032:1040], hfin[:].rearrange("p a b -> p (a b)"), [hfin], [xs])
            k.CP("pool", xs[:, 1040:1048], atot[:].rearrange("p a b -> p (a b)"), [atot], [xs])
            k.ST(s_xsrc.t.ap(), xs[:], [xs], [s_xsrc])
            if USE_CC:
                P.custom("pool", (lambda a_, b_: (lambda e: e.collective_compute("AllGather", ALU.bypass, replica_groups=[list(range(8))],
                                                                               ins=[a_], outs=[b_])))(s_xsrc.t.ap().opt(), s_xdst.t.ap().opt()),
                         reads=[s_xsrc], writes=[s_xdst], inc=1)
            else:
                for r in range(8):
                    k.ST(s_xdst.t.ap()[r * 128:(r + 1) * 128, :], s_xsrc.t.ap(), [s_xsrc], [s_xdst])
            k.LD(xg[:], s_xdst.t.ap().rearrange("(r p) c -> p r c", p=128), [xg], R=[s_xdst])
            k.TS("dve", dm1[:, :, 0:8], xg[:, :, 1024:1032], -1.0, None, ALU.add, None, [xg], [dm1])
            k.TS("dve", dm1[:, :, 8:16], xg[:, :, 1040:1048], -1.0, None, ALU.add, None, [xg], [dm1])
            k.CP("pool", Sin[:], Sctx[:], [Sctx], [Sin])
            k.CP("pool", hin[:], hctx[:], [hctx], [hin])
            for dr in range(2):
                ranks = list(range(0, 7)) if dr == 0 else list(range(7, 0, -1))
                for i in ranks:
                    fl = flags[:, dr * 8 + i:dr * 8 + i + 1]
                    for h in range(4):
                        c0 = (dr * 4 + h) * 128
                        k.STT(tS[:], Sin[:, dr, h, :], dm1[:, i, dr * 4 + h:dr * 4 + h + 1], xg[:, i, c0:c0 + 128], ALU.mult, ALU.add,
                              [Sin, dm1, xg], [tS])
                        k.STT(Sin[:, dr, h, :], tS[:], fl, Sin[:, dr, h, :], ALU.mult, ALU.add, [tS, flags, Sin], [Sin])
                    k.TT("dve", tH[:], hin[:, dr, :], dm1[:, i, 8 + dr * 4:8 + dr * 4 + 4], ALU.mult, [hin, dm1], [tH])
                    k.TT("dve", tH[:], tH[:], xg[:, i, 1032 + dr * 4:1032 + dr * 4 + 4], ALU.add, [tH, xg], [tH])
                    k.STT(hin[:, dr, :], tH[:], fl, hin[:, dr, :], ALU.mult, ALU.add, [tH, flags, hin], [hin])
            k.CP("dve", Sinb[:], Sin[:], [Sin], [Sinb])
            if DEBUG:
                k.ST(dbg["sst"].t.ap(), Sst[:].rearrange("p a b c -> p (a b c)"), [Sst], [dbg["sst"]])
                k.ST(dbg["sctx"].t.ap(), Sctx[:].rearrange("p a b c -> p (a b c)"), [Sctx], [dbg["sctx"]])
                k.ST(dbg["sin"].t.ap(), Sin[:].rearrange("p a b c -> p (a b c)"), [Sin], [dbg["sin"]])
            P.end_phase()
            esx.close()
        if STAGE < 5:
            break

        es4 = ExitStack()
        wob = P.sbuf("wob", [128, KD, D], BF16, es4)
        wst = P.sbuf("wst", [128, D], F32, es4)
        for kc in range(KD):
            if kc < 4:
                k.LD(wst[:], w_out.ap()[L, kc * 128:(kc + 1) * 128, :], [wst])
                k.TS("dve", wob[:, kc, :], wst[:], gnv[:, 0:1], None, ALU.mult, None, [wst, gnv], [wob])
            else:
                k.LD(wob[:, kc, :], w_out.ap()[L, kc * 128:(kc + 1) * 128, :], [wob], q="pool")
        osm = P.sbuf("osm", [64, 8, HGW], F32, es4)
        gtt = P.sbuf("gtt", [64, 8, HGW], BF16, es4)
        qsf = P.sbuf("qsf", [128, 4, 512], BF16, es4)
        qsb = P.sbuf("qsb", [128, 4, 512], BF16, es4)
        hst = P.sbuf("hst", [128, 4, 512], F32, es4)
        acf = P.sbuf("acf", [128, 4, 512], F32, es4)
        acb = P.sbuf("acb", [128, 4, 512], F32, es4)
        gat = P.sbuf("gat", [128, 4, 512], BF16, es4)
        xt3 = P.sbuf("xt3", [128, 4, D], F32, es4)
        ot3 = P.sbuf("ot3", [64, 8, HGW], F32, es4)
        mixb = P.sbuf("mixb", [64, 8, HGW], BF16, es4)
        mixT = P.sbuf("mixT", [128, KD, 512], BF16, es4)
        tA = P.sbuf("tA", [128, 512], F32, es4)
        tmp3 = P.sbuf("tmp3", [128, 512], F32, es4)
        junk3 = P.sbuf("junk3", [128, 512], BF16, es4)
        ssh = P.sbuf("ssh", [64, 32], F32, es4)
        ss2 = P.sbuf("ss2", [128, 4], F32, es4)
        for ui, (tok0, T, isctx) in enumerate(units):
            if isctx and last:
                continue
            nt = T // 128
            nch = T // CH
            j = 1 if isctx else 0
            k.LD(osm[:, 0:nch, :], s_osum.t[tok0:tok0 + T, :].rearrange("(c p) n -> p c n", p=CH), [osm], R=[s_osum])
            k.LD(gtt[:, 0:nch, :], s_g.t[tok0:tok0 + T, :].rearrange("(c p) n -> p c n", p=CH), [gtt], R=[s_g])
            k.LD(hst[:, :, 0:T], s_hsum.t[:, :, tok0:tok0 + T].rearrange("c p t -> p c t"), [hst], R=[s_hsum])
            k.LD(gat[:, :, 0:T], s_gate.t[:, :, tok0:tok0 + T].rearrange("c p t -> p c t"), [gat], R=[s_gate])
            fix = (not isctx) and (not CHAIN)
            if fix:
                k.LD(qsf[:, :, 0:T], s_qseg.t[0, :, :, tok0:tok0 + T].rearrange("h p t -> p h t"), [qsf], R=[s_qseg])
                k.LD(qsb[:, :, 0:T], s_qseg.t[1, :, :, tok0:tok0 + T].rearrange("h p t -> p h t"), [qsb], R=[s_qseg])
                k.LD(acf[:, :, 0:T], s_acum.t[0, :, :, tok0:tok0 + T].rearrange("c p t -> p c t"), [acf], R=[s_acum])
                k.LD(acb[:, :, 0:T], s_acum.t[1, :, :, tok0:tok0 + T].rearrange("c p t -> p c t"), [acb], R=[s_acum])
            src, srcbuf = x_src(L, tok0, T)
            k.LD(xt3[:, 0:nt, :], src.rearrange("(j p) d -> p j d", p=128), [xt3], R=[srcbuf])
            for n in range(nch):
                if fix:
                    pf = k.ps()
                    for h in range(4):
                        hs_ = slice(h * 128, (h + 1) * 128)
                        k.MM(pf.t[0:CH, hs_], qsf[:, h, n * CH:(n + 1) * CH], Sinb[:, 0, h, :], True, False, [qsf, Sinb], [pf])
                        k.MM(pf.t[0:CH, hs_], qsb[:, h, n * CH:(n + 1) * CH], Sinb[:, 1, h, :], False, True, [qsb, Sinb], [pf])
                    k.TT("dve", ot3[:, n, :], pf.t[0:CH, 0:512], osm[:, n, :], ALU.add, [pf, osm], [ot3])
                else:
                    k.CP("pool", ot3[:, n, :], osm[:, n, :], [osm], [ot3])
            ov = ot3[:, 0:nch, :].rearrange("p n (h v) -> p (n h) v", v=128)
            sq = osm[:, 0:nch, :].rearrange("p n (h v) -> p (n h) v", v=128)
            k.TT("pool", sq, ov, ov, ALU.mult, [ot3], [osm])
            P.op("dve", (lambda o_, i_: (lambda e: e.tensor_reduce(out=o_, in_=i_, axis=AX.X, op=ALU.add)))(ssh[:, 0:nch * 4], sq), [osm], [ssh])
            k.TS("dve", ssh[:, 0:nch * 4], ssh[:, 0:nch * 4], 1.0 / 128, EPS, ALU.mult, ALU.add, [ssh], [ssh])
            k.ACT(ssh[:, 0:nch * 4], ssh[:, 0:nch * 4], AF.Sqrt, [ssh], [ssh])
            P.op("dve", (lambda o_: (lambda e: e.reciprocal(out=o_, in_=o_)))(ssh[:, 0:nch * 4]), [ssh], [ssh])
            k.TT("dve", ov, ov, ssh[:, 0:nch * 4].unsqueeze(2).to_broadcast([CH, nch * 4, 128]), ALU.mult, [ot3, ssh], [ot3])
            k.TT("pool", mixb[:, 0:nch, :], ot3[:, 0:nch, :], gtt[:, 0:nch, :], ALU.mult, [ot3, gtt], [mixb])
            for h in range(4):
                pbT = k.ps()
                pT = pbT.t[:, :].bitcast(BF16)
                for n in range(nch):
                    k.TR(pT[:, n * CH:(n + 1) * CH], mixb[:, n, h * 128:(h + 1) * 128], ident[0:CH, 0:CH], [mixb, ident], [pbT])
                k.CP("act", mixT[:, h, 0:T], pT[:, 0:T], [pbT], [mixT])
            for c in range(4):
                if fix:
                    k.STT(tA[:, 0:T], acf[:, c, 0:T], hin[:, 0, c:c + 1], hst[:, c, 0:T], ALU.mult, ALU.add, [acf, hin, hst], [tA])
                    k.STT(tA[:, 0:T], acb[:, c, 0:T], hin[:, 1, c:c + 1], tA[:, 0:T], ALU.mult, ALU.add, [acb, hin, tA], [tA])
                    k.TT("pool", mixT[:, 4 + c, 0:T], tA[:, 0:T], gat[:, c, 0:T], ALU.mult, [tA, gat], [mixT])
                else:
                    k.TT("pool", mixT[:, 4 + c, 0:T], hst[:, c, 0:T], gat[:, c, 0:T], ALU.mult, [hst, gat], [mixT])
            for jj in range(nt):
                pps = [k.ps(), k.ps()]
                for half in range(2):
                    for kc in range(KD):
                        k.MM(pps[half].t[:, 0:512], mixT[:, kc, jj * 128:(jj + 1) * 128], wob[:, kc, half * 512:(half + 1) * 512],
                             kc == 0, kc == KD - 1, [mixT, wob], [pps[half]])
                    k.ACT(junk3[:], pps[half].t[:, 0:512], AF.Square, [pps[half]], [junk3, ss2], accum=ss2[:, half:half + 1])
                k.TT("dve", ss2[:, 2:3], ss2[:, 0:1], ss2[:, 1:2], ALU.add, [ss2], [ss2])
                k.TS("dve", ss2[:, 2:3], ss2[:, 2:3], 1.0 / D, EPS, ALU.mult, ALU.add, [ss2], [ss2])
                k.ACT(ss2[:, 2:3], ss2[:, 2:3], AF.Sqrt, [ss2], [ss2])
                P.op("dve", (lambda o_: (lambda e: e.reciprocal(out=o_, in_=o_)))(ss2[:, 2:3]), [ss2], [ss2])
                for half in range(2):
                    hsl = slice(half * 512, (half + 1) * 512)
                    k.STT(tmp3[:], pps[half].t[:, 0:512], ss2[:, 2:3], grow[0][j][:, hsl], ALU.mult, ALU.mult,
                          [pps[half], ss2, grow[0][j]], [tmp3])
                    k.TT("pool", xt3[:, jj, hsl], xt3[:, jj, hsl], tmp3[:], ALU.add, [xt3, tmp3], [xt3])
            k.ST(s_xmid.t[tok0:tok0 + T, :].rearrange("(j p) d -> p j d", p=128), xt3[:, 0:nt, :], [xt3], [s_xmid])
        P.end_phase()
        es4.close()
        esm.close()
        if STAGE < 6:
            break

        es5 = ExitStack()
        wgb = P.sbuf("wgb", [128, KD, DFF], BF16, es5)
        wub = P.sbuf("wub", [128, KD, DFF], BF16, es5)
        wdb = P.sbuf("wdb", [128, NFF, D], BF16, es5)
        for kc in range(KD):
            k.LD(wgb[:, kc, :], w_gate.ap()[L, kc * 128:(kc + 1) * 128, :], [wgb], q="pool")
            k.LD(wub[:, kc, :], w_up.ap()[L, kc * 128:(kc + 1) * 128, :], [wub], q="pool")
        for jf in range(NFF):
            k.LD(wdb[:, jf, :], w_down.ap()[L, jf * 128:(jf + 1) * 128, :], [wdb], q="pool")
        xt5 = P.sbuf("xt5", [128, 2, D], F32, es5)
        fT5 = P.sbuf("fT5", [128, KD, 256], BF16, es5)
        hid = P.sbuf("hid", [128, NFF, 256], BF16, es5)
        sl5 = P.sbuf("sl5", [128, 256], F32, es5)
        tmp5 = P.sbuf("tmp5", [128, 512], F32, es5)
        junk5 = P.sbuf("junk5", [128, 512], BF16, es5)
        ss5 = P.sbuf("ss5", [128, 4], F32, es5)
        nb5 = norm_bufs(es5, "p5", 2)
        for tok0 in range(0, NT, 256):
            isctx = tok0 < NCTX
            if isctx and last:
                continue
            j = 1 if isctx else 0
            T = 256
            k.LD(xt5[:], s_xmid.t[tok0:tok0 + T, :].rearrange("(j p) d -> p j d", p=128), [xt5], R=[s_xmid])
            norm_mod(nb5, xt5, 2, 1, j, fT5)
            for jf in range(NFF):
                pg = k.ps()
                pu = k.ps()
                for kc in range(KD):
                    k.MM(pg.t[:, 0:T], wgb[:, kc, jf * 128:(jf + 1) * 128], fT5[:, kc, 0:T], kc == 0, kc == KD - 1, [wgb, fT5], [pg])
                for kc in range(KD):
                    k.MM(pu.t[:, 0:T], wub[:, kc, jf * 128:(jf + 1) * 128], fT5[:, kc, 0:T], kc == 0, kc == KD - 1, [wub, fT5], [pu])
                k.ACT(sl5[:], pg.t[:, 0:T], AF.Silu, [pg], [sl5])
                k.TT("dve", hid[:, jf, :], sl5[:], pu.t[:, 0:T], ALU.mult, [sl5, pu], [hid])
            for jj in range(2):
                pps = [k.ps(), k.ps()]
                for half in range(2):
                    for jf in range(NFF):
                        k.MM(pps[half].t[:, 0:512], hid[:, jf, jj * 128:(jj + 1) * 128], wdb[:, jf, half * 512:(half + 1) * 512],
                             jf == 0, jf == NFF - 1, [hid, wdb], [pps[half]])
                    k.ACT(junk5[:], pps[half].t[:, 0:512], AF.Square, [pps[half]], [junk5, ss5], accum=ss5[:, half:half + 1])
                k.TT("dve", ss5[:, 2:3], ss5[:, 0:1], ss5[:, 1:2], ALU.add, [ss5], [ss5])
                k.TS("dve", ss5[:, 2:3], ss5[:, 2:3], 1.0 / D, EPS, ALU.mult, ALU.add, [ss5], [ss5])
                k.ACT(ss5[:, 2:3], ss5[:, 2:3], AF.Sqrt, [ss5], [ss5])
                P.op("dve", (lambda o_: (lambda e: e.reciprocal(out=o_, in_=o_)))(ss5[:, 2:3]), [ss5], [ss5])
                for half in range(2):
                    hsl = slice(half * 512, (half + 1) * 512)
                    k.STT(tmp5[:], pps[half].t[:, 0:512], ss5[:, 2:3], grow[1][j][:, hsl], ALU.mult, ALU.mult,
                          [pps[half], ss5, grow[1][j]], [tmp5])
                    k.TT("pool", xt5[:, jj, hsl], xt5[:, jj, hsl], tmp5[:], ALU.add, [xt5, tmp5], [xt5])
            if last:
                k.ST(out_t.t.ap()[tok0 - NCTX:tok0 - NCTX + T, :].rearrange("(j p) d -> p j d", p=128), xt5[:], [xt5], [out_t])
            else:
                k.ST(s_xres.t[tok0:tok0 + T, :].rearrange("(j p) d -> p j d", p=128), xt5[:], [xt5], [s_xres])
        P.end_phase()
        es5.close()

    fin = []
    if DEBUG:
        pairs = [("qT", s_qT), ("fT", s_fT), ("v", s_v), ("g", s_g), ("u", s_u), ("gate", s_gate)]
        if STAGE >= 2:
            pairs += [("hsum", s_hsum), ("acum", s_acum)]
        if STAGE >= 3:
            pairs += [("osum", s_osum), ("qseg", s_qseg)]
        if STAGE >= 5:
            pairs += [("xdst", s_xdst)]
        if STAGE >= 6:
            pairs += [("xmid", s_xmid)]
        if STAGE >= 7:
            pairs += [("xres", s_xres)]
        P.barrier()
        for nm, sb in pairs:
            fin.append(k.ST(dbg[nm].t.ap(), sb.t.ap(), [sb], [dbg[nm]]))
        k.ST(dbg["misc"].t.ap()[:, 0:96], modfm[:].rearrange("p a b -> p (a b)"), [modfm], [dbg["misc"]])
        k.ST(dbg["misc"].t.ap()[:, 96:160], scsh[:].rearrange("p a b c -> p (a b c)"), [scsh], [dbg["misc"]])
        fin.append(k.ST(dbg["misc"].t.ap()[:, 160:168], c1v[:].rearrange("p a b -> p (a b)"), [c1v], [dbg["misc"]]))
        k.ST(dbg["misc"].t.ap()[:, 168:176], hctx[:].rearrange("p a b -> p (a b)"), [hctx], [dbg["misc"]])
        k.ST(dbg["misc"].t.ap()[:, 176:184], hfin[:].rearrange("p a b -> p (a b)"), [hfin], [dbg["misc"]])
        k.ST(dbg["misc"].t.ap()[:, 184:192], atot[:].rearrange("p a b -> p (a b)"), [atot], [dbg["misc"]])
        k.ST(dbg["misc"].t.ap()[:, 192:200], dtot[:].rearrange("p a b -> p (a b)"), [dtot], [dbg["misc"]])
        k.ST(dbg["misc"].t.ap()[:, 200:208], hin[:].rearrange("p a b -> p (a b)"), [hin], [dbg["misc"]])
    P.barrier()
    P.emit()
    return nc


def make_in_maps(inp):
    f = lambda a: np.ascontiguousarray(np.asarray(a, dtype=np.float32))
    x, c, ctx, c_ctx = f(inp["x"]), f(inp["c"]), f(inp["ctx"]), f(inp["c_ctx"])
    b_mod, norm_g = f(inp["b_mod"]), f(inp["norm_g"])
    common = {
        "w_mod": f(inp["w_mod"]),
        "bmod_fm": f(b_mod.reshape(DEPTH, 6, 8, 128).transpose(0, 3, 1, 2).reshape(DEPTH, 128, 48)),
        "bmod_row": f(b_mod.reshape(DEPTH, 1, 6 * D)),
        "normg_fm": f(norm_g.reshape(DEPTH, 4, 8, 128).transpose(0, 3, 1, 2).reshape(DEPTH, 128, 32)),
        "normg_row": norm_g,
        "w_in": f(inp["w_in"]),
        "lb_fm": f(f(inp["hg_lb_logits"]).reshape(DEPTH, 2, 4, 128).transpose(3, 0, 1, 2)),
        "gn_fm": f(f(inp["hg_gnorm"]).reshape(DEPTH, 128, 1)),
        "convw_fm": f(f(inp["rg_conv_w"]).reshape(DEPTH, 4, 4, 128).transpose(0, 3, 2, 1)),
        "convb_fm": f(f(inp["rg_conv_b"]).reshape(DEPTH, 4, 128).transpose(0, 2, 1)),
        "ba_fm": f(f(inp["rg_b_a"]).reshape(DEPTH, 2, 4, 128).transpose(0, 3, 1, 2)),
        "bx_fm": f(f(inp["rg_b_x"]).reshape(DEPTH, 2, 4, 128).transpose(0, 3, 1, 2)),
        "lam_fm": f(f(inp["rg_lambda"]).reshape(DEPTH, 2, 4, 128).transpose(0, 3, 1, 2)),
        "rg_w_a": f(inp["rg_w_a"]),
        "rg_w_x": f(inp["rg_w_x"]),
        "w_out": f(inp["w_out"]),
        "w_ffn_gate": f(inp["w_ffn_gate"]),
        "w_ffn_up": f(inp["w_ffn_up"]),
        "w_ffn_down": f(inp["w_ffn_down"]),
        "ident": np.eye(128, dtype=np.float32),
    }
    tri = np.triu(np.ones((64, 64), np.float32))
    common["masks"] = f(np.stack([np.tile(tri, (1, 8)), np.tile(tri.T, (1, 8))]))
    maps = []
    for core in range(8):
        if CHAIN:
            b, seg = core % 2, 0
        else:
            b, seg = core // 4, core % 4
        m = dict(common)
        m["x"] = f(x[b, seg * NLAT:(seg + 1) * NLAT])
        m["ctx"] = f(ctx[b])
        cv = np.stack([c[b].reshape(8, 128).T, c_ctx.reshape(8, 128).T], axis=-1)
        m["cvec"] = f(cv)
        fl = np.zeros((128, 16), np.float32)
        for r in range(8):
            same = (r // 4 == b) and not CHAIN
            fl[:, r] = 1.0 if (same and r % 4 < seg) else 0.0
            fl[:, 8 + r] = 1.0 if (same and r % 4 > seg) else 0.0
        m["flags"] = fl
        maps.append(m)
    return maps


_NC_CACHE = {}


def kernel(**inputs):
    if "nc" not in _NC_CACHE:
        _NC_CACHE["nc"] = build()
    nc = _NC_CACHE["nc"]
    maps = make_in_maps(inputs)
    res = run_bass_kernel_spmd(nc, maps, core_ids=list(range(8)))
    out = np.empty((2, 16384, D), np.float32)
    for core in range(8):
        if CHAIN:
            if core >= 2:
                continue
            b, seg = core, 0
        else:
            b, seg = core // 4, core % 4
        out[b, seg * NLAT:(seg + 1) * NLAT] = np.asarray(res.results[core]["out"], dtype=np.float32)
    return out
```

```python
import numpy as np
from contextlib import ExitStack
import concourse.bass as bass
import concourse.mybir as mybir
from concourse.bass_utils import run_bass_kernel_spmd

F32 = mybir.dt.float32
BF16 = mybir.dt.bfloat16
AF = mybir.ActivationFunctionType
ALU = mybir.AluOpType
AX = mybir.AxisListType

MODE = "whole2"
CHAIN = (MODE == "whole2")
D = 1024
KD = 8
NLAT = 16384 if CHAIN else 4096
NCTX = 256
NT = NLAT + NCTX
HGW = 512
RGW = 512
INC = 3584
DFF = 2816
NFF = 22
DEPTH = 2
EPS = 1e-6
CH = 64

DEBUG = False
STAGE = 99
USE_CC = True


class Buf:
    def __init__(self, prog, t, name):
        self.prog = prog
        self.t = t
        self.name = name
        self.last_w = None
        self.readers = []
        self.dma_sem = None
        self.dma_cnt = 0
        self.rd_sem = None
        self.rd_cnt = 0

    def __getitem__(self, idx):
        return self.t[idx]


class Prog:
    ENGS = ("pe", "act", "dve", "pool", "sp")

    def __init__(self, nc):
        self.nc = nc
        self.es = ExitStack()
        self.ops = {e: [] for e in self.ENGS}
        self.cnt = {e: 0 for e in self.ENGS}
        self.sems = {}
        for e in self.ENGS:
            self.sems[e] = self.es.enter_context(nc.semaphore("prog_" + e))
        self.known = {e: {} for e in self.ENGS}
        self.final_waits = []
        self._dma_sem_vals = {}
        self.sem_pool = []
        self.cur_bufs = []
        self.nsem = 0

    def sbuf(self, name, shape, dtype, es=None):
        self.nuniq = getattr(self, "nuniq", 0) + 1
        t = (es or self.es).enter_context(self.nc.sbuf_tensor("sb%d_%s" % (self.nuniq, name), list(shape), dtype))
        b = Buf(self, t, name)
        if es is not None:
            self.cur_bufs.append(b)
        return b

    def end_phase(self):
        self.barrier()
        for b in self.cur_bufs:
            if b.dma_sem is not None:
                self.sem_pool.append((b.dma_sem, b.dma_cnt))
                b.dma_sem = None
            if b.rd_sem is not None:
                self.sem_pool.append((b.rd_sem, b.rd_cnt))
                b.rd_sem = None
        self.cur_bufs = []

    def psum(self, name, shape, dtype):
        t = self.es.enter_context(self.nc.psum_tensor(name, list(shape), dtype))
        return Buf(self, t, name)

    def dram(self, name, shape, dtype):
        t = self.nc.dram_tensor(name, list(shape), dtype)
        return Buf(self, t, name)

    def _new_sem(self, name):
        if self.sem_pool:
            return self.sem_pool.pop()
        self.nsem += 1
        return (self.es.enter_context(self.nc.semaphore("s%d" % self.nsem)), 0)

    def _deps(self, eng, reads, writes):
        need = {}

        def add(ev):
            if ev is None:
                return
            key, val, src = ev
            if src == eng and eng == "pe":
                return
            if key not in need or need[key][0] < val:
                need[key] = (val, src)

        for b in reads:
            add(b.last_w)
        for b in writes:
            add(b.last_w)
            for r in b.readers:
                if r[2] == eng:
                    continue
                add(r)
        waits = []
        kn = self.known[eng]
        for key, (val, src) in need.items():
            if kn.get(key, 0) >= val:
                continue
            kn[key] = val
            waits.append((key, val))
        return waits

    def _semobj(self, key):
        if isinstance(key, str):
            return self.sems[key]
        return key

    def _mark(self, ev, reads, writes):
        for b in writes:
            b.last_w = ev
            b.readers = []
        for b in reads:
            if b in writes:
                continue
            b.readers.append(ev)
            if len(b.readers) > 48:
                latest = {}
                for r in b.readers:
                    k = r[0] if isinstance(r[0], str) else id(r[0])
                    if k not in latest or latest[k][1] < r[1]:
                        latest[k] = r
                b.readers = list(latest.values())

    def op(self, eng, fn, reads=(), writes=()):
        reads = [b for b in reads if b is not None]
        writes = [b for b in writes if b is not None]
        waits = self._deps(eng, reads, writes)
        self.cnt[eng] += 1
        ev = (eng, self.cnt[eng], eng)
        self.ops[eng].append(("op", waits, fn))
        self._mark(ev, reads, writes)
        return ev

    def dma(self, q, out_ap, in_ap, reads=(), writes=(), **kw):
        reads = [b for b in reads if b is not None]
        writes = [b for b in writes if b is not None]
        waits = self._deps(q, reads, writes)
        owner = writes[0] if writes else reads[0]
        if writes:
            if owner.dma_sem is None:
                owner.dma_sem, owner.dma_cnt = self._new_sem("dw_" + owner.name)
            owner.dma_cnt += 16
            sem, val = owner.dma_sem, owner.dma_cnt
        else:
            if owner.rd_sem is None:
                owner.rd_sem, owner.rd_cnt = self._new_sem("dr_" + owner.name)
            owner.rd_cnt += 16
            sem, val = owner.rd_sem, owner.rd_cnt
        ev = (sem, val, "dma")
        self._dma_sem_vals[sem] = val
        self.ops[q].append(("dma", waits, (out_ap, in_ap, sem, kw)))
        self._mark(ev, reads, writes)
        return ev

    def custom(self, q, fn, reads=(), writes=(), inc=16):
        reads = [b for b in reads if b is not None]
        writes = [b for b in writes if b is not None]
        waits = self._deps(q, reads, writes)
        owner = writes[0]
        if owner.dma_sem is None:
            owner.dma_sem, owner.dma_cnt = self._new_sem("dw_" + owner.name)
        owner.dma_cnt += inc
        sem, val = owner.dma_sem, owner.dma_cnt
        ev = (sem, val, "dma")
        self._dma_sem_vals[sem] = val
        self.ops[q].append(("custom", waits, (fn, sem, inc)))
        self._mark(ev, reads, writes)
        return ev

    def barrier(self):
        evs = [(e, self.cnt[e]) for e in self.ENGS if self.cnt[e] > 0]
        evs += list(self._dma_sem_vals.items())
        for e in self.ENGS:
            waits = []
            kn = self.known[e]
            for key, val in evs:
                if kn.get(key, 0) >= val:
                    continue
                kn[key] = val
                waits.append((key, val))
            if waits:
                self.ops[e].append(("wait", waits, None))

    def emit(self):
        nc = self.nc
        engmap = {"pe": "tensor", "act": "scalar", "dve": "vector", "pool": "gpsimd", "sp": "sync"}
        with nc.Block() as block:
            for e in self.ENGS:
                ops = self.ops[e]
                if not ops:
                    continue
                semself = self.sems[e]

                def body(engine, ops=ops, semself=semself):
                    for kind, waits, payload in ops:
                        for key, val in waits:
                            engine.wait_ge(self._semobj(key), val)
                        if kind == "op":
                            payload(engine).then_inc(semself, 1)
                        elif kind == "dma":
                            out_ap, in_ap, sem, kw = payload
                            engine.dma_start(out=out_ap, in_=in_ap, **kw).then_inc(sem, 16)
                        elif kind == "custom":
                            fn, sem, inc = payload
                            fn(engine).then_inc(sem, inc)

                getattr(block, engmap[e])(body)
        self.es.close()


class K:
    def __init__(self):
        nc = bass.Bass("TRN2", target_bir_lowering=False)
        self.nc = nc
        self.P = Prog(nc)
        self.ins = {}
        self.outs = {}
        self.psn = 0

    def inp(self, name, shape, dtype=F32):
        t = self.nc.dram_tensor(name, list(shape), dtype, kind="ExternalInput")
        self.ins[name] = t
        return t

    def outp(self, name, shape, dtype=F32):
        t = self.nc.dram_tensor(name, list(shape), dtype, kind="ExternalOutput")
        b = Buf(self.P, t, name)
        self.outs[name] = b
        return b

    def ACT(self, out, in_, func, R, W, bias=None, scale=None, accum=None):
        kw = {}
        if bias is not None:
            kw["bias"] = bias
        if scale is not None:
            kw["scale"] = scale
        if accum is not None:
            kw["accum_out"] = accum
        return self.P.op("act", lambda e: e.activation(out=out, in_=in_, func=func, **kw), R, W)

    def TS(self, eng, out, in0, s1, s2, op0, op1, R, W):
        if op1 is None:
            return self.P.op(eng, lambda e: e.tensor_scalar(out=out, in0=in0, scalar1=s1, scalar2=None, op0=op0), R, W)
        return self.P.op(eng, lambda e: e.tensor_scalar(out=out, in0=in0, scalar1=s1, scalar2=s2, op0=op0, op1=op1), R, W)

    def TT(self, eng, out, in0, in1, op, R, W):
        return self.P.op(eng, lambda e: e.tensor_tensor(out=out, in0=in0, in1=in1, op=op), R, W)

    def STT(self, out, in0, scalar, in1, op0, op1, R, W):
        return self.P.op("dve", lambda e: e.scalar_tensor_tensor(out=out, in0=in0, scalar=scalar, in1=in1, op0=op0, op1=op1), R, W)

    def MM(self, out, lhsT, rhs, start, stop, R, W):
        return self.P.op("pe", lambda e: e.matmul(out, lhsT=lhsT, rhs=rhs, start=start, stop=stop), R, W)

    def TR(self, out, in_, ident, R, W):
        return self.P.op("pe", lambda e: e.transpose(out, in_, ident), R, W)

    def CP(self, eng, out, in_, R, W):
        if eng == "act":
            return self.P.op("act", lambda e: e.copy(out=out, in_=in_), R, W)
        return self.P.op(eng, lambda e: e.tensor_copy(out=out, in_=in_), R, W)

    def SCAN(self, out, d0, d1, init, R, W):
        return self.P.op("dve", lambda e: e.tensor_tensor_scan(out=out, data0=d0, data1=d1, initial=init, op0=ALU.mult, op1=ALU.add), R, W)

    def MEMSET(self, eng, ap, val, W):
        return self.P.op(eng, lambda e: e.memset(ap, val), [], W)

    def LD(self, out, in_, W, R=(), q="sp"):
        return self.P.dma(q, out, in_, reads=list(R), writes=list(W))

    def ST(self, out, in_, R, W=(), q="sp"):
        return self.P.dma(q, out, in_, reads=list(R), writes=list(W))

    def ps(self):
        b = self.psb[self.psn % 8]
        self.psn += 1
        return b


def build(nlayers=DEPTH):
    k = K()
    nc, P = k.nc, k.P
    x_in = k.inp("x", [NLAT, D])
    ctx_in = k.inp("ctx", [NCTX, D])
    cvec = k.inp("cvec", [128, KD, 2])
    w_mod = k.inp("w_mod", [DEPTH, D, 6 * D])
    bmod_fm = k.inp("bmod_fm", [DEPTH, 128, 48])
    bmod_row = k.inp("bmod_row", [DEPTH, 1, 6 * D])
    normg_fm = k.inp("normg_fm", [DEPTH, 128, 32])
    normg_row = k.inp("normg_row", [DEPTH, 4, D])
    w_in = k.inp("w_in", [DEPTH, D, INC])
    lb_fm = k.inp("lb_fm", [128, DEPTH, 2, 4])
    gn_fm = k.inp("gn_fm", [DEPTH, 128, 1])
    convw_fm = k.inp("convw_fm", [DEPTH, 128, 4, 4])
    convb_fm = k.inp("convb_fm", [DEPTH, 128, 4])
    ba_fm = k.inp("ba_fm", [DEPTH, 128, 2, 4])
    bx_fm = k.inp("bx_fm", [DEPTH, 128, 2, 4])
    lam_fm = k.inp("lam_fm", [DEPTH, 128, 2, 4])
    rg_w_a = k.inp("rg_w_a", [DEPTH, 2, 8, 64, 64])
    rg_w_x = k.inp("rg_w_x", [DEPTH, 2, 8, 64, 64])
    w_out = k.inp("w_out", [DEPTH, D, D])
    w_gate = k.inp("w_ffn_gate", [DEPTH, D, DFF])
    w_up = k.inp("w_ffn_up", [DEPTH, D, DFF])
    w_down = k.inp("w_ffn_down", [DEPTH, DFF, D])
    flags_in = k.inp("flags", [128, 16])
    ident_in = k.inp("ident", [128, 128])
    masks_in = k.inp("masks", [2, 64, 512])
    out_t = k.outp("out", [NLAT, D])

    s_qT = P.dram("s_qT", [4, 128, NT], BF16)
    s_fT = P.dram("s_fT", [2, 4, 128, NT], F32)
    s_v = P.dram("s_v", [NT, HGW], BF16)
    s_g = P.dram("s_g", [NT, HGW], BF16)
    s_u = P.dram("s_u", [4, 128, NT], BF16)
    s_gate = P.dram("s_gate", [4, 128, NT], BF16)
    s_hsum = P.dram("s_hsum", [4, 128, NT], F32)
    s_hsumf = P.dram("s_hsumf", [4, 128, NT], F32)
    s_acum = P.dram("s_acum", [2, 4, 128, NT], F32)
    s_osum = P.dram("s_osum", [NT, HGW], F32)
    s_of = P.dram("s_of", [NT, HGW], F32)
    s_qseg = P.dram("s_qseg", [2, 4, 128, NT], BF16)
    s_xres = P.dram("s_xres", [NT, D], F32)
    s_xmid = P.dram("s_xmid", [NT, D], F32)
    XW = 1024 + 8 + 8 + 8
    s_xsrc = P.dram("s_xsrc", [128, XW], F32)
    s_xdst = P.dram("s_xdst", [8 * 128, XW], F32)
    s_xdst.dma_sem = P.es.enter_context(nc.semaphore("cc_sem"))
    s_xdst.dma_cnt = 0

    dbg = {}
    if DEBUG:
        dbg["qT"] = k.outp("d_qT", [4, 128, NT], BF16)
        dbg["fT"] = k.outp("d_fT", [2, 4, 128, NT], F32)
        dbg["v"] = k.outp("d_v", [NT, HGW], BF16)
        dbg["g"] = k.outp("d_g", [NT, HGW], BF16)
        dbg["u"] = k.outp("d_u", [4, 128, NT], BF16)
        dbg["gate"] = k.outp("d_gate", [4, 128, NT], BF16)
        dbg["hsum"] = k.outp("d_hsum", [4, 128, NT], F32)
        dbg["acum"] = k.outp("d_acum", [2, 4, 128, NT], F32)
        dbg["osum"] = k.outp("d_osum", [NT, HGW], F32)
        dbg["qseg"] = k.outp("d_qseg", [2, 4, 128, NT], BF16)
        dbg["xmid"] = k.outp("d_xmid", [NT, D], F32)
        dbg["xres"] = k.outp("d_xres", [NT, D], F32)
        dbg["xdst"] = k.outp("d_xdst", [8 * 128, XW], F32)
        dbg["misc"] = k.outp("d_misc", [128, 256], F32)
        dbg["sst"] = k.outp("d_sst", [128, 1024], F32)
        dbg["sctx"] = k.outp("d_sctx", [128, 1024], F32)
        dbg["sin"] = k.outp("d_sin", [128, 1024], F32)

    k.psb = [P.psum("psb%d" % i, [128, 512], F32) for i in range(8)]

    ident = P.sbuf("ident", [128, 128], BF16)
    maski = P.sbuf("maski", [64, 2, 512], mybir.dt.int32)
    ones_row = P.sbuf("ones_row", [1, 128], F32)
    ones_t = P.sbuf("ones_t", [128, 512], F32)
    flags = P.sbuf("flags", [128, 16], F32)
    cv_f = P.sbuf("cv_f", [128, KD, 2], F32)
    scbf = P.sbuf("scbf", [128, KD, 2], BF16)
    modfm = P.sbuf("modfm", [128, 48, 2], F32)
    bmodfm = P.sbuf("bmodfm", [128, 48], F32)
    gfm = P.sbuf("gfm", [128, 32], F32)
    scsh = P.sbuf("scsh", [128, 4, KD, 2], F32)
    grow = [[P.sbuf("grow%d%d" % (a, j), [128, D], F32) for j in range(2)] for a in range(2)]
    lbt = P.sbuf("lbt", [128, DEPTH, 2, 4], F32)
    lbv = P.sbuf("lbv", [128, 2, 4], F32)
    omlv = P.sbuf("omlv", [128, 2, 4], F32)
    gnv = P.sbuf("gnv", [128, 1], F32)
    cwv = P.sbuf("cwv", [128, 4, 4], F32)
    cbv = P.sbuf("cbv", [128, 4], F32)
    bav = P.sbuf("bav", [128, 2, 4], F32)
    bxv = P.sbuf("bxv", [128, 2, 4], F32)
    c1v = P.sbuf("c1v", [128, 2, 4], F32)
    Sst = P.sbuf("Sst", [128, 2, 4, 128], F32)
    dtot = P.sbuf("dtot", [128, 2, 4], F32)
    hctx = P.sbuf("hctx", [128, 2, 4], F32)
    hfin = P.sbuf("hfin", [128, 2, 4], F32)
    atot = P.sbuf("atot", [128, 2, 4], F32)
    hin = P.sbuf("hin", [128, 2, 4], F32)

    ess = ExitStack()
    ident_f = P.sbuf("ident_f", [128, 128], F32, ess)
    masks_f = P.sbuf("masks_f", [64, 2, 512], F32, ess)
    k.LD(ident_f[:], ident_in.ap(), [ident_f])
    k.CP("dve", ident[:], ident_f[:], [ident_f], [ident])
    k.LD(masks_f[:], masks_in.ap().rearrange("a s t -> s a t"), [masks_f])
    k.CP("dve", maski[:], masks_f[:], [masks_f], [maski])
    k.MEMSET("pool", ones_row[:], 1.0, [ones_row])
    k.MEMSET("pool", ones_t[:], 1.0, [ones_t])
    k.LD(flags[:], flags_in.ap(), [flags])
    k.LD(cv_f[:], cvec.ap(), [cv_f])
    k.ACT(scbf[:], cv_f[:], AF.Silu, [cv_f], [scbf])
    k.LD(lbt[:], lb_fm.ap(), [lbt])

    P.end_phase()
    ess.close()

    units = [(0, NCTX, True)] + [(NCTX + 512 * i, 512, False) for i in range(NLAT // 512)]

    def x_src(L, tok0, T):
        if L == 0:
            if tok0 < NCTX:
                return ctx_in.ap()[tok0:tok0 + T, :], None
            return x_in.ap()[tok0 - NCTX:tok0 - NCTX + T, :], None
        return s_xres.t[tok0:tok0 + T, :], s_xres

    def norm_bufs(es, tag, ntmax):
        return (P.sbuf("ssq_" + tag, [128, 4], F32, es), P.sbuf("rstd_" + tag, [128, 4], F32, es),
                P.sbuf("junk_" + tag, [128, D], BF16, es), P.sbuf("xn_" + tag, [128, ntmax, D], BF16, es))

    def norm_mod(nb, xt, nt, a, j, hT):
        T = nt * 128
        ssq, rstd, junk, xn = nb
        for jj in range(nt):
            k.ACT(junk[:], xt[:, jj, :], AF.Square, [xt], [junk, ssq], accum=ssq[:, jj:jj + 1])
        k.TS("dve", rstd[:, 0:nt], ssq[:, 0:nt], 1.0 / D, EPS, ALU.mult, ALU.add, [ssq], [rstd])
        k.ACT(rstd[:, 0:nt], rstd[:, 0:nt], AF.Sqrt, [rstd], [rstd])
        P.op("dve", lambda e: e.reciprocal(out=rstd[:, 0:nt], in_=rstd[:, 0:nt]), [rstd], [rstd])
        for jj in range(nt):
            k.ACT(xn[:, jj, :], xt[:, jj, :], AF.Copy, [xt, rstd], [xn], scale=rstd[:, jj:jj + 1])
        for kc in range(KD):
            pb = k.ps()
            pst = pb.t[:, :].bitcast(BF16)
            for jj in range(nt):
                k.TR(pst[:, jj * 128:(jj + 1) * 128], xn[:, jj, kc * 128:(kc + 1) * 128], ident[:], [xn, ident], [pb])
            k.TS("dve", hT[:, kc, 0:T], pst[:, 0:T], scsh[:, 2 * a, kc, j:j + 1], scsh[:, 2 * a + 1, kc, j:j + 1],
                 ALU.mult, ALU.add, [pb, scsh], [hT])

    for L in range(nlayers):
        last = (L == DEPTH - 1)
        es0 = ExitStack()
        wblk = P.sbuf("wblk", [128, KD, D], BF16, es0)
        rowt = P.sbuf("rowt", [1, 512], F32, es0)
        brow = P.sbuf("brow", [1, 6 * D], F32, es0)
        g1row = P.sbuf("g1row", [1, D], F32, es0)
        g3row = P.sbuf("g3row", [1, D], F32, es0)
        lamt = P.sbuf("lamt", [128, 2, 4], F32, es0)
        k.LD(bmodfm[:], bmod_fm.ap()[L], [bmodfm])
        k.LD(gfm[:], normg_fm.ap()[L], [gfm])
        k.LD(brow[:], bmod_row.ap()[L], [brow])
        k.LD(g1row[:], normg_row.ap()[L, 1:2, :], [g1row])
        k.LD(g3row[:], normg_row.ap()[L, 3:4, :], [g3row])
        k.LD(gnv[:], gn_fm.ap()[L], [gnv])
        k.LD(cwv[:], convw_fm.ap()[L], [cwv])
        k.LD(cbv[:], convb_fm.ap()[L], [cbv])
        k.LD(bav[:], ba_fm.ap()[L], [bav])
        k.LD(bxv[:], bx_fm.ap()[L], [bxv])
        k.LD(lamt[:], lam_fm.ap()[L], [lamt])
        k.ACT(c1v[:], lamt[:], AF.Exp, [lamt], [c1v], scale=-1.0)
        k.ACT(c1v[:], c1v[:], AF.Ln, [c1v], [c1v], bias=1.0)
        k.TS("dve", c1v[:], c1v[:], -8.0, None, ALU.mult, None, [c1v], [c1v])
        if L == 0:
            k.MEMSET("pool", lbv[:], 0.0, [lbv])
        else:
            k.TT("dve", lbv[:], lbt[:, 1], lbt[:, 0], ALU.subtract, [lbt], [lbv])
            k.ACT(lbv[:], lbv[:], AF.Sigmoid, [lbv], [lbv])
        k.TS("dve", omlv[:], lbv[:], -1.0, 1.0, ALU.mult, ALU.add, [lbv], [omlv])
        for n in range(6):
            k.LD(wblk[:], w_mod.ap()[L, :, n * D:(n + 1) * D].rearrange("(kc p) c -> p kc c", p=128), [wblk], q="pool")
            pb = k.ps()
            for kd in range(KD):
                for kc in range(KD):
                    k.MM(pb.t[:, kd * 2:kd * 2 + 2], wblk[:, kc, kd * 128:(kd + 1) * 128], scbf[:, kc, :],
                         kc == 0, kc == KD - 1, [wblk, scbf], [pb])
            k.TT("dve", modfm[:, n * 8:(n + 1) * 8, :], pb.t[:, 0:16].rearrange("p (a b) -> p a b", b=2),
                 bmodfm[:, n * 8:(n + 1) * 8].unsqueeze(2).to_broadcast([128, 8, 2]), ALU.add, [pb, bmodfm], [modfm])
            if n in (2, 5):
                a = 0 if n == 2 else 1
                grow_g = g1row if n == 2 else g3row
                for j in range(2):
                    for half in range(2):
                        pb2 = k.ps()
                        for kc in range(KD):
                            k.MM(pb2.t[0:1, 0:512], scbf[:, kc, j:j + 1], wblk[:, kc, half * 512:(half + 1) * 512],
                                 kc == 0, kc == KD - 1, [wblk, scbf], [pb2])
                        k.TT("dve", rowt[:], pb2.t[0:1, 0:512], brow[0:1, n * D + half * 512:n * D + (half + 1) * 512],
                             ALU.add, [pb2, brow], [rowt])
                        k.TT("dve", rowt[:], rowt[:], grow_g[0:1, half * 512:(half + 1) * 512], ALU.mult,
                             [rowt, grow_g], [rowt])
                        pb3 = k.ps()
                        k.MM(pb3.t[:, 0:512], ones_row[0:1, :], rowt[0:1, :], True, True, [ones_row, rowt], [pb3])
                        k.CP("act", grow[a][j][:, half * 512:(half + 1) * 512], pb3.t[:, 0:512], [pb3], [grow[a][j]])
        for a, (nsc, nsh, gi) in enumerate(((1, 0, 0), (4, 3, 2))):
            k.TS("dve", scsh[:, 2 * a], modfm[:, nsc * 8:(nsc + 1) * 8, :], 1.0, None, ALU.add, None, [modfm], [scsh])
            k.TT("dve", scsh[:, 2 * a], scsh[:, 2 * a], gfm[:, gi * 8:(gi + 1) * 8].unsqueeze(2).to_broadcast([128, 8, 2]),
                 ALU.mult, [scsh, gfm], [scsh])
            k.CP("dve", scsh[:, 2 * a + 1], modfm[:, nsh * 8:(nsh + 1) * 8, :], [modfm], [scsh])
        P.end_phase()
        es0.close()
        if STAGE < 1:
            break

        es1 = ExitStack()
        winb = P.sbuf("winb", [128, KD, INC], BF16, es1)
        for kc in range(KD):
            k.LD(winb[:, kc, :], w_in.ap()[L, kc * 128:(kc + 1) * 128, :], [winb], q="pool")
        xts = [P.sbuf("xt%d" % i, [128, 4, D], F32, es1) for i in range(1)]
        hTs = [P.sbuf("hT%d" % i, [128, KD, 512], BF16, es1) for i in range(2)]
        qs = P.sbuf("qs", [128, 4, 512], BF16, es1)
        sg = P.sbuf("sg", [128, 512], F32, es1)
        ft = P.sbuf("ft", [128, 2, 4, 512], F32, es1)
        uf = P.sbuf("uf", [128, 512], F32, es1)
        uc = P.sbuf("uc", [128, 512], F32, es1)
        ub = P.sbuf("ub", [128, 4, 512], BF16, es1)
        gb = P.sbuf("gb", [128, 4, 512], BF16, es1)
        vt = P.sbuf("vt", [64, 8, HGW], BF16, es1)
        gt = P.sbuf("gt", [64, 8, HGW], BF16, es1)
        nb1 = norm_bufs(es1, "p1", 4)
        for ui, (tok0, T, isctx) in enumerate(units):
            nt = T // 128
            nch = T // CH
            j = 1 if isctx else 0
            xt = xts[0]
            hT = hTs[ui % 2]
            src, srcbuf = x_src(L, tok0, T)
            k.LD(xt[:, 0:nt, :], src.rearrange("(j p) d -> p j d", p=128), [xt], R=[srcbuf])
            norm_mod(nb1, xt, nt, 0, j, hT)
            for ct in range(20):
                if ct < 12:
                    c0 = ct * 128
                else:
                    c0 = 5 * HGW + (ct - 12) * 128
                pb = k.ps()
                for kc in range(KD):
                    k.MM(pb.t[:, 0:T], winb[:, kc, c0:c0 + 128], hT[:, kc, 0:T], kc == 0, kc == KD - 1, [winb, hT], [pb])
                if ct < 4:
                    k.ACT(qs[:, ct, 0:T], pb.t[:, 0:T], AF.Silu, [pb], [qs])
                elif ct < 12:
                    dr, h = (ct - 4) // 4, (ct - 4) % 4
                    k.ACT(sg[:, 0:T], pb.t[:, 0:T], AF.Sigmoid, [pb], [sg])
                    k.TS("dve", ft[:, dr, h, 0:T], sg[:, 0:T], omlv[:, dr, h:h + 1], lbv[:, dr, h:h + 1], ALU.mult, ALU.add,
                         [sg, omlv, lbv], [ft])
                elif ct < 16:
                    c = ct - 12
                    k.CP("act", uf[:, 0:T], pb.t[:, 0:T], [pb], [uf])
                    RW = T if isctx else 64
                    ufv = uf[:, 0:T].rearrange("p (r w) -> p r w", w=RW)
                    ucv = uc[:, 0:T].rearrange("p (r w) -> p r w", w=RW)
                    k.TS("dve", uc[:, 0:T], uf[:, 0:T], cwv[:, c, 2:3], cbv[:, c:c + 1], ALU.mult, ALU.add, [uf, cwv, cbv], [uc])
                    for tap in (0, 1, 3):
                        s = tap - 2
                        if s < 0:
                            o_sl = ucv[:, :, -s:RW]
                            i_sl = ufv[:, :, 0:RW + s]
                        else:
                            o_sl = ucv[:, :, 0:RW - s]
                            i_sl = ufv[:, :, s:RW]
                        k.STT(o_sl, i_sl, cwv[:, c, tap:tap + 1], o_sl, ALU.mult, ALU.add, [uf, uc, cwv], [uc])
                    k.CP("act", ub[:, c, 0:T], uc[:, 0:T], [uc], [ub])
                else:
                    c = ct - 16
                    k.ACT(gb[:, c, 0:T], pb.t[:, 0:T], AF.Gelu_apprx_tanh, [pb], [gb])
            for cc in range(nch):
                for which in range(2):
                    c0 = (3 + which) * HGW
                    pb = k.ps()
                    for kc in range(KD):
                        k.MM(pb.t[0:CH, 0:512], hT[:, kc, cc * CH:(cc + 1) * CH], winb[:, kc, c0:c0 + 512],
                             kc == 0, kc == KD - 1, [winb, hT], [pb])
                    if which == 0:
                        k.CP("act", vt[:, cc, :], pb.t[0:CH, 0:512], [pb], [vt])
                    else:
                        k.ACT(gt[:, cc, :], pb.t[0:CH, 0:512], AF.Silu, [pb], [gt])
            k.ST(s_qT.t[:, :, tok0:tok0 + T].rearrange("h p t -> p h t"), qs[:, :, 0:T], [qs], [s_qT])
            for dr in range(2):
                k.ST(s_fT.t[dr, :, :, tok0:tok0 + T].rearrange("h p t -> p h t"), ft[:, dr, :, 0:T], [ft], [s_fT])
            k.ST(s_u.t[:, :, tok0:tok0 + T].rearrange("h p t -> p h t"), ub[:, :, 0:T], [ub], [s_u])
            k.ST(s_gate.t[:, :, tok0:tok0 + T].rearrange("h p t -> p h t"), gb[:, :, 0:T], [gb], [s_gate])
            k.ST(s_v.t[tok0:tok0 + T, :].rearrange("(c p) n -> p c n", p=CH), vt[:, 0:nch, :], [vt], [s_v])
            k.ST(s_g.t[tok0:tok0 + T, :].rearrange("(c p) n -> p c n", p=CH), gt[:, 0:nch, :], [gt], [s_g])
        P.end_phase()
        es1.close()
        if STAGE < 2:
            break


        esm = ExitStack()
        Sst = P.sbuf("Sst", [128, 2, 4, 128], F32, esm)
        Sctx = P.sbuf("Sctx", [128, 2, 4, 128], F32, esm)
        Sin = P.sbuf("Sin", [128, 2, 4, 128], F32, esm)
        Sinb = P.sbuf("Sinb", [128, 2, 4, 128], BF16, esm)
        es2 = ExitStack()
        wbd = P.sbuf("wbd", [128, 2, 2, 4, 128], BF16, es2)
        wstage = P.sbuf("wstage", [128, 2, 2, 4, 128], F32, es2)
        zeros_t = P.sbuf("zeros_t", [128, 512], F32, es2)
        k.MEMSET("pool", zeros_t[:], 0.0, [zeros_t])
        k.MEMSET("pool", wstage[:], 0.0, [wstage])
        for gi, wsrc in enumerate((rg_w_a, rg_w_x)):
            for dr in range(2):
                for half in range(2):
                    src = wsrc.ap()[L, dr].rearrange("(ct h) i j -> h i ct j", h=2)[half]
                    k.LD(wstage[half * 64:(half + 1) * 64, gi, dr, :, half * 64:(half + 1) * 64], src, [wstage])
        k.CP("dve", wbd[:], wstage[:], [wstage], [wbd])
        ut = P.sbuf("ut", [128, 4, 512], BF16, es2)
        rt = P.sbuf("rt", [128, 4, 512], F32, es2)
        it = P.sbuf("it", [128, 4, 512], F32, es2)
        at = P.sbuf("at", [128, 4, 512], F32, es2)
        a2t = P.sbuf("a2t", [128, 4, 512], F32, es2)
        bxt = P.sbuf("bxt", [128, 4, 512], F32, es2)
        hl = P.sbuf("hl", [128, 4, 512], F32, es2)
        ac = P.sbuf("ac", [128, 4, 512], F32, es2)
        hs = P.sbuf("hs", [128, 4, 512], F32, es2)
        car_h = P.sbuf("car_h", [128, 4], F32, es2)
        car_a = P.sbuf("car_a", [128, 4], F32, es2)
        for dr in range(2):
            order = [units[0]] + (units[1:] if dr == 0 else units[1:][::-1])
            for ui, (tok0, T, isctx) in enumerate(order):
                fresh = (ui == 0) if CHAIN else (ui <= 1)
                k.LD(ut[:, :, 0:T], s_u.t[:, :, tok0:tok0 + T].rearrange("c p t -> p c t"), [ut], R=[s_u])
                if dr == 1:
                    k.LD(hs[:, :, 0:T], s_hsumf.t[:, :, tok0:tok0 + T].rearrange("c p t -> p c t"), [hs], R=[s_hsumf])
                pbs = []
                for c in range(4):
                    for gi in range(2):
                        pb = k.ps()
                        k.MM(pb.t[:, 0:T], wbd[:, gi, dr, c, :], ut[:, c, 0:T], True, True, [wbd, ut], [pb])
                        pbs.append(pb)
                for c in range(4):
                    k.ACT(rt[:, c, 0:T], pbs[2 * c].t[:, 0:T], AF.Sigmoid, [pbs[2 * c], bav], [rt], bias=bav[:, dr, c:c + 1])
                    k.ACT(it[:, c, 0:T], pbs[2 * c + 1].t[:, 0:T], AF.Sigmoid, [pbs[2 * c + 1], bxv], [it], bias=bxv[:, dr, c:c + 1])
                for c in range(4):
                    k.ACT(at[:, c, 0:T], rt[:, c, 0:T], AF.Exp, [rt, c1v], [at], scale=c1v[:, dr, c:c + 1])
                k.ACT(a2t[:, :, 0:T], at[:, :, 0:T], AF.Square, [at], [a2t])
                k.ACT(a2t[:, :, 0:T], a2t[:, :, 0:T], AF.Sqrt, [a2t], [a2t], scale=-1.0, bias=1.0)
                k.TT("pool", it[:, :, 0:T], it[:, :, 0:T], ut[:, :, 0:T], ALU.mult, [it, ut], [it])
                k.TT("dve", bxt[:, :, 0:T], it[:, :, 0:T], a2t[:, :, 0:T], ALU.mult, [it, a2t], [bxt])
                for c in range(4):
                    ih = 0.0 if fresh else car_h[:, c:c + 1]
                    ia = 1.0 if fresh else car_a[:, c:c + 1]
                    if dr == 0:
                        k.SCAN(hl[:, c, 0:T], at[:, c, 0:T], bxt[:, c, 0:T], ih, [at, bxt, car_h], [hl])
                        if not CHAIN:
                            k.SCAN(ac[:, c, 0:T], at[:, c, 0:T], zeros_t[:, 0:T], ia, [at, zeros_t, car_a], [ac])
                        lastcol = slice(T - 1, T)
                    else:
                        k.SCAN(hl[:, c, T - 1::-1] if False else hl[:, c, 0:T][:, ::-1], at[:, c, 0:T][:, ::-1], bxt[:, c, 0:T][:, ::-1], ih,
                               [at, bxt, car_h], [hl])
                        if not CHAIN:
                            k.SCAN(ac[:, c, 0:T][:, ::-1], at[:, c, 0:T][:, ::-1], zeros_t[:, 0:T], ia, [at, zeros_t, car_a], [ac])
                        lastcol = slice(0, 1)
                    if isctx:
                        k.CP("act", hctx[:, dr, c:c + 1], hl[:, c, lastcol], [hl], [hctx])
                    if CHAIN or not isctx:
                        k.CP("act", car_h[:, c:c + 1], hl[:, c, lastcol], [hl], [car_h])
                        if not CHAIN:
                            k.CP("act", car_a[:, c:c + 1], ac[:, c, lastcol], [ac], [car_a])
                if dr == 0:
                    k.ST(s_hsumf.t[:, :, tok0:tok0 + T].rearrange("c p t -> p c t"), hl[:, :, 0:T], [hl], [s_hsumf])
                else:
                    k.TT("dve", hl[:, :, 0:T], hl[:, :, 0:T], hs[:, :, 0:T], ALU.add, [hl, hs], [hl])
                    k.ST(s_hsum.t[:, :, tok0:tok0 + T].rearrange("c p t -> p c t"), hl[:, :, 0:T], [hl], [s_hsum])
                if not CHAIN:
                    k.ST(s_acum.t[dr, :, :, tok0:tok0 + T].rearrange("c p t -> p c t"), ac[:, :, 0:T], [ac], [s_acum])
            if not CHAIN:
                k.CP("pool", hfin[:, dr, :], car_h[:], [car_h], [hfin])
                k.CP("pool", atot[:, dr, :], car_a[:], [car_a], [atot])
        P.end_phase()
        es2.close()
        if STAGE < 3:
            esm.close()
            break

        es3 = ExitStack()
        fTt = P.sbuf("fTt", [128, 4, 512], F32, es3)
        qTt = P.sbuf("qTt", [128, 4, 512], BF16, es3)
        kTt = P.sbuf("kTt", [128, 4, 512], F32, es3)
        Ct = P.sbuf("Ct", [128, 4, 512], F32, es3)
        crel = P.sbuf("crel", [128, 4, 512], F32, es3)
        e1 = P.sbuf("e1", [128, 4, 512], F32, es3)
        ktl = P.sbuf("ktl", [128, 4, 512], BF16, es3)
        khl = P.sbuf("khl", [128, 4, 512], BF16, es3)
        qsg = None if CHAIN else P.sbuf("qsg", [128, 4, 512], BF16, es3)
        vtts = [P.sbuf("vtt%d" % i, [64, 8, HGW], BF16, es3) for i in range(2)]
        qtls = [P.sbuf("qtl%d" % i, [128, 4, 512], BF16, es3) for i in range(2)]
        khTs = [P.sbuf("khT%d" % i, [64, 4, 8, 128], BF16, es3) for i in range(2)]
        scTs = [[P.sbuf("scT%d%d" % (d_, i), [64, 4, 512], BF16, es3) for i in range(2)] for d_ in range(2)]
        BDs = [P.sbuf("BD%d" % i, [128, 4, 2, 8], F32, es3) for i in range(2)]
        for d_ in range(2):
            for i in range(2):
                k.MEMSET("pool", scTs[d_][i][:], 0.0, [scTs[d_][i]])
        ot = P.sbuf("ot", [64, 8, HGW], F32, es3)
        ofts = [P.sbuf("oft%d" % i, [64, 8, HGW], F32, es3) for i in range(2)]
        Spb = [P.sbuf("Spb%d" % i, [128, 4, 128], BF16, es3) for i in range(2)]
        carC = P.sbuf("carC", [128, 4], F32, es3)
        cprev = P.sbuf("cprev", [128, 4, 8], F32, es3)
        dif = P.sbuf("dif", [128, 4, 2, 8], F32, es3)

        def hgA(dr, ui, tok0, T, isctx, sset):
            fresh = (ui == 0) if CHAIN else (ui <= 1)
            nch = T // CH
            vtt, qtl, khT, scT, BD = vtts[sset], qtls[sset], khTs[sset], scTs[dr][sset], BDs[sset]
            oft = ofts[sset]
            k.LD(fTt[:, :, 0:T], s_fT.t[dr, :, :, tok0:tok0 + T].rearrange("h p t -> p h t"), [fTt], R=[s_fT])
            k.LD(qTt[:, :, 0:T], s_qT.t[:, :, tok0:tok0 + T].rearrange("h p t -> p h t"), [qTt], R=[s_qT])
            k.LD(vtt[:, 0:nch, :], s_v.t[tok0:tok0 + T, :].rearrange("(c p) n -> p c n", p=CH), [vtt], R=[s_v])
            k.ACT(kTt[:, :, 0:T], fTt[:, :, 0:T], AF.Copy, [fTt], [kTt], scale=-1.0, bias=1.0)
            k.ACT(fTt[:, :, 0:T], fTt[:, :, 0:T], AF.Ln, [fTt], [fTt])
            C4 = Ct[:, :, 0:T].rearrange("p h (n w) -> p h n w", w=CH)
            edge = 0 if dr == 0 else nch - 1
            if fresh:
                k.MEMSET("dve", cprev[:, :, edge:edge + 1], 0.0, [cprev])
            else:
                k.CP("act", cprev[:, :, edge:edge + 1], carC[:].unsqueeze(2), [carC], [cprev])
            for h in range(4):
                init = 0.0 if fresh else carC[:, h:h + 1]
                if dr == 0:
                    k.SCAN(Ct[:, h, 0:T], ones_t[:, 0:T], fTt[:, h, 0:T], init, [ones_t, fTt, carC], [Ct])
                else:
                    k.SCAN(Ct[:, h, 0:T][:, ::-1], ones_t[:, 0:T], fTt[:, h, 0:T][:, ::-1], init, [ones_t, fTt, carC], [Ct])
            if dr == 0:
                k.CP("act", carC[:].unsqueeze(2), Ct[:, :, T - 1:T], [Ct], [carC])
                Aanc = C4[:, :, :, 31]
                Cend = C4[:, :, :, 63]
                if nch > 1:
                    k.CP("act", cprev[:, :, 1:nch], C4[:, :, 0:nch - 1, 63], [Ct], [cprev])
            else:
                k.CP("act", carC[:].unsqueeze(2), Ct[:, :, 0:1], [Ct], [carC])
                Aanc = C4[:, :, :, 32]
                Cend = C4[:, :, :, 0]
                if nch > 1:
                    k.CP("act", cprev[:, :, 0:nch - 1], C4[:, :, 1:nch, 0], [Ct], [cprev])
            k.TT("dve", dif[:, :, 0, 0:nch], Aanc, cprev[:, :, 0:nch], ALU.subtract, [Ct, cprev], [dif])
            k.TT("dve", dif[:, :, 1, 0:nch], Cend, cprev[:, :, 0:nch], ALU.subtract, [Ct, cprev], [dif])
            k.ACT(BD[:, :, :, 0:nch], dif[:, :, :, 0:nch], AF.Exp, [dif], [BD])
            yield
            cr4 = crel[:, :, 0:T].rearrange("p h (n w) -> p h n w", w=CH)
            e14 = e1[:, :, 0:T].rearrange("p h (n w) -> p h n w", w=CH)
            k.TT("dve", cr4, C4, Aanc.unsqueeze(3).to_broadcast([128, 4, nch, CH]), ALU.subtract, [Ct], [crel])
            k.ACT(e1[:, :, 0:T], crel[:, :, 0:T], AF.Exp, [crel], [e1])
            k.ACT(crel[:, :, 0:T], crel[:, :, 0:T], AF.Exp, [crel], [crel], scale=-1.0)
            k.TT("pool", qtl[:, :, 0:T], qTt[:, :, 0:T], e1[:, :, 0:T], ALU.mult, [qTt, e1], [qtl])
            k.TT("dve", ktl[:, :, 0:T], kTt[:, :, 0:T], crel[:, :, 0:T], ALU.mult, [kTt, crel], [ktl])
            yield
            if not isctx and not CHAIN:
                k.ACT(e1[:, :, 0:T], Ct[:, :, 0:T], AF.Exp, [Ct], [e1])
                k.TT("pool", qsg[:, :, 0:T], qTt[:, :, 0:T], e1[:, :, 0:T], ALU.mult, [qTt, e1], [qsg])
                k.ST(s_qseg.t[dr, :, :, tok0:tok0 + T].rearrange("h p t -> p h t"), qsg[:, :, 0:T], [qsg], [s_qseg])
            k.TT("dve", e14, C4, Cend.unsqueeze(3).to_broadcast([128, 4, nch, CH]), ALU.subtract, [Ct], [e1])
            k.ACT(e1[:, :, 0:T], e1[:, :, 0:T], AF.Exp, [e1], [e1], scale=-1.0)
            k.TT("pool", khl[:, :, 0:T], kTt[:, :, 0:T], e1[:, :, 0:T], ALU.mult, [kTt, e1], [khl])
            yield
            for h in range(4):
                pbT = k.ps()
                pT = pbT.t[:, :].bitcast(BF16)
                for n in range(nch):
                    k.TR(pT[0:CH, n * 128:(n + 1) * 128], khl[:, h, n * CH:(n + 1) * CH], ident[:], [khl, ident], [pbT])
                k.CP("act", khT[:, h, 0:nch, :], pT[0:CH, 0:nch * 128].rearrange("p (n k) -> p n k", k=128), [pbT], [khT])
                pbS = k.ps()
                for n in range(nch):
                    k.MM(pbS.t[0:CH, n * CH:(n + 1) * CH], ktl[:, h, n * CH:(n + 1) * CH], qtl[:, h, n * CH:(n + 1) * CH],
                         True, True, [ktl, qtl], [pbS])
                P.op("dve", (lambda sc_, m_, p_: (lambda e: e.copy_predicated(sc_, m_, p_)))(scT[:, h, 0:T], maski[:, dr, 0:T], pbS.t[0:CH, 0:T]),
                     [pbS, maski], [scT])
                if h % 2 == 1:
                    yield
            if dr == 1:
                k.LD(oft[:, 0:nch, :], s_of.t[tok0:tok0 + T, :].rearrange("(c p) n -> p c n", p=CH), [oft], R=[s_of])

        def hgB(dr, ui, tok0, T, isctx, sset):
            nch = T // CH
            vtt, qtl, khT, scT, BD = vtts[sset], qtls[sset], khTs[sset], scTs[dr][sset], BDs[sset]
            oft = ofts[sset]
            if ui == 0:
                k.MEMSET("pool", Sst[:, dr], 0.0, [Sst])
            chunks = list(range(nch)) if dr == 0 else list(range(nch))[::-1]
            for ci, n in enumerate(chunks):
                sp = Spb[ci % 2]
                for h in range(4):
                    k.ACT(sp[:, h, :], Sst[:, dr, h, :], AF.Copy, [Sst, BD], [sp], scale=BD[:, h, 0, n:n + 1])
                po = k.ps()
                pk = k.ps()
                for h in range(4):
                    hs_ = slice(h * 128, (h + 1) * 128)
                    k.MM(po.t[0:CH, hs_], scT[:, h, n * CH:(n + 1) * CH], vtt[:, n, hs_], True, False, [scT, vtt], [po])
                    k.MM(po.t[0:CH, hs_], qtl[:, h, n * CH:(n + 1) * CH], sp[:, h, :], False, True, [qtl, sp], [po])
                    k.MM(pk.t[:, hs_], khT[:, h, n, :], vtt[:, n, hs_], True, True, [khT, vtt], [pk])
                for h in range(4):
                    hs_ = slice(h * 128, (h + 1) * 128)
                    k.STT(Sst[:, dr, h, :], Sst[:, dr, h, :], BD[:, h, 1, n:n + 1], pk.t[:, hs_], ALU.mult, ALU.add,
                          [Sst, BD, pk], [Sst])
                if dr == 0:
                    k.CP("act", ot[:, n, :], po.t[0:CH, 0:512], [po], [ot])
                else:
                    k.TT("dve", ot[:, n, :], po.t[0:CH, 0:512], oft[:, n, :], ALU.add, [po, oft], [ot])
                yield
            dst = s_of if dr == 0 else s_osum
            k.ST(dst.t[tok0:tok0 + T, :].rearrange("(c p) n -> p c n", p=CH), ot[:, 0:nch, :], [ot], [dst])
            if isctx and not CHAIN:
                k.CP("pool", Sctx[:, dr], Sst[:, dr], [Sst], [Sctx])
                k.MEMSET("pool", Sst[:, dr], 0.0, [Sst])

        def drain(g):
            for _ in g:
                pass

        def interleave(g1, g2):
            a1, a2 = g1 is not None, g2 is not None
            while a1 or a2:
                if a1:
                    try:
                        next(g1)
                    except StopIteration:
                        a1 = False
                if a2:
                    try:
                        next(g2)
                    except StopIteration:
                        a2 = False

        seq = []
        for dr in range(2):
            order = [units[0]] + (units[1:] if dr == 0 else units[1:][::-1])
            for ui, (tok0, T, isctx) in enumerate(order):
                seq.append((dr, ui, tok0, T, isctx))
        drain(hgA(*seq[0], 0))
        for i, item in enumerate(seq):
            nxt = hgA(*seq[i + 1], (i + 1) % 2) if i + 1 < len(seq) else None
            if nxt is not None and seq[i + 1][0] != item[0]:
                if not CHAIN:
                    k.ACT(dtot[:, item[0], :], carC[:], AF.Exp, [carC], [dtot])
            interleave(nxt, hgB(*item, i % 2))
        if not CHAIN:
            k.ACT(dtot[:, 1, :], carC[:], AF.Exp, [carC], [dtot])
        P.end_phase()
        es3.close()
        if STAGE < 4:
            esm.close()
            break

        if not CHAIN:
            esx = ExitStack()
            xs = P.sbuf("xs", [128, XW], F32, esx)
            xg = P.sbuf("xg", [128, 8, XW], F32, esx)
            dm1 = P.sbuf("dm1", [128, 8, 16], F32, esx)
            tS = P.sbuf("tS", [128, 128], F32, esx)
            tH = P.sbuf("tH", [128, 4], F32, esx)
            k.CP("pool", xs[:, 0:1024], Sst[:].rearrange("p a b c -> p (a b c)"), [Sst], [xs])
            k.CP("pool", xs[:, 1024:1032], dtot[:].rearrange("p a b -> p (a b)"), [dtot], [xs])
            k.CP("pool", xs[:, 1032:1040], hfin[:].rearrange("p a b -> p (a b)"), [hfin], [xs])
            k.CP("pool", xs[:, 1040:1048], atot[:].rearrange("p a b -> p (a b)"), [atot], [xs])
            k.ST(s_xsrc.t.ap(), xs[:], [xs], [s_xsrc])
            if USE_CC:
                P.custom("pool", (lambda a_, b_: (lambda e: e.collective_compute("AllGather", ALU.bypass, replica_groups=[list(range(8))],
                                                                               ins=[a_], outs=[b_])))(s_xsrc.t.ap().opt(), s_xdst.t.ap().opt()),
                         reads=[s_xsrc], writes=[s_xdst], inc=1)
            else:
                for r in range(8):
                    k.ST(s_xdst.t.ap()[r * 128:(r + 1) * 128, :], s_xsrc.t.ap(), [s_xsrc], [s_xdst])
            k.LD(xg[:], s_xdst.t.ap().rearrange("(r p) c -> p r c", p=128), [xg], R=[s_xdst])
            k.TS("dve", dm1[:, :, 0:8], xg[:, :, 1024:1032], -1.0, None, ALU.add, None, [xg], [dm1])
            k.TS("dve", dm1[:, :, 8:16], xg[:, :, 1040:1048], -1.0, None, ALU.add, None, [xg], [dm1])
            k.CP("pool", Sin[:], Sctx[:], [Sctx], [Sin])
            k.CP("pool", hin[:], hctx[:], [hctx], [hin])
            for dr in range(2):
                ranks = list(range(0, 7)) if dr == 0 else list(range(7, 0, -1))
                for i in ranks:
                    fl = flags[:, dr * 8 + i:dr * 8 + i + 1]
                    for h in range(4):
                        c0 = (dr * 4 + h) * 128
                        k.STT(tS[:], Sin[:, dr, h, :], dm1[:, i, dr * 4 + h:dr * 4 + h + 1], xg[:, i, c0:c0 + 128], ALU.mult, ALU.add,
                              [Sin, dm1, xg], [tS])
                        k.STT(Sin[:, dr, h, :], tS[:], fl, Sin[:, dr, h, :], ALU.mult, ALU.add, [tS, flags, Sin], [Sin])
                    k.TT("dve", tH[:], hin[:, dr, :], dm1[:, i, 8 + dr * 4:8 + dr * 4 + 4], ALU.mult, [hin, dm1], [tH])
                    k.TT("dve", tH[:], tH[:], xg[:, i, 1032 + dr * 4:1032 + dr * 4 + 4], ALU.add, [tH, xg], [tH])
                    k.STT(hin[:, dr, :], tH[:], fl, hin[:, dr, :], ALU.mult, ALU.add, [tH, flags, hin], [hin])
            k.CP("dve", Sinb[:], Sin[:], [Sin], [Sinb])
            if DEBUG:
                k.ST(dbg["sst"].t.ap(), Sst[:].rearrange("p a b c -> p (a b c)"), [Sst], [dbg["sst"]])
                k.ST(dbg["sctx"].t.ap(), Sctx[:].rearrange("p a b c -> p (a b c)"), [Sctx], [dbg["sctx"]])
                k.ST(dbg["sin"].t.ap(), Sin[:].rearrange("p a b c -> p (a b c)"), [Sin], [dbg["sin"]])
            P.end_phase()
            esx.close()
        if STAGE < 5:
            esm.close()
            break

        es4 = ExitStack()
        wob = P.sbuf("wob", [128, KD, D], BF16, es4)
        wst = P.sbuf("wst", [128, D], F32, es4)
        for kc in range(KD):
            if kc < 4:
                k.LD(wst[:], w_out.ap()[L, kc * 128:(kc + 1) * 128, :], [wst])
                k.TS("dve", wob[:, kc, :], wst[:], gnv[:, 0:1], None, ALU.mult, None, [wst, gnv], [wob])
            else:
                k.LD(wob[:, kc, :], w_out.ap()[L, kc * 128:(kc + 1) * 128, :], [wob], q="pool")
        osm = P.sbuf("osm", [64, 8, HGW], F32, es4)
        gtt = P.sbuf("gtt", [64, 8, HGW], BF16, es4)
        qsf = P.sbuf("qsf", [128, 4, 512], BF16, es4)
        qsb = P.sbuf("qsb", [128, 4, 512], BF16, es4)
        hst = P.sbuf("hst", [128, 4, 512], F32, es4)
        acf = P.sbuf("acf", [128, 4, 512], F32, es4)
        acb = P.sbuf("acb", [128, 4, 512], F32, es4)
        gat = P.sbuf("gat", [128, 4, 512], BF16, es4)
        xt3 = P.sbuf("xt3", [128, 4, D], F32, es4)
        ot3 = P.sbuf("ot3", [64, 8, HGW], F32, es4)
        mixb = P.sbuf("mixb", [64, 8, HGW], BF16, es4)
        mixT = P.sbuf("mixT", [128, KD, 512], BF16, es4)
        tA = P.sbuf("tA", [128, 512], F32, es4)
        tmp3 = P.sbuf("tmp3", [128, 512], F32, es4)
        junk3 = P.sbuf("junk3", [128, 512], BF16, es4)
        ssh = P.sbuf("ssh", [64, 32], F32, es4)
        ss2 = P.sbuf("ss2", [128, 4], F32, es4)
        for ui, (tok0, T, isctx) in enumerate(units):
            if isctx and last:
                continue
            nt = T // 128
            nch = T // CH
            j = 1 if isctx else 0
            k.LD(osm[:, 0:nch, :], s_osum.t[tok0:tok0 + T, :].rearrange("(c p) n -> p c n", p=CH), [osm], R=[s_osum])
            k.LD(gtt[:, 0:nch, :], s_g.t[tok0:tok0 + T, :].rearrange("(c p) n -> p c n", p=CH), [gtt], R=[s_g])
            k.LD(hst[:, :, 0:T], s_hsum.t[:, :, tok0:tok0 + T].rearrange("c p t -> p c t"), [hst], R=[s_hsum])
            k.LD(gat[:, :, 0:T], s_gate.t[:, :, tok0:tok0 + T].rearrange("c p t -> p c t"), [gat], R=[s_gate])
            fix = (not isctx) and (not CHAIN)
            if fix:
                k.LD(qsf[:, :, 0:T], s_qseg.t[0, :, :, tok0:tok0 + T].rearrange("h p t -> p h t"), [qsf], R=[s_qseg])
                k.LD(qsb[:, :, 0:T], s_qseg.t[1, :, :, tok0:tok0 + T].rearrange("h p t -> p h t"), [qsb], R=[s_qseg])
                k.LD(acf[:, :, 0:T], s_acum.t[0, :, :, tok0:tok0 + T].rearrange("c p t -> p c t"), [acf], R=[s_acum])
                k.LD(acb[:, :, 0:T], s_acum.t[1, :, :, tok0:tok0 + T].rearrange("c p t -> p c t"), [acb], R=[s_acum])
            src, srcbuf = x_src(L, tok0, T)
            k.LD(xt3[:, 0:nt, :], src.rearrange("(j p) d -> p j d", p=128), [xt3], R=[srcbuf])
            for n in range(nch):
                if fix:
                    pf = k.ps()
                    for h in range(4):
                        hs_ = slice(h * 128, (h + 1) * 128)
                        k.MM(pf.t[0:CH, hs_], qsf[:, h, n * CH:(n + 1) * CH], Sinb[:, 0, h, :], True, False, [qsf, Sinb], [pf])
                        k.MM(pf.t[0:CH, hs_], qsb[:, h, n * CH:(n + 1) * CH], Sinb[:, 1, h, :], False, True, [qsb, Sinb], [pf])
                    k.TT("dve", ot3[:, n, :], pf.t[0:CH, 0:512], osm[:, n, :], ALU.add, [pf, osm], [ot3])
                else:
                    k.CP("act", ot3[:, n, :], osm[:, n, :], [osm], [ot3])
            ov = ot3[:, 0:nch, :].rearrange("p n (h v) -> p (n h) v", v=128)
            sq = osm[:, 0:nch, :].rearrange("p n (h v) -> p (n h) v", v=128)
            k.TT("pool", sq, ov, ov, ALU.mult, [ot3], [osm])
            P.op("dve", (lambda o_, i_: (lambda e: e.tensor_reduce(out=o_, in_=i_, axis=AX.X, op=ALU.add)))(ssh[:, 0:nch * 4], sq), [osm], [ssh])
            k.TS("dve", ssh[:, 0:nch * 4], ssh[:, 0:nch * 4], 1.0 / 128, EPS, ALU.mult, ALU.add, [ssh], [ssh])
            k.ACT(ssh[:, 0:nch * 4], ssh[:, 0:nch * 4], AF.Sqrt, [ssh], [ssh])
            P.op("dve", (lambda o_: (lambda e: e.reciprocal(out=o_, in_=o_)))(ssh[:, 0:nch * 4]), [ssh], [ssh])
            k.TT("dve", ov, ov, ssh[:, 0:nch * 4].unsqueeze(2).to_broadcast([CH, nch * 4, 128]), ALU.mult, [ot3, ssh], [ot3])
            k.TT("pool", mixb[:, 0:nch, :], ot3[:, 0:nch, :], gtt[:, 0:nch, :], ALU.mult, [ot3, gtt], [mixb])
            for h in range(4):
                pbT = k.ps()
                pT = pbT.t[:, :].bitcast(BF16)
                for n in range(nch):
                    k.TR(pT[:, n * CH:(n + 1) * CH], mixb[:, n, h * 128:(h + 1) * 128], ident[0:CH, 0:CH], [mixb, ident], [pbT])
                k.CP("act", mixT[:, h, 0:T], pT[:, 0:T], [pbT], [mixT])
            for c in range(4):
                if fix:
                    k.STT(tA[:, 0:T], acf[:, c, 0:T], hin[:, 0, c:c + 1], hst[:, c, 0:T], ALU.mult, ALU.add, [acf, hin, hst], [tA])
                    k.STT(tA[:, 0:T], acb[:, c, 0:T], hin[:, 1, c:c + 1], tA[:, 0:T], ALU.mult, ALU.add, [acb, hin, tA], [tA])
                    k.TT("pool", mixT[:, 4 + c, 0:T], tA[:, 0:T], gat[:, c, 0:T], ALU.mult, [tA, gat], [mixT])
                else:
                    k.TT("pool", mixT[:, 4 + c, 0:T], hst[:, c, 0:T], gat[:, c, 0:T], ALU.mult, [hst, gat], [mixT])
            for jj in range(nt):
                pps = [k.ps(), k.ps()]
                for half in range(2):
                    for kc in range(KD):
                        k.MM(pps[half].t[:, 0:512], mixT[:, kc, jj * 128:(jj + 1) * 128], wob[:, kc, half * 512:(half + 1) * 512],
                             kc == 0, kc == KD - 1, [mixT, wob], [pps[half]])
                    k.ACT(junk3[:], pps[half].t[:, 0:512], AF.Square, [pps[half]], [junk3, ss2], accum=ss2[:, half:half + 1])
                k.TT("dve", ss2[:, 2:3], ss2[:, 0:1], ss2[:, 1:2], ALU.add, [ss2], [ss2])
                k.TS("dve", ss2[:, 2:3], ss2[:, 2:3], 1.0 / D, EPS, ALU.mult, ALU.add, [ss2], [ss2])
                k.ACT(ss2[:, 2:3], ss2[:, 2:3], AF.Sqrt, [ss2], [ss2])
                P.op("dve", (lambda o_: (lambda e: e.reciprocal(out=o_, in_=o_)))(ss2[:, 2:3]), [ss2], [ss2])
                for half in range(2):
                    hsl = slice(half * 512, (half + 1) * 512)
                    k.STT(tmp3[:], pps[half].t[:, 0:512], ss2[:, 2:3], grow[0][j][:, hsl], ALU.mult, ALU.mult,
                          [pps[half], ss2, grow[0][j]], [tmp3])
                    k.TT("dve", xt3[:, jj, hsl], xt3[:, jj, hsl], tmp3[:], ALU.add, [xt3, tmp3], [xt3])
            k.ST(s_xmid.t[tok0:tok0 + T, :].rearrange("(j p) d -> p j d", p=128), xt3[:, 0:nt, :], [xt3], [s_xmid])
        P.end_phase()
        es4.close()
        esm.close()
        if STAGE < 6:
            break

        es5 = ExitStack()
        wgb = P.sbuf("wgb", [128, KD, DFF], BF16, es5)
        wub = P.sbuf("wub", [128, KD, DFF], BF16, es5)
        wdb = P.sbuf("wdb", [128, NFF, D], BF16, es5)
        for kc in range(KD):
            k.LD(wgb[:, kc, :], w_gate.ap()[L, kc * 128:(kc + 1) * 128, :], [wgb], q="pool")
            k.LD(wub[:, kc, :], w_up.ap()[L, kc * 128:(kc + 1) * 128, :], [wub], q="pool")
        for jf in range(NFF):
            k.LD(wdb[:, jf, :], w_down.ap()[L, jf * 128:(jf + 1) * 128, :], [wdb], q="pool")
        xt5 = P.sbuf("xt5", [128, 2, D], F32, es5)
        fT5 = P.sbuf("fT5", [128, KD, 256], BF16, es5)
        hid = P.sbuf("hid", [128, NFF, 256], BF16, es5)
        sl5 = P.sbuf("sl5", [128, 256], F32, es5)
        tmp5 = P.sbuf("tmp5", [128, 512], F32, es5)
        junk5 = P.sbuf("junk5", [128, 512], BF16, es5)
        ss5 = P.sbuf("ss5", [128, 4], F32, es5)
        nb5 = norm_bufs(es5, "p5", 2)
        for tok0 in range(0, NT, 256):
            isctx = tok0 < NCTX
            if isctx and last:
                continue
            j = 1 if isctx else 0
            T = 256
            k.LD(xt5[:], s_xmid.t[tok0:tok0 + T, :].rearrange("(j p) d -> p j d", p=128), [xt5], R=[s_xmid])
            norm_mod(nb5, xt5, 2, 1, j, fT5)
            for jf in range(NFF):
                pg = k.ps()
                pu = k.ps()
                for kc in range(KD):
                    k.MM(pg.t[:, 0:T], wgb[:, kc, jf * 128:(jf + 1) * 128], fT5[:, kc, 0:T], kc == 0, kc == KD - 1, [wgb, fT5], [pg])
                for kc in range(KD):
                    k.MM(pu.t[:, 0:T], wub[:, kc, jf * 128:(jf + 1) * 128], fT5[:, kc, 0:T], kc == 0, kc == KD - 1, [wub, fT5], [pu])
                k.ACT(sl5[:], pg.t[:, 0:T], AF.Silu, [pg], [sl5])
                k.TT("dve", hid[:, jf, :], sl5[:], pu.t[:, 0:T], ALU.mult, [sl5, pu], [hid])
            for jj in range(2):
                pps = [k.ps(), k.ps()]
                for half in range(2):
                    for jf in range(NFF):
                        k.MM(pps[half].t[:, 0:512], hid[:, jf, jj * 128:(jj + 1) * 128], wdb[:, jf, half * 512:(half + 1) * 512],
                             jf == 0, jf == NFF - 1, [hid, wdb], [pps[half]])
                    k.ACT(junk5[:], pps[half].t[:, 0:512], AF.Square, [pps[half]], [junk5, ss5], accum=ss5[:, half:half + 1])
                k.TT("dve", ss5[:, 2:3], ss5[:, 0:1], ss5[:, 1:2], ALU.add, [ss5], [ss5])
                k.TS("dve", ss5[:, 2:3], ss5[:, 2:3], 1.0 / D, EPS, ALU.mult, ALU.add, [ss5], [ss5])
                k.ACT(ss5[:, 2:3], ss5[:, 2:3], AF.Sqrt, [ss5], [ss5])
                P.op("dve", (lambda o_: (lambda e: e.reciprocal(out=o_, in_=o_)))(ss5[:, 2:3]), [ss5], [ss5])
                for half in range(2):
                    hsl = slice(half * 512, (half + 1) * 512)
                    k.STT(tmp5[:], pps[half].t[:, 0:512], ss5[:, 2:3], grow[1][j][:, hsl], ALU.mult, ALU.mult,
                          [pps[half], ss5, grow[1][j]], [tmp5])
                    k.TT("dve", xt5[:, jj, hsl], xt5[:, jj, hsl], tmp5[:], ALU.add, [xt5, tmp5], [xt5])
            if last:
                k.ST(out_t.t.ap()[tok0 - NCTX:tok0 - NCTX + T, :].rearrange("(j p) d -> p j d", p=128), xt5[:], [xt5], [out_t])
            else:
                k.ST(s_xres.t[tok0:tok0 + T, :].rearrange("(j p) d -> p j d", p=128), xt5[:], [xt5], [s_xres])
        P.end_phase()
        es5.close()

    fin = []
    if DEBUG:
        pairs = [("qT", s_qT), ("fT", s_fT), ("v", s_v), ("g", s_g), ("u", s_u), ("gate", s_gate)]
        if STAGE >= 2:
            pairs += [("hsum", s_hsum), ("acum", s_acum)]
        if STAGE >= 3:
            pairs += [("osum", s_osum), ("qseg", s_qseg)]
        if STAGE >= 5:
            pairs += [("xdst", s_xdst)]
        if STAGE >= 6:
            pairs += [("xmid", s_xmid)]
        if STAGE >= 7:
            pairs += [("xres", s_xres)]
        P.barrier()
        for nm, sb in pairs:
            fin.append(k.ST(dbg[nm].t.ap(), sb.t.ap(), [sb], [dbg[nm]]))
        k.ST(dbg["misc"].t.ap()[:, 0:96], modfm[:].rearrange("p a b -> p (a b)"), [modfm], [dbg["misc"]])
        k.ST(dbg["misc"].t.ap()[:, 96:160], scsh[:].rearrange("p a b c -> p (a b c)"), [scsh], [dbg["misc"]])
        fin.append(k.ST(dbg["misc"].t.ap()[:, 160:168], c1v[:].rearrange("p a b -> p (a b)"), [c1v], [dbg["misc"]]))
        k.ST(dbg["misc"].t.ap()[:, 168:176], hctx[:].rearrange("p a b -> p (a b)"), [hctx], [dbg["misc"]])
        k.ST(dbg["misc"].t.ap()[:, 176:184], hfin[:].rearrange("p a b -> p (a b)"), [hfin], [dbg["misc"]])
        k.ST(dbg["misc"].t.ap()[:, 184:192], atot[:].rearrange("p a b -> p (a b)"), [atot], [dbg["misc"]])
        k.ST(dbg["misc"].t.ap()[:, 192:200], dtot[:].rearrange("p a b -> p (a b)"), [dtot], [dbg["misc"]])
        k.ST(dbg["misc"].t.ap()[:, 200:208], hin[:].rearrange("p a b -> p (a b)"), [hin], [dbg["misc"]])
    P.barrier()
    P.emit()
    return nc


def make_in_maps(inp):
    f = lambda a: np.ascontiguousarray(np.asarray(a, dtype=np.float32))
    x, c, ctx, c_ctx = f(inp["x"]), f(inp["c"]), f(inp["ctx"]), f(inp["c_ctx"])
    b_mod, norm_g = f(inp["b_mod"]), f(inp["norm_g"])
    common = {
        "w_mod": f(inp["w_mod"]),
        "bmod_fm": f(b_mod.reshape(DEPTH, 6, 8, 128).transpose(0, 3, 1, 2).reshape(DEPTH, 128, 48)),
        "bmod_row": f(b_mod.reshape(DEPTH, 1, 6 * D)),
        "normg_fm": f(norm_g.reshape(DEPTH, 4, 8, 128).transpose(0, 3, 1, 2).reshape(DEPTH, 128, 32)),
        "normg_row": norm_g,
        "w_in": f(inp["w_in"]),
        "lb_fm": f(f(inp["hg_lb_logits"]).reshape(DEPTH, 2, 4, 128).transpose(3, 0, 1, 2)),
        "gn_fm": f(f(inp["hg_gnorm"]).reshape(DEPTH, 128, 1)),
        "convw_fm": f(f(inp["rg_conv_w"]).reshape(DEPTH, 4, 4, 128).transpose(0, 3, 2, 1)),
        "convb_fm": f(f(inp["rg_conv_b"]).reshape(DEPTH, 4, 128).transpose(0, 2, 1)),
        "ba_fm": f(f(inp["rg_b_a"]).reshape(DEPTH, 2, 4, 128).transpose(0, 3, 1, 2)),
        "bx_fm": f(f(inp["rg_b_x"]).reshape(DEPTH, 2, 4, 128).transpose(0, 3, 1, 2)),
        "lam_fm": f(f(inp["rg_lambda"]).reshape(DEPTH, 2, 4, 128).transpose(0, 3, 1, 2)),
        "rg_w_a": f(inp["rg_w_a"]),
        "rg_w_x": f(inp["rg_w_x"]),
        "w_out": f(inp["w_out"]),
        "w_ffn_gate": f(inp["w_ffn_gate"]),
        "w_ffn_up": f(inp["w_ffn_up"]),
        "w_ffn_down": f(inp["w_ffn_down"]),
        "ident": np.eye(128, dtype=np.float32),
    }
    tri = np.triu(np.ones((64, 64), np.float32))
    common["masks"] = f(np.stack([np.tile(tri, (1, 8)), np.tile(tri.T, (1, 8))]))
    maps = []
    for core in range(8):
        if CHAIN:
            b, seg = core % 2, 0
        else:
            b, seg = core // 4, core % 4
        m = dict(common)
        m["x"] = f(x[b, seg * NLAT:(seg + 1) * NLAT])
        m["ctx"] = f(ctx[b])
        cv = np.stack([c[b].reshape(8, 128).T, c_ctx.reshape(8, 128).T], axis=-1)
        m["cvec"] = f(cv)
        fl = np.zeros((128, 16), np.float32)
        for r in range(8):
            same = (r // 4 == b) and not CHAIN
            fl[:, r] = 1.0 if (same and r % 4 < seg) else 0.0
            fl[:, 8 + r] = 1.0 if (same and r % 4 > seg) else 0.0
        m["flags"] = fl
        maps.append(m)
    return maps


_NC_CACHE = {}


def kernel(**inputs):
    if "nc" not in _NC_CACHE:
        _NC_CACHE["nc"] = build()
    nc = _NC_CACHE["nc"]
    maps = make_in_maps(inputs)
    res = run_bass_kernel_spmd(nc, maps, core_ids=list(range(8)))
    out = np.empty((2, 16384, D), np.float32)
    for core in range(8):
        if CHAIN:
            if core >= 2:
                continue
            b, seg = core, 0
        else:
            b, seg = core // 4, core % 4
        out[b, seg * NLAT:(seg + 1) * NLAT] = np.asarray(res.results[core]["out"], dtype=np.float32)
    return out
```

```python
import numpy as np
from contextlib import ExitStack
import concourse.bass as bass
import concourse.mybir as mybir
from concourse.bass_utils import run_bass_kernel_spmd

F32 = mybir.dt.float32
BF16 = mybir.dt.bfloat16
AF = mybir.ActivationFunctionType
ALU = mybir.AluOpType
AX = mybir.AxisListType

MODE = "whole2"
CHAIN = (MODE == "whole2")
D = 1024
KD = 8
NLAT = 16384 if CHAIN else 4096
NCTX = 256
NT = NLAT + NCTX
HGW = 512
RGW = 512
INC = 3584
DFF = 2816
NFF = 22
DEPTH = 2
EPS = 1e-6
CH = 64

DEBUG = False
STAGE = 99
USE_CC = True


class Buf:
    def __init__(self, prog, t, name):
        self.prog = prog
        self.t = t
        self.name = name
        self.last_w = None
        self.readers = []
        self.dma_sem = None
        self.dma_cnt = 0
        self.rd_sem = None
        self.rd_cnt = 0

    def __getitem__(self, idx):
        return self.t[idx]


class Prog:
    ENGS = ("pe", "act", "dve", "pool", "sp")

    def __init__(self, nc):
        self.nc = nc
        self.es = ExitStack()
        self.ops = {e: [] for e in self.ENGS}
        self.cnt = {e: 0 for e in self.ENGS}
        self.sems = {}
        for e in self.ENGS:
            self.sems[e] = self.es.enter_context(nc.semaphore("prog_" + e))
        self.known = {e: {} for e in self.ENGS}
        self.final_waits = []
        self._dma_sem_vals = {}
        self.sem_pool = []
        self.cur_bufs = []
        self.nsem = 0

    def sbuf(self, name, shape, dtype, es=None):
        self.nuniq = getattr(self, "nuniq", 0) + 1
        t = (es or self.es).enter_context(self.nc.sbuf_tensor("sb%d_%s" % (self.nuniq, name), list(shape), dtype))
        b = Buf(self, t, name)
        if es is not None:
            self.cur_bufs.append(b)
        return b

    def end_phase(self):
        self.barrier()
        for b in self.cur_bufs:
            if b.dma_sem is not None:
                self.sem_pool.append((b.dma_sem, b.dma_cnt))
                b.dma_sem = None
            if b.rd_sem is not None:
                self.sem_pool.append((b.rd_sem, b.rd_cnt))
                b.rd_sem = None
        self.cur_bufs = []

    def psum(self, name, shape, dtype):
        t = self.es.enter_context(self.nc.psum_tensor(name, list(shape), dtype))
        return Buf(self, t, name)

    def dram(self, name, shape, dtype):
        t = self.nc.dram_tensor(name, list(shape), dtype)
        return Buf(self, t, name)

    def _new_sem(self, name):
        if self.sem_pool:
            return self.sem_pool.pop()
        self.nsem += 1
        return (self.es.enter_context(self.nc.semaphore("s%d" % self.nsem)), 0)

    def _deps(self, eng, reads, writes):
        need = {}

        def add(ev):
            if ev is None:
                return
            key, val, src = ev
            if src == eng and eng == "pe":
                return
            if key not in need or need[key][0] < val:
                need[key] = (val, src)

        for b in reads:
            add(b.last_w)
        for b in writes:
            add(b.last_w)
            for r in b.readers:
                if r[2] == eng:
                    continue
                add(r)
        waits = []
        kn = self.known[eng]
        for key, (val, src) in need.items():
            if kn.get(key, 0) >= val:
                continue
            kn[key] = val
            waits.append((key, val))
        return waits

    def _semobj(self, key):
        if isinstance(key, str):
            return self.sems[key]
        return key

    def _mark(self, ev, reads, writes):
        for b in writes:
            b.last_w = ev
            b.readers = []
        for b in reads:
            if b in writes:
                continue
            b.readers.append(ev)
            if len(b.readers) > 48:
                latest = {}
                for r in b.readers:
                    k = r[0] if isinstance(r[0], str) else id(r[0])
                    if k not in latest or latest[k][1] < r[1]:
                        latest[k] = r
                b.readers = list(latest.values())

    def op(self, eng, fn, reads=(), writes=()):
        reads = [b for b in reads if b is not None]
        writes = [b for b in writes if b is not None]
        waits = self._deps(eng, reads, writes)
        self.cnt[eng] += 1
        ev = (eng, self.cnt[eng], eng)
        self.ops[eng].append(("op", waits, fn))
        self._mark(ev, reads, writes)
        return ev

    def dma(self, q, out_ap, in_ap, reads=(), writes=(), **kw):
        reads = [b for b in reads if b is not None]
        writes = [b for b in writes if b is not None]
        waits = self._deps(q, reads, writes)
        owner = writes[0] if writes else reads[0]
        if writes:
            if owner.dma_sem is None:
                owner.dma_sem, owner.dma_cnt = self._new_sem("dw_" + owner.name)
            owner.dma_cnt += 16
            sem, val = owner.dma_sem, owner.dma_cnt
        else:
            if owner.rd_sem is None:
                owner.rd_sem, owner.rd_cnt = self._new_sem("dr_" + owner.name)
            owner.rd_cnt += 16
            sem, val = owner.rd_sem, owner.rd_cnt
        ev = (sem, val, "dma")
        self._dma_sem_vals[sem] = val
        self.ops[q].append(("dma", waits, (out_ap, in_ap, sem, kw)))
        self._mark(ev, reads, writes)
        return ev

    def custom(self, q, fn, reads=(), writes=(), inc=16):
        reads = [b for b in reads if b is not None]
        writes = [b for b in writes if b is not None]
        waits = self._deps(q, reads, writes)
        owner = writes[0]
        if owner.dma_sem is None:
            owner.dma_sem, owner.dma_cnt = self._new_sem("dw_" + owner.name)
        owner.dma_cnt += inc
        sem, val = owner.dma_sem, owner.dma_cnt
        ev = (sem, val, "dma")
        self._dma_sem_vals[sem] = val
        self.ops[q].append(("custom", waits, (fn, sem, inc)))
        self._mark(ev, reads, writes)
        return ev

    def barrier(self):
        evs = [(e, self.cnt[e]) for e in self.ENGS if self.cnt[e] > 0]
        evs += list(self._dma_sem_vals.items())
        for e in self.ENGS:
            waits = []
            kn = self.known[e]
            for key, val in evs:
                if kn.get(key, 0) >= val:
                    continue
                kn[key] = val
                waits.append((key, val))
            if waits:
                self.ops[e].append(("wait", waits, None))

    def emit(self):
        nc = self.nc
        engmap = {"pe": "tensor", "act": "scalar", "dve": "vector", "pool": "gpsimd", "sp": "sync"}
        with nc.Block() as block:
            for e in self.ENGS:
                ops = self.ops[e]
                if not ops:
                    continue
                semself = self.sems[e]

                def body(engine, ops=ops, semself=semself):
                    for kind, waits, payload in ops:
                        for key, val in waits:
                            engine.wait_ge(self._semobj(key), val)
                        if kind == "op":
                            payload(engine).then_inc(semself, 1)
                        elif kind == "dma":
                            out_ap, in_ap, sem, kw = payload
                            engine.dma_start(out=out_ap, in_=in_ap, **kw).then_inc(sem, 16)
                        elif kind == "custom":
                            fn, sem, inc = payload
                            fn(engine).then_inc(sem, inc)

                getattr(block, engmap[e])(body)
        self.es.close()


class K:
    def __init__(self):
        nc = bass.Bass("TRN2", target_bir_lowering=False)
        self.nc = nc
        self.P = Prog(nc)
        self.ins = {}
        self.outs = {}
        self.psn = 0

    def inp(self, name, shape, dtype=F32):
        t = self.nc.dram_tensor(name, list(shape), dtype, kind="ExternalInput")
        self.ins[name] = t
        return t

    def outp(self, name, shape, dtype=F32):
        t = self.nc.dram_tensor(name, list(shape), dtype, kind="ExternalOutput")
        b = Buf(self.P, t, name)
        self.outs[name] = b
        return b

    def ACT(self, out, in_, func, R, W, bias=None, scale=None, accum=None):
        kw = {}
        if bias is not None:
            kw["bias"] = bias
        if scale is not None:
            kw["scale"] = scale
        if accum is not None:
            kw["accum_out"] = accum
        return self.P.op("act", lambda e: e.activation(out=out, in_=in_, func=func, **kw), R, W)

    def TS(self, eng, out, in0, s1, s2, op0, op1, R, W):
        if op1 is None:
            return self.P.op(eng, lambda e: e.tensor_scalar(out=out, in0=in0, scalar1=s1, scalar2=None, op0=op0), R, W)
        return self.P.op(eng, lambda e: e.tensor_scalar(out=out, in0=in0, scalar1=s1, scalar2=s2, op0=op0, op1=op1), R, W)

    def TT(self, eng, out, in0, in1, op, R, W):
        return self.P.op(eng, lambda e: e.tensor_tensor(out=out, in0=in0, in1=in1, op=op), R, W)

    def STT(self, out, in0, scalar, in1, op0, op1, R, W):
        return self.P.op("dve", lambda e: e.scalar_tensor_tensor(out=out, in0=in0, scalar=scalar, in1=in1, op0=op0, op1=op1), R, W)

    def MM(self, out, lhsT, rhs, start, stop, R, W):
        return self.P.op("pe", lambda e: e.matmul(out, lhsT=lhsT, rhs=rhs, start=start, stop=stop), R, W)

    def TR(self, out, in_, ident, R, W):
        return self.P.op("pe", lambda e: e.transpose(out, in_, ident), R, W)

    def CP(self, eng, out, in_, R, W):
        if eng == "act":
            return self.P.op("act", lambda e: e.copy(out=out, in_=in_), R, W)
        return self.P.op(eng, lambda e: e.tensor_copy(out=out, in_=in_), R, W)

    def SCAN(self, out, d0, d1, init, R, W):
        return self.P.op("dve", lambda e: e.tensor_tensor_scan(out=out, data0=d0, data1=d1, initial=init, op0=ALU.mult, op1=ALU.add), R, W)

    def MEMSET(self, eng, ap, val, W):
        return self.P.op(eng, lambda e: e.memset(ap, val), [], W)

    def LD(self, out, in_, W, R=(), q="sp"):
        return self.P.dma(q, out, in_, reads=list(R), writes=list(W))

    def ST(self, out, in_, R, W=(), q="sp"):
        return self.P.dma(q, out, in_, reads=list(R), writes=list(W))

    def ps(self):
        b = self.psb[self.psn % 8]
        self.psn += 1
        return b


def build(nlayers=DEPTH):
    k = K()
    nc, P = k.nc, k.P
    x_in = k.inp("x", [NLAT, D])
    ctx_in = k.inp("ctx", [NCTX, D])
    cvec = k.inp("cvec", [128, KD, 2])
    w_mod = k.inp("w_mod", [DEPTH, D, 6 * D])
    bmod_fm = k.inp("bmod_fm", [DEPTH, 128, 48])
    bmod_row = k.inp("bmod_row", [DEPTH, 1, 6 * D])
    normg_fm = k.inp("normg_fm", [DEPTH, 128, 32])
    normg_row = k.inp("normg_row", [DEPTH, 4, D])
    w_in = k.inp("w_in", [DEPTH, D, INC])
    lb_fm = k.inp("lb_fm", [128, DEPTH, 2, 4])
    gn_fm = k.inp("gn_fm", [DEPTH, 128, 1])
    convw_fm = k.inp("convw_fm", [DEPTH, 128, 4, 4])
    convb_fm = k.inp("convb_fm", [DEPTH, 128, 4])
    ba_fm = k.inp("ba_fm", [DEPTH, 128, 2, 4])
    bx_fm = k.inp("bx_fm", [DEPTH, 128, 2, 4])
    lam_fm = k.inp("lam_fm", [DEPTH, 128, 2, 4])
    rg_w_a = k.inp("rg_w_a", [DEPTH, 2, 8, 64, 64])
    rg_w_x = k.inp("rg_w_x", [DEPTH, 2, 8, 64, 64])
    w_out = k.inp("w_out", [DEPTH, D, D])
    w_gate = k.inp("w_ffn_gate", [DEPTH, D, DFF])
    w_up = k.inp("w_ffn_up", [DEPTH, D, DFF])
    w_down = k.inp("w_ffn_down", [DEPTH, DFF, D])
    flags_in = k.inp("flags", [128, 16])
    ident_in = k.inp("ident", [128, 128])
    masks_in = k.inp("masks", [2, 64, 512])
    out_t = k.outp("out", [NLAT, D])

    s_qT = P.dram("s_qT", [4, 128, NT], BF16)
    s_fT = P.dram("s_fT", [2, 4, 128, NT], F32)
    s_v = P.dram("s_v", [NT, HGW], BF16)
    s_g = P.dram("s_g", [NT, HGW], BF16)
    s_u = P.dram("s_u", [4, 128, NT], BF16)
    s_gate = P.dram("s_gate", [4, 128, NT], BF16)
    s_hsum = P.dram("s_hsum", [4, 128, NT], F32)
    s_hsumf = P.dram("s_hsumf", [4, 128, NT], F32)
    s_acum = P.dram("s_acum", [2, 4, 128, NT], F32)
    s_osum = P.dram("s_osum", [NT, HGW], F32)
    s_of = P.dram("s_of", [NT, HGW], F32)
    s_qseg = P.dram("s_qseg", [2, 4, 128, NT], BF16)
    s_xres = P.dram("s_xres", [NT, D], F32)
    s_xmid = P.dram("s_xmid", [NT, D], F32)
    s_grow = P.dram("s_grow", [4, 128, D], F32)
    XW = 1024 + 8 + 8 + 8
    s_xsrc = P.dram("s_xsrc", [128, XW], F32)
    s_xdst = P.dram("s_xdst", [8 * 128, XW], F32)
    s_xdst.dma_sem = P.es.enter_context(nc.semaphore("cc_sem"))
    s_xdst.dma_cnt = 0

    dbg = {}
    if DEBUG:
        dbg["qT"] = k.outp("d_qT", [4, 128, NT], BF16)
        dbg["fT"] = k.outp("d_fT", [2, 4, 128, NT], F32)
        dbg["v"] = k.outp("d_v", [NT, HGW], BF16)
        dbg["g"] = k.outp("d_g", [NT, HGW], BF16)
        dbg["u"] = k.outp("d_u", [4, 128, NT], BF16)
        dbg["gate"] = k.outp("d_gate", [4, 128, NT], BF16)
        dbg["hsum"] = k.outp("d_hsum", [4, 128, NT], F32)
        dbg["acum"] = k.outp("d_acum", [2, 4, 128, NT], F32)
        dbg["osum"] = k.outp("d_osum", [NT, HGW], F32)
        dbg["qseg"] = k.outp("d_qseg", [2, 4, 128, NT], BF16)
        dbg["xmid"] = k.outp("d_xmid", [NT, D], F32)
        dbg["xres"] = k.outp("d_xres", [NT, D], F32)
        dbg["xdst"] = k.outp("d_xdst", [8 * 128, XW], F32)
        dbg["misc"] = k.outp("d_misc", [128, 256], F32)
        dbg["sst"] = k.outp("d_sst", [128, 1024], F32)
        dbg["sctx"] = k.outp("d_sctx", [128, 1024], F32)
        dbg["sin"] = k.outp("d_sin", [128, 1024], F32)

    k.psb = [P.psum("psb%d" % i, [128, 512], F32) for i in range(8)]

    ident = P.sbuf("ident", [128, 128], BF16)
    maski = P.sbuf("maski", [64, 2, 512], mybir.dt.int32)
    ones_row = P.sbuf("ones_row", [1, 128], F32)
    ones_t = P.sbuf("ones_t", [128, 512], F32)
    flags = P.sbuf("flags", [128, 16], F32)
    cv_f = P.sbuf("cv_f", [128, KD, 2], F32)
    scbf = P.sbuf("scbf", [128, KD, 2], BF16)
    modfm = P.sbuf("modfm", [128, 48, 2], F32)
    bmodfm = P.sbuf("bmodfm", [128, 48], F32)
    gfm = P.sbuf("gfm", [128, 32], F32)
    scsh = P.sbuf("scsh", [128, 4, KD, 2], F32)
    lbt = P.sbuf("lbt", [128, DEPTH, 2, 4], F32)
    lbv = P.sbuf("lbv", [128, 2, 4], F32)
    omlv = P.sbuf("omlv", [128, 2, 4], F32)
    gnv = P.sbuf("gnv", [128, 1], F32)
    cwv = P.sbuf("cwv", [128, 4, 4], F32)
    cbv = P.sbuf("cbv", [128, 4], F32)
    bav = P.sbuf("bav", [128, 2, 4], F32)
    bxv = P.sbuf("bxv", [128, 2, 4], F32)
    c1v = P.sbuf("c1v", [128, 2, 4], F32)
    Sst = P.sbuf("Sst", [128, 2, 4, 128], F32)
    dtot = P.sbuf("dtot", [128, 2, 4], F32)
    hctx = P.sbuf("hctx", [128, 2, 4], F32)
    hfin = P.sbuf("hfin", [128, 2, 4], F32)
    atot = P.sbuf("atot", [128, 2, 4], F32)
    hin = P.sbuf("hin", [128, 2, 4], F32)

    ess = ExitStack()
    ident_f = P.sbuf("ident_f", [128, 128], F32, ess)
    masks_f = P.sbuf("masks_f", [64, 2, 512], F32, ess)
    k.LD(ident_f[:], ident_in.ap(), [ident_f])
    k.CP("dve", ident[:], ident_f[:], [ident_f], [ident])
    k.LD(masks_f[:], masks_in.ap().rearrange("a s t -> s a t"), [masks_f])
    k.CP("dve", maski[:], masks_f[:], [masks_f], [maski])
    k.MEMSET("pool", ones_row[:], 1.0, [ones_row])
    k.MEMSET("pool", ones_t[:], 1.0, [ones_t])
    k.LD(flags[:], flags_in.ap(), [flags])
    k.LD(cv_f[:], cvec.ap(), [cv_f])
    k.ACT(scbf[:], cv_f[:], AF.Silu, [cv_f], [scbf])
    k.LD(lbt[:], lb_fm.ap(), [lbt])

    P.end_phase()
    ess.close()

    units = [(0, NCTX, True)] + [(NCTX + 512 * i, 512, False) for i in range(NLAT // 512)]

    def x_src(L, tok0, T):
        if L == 0:
            if tok0 < NCTX:
                return ctx_in.ap()[tok0:tok0 + T, :], None
            return x_in.ap()[tok0 - NCTX:tok0 - NCTX + T, :], None
        return s_xres.t[tok0:tok0 + T, :], s_xres

    def norm_bufs(es, tag, ntmax):
        return (P.sbuf("ssq_" + tag, [128, 4], F32, es), P.sbuf("rstd_" + tag, [128, 4], F32, es),
                P.sbuf("junk_" + tag, [128, D], BF16, es), P.sbuf("xn_" + tag, [128, ntmax, D], BF16, es))

    def norm_mod(nb, xt, nt, a, j, hT):
        drain(norm_mod_g(nb, xt, nt, a, j, hT))

    def norm_mod_g(nb, xt, nt, a, j, hT):
        T = nt * 128
        ssq, rstd, junk, xn = nb
        for jj in range(nt):
            k.ACT(junk[:], xt[:, jj, :], AF.Square, [xt], [junk, ssq], accum=ssq[:, jj:jj + 1])
        k.TS("dve", rstd[:, 0:nt], ssq[:, 0:nt], 1.0 / D, EPS, ALU.mult, ALU.add, [ssq], [rstd])
        k.ACT(rstd[:, 0:nt], rstd[:, 0:nt], AF.Sqrt, [rstd], [rstd])
        P.op("dve", lambda e: e.reciprocal(out=rstd[:, 0:nt], in_=rstd[:, 0:nt]), [rstd], [rstd])
        yield
        for jj in range(nt):
            k.ACT(xn[:, jj, :], xt[:, jj, :], AF.Copy, [xt, rstd], [xn], scale=rstd[:, jj:jj + 1])
        yield
        yield
        for kc in range(KD):
            if kc == 4:
                yield
            pb = k.ps()
            pst = pb.t[:, :].bitcast(BF16)
            for jj in range(nt):
                k.TR(pst[:, jj * 128:(jj + 1) * 128], xn[:, jj, kc * 128:(kc + 1) * 128], ident[:], [xn, ident], [pb])
            k.TS("dve", hT[:, kc, 0:T], pst[:, 0:T], scsh[:, 2 * a, kc, j:j + 1], scsh[:, 2 * a + 1, kc, j:j + 1],
                 ALU.mult, ALU.add, [pb, scsh], [hT])

    def drain(g):
        for _ in g:
            pass

    def interleave(g1, g2):
        a1, a2 = g1 is not None, g2 is not None
        while a1 or a2:
            if a1:
                try:
                    next(g1)
                except StopIteration:
                    a1 = False
            if a2:
                try:
                    next(g2)
                except StopIteration:
                    a2 = False

    def pipeline(items, genA, genB):
        if not items:
            return
        drain(genA(0, items[0]))
        for i, it_ in enumerate(items):
            nxt = genA(i + 1, items[i + 1]) if i + 1 < len(items) else None
            interleave(nxt, genB(i, it_))

    for L in range(nlayers):
        last = (L == DEPTH - 1)
        es0 = ExitStack()
        wblk = P.sbuf("wblk", [128, KD, D], BF16, es0)
        rowt = P.sbuf("rowt", [1, 512], F32, es0)
        growt = P.sbuf("growt", [128, D], F32, es0)
        brow = P.sbuf("brow", [1, 6 * D], F32, es0)
        g1row = P.sbuf("g1row", [1, D], F32, es0)
        g3row = P.sbuf("g3row", [1, D], F32, es0)
        lamt = P.sbuf("lamt", [128, 2, 4], F32, es0)
        k.LD(bmodfm[:], bmod_fm.ap()[L], [bmodfm])
        k.LD(gfm[:], normg_fm.ap()[L], [gfm])
        k.LD(brow[:], bmod_row.ap()[L], [brow])
        k.LD(g1row[:], normg_row.ap()[L, 1:2, :], [g1row])
        k.LD(g3row[:], normg_row.ap()[L, 3:4, :], [g3row])
        k.LD(gnv[:], gn_fm.ap()[L], [gnv])
        k.LD(cwv[:], convw_fm.ap()[L], [cwv])
        k.LD(cbv[:], convb_fm.ap()[L], [cbv])
        k.LD(bav[:], ba_fm.ap()[L], [bav])
        k.LD(bxv[:], bx_fm.ap()[L], [bxv])
        k.LD(lamt[:], lam_fm.ap()[L], [lamt])
        k.ACT(c1v[:], lamt[:], AF.Exp, [lamt], [c1v], scale=-1.0)
        k.ACT(c1v[:], c1v[:], AF.Ln, [c1v], [c1v], bias=1.0)
        k.TS("dve", c1v[:], c1v[:], -8.0, None, ALU.mult, None, [c1v], [c1v])
        if L == 0:
            k.MEMSET("pool", lbv[:], 0.0, [lbv])
        else:
            k.TT("dve", lbv[:], lbt[:, 1], lbt[:, 0], ALU.subtract, [lbt], [lbv])
            k.ACT(lbv[:], lbv[:], AF.Sigmoid, [lbv], [lbv])
        k.TS("dve", omlv[:], lbv[:], -1.0, 1.0, ALU.mult, ALU.add, [lbv], [omlv])
        for n in range(6):
            k.LD(wblk[:], w_mod.ap()[L, :, n * D:(n + 1) * D].rearrange("(kc p) c -> p kc c", p=128), [wblk], q="pool")
            pb = k.ps()
            for kd in range(KD):
                for kc in range(KD):
                    k.MM(pb.t[:, kd * 2:kd * 2 + 2], wblk[:, kc, kd * 128:(kd + 1) * 128], scbf[:, kc, :],
                         kc == 0, kc == KD - 1, [wblk, scbf], [pb])
            k.TT("dve", modfm[:, n * 8:(n + 1) * 8, :], pb.t[:, 0:16].rearrange("p (a b) -> p a b", b=2),
                 bmodfm[:, n * 8:(n + 1) * 8].unsqueeze(2).to_broadcast([128, 8, 2]), ALU.add, [pb, bmodfm], [modfm])
            if n in (2, 5):
                a = 0 if n == 2 else 1
                grow_g = g1row if n == 2 else g3row
                for j in range(2):
                    for half in range(2):
                        pb2 = k.ps()
                        for kc in range(KD):
                            k.MM(pb2.t[0:1, 0:512], scbf[:, kc, j:j + 1], wblk[:, kc, half * 512:(half + 1) * 512],
                                 kc == 0, kc == KD - 1, [wblk, scbf], [pb2])
                        k.TT("dve", rowt[:], pb2.t[0:1, 0:512], brow[0:1, n * D + half * 512:n * D + (half + 1) * 512],
                             ALU.add, [pb2, brow], [rowt])
                        k.TT("dve", rowt[:], rowt[:], grow_g[0:1, half * 512:(half + 1) * 512], ALU.mult,
                             [rowt, grow_g], [rowt])
                        pb3 = k.ps()
                        k.MM(pb3.t[:, 0:512], ones_row[0:1, :], rowt[0:1, :], True, True, [ones_row, rowt], [pb3])
                        k.CP("act", growt[:, half * 512:(half + 1) * 512], pb3.t[:, 0:512], [pb3], [growt])
                    k.ST(s_grow.t[2 * a + j], growt[:], [growt], [s_grow])
        for a, (nsc, nsh, gi) in enumerate(((1, 0, 0), (4, 3, 2))):
            k.TS("dve", scsh[:, 2 * a], modfm[:, nsc * 8:(nsc + 1) * 8, :], 1.0, None, ALU.add, None, [modfm], [scsh])
            k.TT("dve", scsh[:, 2 * a], scsh[:, 2 * a], gfm[:, gi * 8:(gi + 1) * 8].unsqueeze(2).to_broadcast([128, 8, 2]),
                 ALU.mult, [scsh, gfm], [scsh])
            k.CP("dve", scsh[:, 2 * a + 1], modfm[:, nsh * 8:(nsh + 1) * 8, :], [modfm], [scsh])
        P.end_phase()
        es0.close()
        if STAGE < 1:
            break

        es1 = ExitStack()
        winb = P.sbuf("winb", [128, KD, INC], BF16, es1)
        for kc in range(KD):
            k.LD(winb[:, kc, :], w_in.ap()[L, kc * 128:(kc + 1) * 128, :], [winb], q="pool")
        xts = [P.sbuf("xt%d" % i, [128, 4, D], F32, es1) for i in range(1)]
        hTs = [P.sbuf("hT%d" % i, [128, KD, 512], BF16, es1) for i in range(2)]
        qs = P.sbuf("qs", [128, 4, 512], BF16, es1)
        sg = P.sbuf("sg", [128, 512], F32, es1)
        ft = P.sbuf("ft", [128, 2, 4, 512], F32, es1)
        uf = P.sbuf("uf", [128, 512], F32, es1)
        uc = P.sbuf("uc", [128, 512], F32, es1)
        ub = P.sbuf("ub", [128, 4, 512], BF16, es1)
        gb = P.sbuf("gb", [128, 4, 512], BF16, es1)
        vt = P.sbuf("vt", [64, 8, HGW], BF16, es1)
        gt = P.sbuf("gt", [64, 8, HGW], BF16, es1)
        nb1 = norm_bufs(es1, "p1", 4)
        def p1A(ui, unit):
            tok0, T, isctx = unit
            nt = T // 128
            j = 1 if isctx else 0
            xt = xts[0]
            src, srcbuf = x_src(L, tok0, T)
            k.LD(xt[:, 0:nt, :], src.rearrange("(j p) d -> p j d", p=128), [xt], R=[srcbuf])
            yield from norm_mod_g(nb1, xt, nt, 0, j, hTs[ui % 2])

        def p1B(ui, unit):
            tok0, T, isctx = unit
            nt = T // 128
            nch = T // CH
            j = 1 if isctx else 0
            hT = hTs[ui % 2]
            for ct in range(20):
                if ct < 12:
                    c0 = ct * 128
                else:
                    c0 = 5 * HGW + (ct - 12) * 128
                pb = k.ps()
                for kc in range(KD):
                    k.MM(pb.t[:, 0:T], winb[:, kc, c0:c0 + 128], hT[:, kc, 0:T], kc == 0, kc == KD - 1, [winb, hT], [pb])
                if ct < 4:
                    k.ACT(qs[:, ct, 0:T], pb.t[:, 0:T], AF.Silu, [pb], [qs])
                elif ct < 12:
                    dr, h = (ct - 4) // 4, (ct - 4) % 4
                    k.ACT(sg[:, 0:T], pb.t[:, 0:T], AF.Sigmoid, [pb], [sg])
                    k.TS("dve", ft[:, dr, h, 0:T], sg[:, 0:T], omlv[:, dr, h:h + 1], lbv[:, dr, h:h + 1], ALU.mult, ALU.add,
                         [sg, omlv, lbv], [ft])
                elif ct < 16:
                    c = ct - 12
                    k.CP("act", uf[:, 0:T], pb.t[:, 0:T], [pb], [uf])
                    RW = T if isctx else 64
                    ufv = uf[:, 0:T].rearrange("p (r w) -> p r w", w=RW)
                    ucv = uc[:, 0:T].rearrange("p (r w) -> p r w", w=RW)
                    k.TS("dve", uc[:, 0:T], uf[:, 0:T], cwv[:, c, 2:3], cbv[:, c:c + 1], ALU.mult, ALU.add, [uf, cwv, cbv], [uc])
                    for tap in (0, 1, 3):
                        s = tap - 2
                        if s < 0:
                            o_sl = ucv[:, :, -s:RW]
                            i_sl = ufv[:, :, 0:RW + s]
                        else:
                            o_sl = ucv[:, :, 0:RW - s]
                            i_sl = ufv[:, :, s:RW]
                        k.STT(o_sl, i_sl, cwv[:, c, tap:tap + 1], o_sl, ALU.mult, ALU.add, [uf, uc, cwv], [uc])
                    k.CP("act", ub[:, c, 0:T], uc[:, 0:T], [uc], [ub])
                else:
                    c = ct - 16
                    k.ACT(gb[:, c, 0:T], pb.t[:, 0:T], AF.Gelu_apprx_tanh, [pb], [gb])
                if ct % 2 == 1:
                    yield
            for cc in range(nch):
                for which in range(2):
                    c0 = (3 + which) * HGW
                    pb = k.ps()
                    for kc in range(KD):
                        k.MM(pb.t[0:CH, 0:512], hT[:, kc, cc * CH:(cc + 1) * CH], winb[:, kc, c0:c0 + 512],
                             kc == 0, kc == KD - 1, [winb, hT], [pb])
                    if which == 0:
                        k.CP("act", vt[:, cc, :], pb.t[0:CH, 0:512], [pb], [vt])
                    else:
                        k.ACT(gt[:, cc, :], pb.t[0:CH, 0:512], AF.Silu, [pb], [gt])
            k.ST(s_qT.t[:, :, tok0:tok0 + T].rearrange("h p t -> p h t"), qs[:, :, 0:T], [qs], [s_qT])
            for dr in range(2):
                k.ST(s_fT.t[dr, :, :, tok0:tok0 + T].rearrange("h p t -> p h t"), ft[:, dr, :, 0:T], [ft], [s_fT])
            k.ST(s_u.t[:, :, tok0:tok0 + T].rearrange("h p t -> p h t"), ub[:, :, 0:T], [ub], [s_u])
            k.ST(s_gate.t[:, :, tok0:tok0 + T].rearrange("h p t -> p h t"), gb[:, :, 0:T], [gb], [s_gate])
            k.ST(s_v.t[tok0:tok0 + T, :].rearrange("(c p) n -> p c n", p=CH), vt[:, 0:nch, :], [vt], [s_v])
            k.ST(s_g.t[tok0:tok0 + T, :].rearrange("(c p) n -> p c n", p=CH), gt[:, 0:nch, :], [gt], [s_g])
        pipeline(units, p1A, p1B)
        P.end_phase()
        es1.close()
        if STAGE < 2:
            break


        esm = ExitStack()
        Sst = P.sbuf("Sst", [128, 2, 4, 128], F32, esm)
        Sctx = P.sbuf("Sctx", [128, 2, 4, 128], F32, esm)
        Sin = P.sbuf("Sin", [128, 2, 4, 128], F32, esm)
        Sinb = P.sbuf("Sinb", [128, 2, 4, 128], BF16, esm)
        es2 = ExitStack()
        wbd = P.sbuf("wbd", [128, 2, 2, 4, 128], BF16, es2)
        wstage = P.sbuf("wstage", [128, 2, 2, 4, 128], F32, es2)
        zeros_t = P.sbuf("zeros_t", [128, 512], F32, es2)
        k.MEMSET("pool", zeros_t[:], 0.0, [zeros_t])
        k.MEMSET("pool", wstage[:], 0.0, [wstage])
        for gi, wsrc in enumerate((rg_w_a, rg_w_x)):
            for dr in range(2):
                for half in range(2):
                    src = wsrc.ap()[L, dr].rearrange("(ct h) i j -> h i ct j", h=2)[half]
                    k.LD(wstage[half * 64:(half + 1) * 64, gi, dr, :, half * 64:(half + 1) * 64], src, [wstage])
        k.CP("dve", wbd[:], wstage[:], [wstage], [wbd])
        uts = [P.sbuf("ut%d" % i, [128, 4, 512], BF16, es2) for i in range(2)]
        rt = P.sbuf("rt", [128, 4, 512], F32, es2)
        it = P.sbuf("it", [128, 4, 512], F32, es2)
        at = P.sbuf("at", [128, 4, 512], F32, es2)
        a2t = P.sbuf("a2t", [128, 4, 512], F32, es2)
        bxt = P.sbuf("bxt", [128, 4, 512], F32, es2)
        hl = P.sbuf("hl", [128, 4, 512], F32, es2)
        ac = P.sbuf("ac", [128, 4, 512], F32, es2)
        hss = [P.sbuf("hs%d" % i, [128, 4, 512], F32, es2) for i in range(2)]
        car_h = P.sbuf("car_h", [128, 4], F32, es2)
        car_a = P.sbuf("car_a", [128, 4], F32, es2)
        for dr in range(2):
            order = [units[0]] + (units[1:] if dr == 0 else units[1:][::-1])

            def rgA(ui, unit, dr=dr):
                tok0, T, isctx = unit
                k.LD(uts[ui % 2][:, :, 0:T], s_u.t[:, :, tok0:tok0 + T].rearrange("c p t -> p c t"), [uts[ui % 2]], R=[s_u])
                if dr == 1:
                    k.LD(hss[ui % 2][:, :, 0:T], s_hsumf.t[:, :, tok0:tok0 + T].rearrange("c p t -> p c t"), [hss[ui % 2]], R=[s_hsumf])
                yield

            def rgB(ui, unit, dr=dr):
                tok0, T, isctx = unit
                ut, hs = uts[ui % 2], hss[ui % 2]
                fresh = (ui == 0) if CHAIN else (ui <= 1)
                pbs = []
                for c in range(4):
                    for gi in range(2):
                        pb = k.ps()
                        k.MM(pb.t[:, 0:T], wbd[:, gi, dr, c, :], ut[:, c, 0:T], True, True, [wbd, ut], [pb])
                        pbs.append(pb)
                for c in range(4):
                    k.ACT(rt[:, c, 0:T], pbs[2 * c].t[:, 0:T], AF.Sigmoid, [pbs[2 * c], bav], [rt], bias=bav[:, dr, c:c + 1])
                    k.ACT(it[:, c, 0:T], pbs[2 * c + 1].t[:, 0:T], AF.Sigmoid, [pbs[2 * c + 1], bxv], [it], bias=bxv[:, dr, c:c + 1])
                for c in range(4):
                    k.ACT(at[:, c, 0:T], rt[:, c, 0:T], AF.Exp, [rt, c1v], [at], scale=c1v[:, dr, c:c + 1])
                k.ACT(a2t[:, :, 0:T], at[:, :, 0:T], AF.Square, [at], [a2t])
                k.ACT(a2t[:, :, 0:T], a2t[:, :, 0:T], AF.Sqrt, [a2t], [a2t], scale=-1.0, bias=1.0)
                k.TT("pool", it[:, :, 0:T], it[:, :, 0:T], ut[:, :, 0:T], ALU.mult, [it, ut], [it])
                k.TT("dve", bxt[:, :, 0:T], it[:, :, 0:T], a2t[:, :, 0:T], ALU.mult, [it, a2t], [bxt])
                for c in range(4):
                    ih = 0.0 if fresh else car_h[:, c:c + 1]
                    ia = 1.0 if fresh else car_a[:, c:c + 1]
                    if dr == 0:
                        k.SCAN(hl[:, c, 0:T], at[:, c, 0:T], bxt[:, c, 0:T], ih, [at, bxt, car_h], [hl])
                        if not CHAIN:
                            k.SCAN(ac[:, c, 0:T], at[:, c, 0:T], zeros_t[:, 0:T], ia, [at, zeros_t, car_a], [ac])
                        lastcol = slice(T - 1, T)
                    else:
                        k.SCAN(hl[:, c, T - 1::-1] if False else hl[:, c, 0:T][:, ::-1], at[:, c, 0:T][:, ::-1], bxt[:, c, 0:T][:, ::-1], ih,
                               [at, bxt, car_h], [hl])
                        if not CHAIN:
                            k.SCAN(ac[:, c, 0:T][:, ::-1], at[:, c, 0:T][:, ::-1], zeros_t[:, 0:T], ia, [at, zeros_t, car_a], [ac])
                        lastcol = slice(0, 1)
                    if isctx:
                        k.CP("act", hctx[:, dr, c:c + 1], hl[:, c, lastcol], [hl], [hctx])
                    if CHAIN or not isctx:
                        k.CP("act", car_h[:, c:c + 1], hl[:, c, lastcol], [hl], [car_h])
                        if not CHAIN:
                            k.CP("act", car_a[:, c:c + 1], ac[:, c, lastcol], [ac], [car_a])
                if dr == 0:
                    k.ST(s_hsumf.t[:, :, tok0:tok0 + T].rearrange("c p t -> p c t"), hl[:, :, 0:T], [hl], [s_hsumf])
                else:
                    k.TT("dve", hl[:, :, 0:T], hl[:, :, 0:T], hs[:, :, 0:T], ALU.add, [hl, hs], [hl])
                    k.ST(s_hsum.t[:, :, tok0:tok0 + T].rearrange("c p t -> p c t"), hl[:, :, 0:T], [hl], [s_hsum])
                if not CHAIN:
                    k.ST(s_acum.t[dr, :, :, tok0:tok0 + T].rearrange("c p t -> p c t"), ac[:, :, 0:T], [ac], [s_acum])
                yield

            pipeline(order, rgA, rgB)
            if not CHAIN:
                k.CP("pool", hfin[:, dr, :], car_h[:], [car_h], [hfin])
                k.CP("pool", atot[:, dr, :], car_a[:], [car_a], [atot])
        P.end_phase()
        es2.close()
        if STAGE < 3:
            esm.close()
            break

        es3 = ExitStack()
        fTt = P.sbuf("fTt", [128, 4, 512], F32, es3)
        qTt = P.sbuf("qTt", [128, 4, 512], BF16, es3)
        kTt = P.sbuf("kTt", [128, 4, 512], F32, es3)
        Ct = P.sbuf("Ct", [128, 4, 512], F32, es3)
        crel = P.sbuf("crel", [128, 4, 512], F32, es3)
        e1 = P.sbuf("e1", [128, 4, 512], F32, es3)
        ktl = P.sbuf("ktl", [128, 4, 512], BF16, es3)
        khl = P.sbuf("khl", [128, 4, 512], BF16, es3)
        qsg = None if CHAIN else P.sbuf("qsg", [128, 4, 512], BF16, es3)
        vtts = [P.sbuf("vtt%d" % i, [64, 8, HGW], BF16, es3) for i in range(2)]
        qtls = [P.sbuf("qtl%d" % i, [128, 4, 512], BF16, es3) for i in range(2)]
        khTs = [P.sbuf("khT%d" % i, [64, 4, 8, 128], BF16, es3) for i in range(2)]
        scTs = [[P.sbuf("scT%d%d" % (d_, i), [64, 4, 512], BF16, es3) for i in range(2)] for d_ in range(2)]
        BDs = [P.sbuf("BD%d" % i, [128, 4, 2, 8], F32, es3) for i in range(2)]
        for d_ in range(2):
            for i in range(2):
                k.MEMSET("pool", scTs[d_][i][:], 0.0, [scTs[d_][i]])
        ot = P.sbuf("ot", [64, 8, HGW], F32, es3)
        ofts = [P.sbuf("oft%d" % i, [64, 8, HGW], F32, es3) for i in range(2)]
        Spb = [P.sbuf("Spb%d" % i, [128, 4, 128], BF16, es3) for i in range(2)]
        carC = P.sbuf("carC", [128, 4], F32, es3)
        cprev = P.sbuf("cprev", [128, 4, 8], F32, es3)
        dif = P.sbuf("dif", [128, 4, 2, 8], F32, es3)

        def hgA(dr, ui, tok0, T, isctx, sset):
            fresh = (ui == 0) if CHAIN else (ui <= 1)
            nch = T // CH
            vtt, qtl, khT, scT, BD = vtts[sset], qtls[sset], khTs[sset], scTs[dr][sset], BDs[sset]
            oft = ofts[sset]
            k.LD(fTt[:, :, 0:T], s_fT.t[dr, :, :, tok0:tok0 + T].rearrange("h p t -> p h t"), [fTt], R=[s_fT])
            k.LD(qTt[:, :, 0:T], s_qT.t[:, :, tok0:tok0 + T].rearrange("h p t -> p h t"), [qTt], R=[s_qT])
            k.LD(vtt[:, 0:nch, :], s_v.t[tok0:tok0 + T, :].rearrange("(c p) n -> p c n", p=CH), [vtt], R=[s_v])
            k.ACT(kTt[:, :, 0:T], fTt[:, :, 0:T], AF.Copy, [fTt], [kTt], scale=-1.0, bias=1.0)
            k.ACT(fTt[:, :, 0:T], fTt[:, :, 0:T], AF.Ln, [fTt], [fTt])
            C4 = Ct[:, :, 0:T].rearrange("p h (n w) -> p h n w", w=CH)
            edge = 0 if dr == 0 else nch - 1
            if fresh:
                k.MEMSET("dve", cprev[:, :, edge:edge + 1], 0.0, [cprev])
            else:
                k.CP("act", cprev[:, :, edge:edge + 1], carC[:].unsqueeze(2), [carC], [cprev])
            for h in range(4):
                init = 0.0 if fresh else carC[:, h:h + 1]
                if dr == 0:
                    k.SCAN(Ct[:, h, 0:T], ones_t[:, 0:T], fTt[:, h, 0:T], init, [ones_t, fTt, carC], [Ct])
                else:
                    k.SCAN(Ct[:, h, 0:T][:, ::-1], ones_t[:, 0:T], fTt[:, h, 0:T][:, ::-1], init, [ones_t, fTt, carC], [Ct])
            if dr == 0:
                k.CP("act", carC[:].unsqueeze(2), Ct[:, :, T - 1:T], [Ct], [carC])
                Aanc = C4[:, :, :, 31]
                Cend = C4[:, :, :, 63]
                if nch > 1:
                    k.CP("act", cprev[:, :, 1:nch], C4[:, :, 0:nch - 1, 63], [Ct], [cprev])
            else:
                k.CP("act", carC[:].unsqueeze(2), Ct[:, :, 0:1], [Ct], [carC])
                Aanc = C4[:, :, :, 32]
                Cend = C4[:, :, :, 0]
                if nch > 1:
                    k.CP("act", cprev[:, :, 0:nch - 1], C4[:, :, 1:nch, 0], [Ct], [cprev])
            k.TT("dve", dif[:, :, 0, 0:nch], Aanc, cprev[:, :, 0:nch], ALU.subtract, [Ct, cprev], [dif])
            k.TT("dve", dif[:, :, 1, 0:nch], Cend, cprev[:, :, 0:nch], ALU.subtract, [Ct, cprev], [dif])
            k.ACT(BD[:, :, :, 0:nch], dif[:, :, :, 0:nch], AF.Exp, [dif], [BD])
            yield
            cr4 = crel[:, :, 0:T].rearrange("p h (n w) -> p h n w", w=CH)
            e14 = e1[:, :, 0:T].rearrange("p h (n w) -> p h n w", w=CH)
            k.TT("dve", cr4, C4, Aanc.unsqueeze(3).to_broadcast([128, 4, nch, CH]), ALU.subtract, [Ct], [crel])
            k.ACT(e1[:, :, 0:T], crel[:, :, 0:T], AF.Exp, [crel], [e1])
            k.ACT(crel[:, :, 0:T], crel[:, :, 0:T], AF.Exp, [crel], [crel], scale=-1.0)
            k.TT("pool", qtl[:, :, 0:T], qTt[:, :, 0:T], e1[:, :, 0:T], ALU.mult, [qTt, e1], [qtl])
            k.TT("dve", ktl[:, :, 0:T], kTt[:, :, 0:T], crel[:, :, 0:T], ALU.mult, [kTt, crel], [ktl])
            yield
            if not isctx and not CHAIN:
                k.ACT(e1[:, :, 0:T], Ct[:, :, 0:T], AF.Exp, [Ct], [e1])
                k.TT("pool", qsg[:, :, 0:T], qTt[:, :, 0:T], e1[:, :, 0:T], ALU.mult, [qTt, e1], [qsg])
                k.ST(s_qseg.t[dr, :, :, tok0:tok0 + T].rearrange("h p t -> p h t"), qsg[:, :, 0:T], [qsg], [s_qseg])
            k.TT("dve", e14, C4, Cend.unsqueeze(3).to_broadcast([128, 4, nch, CH]), ALU.subtract, [Ct], [e1])
            k.ACT(e1[:, :, 0:T], e1[:, :, 0:T], AF.Exp, [e1], [e1], scale=-1.0)
            k.TT("pool", khl[:, :, 0:T], kTt[:, :, 0:T], e1[:, :, 0:T], ALU.mult, [kTt, e1], [khl])
            yield
            for h in range(4):
                pbT = k.ps()
                pT = pbT.t[:, :].bitcast(BF16)
                for n in range(nch):
                    k.TR(pT[0:CH, n * 128:(n + 1) * 128], khl[:, h, n * CH:(n + 1) * CH], ident[:], [khl, ident], [pbT])
                k.CP("act", khT[:, h, 0:nch, :], pT[0:CH, 0:nch * 128].rearrange("p (n k) -> p n k", k=128), [pbT], [khT])
                pbS = k.ps()
                for n in range(nch):
                    k.MM(pbS.t[0:CH, n * CH:(n + 1) * CH], ktl[:, h, n * CH:(n + 1) * CH], qtl[:, h, n * CH:(n + 1) * CH],
                         True, True, [ktl, qtl], [pbS])
                P.op("dve", (lambda sc_, m_, p_: (lambda e: e.copy_predicated(sc_, m_, p_)))(scT[:, h, 0:T], maski[:, dr, 0:T], pbS.t[0:CH, 0:T]),
                     [pbS, maski], [scT])
                if h % 2 == 1:
                    yield
            if dr == 1:
                k.LD(oft[:, 0:nch, :], s_of.t[tok0:tok0 + T, :].rearrange("(c p) n -> p c n", p=CH), [oft], R=[s_of])

        def hgB(dr, ui, tok0, T, isctx, sset):
            nch = T // CH
            vtt, qtl, khT, scT, BD = vtts[sset], qtls[sset], khTs[sset], scTs[dr][sset], BDs[sset]
            oft = ofts[sset]
            if ui == 0:
                k.MEMSET("pool", Sst[:, dr], 0.0, [Sst])
            chunks = list(range(nch)) if dr == 0 else list(range(nch))[::-1]
            for ci, n in enumerate(chunks):
                sp = Spb[ci % 2]
                for h in range(4):
                    k.ACT(sp[:, h, :], Sst[:, dr, h, :], AF.Copy, [Sst, BD], [sp], scale=BD[:, h, 0, n:n + 1])
                po = k.ps()
                pk = k.ps()
                for h in range(4):
                    hs_ = slice(h * 128, (h + 1) * 128)
                    k.MM(po.t[0:CH, hs_], scT[:, h, n * CH:(n + 1) * CH], vtt[:, n, hs_], True, False, [scT, vtt], [po])
                    k.MM(po.t[0:CH, hs_], qtl[:, h, n * CH:(n + 1) * CH], sp[:, h, :], False, True, [qtl, sp], [po])
                    k.MM(pk.t[:, hs_], khT[:, h, n, :], vtt[:, n, hs_], True, True, [khT, vtt], [pk])
                for h in range(4):
                    hs_ = slice(h * 128, (h + 1) * 128)
                    k.STT(Sst[:, dr, h, :], Sst[:, dr, h, :], BD[:, h, 1, n:n + 1], pk.t[:, hs_], ALU.mult, ALU.add,
                          [Sst, BD, pk], [Sst])
                if dr == 0:
                    k.CP("act", ot[:, n, :], po.t[0:CH, 0:512], [po], [ot])
                else:
                    k.TT("dve", ot[:, n, :], po.t[0:CH, 0:512], oft[:, n, :], ALU.add, [po, oft], [ot])
                yield
            dst = s_of if dr == 0 else s_osum
            k.ST(dst.t[tok0:tok0 + T, :].rearrange("(c p) n -> p c n", p=CH), ot[:, 0:nch, :], [ot], [dst])
            if isctx and not CHAIN:
                k.CP("pool", Sctx[:, dr], Sst[:, dr], [Sst], [Sctx])
                k.MEMSET("pool", Sst[:, dr], 0.0, [Sst])

        def drain(g):
            for _ in g:
                pass

        def interleave(g1, g2):
            a1, a2 = g1 is not None, g2 is not None
            while a1 or a2:
                if a1:
                    try:
                        next(g1)
                    except StopIteration:
                        a1 = False
                if a2:
                    try:
                        next(g2)
                    except StopIteration:
                        a2 = False

        seq = []
        for dr in range(2):
            order = [units[0]] + (units[1:] if dr == 0 else units[1:][::-1])
            for ui, (tok0, T, isctx) in enumerate(order):
                seq.append((dr, ui, tok0, T, isctx))
        drain(hgA(*seq[0], 0))
        for i, item in enumerate(seq):
            nxt = hgA(*seq[i + 1], (i + 1) % 2) if i + 1 < len(seq) else None
            if nxt is not None and seq[i + 1][0] != item[0]:
                if not CHAIN:
                    k.ACT(dtot[:, item[0], :], carC[:], AF.Exp, [carC], [dtot])
            interleave(nxt, hgB(*item, i % 2))
        if not CHAIN:
            k.ACT(dtot[:, 1, :], carC[:], AF.Exp, [carC], [dtot])
        P.end_phase()
        es3.close()
        if STAGE < 4:
            esm.close()
            break

        if not CHAIN:
            esx = ExitStack()
            xs = P.sbuf("xs", [128, XW], F32, esx)
            xg = P.sbuf("xg", [128, 8, XW], F32, esx)
            dm1 = P.sbuf("dm1", [128, 8, 16], F32, esx)
            tS = P.sbuf("tS", [128, 128], F32, esx)
            tH = P.sbuf("tH", [128, 4], F32, esx)
            k.CP("pool", xs[:, 0:1024], Sst[:].rearrange("p a b c -> p (a b c)"), [Sst], [xs])
            k.CP("pool", xs[:, 1024:1032], dtot[:].rearrange("p a b -> p (a b)"), [dtot], [xs])
            k.CP("pool", xs[:, 1032:1040], hfin[:].rearrange("p a b -> p (a b)"), [hfin], [xs])
            k.CP("pool", xs[:, 1040:1048], atot[:].rearrange("p a b -> p (a b)"), [atot], [xs])
            k.ST(s_xsrc.t.ap(), xs[:], [xs], [s_xsrc])
            if USE_CC:
                P.custom("pool", (lambda a_, b_: (lambda e: e.collective_compute("AllGather", ALU.bypass, replica_groups=[list(range(8))],
                                                                               ins=[a_], outs=[b_])))(s_xsrc.t.ap().opt(), s_xdst.t.ap().opt()),
                         reads=[s_xsrc], writes=[s_xdst], inc=1)
            else:
                for r in range(8):
                    k.ST(s_xdst.t.ap()[r * 128:(r + 1) * 128, :], s_xsrc.t.ap(), [s_xsrc], [s_xdst])
            k.LD(xg[:], s_xdst.t.ap().rearrange("(r p) c -> p r c", p=128), [xg], R=[s_xdst])
            k.TS("dve", dm1[:, :, 0:8], xg[:, :, 1024:1032], -1.0, None, ALU.add, None, [xg], [dm1])
            k.TS("dve", dm1[:, :, 8:16], xg[:, :, 1040:1048], -1.0, None, ALU.add, None, [xg], [dm1])
            k.CP("pool", Sin[:], Sctx[:], [Sctx], [Sin])
            k.CP("pool", hin[:], hctx[:], [hctx], [hin])
            for dr in range(2):
                ranks = list(range(0, 7)) if dr == 0 else list(range(7, 0, -1))
                for i in ranks:
                    fl = flags[:, dr * 8 + i:dr * 8 + i + 1]
                    for h in range(4):
                        c0 = (dr * 4 + h) * 128
                        k.STT(tS[:], Sin[:, dr, h, :], dm1[:, i, dr * 4 + h:dr * 4 + h + 1], xg[:, i, c0:c0 + 128], ALU.mult, ALU.add,
                              [Sin, dm1, xg], [tS])
                        k.STT(Sin[:, dr, h, :], tS[:], fl, Sin[:, dr, h, :], ALU.mult, ALU.add, [tS, flags, Sin], [Sin])
                    k.TT("dve", tH[:], hin[:, dr, :], dm1[:, i, 8 + dr * 4:8 + dr * 4 + 4], ALU.mult, [hin, dm1], [tH])
                    k.TT("dve", tH[:], tH[:], xg[:, i, 1032 + dr * 4:1032 + dr * 4 + 4], ALU.add, [tH, xg], [tH])
                    k.STT(hin[:, dr, :], tH[:], fl, hin[:, dr, :], ALU.mult, ALU.add, [tH, flags, hin], [hin])
            k.CP("dve", Sinb[:], Sin[:], [Sin], [Sinb])
            if DEBUG:
                k.ST(dbg["sst"].t.ap(), Sst[:].rearrange("p a b c -> p (a b c)"), [Sst], [dbg["sst"]])
                k.ST(dbg["sctx"].t.ap(), Sctx[:].rearrange("p a b c -> p (a b c)"), [Sctx], [dbg["sctx"]])
                k.ST(dbg["sin"].t.ap(), Sin[:].rearrange("p a b c -> p (a b c)"), [Sin], [dbg["sin"]])
            P.end_phase()
            esx.close()
        if STAGE < 5:
            esm.close()
            break

        es4 = ExitStack()
        wob = P.sbuf("wob", [128, KD, D], BF16, es4)
        wst = P.sbuf("wst", [128, D], F32, es4)
        for kc in range(KD):
            if kc < 4:
                k.LD(wst[:], w_out.ap()[L, kc * 128:(kc + 1) * 128, :], [wst])
                k.TS("dve", wob[:, kc, :], wst[:], gnv[:, 0:1], None, ALU.mult, None, [wst, gnv], [wob])
            else:
                k.LD(wob[:, kc, :], w_out.ap()[L, kc * 128:(kc + 1) * 128, :], [wob], q="pool")
        osm = P.sbuf("osm", [64, 8, HGW], F32, es4)
        gtt = P.sbuf("gtt", [64, 8, HGW], BF16, es4)
        qsf = P.sbuf("qsf", [128, 4, 512], BF16, es4)
        qsb = P.sbuf("qsb", [128, 4, 512], BF16, es4)
        hst = P.sbuf("hst", [128, 4, 512], F32, es4)
        acf = P.sbuf("acf", [128, 4, 512], F32, es4)
        acb = P.sbuf("acb", [128, 4, 512], F32, es4)
        gat = P.sbuf("gat", [128, 4, 512], BF16, es4)
        xt3 = P.sbuf("xt3", [128, 4, D], F32, es4)
        grow0 = [P.sbuf("grow0_%d" % i, [128, D], F32, es4) for i in range(2)]
        for i in range(2):
            k.LD(grow0[i][:], s_grow.t[i], [grow0[i]], R=[s_grow])
        ot3 = P.sbuf("ot3", [64, 8, HGW], F32, es4)
        mixb = P.sbuf("mixb", [64, 8, HGW], BF16, es4)
        mixT = P.sbuf("mixT", [128, KD, 512], BF16, es4)
        tA = P.sbuf("tA", [128, 512], F32, es4)
        tmp3 = P.sbuf("tmp3", [128, 512], F32, es4)
        junk3 = P.sbuf("junk3", [128, 512], BF16, es4)
        ssh = P.sbuf("ssh", [64, 32], F32, es4)
        ss2 = P.sbuf("ss2", [128, 4], F32, es4)
        for ui, (tok0, T, isctx) in enumerate(units):
            if isctx and last:
                continue
            nt = T // 128
            nch = T // CH
            j = 1 if isctx else 0
            k.LD(osm[:, 0:nch, :], s_osum.t[tok0:tok0 + T, :].rearrange("(c p) n -> p c n", p=CH), [osm], R=[s_osum])
            k.LD(gtt[:, 0:nch, :], s_g.t[tok0:tok0 + T, :].rearrange("(c p) n -> p c n", p=CH), [gtt], R=[s_g])
            k.LD(hst[:, :, 0:T], s_hsum.t[:, :, tok0:tok0 + T].rearrange("c p t -> p c t"), [hst], R=[s_hsum])
            k.LD(gat[:, :, 0:T], s_gate.t[:, :, tok0:tok0 + T].rearrange("c p t -> p c t"), [gat], R=[s_gate])
            fix = (not isctx) and (not CHAIN)
            if fix:
                k.LD(qsf[:, :, 0:T], s_qseg.t[0, :, :, tok0:tok0 + T].rearrange("h p t -> p h t"), [qsf], R=[s_qseg])
                k.LD(qsb[:, :, 0:T], s_qseg.t[1, :, :, tok0:tok0 + T].rearrange("h p t -> p h t"), [qsb], R=[s_qseg])
                k.LD(acf[:, :, 0:T], s_acum.t[0, :, :, tok0:tok0 + T].rearrange("c p t -> p c t"), [acf], R=[s_acum])
                k.LD(acb[:, :, 0:T], s_acum.t[1, :, :, tok0:tok0 + T].rearrange("c p t -> p c t"), [acb], R=[s_acum])
            src, srcbuf = x_src(L, tok0, T)
            k.LD(xt3[:, 0:nt, :], src.rearrange("(j p) d -> p j d", p=128), [xt3], R=[srcbuf])
            for n in range(nch):
                if fix:
                    pf = k.ps()
                    for h in range(4):
                        hs_ = slice(h * 128, (h + 1) * 128)
                        k.MM(pf.t[0:CH, hs_], qsf[:, h, n * CH:(n + 1) * CH], Sinb[:, 0, h, :], True, False, [qsf, Sinb], [pf])
                        k.MM(pf.t[0:CH, hs_], qsb[:, h, n * CH:(n + 1) * CH], Sinb[:, 1, h, :], False, True, [qsb, Sinb], [pf])
                    k.TT("dve", ot3[:, n, :], pf.t[0:CH, 0:512], osm[:, n, :], ALU.add, [pf, osm], [ot3])
                else:
                    k.CP("act", ot3[:, n, :], osm[:, n, :], [osm], [ot3])
            ov = ot3[:, 0:nch, :].rearrange("p n (h v) -> p (n h) v", v=128)
            sq = osm[:, 0:nch, :].rearrange("p n (h v) -> p (n h) v", v=128)
            k.TT("pool", sq, ov, ov, ALU.mult, [ot3], [osm])
            P.op("dve", (lambda o_, i_: (lambda e: e.tensor_reduce(out=o_, in_=i_, axis=AX.X, op=ALU.add)))(ssh[:, 0:nch * 4], sq), [osm], [ssh])
            k.TS("dve", ssh[:, 0:nch * 4], ssh[:, 0:nch * 4], 1.0 / 128, EPS, ALU.mult, ALU.add, [ssh], [ssh])
            k.ACT(ssh[:, 0:nch * 4], ssh[:, 0:nch * 4], AF.Sqrt, [ssh], [ssh])
            P.op("dve", (lambda o_: (lambda e: e.reciprocal(out=o_, in_=o_)))(ssh[:, 0:nch * 4]), [ssh], [ssh])
            k.TT("dve", ov, ov, ssh[:, 0:nch * 4].unsqueeze(2).to_broadcast([CH, nch * 4, 128]), ALU.mult, [ot3, ssh], [ot3])
            k.TT("pool", mixb[:, 0:nch, :], ot3[:, 0:nch, :], gtt[:, 0:nch, :], ALU.mult, [ot3, gtt], [mixb])
            for h in range(4):
                pbT = k.ps()
                pT = pbT.t[:, :].bitcast(BF16)
                for n in range(nch):
                    k.TR(pT[:, n * CH:(n + 1) * CH], mixb[:, n, h * 128:(h + 1) * 128], ident[0:CH, 0:CH], [mixb, ident], [pbT])
                k.CP("act", mixT[:, h, 0:T], pT[:, 0:T], [pbT], [mixT])
            for c in range(4):
                if fix:
                    k.STT(tA[:, 0:T], acf[:, c, 0:T], hin[:, 0, c:c + 1], hst[:, c, 0:T], ALU.mult, ALU.add, [acf, hin, hst], [tA])
                    k.STT(tA[:, 0:T], acb[:, c, 0:T], hin[:, 1, c:c + 1], tA[:, 0:T], ALU.mult, ALU.add, [acb, hin, tA], [tA])
                    k.TT("pool", mixT[:, 4 + c, 0:T], tA[:, 0:T], gat[:, c, 0:T], ALU.mult, [tA, gat], [mixT])
                else:
                    k.TT("pool", mixT[:, 4 + c, 0:T], hst[:, c, 0:T], gat[:, c, 0:T], ALU.mult, [hst, gat], [mixT])
            for jj in range(nt):
                pps = [k.ps(), k.ps()]
                for half in range(2):
                    for kc in range(KD):
                        k.MM(pps[half].t[:, 0:512], mixT[:, kc, jj * 128:(jj + 1) * 128], wob[:, kc, half * 512:(half + 1) * 512],
                             kc == 0, kc == KD - 1, [mixT, wob], [pps[half]])
                    k.ACT(junk3[:], pps[half].t[:, 0:512], AF.Square, [pps[half]], [junk3, ss2], accum=ss2[:, half:half + 1])
                k.TT("dve", ss2[:, 2:3], ss2[:, 0:1], ss2[:, 1:2], ALU.add, [ss2], [ss2])
                k.TS("dve", ss2[:, 2:3], ss2[:, 2:3], 1.0 / D, EPS, ALU.mult, ALU.add, [ss2], [ss2])
                k.ACT(ss2[:, 2:3], ss2[:, 2:3], AF.Sqrt, [ss2], [ss2])
                P.op("dve", (lambda o_: (lambda e: e.reciprocal(out=o_, in_=o_)))(ss2[:, 2:3]), [ss2], [ss2])
                for half in range(2):
                    hsl = slice(half * 512, (half + 1) * 512)
                    k.STT(tmp3[:], pps[half].t[:, 0:512], ss2[:, 2:3], grow0[j][:, hsl], ALU.mult, ALU.mult,
                          [pps[half], ss2, grow0[j]], [tmp3])
                    k.TT("dve", xt3[:, jj, hsl], xt3[:, jj, hsl], tmp3[:], ALU.add, [xt3, tmp3], [xt3])
            k.ST(s_xmid.t[tok0:tok0 + T, :].rearrange("(j p) d -> p j d", p=128), xt3[:, 0:nt, :], [xt3], [s_xmid])
        P.end_phase()
        es4.close()
        esm.close()
        if STAGE < 6:
            break

        es5 = ExitStack()
        wgb = P.sbuf("wgb", [128, KD, DFF], BF16, es5)
        wub = P.sbuf("wub", [128, KD, DFF], BF16, es5)
        wdb = P.sbuf("wdb", [128, NFF, D], BF16, es5)
        for kc in range(KD):
            k.LD(wgb[:, kc, :], w_gate.ap()[L, kc * 128:(kc + 1) * 128, :], [wgb], q="pool")
            k.LD(wub[:, kc, :], w_up.ap()[L, kc * 128:(kc + 1) * 128, :], [wub], q="pool")
        for jf in range(NFF):
            k.LD(wdb[:, jf, :], w_down.ap()[L, jf * 128:(jf + 1) * 128, :], [wdb], q="pool")
        xt5s = [P.sbuf("xt5_%d" % i, [128, 2, D], F32, es5) for i in range(2)]
        fT5s = [P.sbuf("fT5_%d" % i, [128, KD, 256], BF16, es5) for i in range(2)]
        grow1 = [P.sbuf("grow1_%d" % i, [128, D], F32, es5) for i in range(2)]
        for i in range(2):
            k.LD(grow1[i][:], s_grow.t[2 + i], [grow1[i]], R=[s_grow])
        hid = P.sbuf("hid", [128, NFF, 256], BF16, es5)
        sl5 = P.sbuf("sl5", [128, 256], F32, es5)
        tmp5 = P.sbuf("tmp5", [128, 512], F32, es5)
        junk5 = P.sbuf("junk5", [128, 512], BF16, es5)
        ss5 = P.sbuf("ss5", [128, 4], F32, es5)
        nb5 = norm_bufs(es5, "p5", 2)
        toks5 = [t_ for t_ in range(0, NT, 256) if not (t_ < NCTX and last)]

        def p5A(ui, tok0):
            j = 1 if tok0 < NCTX else 0
            xt5 = xt5s[ui % 2]
            k.LD(xt5[:], s_xmid.t[tok0:tok0 + 256, :].rearrange("(j p) d -> p j d", p=128), [xt5], R=[s_xmid])
            yield from norm_mod_g(nb5, xt5, 2, 1, j, fT5s[ui % 2])

        def p5B(ui, tok0):
            isctx = tok0 < NCTX
            j = 1 if isctx else 0
            T = 256
            xt5 = xt5s[ui % 2]
            fT5 = fT5s[ui % 2]
            for jf in range(NFF):
                pg = k.ps()
                pu = k.ps()
                for kc in range(KD):
                    k.MM(pg.t[:, 0:T], wgb[:, kc, jf * 128:(jf + 1) * 128], fT5[:, kc, 0:T], kc == 0, kc == KD - 1, [wgb, fT5], [pg])
                for kc in range(KD):
                    k.MM(pu.t[:, 0:T], wub[:, kc, jf * 128:(jf + 1) * 128], fT5[:, kc, 0:T], kc == 0, kc == KD - 1, [wub, fT5], [pu])
                k.ACT(sl5[:], pg.t[:, 0:T], AF.Silu, [pg], [sl5])
                k.TT("dve", hid[:, jf, :], sl5[:], pu.t[:, 0:T], ALU.mult, [sl5, pu], [hid])
                yield
            for jj in range(2):
                pps = [k.ps(), k.ps()]
                for half in range(2):
                    for jf in range(NFF):
                        k.MM(pps[half].t[:, 0:512], hid[:, jf, jj * 128:(jj + 1) * 128], wdb[:, jf, half * 512:(half + 1) * 512],
                             jf == 0, jf == NFF - 1, [hid, wdb], [pps[half]])
                    k.ACT(junk5[:], pps[half].t[:, 0:512], AF.Square, [pps[half]], [junk5, ss5], accum=ss5[:, half:half + 1])
                k.TT("dve", ss5[:, 2:3], ss5[:, 0:1], ss5[:, 1:2], ALU.add, [ss5], [ss5])
                k.TS("dve", ss5[:, 2:3], ss5[:, 2:3], 1.0 / D, EPS, ALU.mult, ALU.add, [ss5], [ss5])
                k.ACT(ss5[:, 2:3], ss5[:, 2:3], AF.Sqrt, [ss5], [ss5])
                P.op("dve", (lambda o_: (lambda e: e.reciprocal(out=o_, in_=o_)))(ss5[:, 2:3]), [ss5], [ss5])
                for half in range(2):
                    hsl = slice(half * 512, (half + 1) * 512)
                    k.STT(tmp5[:], pps[half].t[:, 0:512], ss5[:, 2:3], grow1[j][:, hsl], ALU.mult, ALU.mult,
                          [pps[half], ss5, grow1[j]], [tmp5])
                    k.TT("dve", xt5[:, jj, hsl], xt5[:, jj, hsl], tmp5[:], ALU.add, [xt5, tmp5], [xt5])
            if last:
                k.ST(out_t.t.ap()[tok0 - NCTX:tok0 - NCTX + T, :].rearrange("(j p) d -> p j d", p=128), xt5[:], [xt5], [out_t])
            else:
                k.ST(s_xres.t[tok0:tok0 + T, :].rearrange("(j p) d -> p j d", p=128), xt5[:], [xt5], [s_xres])
        pipeline(toks5, p5A, p5B)
        P.end_phase()
        es5.close()

    fin = []
    if DEBUG:
        pairs = [("qT", s_qT), ("fT", s_fT), ("v", s_v), ("g", s_g), ("u", s_u), ("gate", s_gate)]
        if STAGE >= 2:
            pairs += [("hsum", s_hsum), ("acum", s_acum)]
        if STAGE >= 3:
            pairs += [("osum", s_osum), ("qseg", s_qseg)]
        if STAGE >= 5:
            pairs += [("xdst", s_xdst)]
        if STAGE >= 6:
            pairs += [("xmid", s_xmid)]
        if STAGE >= 7:
            pairs += [("xres", s_xres)]
        P.barrier()
        for nm, sb in pairs:
            fin.append(k.ST(dbg[nm].t.ap(), sb.t.ap(), [sb], [dbg[nm]]))
        k.ST(dbg["misc"].t.ap()[:, 0:96], modfm[:].rearrange("p a b -> p (a b)"), [modfm], [dbg["misc"]])
        k.ST(dbg["misc"].t.ap()[:, 96:160], scsh[:].rearrange("p a b c -> p (a b c)"), [scsh], [dbg["misc"]])
        fin.append(k.ST(dbg["misc"].t.ap()[:, 160:168], c1v[:].rearrange("p a b -> p (a b)"), [c1v], [dbg["misc"]]))
        k.ST(dbg["misc"].t.ap()[:, 168:176], hctx[:].rearrange("p a b -> p (a b)"), [hctx], [dbg["misc"]])
        k.ST(dbg["misc"].t.ap()[:, 176:184], hfin[:].rearrange("p a b -> p (a b)"), [hfin], [dbg["misc"]])
        k.ST(dbg["misc"].t.ap()[:, 184:192], atot[:].rearrange("p a b -> p (a b)"), [atot], [dbg["misc"]])
        k.ST(dbg["misc"].t.ap()[:, 192:200], dtot[:].rearrange("p a b -> p (a b)"), [dtot], [dbg["misc"]])
        k.ST(dbg["misc"].t.ap()[:, 200:208], hin[:].rearrange("p a b -> p (a b)"), [hin], [dbg["misc"]])
    P.barrier()
    P.emit()
    return nc


def make_in_maps(inp):
    f = lambda a: np.ascontiguousarray(np.asarray(a, dtype=np.float32))
    x, c, ctx, c_ctx = f(inp["x"]), f(inp["c"]), f(inp["ctx"]), f(inp["c_ctx"])
    b_mod, norm_g = f(inp["b_mod"]), f(inp["norm_g"])
    common = {
        "w_mod": f(inp["w_mod"]),
        "bmod_fm": f(b_mod.reshape(DEPTH, 6, 8, 128).transpose(0, 3, 1, 2).reshape(DEPTH, 128, 48)),
        "bmod_row": f(b_mod.reshape(DEPTH, 1, 6 * D)),
        "normg_fm": f(norm_g.reshape(DEPTH, 4, 8, 128).transpose(0, 3, 1, 2).reshape(DEPTH, 128, 32)),
        "normg_row": norm_g,
        "w_in": f(inp["w_in"]),
        "lb_fm": f(f(inp["hg_lb_logits"]).reshape(DEPTH, 2, 4, 128).transpose(3, 0, 1, 2)),
        "gn_fm": f(f(inp["hg_gnorm"]).reshape(DEPTH, 128, 1)),
        "convw_fm": f(f(inp["rg_conv_w"]).reshape(DEPTH, 4, 4, 128).transpose(0, 3, 2, 1)),
        "convb_fm": f(f(inp["rg_conv_b"]).reshape(DEPTH, 4, 128).transpose(0, 2, 1)),
        "ba_fm": f(f(inp["rg_b_a"]).reshape(DEPTH, 2, 4, 128).transpose(0, 3, 1, 2)),
        "bx_fm": f(f(inp["rg_b_x"]).reshape(DEPTH, 2, 4, 128).transpose(0, 3, 1, 2)),
        "lam_fm": f(f(inp["rg_lambda"]).reshape(DEPTH, 2, 4, 128).transpose(0, 3, 1, 2)),
        "rg_w_a": f(inp["rg_w_a"]),
        "rg_w_x": f(inp["rg_w_x"]),
        "w_out": f(inp["w_out"]),
        "w_ffn_gate": f(inp["w_ffn_gate"]),
        "w_ffn_up": f(inp["w_ffn_up"]),
        "w_ffn_down": f(inp["w_ffn_down"]),
        "ident": np.eye(128, dtype=np.float32),
    }
    tri = np.triu(np.ones((64, 64), np.float32))
    common["masks"] = f(np.stack([np.tile(tri, (1, 8)), np.tile(tri.T, (1, 8))]))
    maps = []
    for core in range(8):
        if CHAIN:
            b, seg = core % 2, 0
        else:
            b, seg = core // 4, core % 4
        m = dict(common)
        m["x"] = f(x[b, seg * NLAT:(seg + 1) * NLAT])
        m["ctx"] = f(ctx[b])
        cv = np.stack([c[b].reshape(8, 128).T, c_ctx.reshape(8, 128).T], axis=-1)
        m["cvec"] = f(cv)
        fl = np.zeros((128, 16), np.float32)
        for r in range(8):
            same = (r // 4 == b) and not CHAIN
            fl[:, r] = 1.0 if (same and r % 4 < seg) else 0.0
            fl[:, 8 + r] = 1.0 if (same and r % 4 > seg) else 0.0
        m["flags"] = fl
        maps.append(m)
    return maps


_NC_CACHE = {}


def kernel(**inputs):
    if "nc" not in _NC_CACHE:
        _NC_CACHE["nc"] = build()
    nc = _NC_CACHE["nc"]
    maps = make_in_maps(inputs)
    res = run_bass_kernel_spmd(nc, maps, core_ids=list(range(8)))
    out = np.empty((2, 16384, D), np.float32)
    for core in range(8):
        if CHAIN:
            if core >= 2:
                continue
            b, seg = core, 0
        else:
            b, seg = core // 4, core % 4
        out[b, seg * NLAT:(seg + 1) * NLAT] = np.asarray(res.results[core]["out"], dtype=np.float32)
    return out
```

```python
import numpy as np
from contextlib import ExitStack
import concourse.bass as bass
import concourse.mybir as mybir
from concourse.bass_utils import run_bass_kernel_spmd

F32 = mybir.dt.float32
BF16 = mybir.dt.bfloat16
AF = mybir.ActivationFunctionType
ALU = mybir.AluOpType
AX = mybir.AxisListType

MODE = "whole2"
CHAIN = (MODE == "whole2")
D = 1024
KD = 8
NLAT = 16384 if CHAIN else 4096
NCTX = 256
NT = NLAT + NCTX
HGW = 512
RGW = 512
INC = 3584
DFF = 2816
NFF = 22
DEPTH = 2
EPS = 1e-6
CH = 64

DEBUG = False
STAGE = 99
USE_CC = True


class Buf:
    def __init__(self, prog, t, name):
        self.prog = prog
        self.t = t
        self.name = name
        self.last_w = None
        self.readers = []
        self.dma_sem = None
        self.dma_cnt = 0
        self.rd_sem = None
        self.rd_cnt = 0

    def __getitem__(self, idx):
        return self.t[idx]


class Prog:
    ENGS = ("pe", "act", "dve", "pool", "sp")

    def __init__(self, nc):
        self.nc = nc
        self.es = ExitStack()
        self.ops = {e: [] for e in self.ENGS}
        self.cnt = {e: 0 for e in self.ENGS}
        self.sems = {}
        for e in self.ENGS:
            self.sems[e] = self.es.enter_context(nc.semaphore("prog_" + e))
        self.known = {e: {} for e in self.ENGS}
        self.final_waits = []
        self._dma_sem_vals = {}
        self.sem_pool = []
        self.cur_bufs = []
        self.nsem = 0

    def sbuf(self, name, shape, dtype, es=None):
        self.nuniq = getattr(self, "nuniq", 0) + 1
        t = (es or self.es).enter_context(self.nc.sbuf_tensor("sb%d_%s" % (self.nuniq, name), list(shape), dtype))
        b = Buf(self, t, name)
        if es is not None:
            self.cur_bufs.append(b)
        return b

    def end_phase(self):
        self.barrier()
        for b in self.cur_bufs:
            if b.dma_sem is not None:
                self.sem_pool.append((b.dma_sem, b.dma_cnt))
                b.dma_sem = None
            if b.rd_sem is not None:
                self.sem_pool.append((b.rd_sem, b.rd_cnt))
                b.rd_sem = None
        self.cur_bufs = []

    def psum(self, name, shape, dtype):
        t = self.es.enter_context(self.nc.psum_tensor(name, list(shape), dtype))
        return Buf(self, t, name)

    def dram(self, name, shape, dtype):
        t = self.nc.dram_tensor(name, list(shape), dtype)
        return Buf(self, t, name)

    def _new_sem(self, name):
        if self.sem_pool:
            return self.sem_pool.pop()
        self.nsem += 1
        return (self.es.enter_context(self.nc.semaphore("s%d" % self.nsem)), 0)

    def _deps(self, eng, reads, writes):
        need = {}

        def add(ev):
            if ev is None:
                return
            key, val, src = ev
            if src == eng and eng == "pe":
                return
            if key not in need or need[key][0] < val:
                need[key] = (val, src)

        for b in reads:
            add(b.last_w)
        for b in writes:
            add(b.last_w)
            for r in b.readers:
                if r[2] == eng:
                    continue
                add(r)
        waits = []
        kn = self.known[eng]
        for key, (val, src) in need.items():
            if kn.get(key, 0) >= val:
                continue
            kn[key] = val
            waits.append((key, val))
        return waits

    def _semobj(self, key):
        if isinstance(key, str):
            return self.sems[key]
        return key

    def _mark(self, ev, reads, writes):
        for b in writes:
            b.last_w = ev
            b.readers = []
        for b in reads:
            if b in writes:
                continue
            b.readers.append(ev)
            if len(b.readers) > 48:
                latest = {}
                for r in b.readers:
                    k = r[0] if isinstance(r[0], str) else id(r[0])
                    if k not in latest or latest[k][1] < r[1]:
                        latest[k] = r
                b.readers = list(latest.values())

    def op(self, eng, fn, reads=(), writes=()):
        reads = [b for b in reads if b is not None]
        writes = [b for b in writes if b is not None]
        waits = self._deps(eng, reads, writes)
        self.cnt[eng] += 1
        ev = (eng, self.cnt[eng], eng)
        self.ops[eng].append(("op", waits, fn))
        self._mark(ev, reads, writes)
        return ev

    def dma(self, q, out_ap, in_ap, reads=(), writes=(), **kw):
        reads = [b for b in reads if b is not None]
        writes = [b for b in writes if b is not None]
        waits = self._deps(q, reads, writes)
        owner = writes[0] if writes else reads[0]
        if writes:
            if owner.dma_sem is None:
                owner.dma_sem, owner.dma_cnt = self._new_sem("dw_" + owner.name)
            owner.dma_cnt += 16
            sem, val = owner.dma_sem, owner.dma_cnt
        else:
            if owner.rd_sem is None:
                owner.rd_sem, owner.rd_cnt = self._new_sem("dr_" + owner.name)
            owner.rd_cnt += 16
            sem, val = owner.rd_sem, owner.rd_cnt
        ev = (sem, val, "dma")
        self._dma_sem_vals[sem] = val
        self.ops[q].append(("dma", waits, (out_ap, in_ap, sem, kw)))
        self._mark(ev, reads, writes)
        return ev

    def custom(self, q, fn, reads=(), writes=(), inc=16):
        reads = [b for b in reads if b is not None]
        writes = [b for b in writes if b is not None]
        waits = self._deps(q, reads, writes)
        owner = writes[0]
        if owner.dma_sem is None:
            owner.dma_sem, owner.dma_cnt = self._new_sem("dw_" + owner.name)
        owner.dma_cnt += inc
        sem, val = owner.dma_sem, owner.dma_cnt
        ev = (sem, val, "dma")
        self._dma_sem_vals[sem] = val
        self.ops[q].append(("custom", waits, (fn, sem, inc)))
        self._mark(ev, reads, writes)
        return ev

    def barrier(self):
        evs = [(e, self.cnt[e]) for e in self.ENGS if self.cnt[e] > 0]
        evs += list(self._dma_sem_vals.items())
        for e in self.ENGS:
            waits = []
            kn = self.known[e]
            for key, val in evs:
                if kn.get(key, 0) >= val:
                    continue
                kn[key] = val
                waits.append((key, val))
            if waits:
                self.ops[e].append(("wait", waits, None))

    def emit(self):
        nc = self.nc
        engmap = {"pe": "tensor", "act": "scalar", "dve": "vector", "pool": "gpsimd", "sp": "sync"}
        with nc.Block() as block:
            for e in self.ENGS:
                ops = self.ops[e]
                if not ops:
                    continue
                semself = self.sems[e]

                def body(engine, ops=ops, semself=semself):
                    for kind, waits, payload in ops:
                        for key, val in waits:
                            engine.wait_ge(self._semobj(key), val)
                        if kind == "op":
                            payload(engine).then_inc(semself, 1)
                        elif kind == "dma":
                            out_ap, in_ap, sem, kw = payload
                            engine.dma_start(out=out_ap, in_=in_ap, **kw).then_inc(sem, 16)
                        elif kind == "custom":
                            fn, sem, inc = payload
                            fn(engine).then_inc(sem, inc)

                getattr(block, engmap[e])(body)
        self.es.close()


class K:
    def __init__(self):
        nc = bass.Bass("TRN2", target_bir_lowering=False)
        self.nc = nc
        self.P = Prog(nc)
        self.ins = {}
        self.outs = {}
        self.psn = 0

    def inp(self, name, shape, dtype=F32):
        t = self.nc.dram_tensor(name, list(shape), dtype, kind="ExternalInput")
        self.ins[name] = t
        return t

    def outp(self, name, shape, dtype=F32):
        t = self.nc.dram_tensor(name, list(shape), dtype, kind="ExternalOutput")
        b = Buf(self.P, t, name)
        self.outs[name] = b
        return b

    def ACT(self, out, in_, func, R, W, bias=None, scale=None, accum=None):
        kw = {}
        if bias is not None:
            kw["bias"] = bias
        if scale is not None:
            kw["scale"] = scale
        if accum is not None:
            kw["accum_out"] = accum
        return self.P.op("act", lambda e: e.activation(out=out, in_=in_, func=func, **kw), R, W)

    def TS(self, eng, out, in0, s1, s2, op0, op1, R, W):
        if op1 is None:
            return self.P.op(eng, lambda e: e.tensor_scalar(out=out, in0=in0, scalar1=s1, scalar2=None, op0=op0), R, W)
        return self.P.op(eng, lambda e: e.tensor_scalar(out=out, in0=in0, scalar1=s1, scalar2=s2, op0=op0, op1=op1), R, W)

    def TT(self, eng, out, in0, in1, op, R, W):
        return self.P.op(eng, lambda e: e.tensor_tensor(out=out, in0=in0, in1=in1, op=op), R, W)

    def STT(self, out, in0, scalar, in1, op0, op1, R, W):
        return self.P.op("dve", lambda e: e.scalar_tensor_tensor(out=out, in0=in0, scalar=scalar, in1=in1, op0=op0, op1=op1), R, W)

    def MM(self, out, lhsT, rhs, start, stop, R, W):
        return self.P.op("pe", lambda e: e.matmul(out, lhsT=lhsT, rhs=rhs, start=start, stop=stop), R, W)

    def TR(self, out, in_, ident, R, W):
        return self.P.op("pe", lambda e: e.transpose(out, in_, ident), R, W)

    def CP(self, eng, out, in_, R, W):
        if eng == "act":
            return self.P.op("act", lambda e: e.copy(out=out, in_=in_), R, W)
        return self.P.op(eng, lambda e: e.tensor_copy(out=out, in_=in_), R, W)

    def SCAN(self, out, d0, d1, init, R, W):
        return self.P.op("dve", lambda e: e.tensor_tensor_scan(out=out, data0=d0, data1=d1, initial=init, op0=ALU.mult, op1=ALU.add), R, W)

    def MEMSET(self, eng, ap, val, W):
        return self.P.op(eng, lambda e: e.memset(ap, val), [], W)

    def LD(self, out, in_, W, R=(), q="sp"):
        return self.P.dma(q, out, in_, reads=list(R), writes=list(W))

    def ST(self, out, in_, R, W=(), q="sp"):
        return self.P.dma(q, out, in_, reads=list(R), writes=list(W))

    def ps(self):
        b = self.psb[self.psn % getattr(self, "nrot", 8)]
        self.psn += 1
        return b


def build(nlayers=DEPTH):
    k = K()
    nc, P = k.nc, k.P
    x_in = k.inp("x", [NLAT, D])
    ctx_in = k.inp("ctx", [NCTX, D])
    cvec = k.inp("cvec", [128, KD, 2])
    w_mod = k.inp("w_mod", [DEPTH, D, 6 * D])
    bmod_fm = k.inp("bmod_fm", [DEPTH, 128, 48])
    bmod_row = k.inp("bmod_row", [DEPTH, 1, 6 * D])
    normg_fm = k.inp("normg_fm", [DEPTH, 128, 32])
    normg_row = k.inp("normg_row", [DEPTH, 4, D])
    w_in = k.inp("w_in", [DEPTH, D, INC])
    lb_fm = k.inp("lb_fm", [128, DEPTH, 2, 4])
    gn_fm = k.inp("gn_fm", [DEPTH, 128, 1])
    convw_fm = k.inp("convw_fm", [DEPTH, 128, 4, 4])
    convb_fm = k.inp("convb_fm", [DEPTH, 128, 4])
    ba_fm = k.inp("ba_fm", [DEPTH, 128, 2, 4])
    bx_fm = k.inp("bx_fm", [DEPTH, 128, 2, 4])
    lam_fm = k.inp("lam_fm", [DEPTH, 128, 2, 4])
    rg_w_a = k.inp("rg_w_a", [DEPTH, 2, 8, 64, 64])
    rg_w_x = k.inp("rg_w_x", [DEPTH, 2, 8, 64, 64])
    w_out = k.inp("w_out", [DEPTH, D, D])
    w_gate = k.inp("w_ffn_gate", [DEPTH, D, DFF])
    w_up = k.inp("w_ffn_up", [DEPTH, D, DFF])
    w_down = k.inp("w_ffn_down", [DEPTH, DFF, D])
    flags_in = k.inp("flags", [128, 16])
    ident_in = k.inp("ident", [128, 128])
    masks_in = k.inp("masks", [2, 64, 512])
    out_t = k.outp("out", [NLAT, D])

    s_qT = P.dram("s_qT", [4, 128, NT], BF16)
    s_fT = P.dram("s_fT", [2, 4, 128, NT], F32)
    s_v = P.dram("s_v", [NT, HGW], BF16)
    s_g = P.dram("s_g", [NT, HGW], BF16)
    s_u = P.dram("s_u", [4, 128, NT], BF16)
    s_gate = P.dram("s_gate", [4, 128, NT], BF16)
    s_hsum = P.dram("s_hsum", [4, 128, NT], F32)
    s_hsumf = P.dram("s_hsumf", [4, 128, NT], F32)
    s_acum = P.dram("s_acum", [2, 4, 128, NT], F32)
    s_osum = P.dram("s_osum", [NT, HGW], F32)
    s_of = P.dram("s_of", [NT, HGW], F32)
    s_qseg = P.dram("s_qseg", [2, 4, 128, NT], BF16)
    s_xres = P.dram("s_xres", [NT, D], F32)
    s_xmid = P.dram("s_xmid", [NT, D], F32)
    s_grow = P.dram("s_grow", [4, 128, D], F32)
    XW = 1024 + 8 + 8 + 8
    s_xsrc = P.dram("s_xsrc", [128, XW], F32)
    s_xdst = P.dram("s_xdst", [8 * 128, XW], F32)

    dbg = {}
    if DEBUG:
        dbg["qT"] = k.outp("d_qT", [4, 128, NT], BF16)
        dbg["fT"] = k.outp("d_fT", [2, 4, 128, NT], F32)
        dbg["v"] = k.outp("d_v", [NT, HGW], BF16)
        dbg["g"] = k.outp("d_g", [NT, HGW], BF16)
        dbg["u"] = k.outp("d_u", [4, 128, NT], BF16)
        dbg["gate"] = k.outp("d_gate", [4, 128, NT], BF16)
        dbg["hsum"] = k.outp("d_hsum", [4, 128, NT], F32)
        dbg["acum"] = k.outp("d_acum", [2, 4, 128, NT], F32)
        dbg["osum"] = k.outp("d_osum", [NT, HGW], F32)
        dbg["qseg"] = k.outp("d_qseg", [2, 4, 128, NT], BF16)
        dbg["xmid"] = k.outp("d_xmid", [NT, D], F32)
        dbg["xres"] = k.outp("d_xres", [NT, D], F32)
        dbg["xdst"] = k.outp("d_xdst", [8 * 128, XW], F32)
        dbg["misc"] = k.outp("d_misc", [128, 256], F32)
        dbg["sst"] = k.outp("d_sst", [128, 1024], F32)
        dbg["sctx"] = k.outp("d_sctx", [128, 1024], F32)
        dbg["sin"] = k.outp("d_sin", [128, 1024], F32)

    k.psb = [P.psum("psb%d" % i, [128, 512], F32) for i in range(8)]

    ident = P.sbuf("ident", [128, 128], BF16)
    maski = P.sbuf("maski", [64, 2, 512], mybir.dt.int32)
    ones_row = P.sbuf("ones_row", [1, 128], F32)
    ones_t = P.sbuf("ones_t", [128, 512], F32)
    flags = P.sbuf("flags", [128, 16], F32)
    cv_f = P.sbuf("cv_f", [128, KD, 2], F32)
    scbf = P.sbuf("scbf", [128, KD, 2], BF16)
    modfm = P.sbuf("modfm", [128, 48, 2], F32)
    bmodfm = P.sbuf("bmodfm", [128, 48], F32)
    gfm = P.sbuf("gfm", [128, 32], F32)
    scsh = P.sbuf("scsh", [128, 4, KD, 2], F32)
    lbt = P.sbuf("lbt", [128, DEPTH, 2, 4], F32)
    lbv = P.sbuf("lbv", [128, 2, 4], F32)
    omlv = P.sbuf("omlv", [128, 2, 4], F32)
    gnv = P.sbuf("gnv", [128, 1], F32)
    cwv = P.sbuf("cwv", [128, 4, 4], F32)
    cbv = P.sbuf("cbv", [128, 4], F32)
    bav = P.sbuf("bav", [128, 2, 4], F32)
    bxv = P.sbuf("bxv", [128, 2, 4], F32)
    c1v = P.sbuf("c1v", [128, 2, 4], F32)
    Sst = P.sbuf("Sst", [128, 2, 4, 128], F32)
    dtot = P.sbuf("dtot", [128, 2, 4], F32)
    hctx = P.sbuf("hctx", [128, 2, 4], F32)
    hfin = P.sbuf("hfin", [128, 2, 4], F32)
    atot = P.sbuf("atot", [128, 2, 4], F32)
    hin = P.sbuf("hin", [128, 2, 4], F32)

    ess = ExitStack()
    ident_f = P.sbuf("ident_f", [128, 128], F32, ess)
    masks_f = P.sbuf("masks_f", [64, 2, 512], F32, ess)
    k.LD(ident_f[:], ident_in.ap(), [ident_f])
    k.CP("dve", ident[:], ident_f[:], [ident_f], [ident])
    k.LD(masks_f[:], masks_in.ap().rearrange("a s t -> s a t"), [masks_f])
    k.CP("dve", maski[:], masks_f[:], [masks_f], [maski])
    k.MEMSET("pool", ones_row[:], 1.0, [ones_row])
    k.MEMSET("pool", ones_t[:], 1.0, [ones_t])
    k.LD(flags[:], flags_in.ap(), [flags])
    k.LD(cv_f[:], cvec.ap(), [cv_f])
    k.ACT(scbf[:], cv_f[:], AF.Silu, [cv_f], [scbf])
    k.LD(lbt[:], lb_fm.ap(), [lbt])

    P.end_phase()
    ess.close()

    units = [(0, NCTX, True)] + [(NCTX + 512 * i, 512, False) for i in range(NLAT // 512)]

    def x_src(L, tok0, T):
        if L == 0:
            if tok0 < NCTX:
                return ctx_in.ap()[tok0:tok0 + T, :], None
            return x_in.ap()[tok0 - NCTX:tok0 - NCTX + T, :], None
        return s_xres.t[tok0:tok0 + T, :], s_xres

    def norm_bufs(es, tag, ntmax):
        return (P.sbuf("ssq_" + tag, [128, 4], F32, es), P.sbuf("rstd_" + tag, [128, 4], F32, es),
                P.sbuf("junk_" + tag, [128, D], BF16, es), P.sbuf("xn_" + tag, [128, ntmax, D], BF16, es))

    def norm_mod(nb, xt, nt, a, j, hT):
        drain(norm_mod_g(nb, xt, nt, a, j, hT))

    def norm_mod_g(nb, xt, nt, a, j, hT):
        T = nt * 128
        ssq, rstd, junk, xn = nb
        for jj in range(nt):
            k.ACT(junk[:], xt[:, jj, :], AF.Square, [xt], [junk, ssq], accum=ssq[:, jj:jj + 1])
        k.TS("dve", rstd[:, 0:nt], ssq[:, 0:nt], 1.0 / D, EPS, ALU.mult, ALU.add, [ssq], [rstd])
        k.ACT(rstd[:, 0:nt], rstd[:, 0:nt], AF.Sqrt, [rstd], [rstd])
        P.op("dve", lambda e: e.reciprocal(out=rstd[:, 0:nt], in_=rstd[:, 0:nt]), [rstd], [rstd])
        yield
        for jj in range(nt):
            k.ACT(xn[:, jj, :], xt[:, jj, :], AF.Copy, [xt, rstd], [xn], scale=rstd[:, jj:jj + 1])
        yield
        yield
        for kc in range(KD):
            if kc == 4:
                yield
            pb = k.ps()
            pst = pb.t[:, :].bitcast(BF16)
            for jj in range(nt):
                k.TR(pst[:, jj * 128:(jj + 1) * 128], xn[:, jj, kc * 128:(kc + 1) * 128], ident[:], [xn, ident], [pb])
            k.TS("dve", hT[:, kc, 0:T], pst[:, 0:T], scsh[:, 2 * a, kc, j:j + 1], scsh[:, 2 * a + 1, kc, j:j + 1],
                 ALU.mult, ALU.add, [pb, scsh], [hT])

    def drain(g):
        for _ in g:
            pass

    def interleave(g1, g2):
        a1, a2 = g1 is not None, g2 is not None
        while a1 or a2:
            if a1:
                try:
                    next(g1)
                except StopIteration:
                    a1 = False
            if a2:
                try:
                    next(g2)
                except StopIteration:
                    a2 = False

    def pipeline(items, genA, genB):
        if not items:
            return
        drain(genA(0, items[0]))
        for i, it_ in enumerate(items):
            nxt = genA(i + 1, items[i + 1]) if i + 1 < len(items) else None
            interleave(nxt, genB(i, it_))

    for L in range(nlayers):
        last = (L == DEPTH - 1)
        es0 = ExitStack()
        wblk = P.sbuf("wblk", [128, KD, D], BF16, es0)
        rowt = P.sbuf("rowt", [1, 512], F32, es0)
        growt = P.sbuf("growt", [128, D], F32, es0)
        brow = P.sbuf("brow", [1, 6 * D], F32, es0)
        g1row = P.sbuf("g1row", [1, D], F32, es0)
        g3row = P.sbuf("g3row", [1, D], F32, es0)
        lamt = P.sbuf("lamt", [128, 2, 4], F32, es0)
        k.LD(bmodfm[:], bmod_fm.ap()[L], [bmodfm])
        k.LD(gfm[:], normg_fm.ap()[L], [gfm])
        k.LD(brow[:], bmod_row.ap()[L], [brow])
        k.LD(g1row[:], normg_row.ap()[L, 1:2, :], [g1row])
        k.LD(g3row[:], normg_row.ap()[L, 3:4, :], [g3row])
        k.LD(gnv[:], gn_fm.ap()[L], [gnv])
        k.LD(cwv[:], convw_fm.ap()[L], [cwv])
        k.LD(cbv[:], convb_fm.ap()[L], [cbv])
        k.LD(bav[:], ba_fm.ap()[L], [bav])
        k.LD(bxv[:], bx_fm.ap()[L], [bxv])
        k.LD(lamt[:], lam_fm.ap()[L], [lamt])
        k.ACT(c1v[:], lamt[:], AF.Exp, [lamt], [c1v], scale=-1.0)
        k.ACT(c1v[:], c1v[:], AF.Ln, [c1v], [c1v], bias=1.0)
        k.TS("dve", c1v[:], c1v[:], -8.0, None, ALU.mult, None, [c1v], [c1v])
        if L == 0:
            k.MEMSET("pool", lbv[:], 0.0, [lbv])
        else:
            k.TT("dve", lbv[:], lbt[:, 1], lbt[:, 0], ALU.subtract, [lbt], [lbv])
            k.ACT(lbv[:], lbv[:], AF.Sigmoid, [lbv], [lbv])
        k.TS("dve", omlv[:], lbv[:], -1.0, 1.0, ALU.mult, ALU.add, [lbv], [omlv])
        for n in range(6):
            k.LD(wblk[:], w_mod.ap()[L, :, n * D:(n + 1) * D].rearrange("(kc p) c -> p kc c", p=128), [wblk], q="pool")
            pb = k.ps()
            for kd in range(KD):
                for kc in range(KD):
                    k.MM(pb.t[:, kd * 2:kd * 2 + 2], wblk[:, kc, kd * 128:(kd + 1) * 128], scbf[:, kc, :],
                         kc == 0, kc == KD - 1, [wblk, scbf], [pb])
            k.TT("dve", modfm[:, n * 8:(n + 1) * 8, :], pb.t[:, 0:16].rearrange("p (a b) -> p a b", b=2),
                 bmodfm[:, n * 8:(n + 1) * 8].unsqueeze(2).to_broadcast([128, 8, 2]), ALU.add, [pb, bmodfm], [modfm])
            if n in (2, 5):
                a = 0 if n == 2 else 1
                grow_g = g1row if n == 2 else g3row
                for j in range(2):
                    for half in range(2):
                        pb2 = k.ps()
                        for kc in range(KD):
                            k.MM(pb2.t[0:1, 0:512], scbf[:, kc, j:j + 1], wblk[:, kc, half * 512:(half + 1) * 512],
                                 kc == 0, kc == KD - 1, [wblk, scbf], [pb2])
                        k.TT("dve", rowt[:], pb2.t[0:1, 0:512], brow[0:1, n * D + half * 512:n * D + (half + 1) * 512],
                             ALU.add, [pb2, brow], [rowt])
                        k.TT("dve", rowt[:], rowt[:], grow_g[0:1, half * 512:(half + 1) * 512], ALU.mult,
                             [rowt, grow_g], [rowt])
                        pb3 = k.ps()
                        k.MM(pb3.t[:, 0:512], ones_row[0:1, :], rowt[0:1, :], True, True, [ones_row, rowt], [pb3])
                        k.CP("act", growt[:, half * 512:(half + 1) * 512], pb3.t[:, 0:512], [pb3], [growt])
                    k.ST(s_grow.t[2 * a + j], growt[:], [growt], [s_grow])
        for a, (nsc, nsh, gi) in enumerate(((1, 0, 0), (4, 3, 2))):
            k.TS("dve", scsh[:, 2 * a], modfm[:, nsc * 8:(nsc + 1) * 8, :], 1.0, None, ALU.add, None, [modfm], [scsh])
            k.TT("dve", scsh[:, 2 * a], scsh[:, 2 * a], gfm[:, gi * 8:(gi + 1) * 8].unsqueeze(2).to_broadcast([128, 8, 2]),
                 ALU.mult, [scsh, gfm], [scsh])
            k.CP("dve", scsh[:, 2 * a + 1], modfm[:, nsh * 8:(nsh + 1) * 8, :], [modfm], [scsh])
        P.end_phase()
        es0.close()
        if STAGE < 1:
            break

        es1 = ExitStack()
        winb = P.sbuf("winb", [128, KD, INC], BF16, es1)
        for kc in range(KD):
            k.LD(winb[:, kc, :], w_in.ap()[L, kc * 128:(kc + 1) * 128, :], [winb], q="pool")
        xts = [P.sbuf("xt%d" % i, [128, 4, D], F32, es1) for i in range(1)]
        hTs = [P.sbuf("hT%d" % i, [128, KD, 512], BF16, es1) for i in range(2)]
        qs = P.sbuf("qs", [128, 4, 512], BF16, es1)
        sg = P.sbuf("sg", [128, 512], F32, es1)
        ft = P.sbuf("ft", [128, 2, 4, 512], F32, es1)
        uf = P.sbuf("uf", [128, 512], F32, es1)
        uc = P.sbuf("uc", [128, 512], F32, es1)
        ub = P.sbuf("ub", [128, 4, 512], BF16, es1)
        gb = P.sbuf("gb", [128, 4, 512], BF16, es1)
        vt = P.sbuf("vt", [64, 8, HGW], BF16, es1)
        gt = P.sbuf("gt", [64, 8, HGW], BF16, es1)
        nb1 = norm_bufs(es1, "p1", 4)
        def p1A(ui, unit):
            tok0, T, isctx = unit
            nt = T // 128
            j = 1 if isctx else 0
            xt = xts[0]
            src, srcbuf = x_src(L, tok0, T)
            k.LD(xt[:, 0:nt, :], src.rearrange("(j p) d -> p j d", p=128), [xt], R=[srcbuf])
            yield from norm_mod_g(nb1, xt, nt, 0, j, hTs[ui % 2])

        def p1B(ui, unit):
            tok0, T, isctx = unit
            nt = T // 128
            nch = T // CH
            j = 1 if isctx else 0
            hT = hTs[ui % 2]
            for ct in range(20):
                if ct < 12:
                    c0 = ct * 128
                else:
                    c0 = 5 * HGW + (ct - 12) * 128
                pb = k.ps()
                for kc in range(KD):
                    k.MM(pb.t[:, 0:T], winb[:, kc, c0:c0 + 128], hT[:, kc, 0:T], kc == 0, kc == KD - 1, [winb, hT], [pb])
                if ct < 4:
                    k.ACT(qs[:, ct, 0:T], pb.t[:, 0:T], AF.Silu, [pb], [qs])
                elif ct < 12:
                    dr, h = (ct - 4) // 4, (ct - 4) % 4
                    k.ACT(sg[:, 0:T], pb.t[:, 0:T], AF.Sigmoid, [pb], [sg])
                    k.TS("dve", ft[:, dr, h, 0:T], sg[:, 0:T], omlv[:, dr, h:h + 1], lbv[:, dr, h:h + 1], ALU.mult, ALU.add,
                         [sg, omlv, lbv], [ft])
                elif ct < 16:
                    c = ct - 12
                    k.CP("act", uf[:, 0:T], pb.t[:, 0:T], [pb], [uf])
                    RW = T if isctx else 64
                    ufv = uf[:, 0:T].rearrange("p (r w) -> p r w", w=RW)
                    ucv = uc[:, 0:T].rearrange("p (r w) -> p r w", w=RW)
                    k.TS("dve", uc[:, 0:T], uf[:, 0:T], cwv[:, c, 2:3], cbv[:, c:c + 1], ALU.mult, ALU.add, [uf, cwv, cbv], [uc])
                    for tap in (0, 1, 3):
                        s = tap - 2
                        if s < 0:
                            o_sl = ucv[:, :, -s:RW]
                            i_sl = ufv[:, :, 0:RW + s]
                        else:
                            o_sl = ucv[:, :, 0:RW - s]
                            i_sl = ufv[:, :, s:RW]
                        k.STT(o_sl, i_sl, cwv[:, c, tap:tap + 1], o_sl, ALU.mult, ALU.add, [uf, uc, cwv], [uc])
                    k.CP("act", ub[:, c, 0:T], uc[:, 0:T], [uc], [ub])
                else:
                    c = ct - 16
                    k.ACT(gb[:, c, 0:T], pb.t[:, 0:T], AF.Gelu_apprx_tanh, [pb], [gb])
                if ct % 2 == 1:
                    yield
            for cc in range(nch):
                for which in range(2):
                    c0 = (3 + which) * HGW
                    pb = k.ps()
                    for kc in range(KD):
                        k.MM(pb.t[0:CH, 0:512], hT[:, kc, cc * CH:(cc + 1) * CH], winb[:, kc, c0:c0 + 512],
                             kc == 0, kc == KD - 1, [winb, hT], [pb])
                    if which == 0:
                        k.CP("act", vt[:, cc, :], pb.t[0:CH, 0:512], [pb], [vt])
                    else:
                        k.ACT(gt[:, cc, :], pb.t[0:CH, 0:512], AF.Silu, [pb], [gt])
            k.ST(s_qT.t[:, :, tok0:tok0 + T].rearrange("h p t -> p h t"), qs[:, :, 0:T], [qs], [s_qT])
            for dr in range(2):
                k.ST(s_fT.t[dr, :, :, tok0:tok0 + T].rearrange("h p t -> p h t"), ft[:, dr, :, 0:T], [ft], [s_fT])
            k.ST(s_u.t[:, :, tok0:tok0 + T].rearrange("h p t -> p h t"), ub[:, :, 0:T], [ub], [s_u])
            k.ST(s_gate.t[:, :, tok0:tok0 + T].rearrange("h p t -> p h t"), gb[:, :, 0:T], [gb], [s_gate])
            k.ST(s_v.t[tok0:tok0 + T, :].rearrange("(c p) n -> p c n", p=CH), vt[:, 0:nch, :], [vt], [s_v])
            k.ST(s_g.t[tok0:tok0 + T, :].rearrange("(c p) n -> p c n", p=CH), gt[:, 0:nch, :], [gt], [s_g])
        pipeline(units, p1A, p1B)
        P.end_phase()
        es1.close()
        if STAGE < 2:
            break


        esm = ExitStack()
        Sst = P.sbuf("Sst", [128, 2, 4, 128], F32, esm)
        Sctx = P.sbuf("Sctx", [128, 2, 4, 128], F32, esm)
        Sin = P.sbuf("Sin", [128, 2, 4, 128], F32, esm)
        Sinb = P.sbuf("Sinb", [128, 2, 4, 128], BF16, esm)
        es2 = ExitStack()
        wbd = P.sbuf("wbd", [128, 2, 2, 4, 128], BF16, es2)
        wstage = P.sbuf("wstage", [128, 2, 2, 4, 128], F32, es2)
        zeros_t = P.sbuf("zeros_t", [128, 512], F32, es2)
        k.MEMSET("pool", zeros_t[:], 0.0, [zeros_t])
        k.MEMSET("pool", wstage[:], 0.0, [wstage])
        for gi, wsrc in enumerate((rg_w_a, rg_w_x)):
            for dr in range(2):
                for half in range(2):
                    src = wsrc.ap()[L, dr].rearrange("(ct h) i j -> h i ct j", h=2)[half]
                    k.LD(wstage[half * 64:(half + 1) * 64, gi, dr, :, half * 64:(half + 1) * 64], src, [wstage])
        k.CP("dve", wbd[:], wstage[:], [wstage], [wbd])
        uts = [P.sbuf("ut%d" % i, [128, 4, 512], BF16, es2) for i in range(2)]
        rt = P.sbuf("rt", [128, 4, 512], F32, es2)
        it = P.sbuf("it", [128, 4, 512], F32, es2)
        at = P.sbuf("at", [128, 4, 512], F32, es2)
        a2t = P.sbuf("a2t", [128, 4, 512], F32, es2)
        bxt = P.sbuf("bxt", [128, 4, 512], F32, es2)
        hl = P.sbuf("hl", [128, 4, 512], F32, es2)
        ac = P.sbuf("ac", [128, 4, 512], F32, es2)
        hss = [P.sbuf("hs%d" % i, [128, 4, 512], F32, es2) for i in range(2)]
        car_h = P.sbuf("car_h", [128, 4], F32, es2)
        car_a = P.sbuf("car_a", [128, 4], F32, es2)
        for dr in range(2):
            order = [units[0]] + (units[1:] if dr == 0 else units[1:][::-1])

            def rgA(ui, unit, dr=dr):
                tok0, T, isctx = unit
                k.LD(uts[ui % 2][:, :, 0:T], s_u.t[:, :, tok0:tok0 + T].rearrange("c p t -> p c t"), [uts[ui % 2]], R=[s_u])
                if dr == 1:
                    k.LD(hss[ui % 2][:, :, 0:T], s_hsumf.t[:, :, tok0:tok0 + T].rearrange("c p t -> p c t"), [hss[ui % 2]], R=[s_hsumf])
                yield

            def rgB(ui, unit, dr=dr):
                tok0, T, isctx = unit
                ut, hs = uts[ui % 2], hss[ui % 2]
                fresh = (ui == 0) if CHAIN else (ui <= 1)
                pbs = []
                for c in range(4):
                    for gi in range(2):
                        pb = k.ps()
                        k.MM(pb.t[:, 0:T], wbd[:, gi, dr, c, :], ut[:, c, 0:T], True, True, [wbd, ut], [pb])
                        pbs.append(pb)
                for c in range(4):
                    k.ACT(rt[:, c, 0:T], pbs[2 * c].t[:, 0:T], AF.Sigmoid, [pbs[2 * c], bav], [rt], bias=bav[:, dr, c:c + 1])
                    k.ACT(it[:, c, 0:T], pbs[2 * c + 1].t[:, 0:T], AF.Sigmoid, [pbs[2 * c + 1], bxv], [it], bias=bxv[:, dr, c:c + 1])
                for c in range(4):
                    k.ACT(at[:, c, 0:T], rt[:, c, 0:T], AF.Exp, [rt, c1v], [at], scale=c1v[:, dr, c:c + 1])
                k.ACT(a2t[:, :, 0:T], at[:, :, 0:T], AF.Square, [at], [a2t])
                k.ACT(a2t[:, :, 0:T], a2t[:, :, 0:T], AF.Sqrt, [a2t], [a2t], scale=-1.0, bias=1.0)
                k.TT("pool", it[:, :, 0:T], it[:, :, 0:T], ut[:, :, 0:T], ALU.mult, [it, ut], [it])
                k.TT("dve", bxt[:, :, 0:T], it[:, :, 0:T], a2t[:, :, 0:T], ALU.mult, [it, a2t], [bxt])
                for c in range(4):
                    ih = 0.0 if fresh else car_h[:, c:c + 1]
                    ia = 1.0 if fresh else car_a[:, c:c + 1]
                    if dr == 0:
                        k.SCAN(hl[:, c, 0:T], at[:, c, 0:T], bxt[:, c, 0:T], ih, [at, bxt, car_h], [hl])
                        if not CHAIN:
                            k.SCAN(ac[:, c, 0:T], at[:, c, 0:T], zeros_t[:, 0:T], ia, [at, zeros_t, car_a], [ac])
                        lastcol = slice(T - 1, T)
                    else:
                        k.SCAN(hl[:, c, T - 1::-1] if False else hl[:, c, 0:T][:, ::-1], at[:, c, 0:T][:, ::-1], bxt[:, c, 0:T][:, ::-1], ih,
                               [at, bxt, car_h], [hl])
                        if not CHAIN:
                            k.SCAN(ac[:, c, 0:T][:, ::-1], at[:, c, 0:T][:, ::-1], zeros_t[:, 0:T], ia, [at, zeros_t, car_a], [ac])
                        lastcol = slice(0, 1)
                    if isctx:
                        k.CP("act", hctx[:, dr, c:c + 1], hl[:, c, lastcol], [hl], [hctx])
                    if CHAIN or not isctx:
                        k.CP("act", car_h[:, c:c + 1], hl[:, c, lastcol], [hl], [car_h])
                        if not CHAIN:
                            k.CP("act", car_a[:, c:c + 1], ac[:, c, lastcol], [ac], [car_a])
                if dr == 0:
                    k.ST(s_hsumf.t[:, :, tok0:tok0 + T].rearrange("c p t -> p c t"), hl[:, :, 0:T], [hl], [s_hsumf])
                else:
                    k.TT("dve", hl[:, :, 0:T], hl[:, :, 0:T], hs[:, :, 0:T], ALU.add, [hl, hs], [hl])
                    k.ST(s_hsum.t[:, :, tok0:tok0 + T].rearrange("c p t -> p c t"), hl[:, :, 0:T], [hl], [s_hsum])
                if not CHAIN:
                    k.ST(s_acum.t[dr, :, :, tok0:tok0 + T].rearrange("c p t -> p c t"), ac[:, :, 0:T], [ac], [s_acum])
                yield

            pipeline(order, rgA, rgB)
            if not CHAIN:
                k.CP("pool", hfin[:, dr, :], car_h[:], [car_h], [hfin])
                k.CP("pool", atot[:, dr, :], car_a[:], [car_a], [atot])
        P.end_phase()
        es2.close()
        if STAGE < 3:
            esm.close()
            break

        es3 = ExitStack()
        fTt = P.sbuf("fTt", [128, 4, 512], F32, es3)
        qTt = P.sbuf("qTt", [128, 4, 512], BF16, es3)
        kTt = P.sbuf("kTt", [128, 4, 512], F32, es3)
        Ct = P.sbuf("Ct", [128, 4, 512], F32, es3)
        crel = P.sbuf("crel", [128, 4, 512], F32, es3)
        e1 = P.sbuf("e1", [128, 4, 512], F32, es3)
        ktl = P.sbuf("ktl", [128, 4, 512], BF16, es3)
        khl = P.sbuf("khl", [128, 4, 512], BF16, es3)
        qsg = None if CHAIN else P.sbuf("qsg", [128, 4, 512], BF16, es3)
        vtts = [P.sbuf("vtt%d" % i, [64, 8, HGW], BF16, es3) for i in range(2)]
        qtls = [P.sbuf("qtl%d" % i, [128, 4, 512], BF16, es3) for i in range(2)]
        khTs = [P.sbuf("khT%d" % i, [64, 4, 8, 128], BF16, es3) for i in range(2)]
        scTs = [[P.sbuf("scT%d%d" % (d_, i), [64, 4, 512], BF16, es3) for i in range(2)] for d_ in range(2)]
        BDs = [P.sbuf("BD%d" % i, [128, 4, 2, 8], F32, es3) for i in range(2)]
        for d_ in range(2):
            for i in range(2):
                k.MEMSET("pool", scTs[d_][i][:], 0.0, [scTs[d_][i]])
        ot = P.sbuf("ot", [64, 8, HGW], F32, es3)
        ofts = [P.sbuf("oft%d" % i, [64, 8, HGW], F32, es3) for i in range(2)]
        SpbV = [[P.sbuf("Spb%d_%d" % (i, h), [128, 128], BF16, es3) for h in range(4)] for i in range(2)]
        SstV = [[Buf(P, Sst.t, "SstV%d%d" % (d_, h)) for h in range(4)] for d_ in range(2)]
        carC = P.sbuf("carC", [128, 4], F32, es3)
        cprev = P.sbuf("cprev", [128, 4, 8], F32, es3)
        dif = P.sbuf("dif", [128, 4, 2, 8], F32, es3)

        def hgA(dr, ui, tok0, T, isctx, sset):
            fresh = (ui == 0) if CHAIN else (ui <= 1)
            nch = T // CH
            vtt, qtl, khT, scT, BD = vtts[sset], qtls[sset], khTs[sset], scTs[dr][sset], BDs[sset]
            oft = ofts[sset]
            k.LD(fTt[:, :, 0:T], s_fT.t[dr, :, :, tok0:tok0 + T].rearrange("h p t -> p h t"), [fTt], R=[s_fT])
            k.LD(qTt[:, :, 0:T], s_qT.t[:, :, tok0:tok0 + T].rearrange("h p t -> p h t"), [qTt], R=[s_qT])
            k.LD(vtt[:, 0:nch, :], s_v.t[tok0:tok0 + T, :].rearrange("(c p) n -> p c n", p=CH), [vtt], R=[s_v])
            k.ACT(kTt[:, :, 0:T], fTt[:, :, 0:T], AF.Copy, [fTt], [kTt], scale=-1.0, bias=1.0)
            k.ACT(fTt[:, :, 0:T], fTt[:, :, 0:T], AF.Ln, [fTt], [fTt])
            C4 = Ct[:, :, 0:T].rearrange("p h (n w) -> p h n w", w=CH)
            edge = 0 if dr == 0 else nch - 1
            if fresh:
                k.MEMSET("dve", cprev[:, :, edge:edge + 1], 0.0, [cprev])
            else:
                k.CP("act", cprev[:, :, edge:edge + 1], carC[:].unsqueeze(2), [carC], [cprev])
            for h in range(4):
                init = 0.0 if fresh else carC[:, h:h + 1]
                if dr == 0:
                    k.SCAN(Ct[:, h, 0:T], ones_t[:, 0:T], fTt[:, h, 0:T], init, [ones_t, fTt, carC], [Ct])
                else:
                    k.SCAN(Ct[:, h, 0:T][:, ::-1], ones_t[:, 0:T], fTt[:, h, 0:T][:, ::-1], init, [ones_t, fTt, carC], [Ct])
            if dr == 0:
                k.CP("act", carC[:].unsqueeze(2), Ct[:, :, T - 1:T], [Ct], [carC])
                Aanc = C4[:, :, :, 31]
                Cend = C4[:, :, :, 63]
                if nch > 1:
                    k.CP("act", cprev[:, :, 1:nch], C4[:, :, 0:nch - 1, 63], [Ct], [cprev])
            else:
                k.CP("act", carC[:].unsqueeze(2), Ct[:, :, 0:1], [Ct], [carC])
                Aanc = C4[:, :, :, 32]
                Cend = C4[:, :, :, 0]
                if nch > 1:
                    k.CP("act", cprev[:, :, 0:nch - 1], C4[:, :, 1:nch, 0], [Ct], [cprev])
            k.TT("dve", dif[:, :, 0, 0:nch], Aanc, cprev[:, :, 0:nch], ALU.subtract, [Ct, cprev], [dif])
            k.TT("dve", dif[:, :, 1, 0:nch], Cend, cprev[:, :, 0:nch], ALU.subtract, [Ct, cprev], [dif])
            k.ACT(BD[:, :, :, 0:nch], dif[:, :, :, 0:nch], AF.Exp, [dif], [BD])
            yield
            cr4 = crel[:, :, 0:T].rearrange("p h (n w) -> p h n w", w=CH)
            e14 = e1[:, :, 0:T].rearrange("p h (n w) -> p h n w", w=CH)
            k.TT("dve", cr4, C4, Aanc.unsqueeze(3).to_broadcast([128, 4, nch, CH]), ALU.subtract, [Ct], [crel])
            k.ACT(e1[:, :, 0:T], crel[:, :, 0:T], AF.Exp, [crel], [e1])
            k.ACT(crel[:, :, 0:T], crel[:, :, 0:T], AF.Exp, [crel], [crel], scale=-1.0)
            k.TT("pool", qtl[:, :, 0:T], qTt[:, :, 0:T], e1[:, :, 0:T], ALU.mult, [qTt, e1], [qtl])
            k.TT("dve", ktl[:, :, 0:T], kTt[:, :, 0:T], crel[:, :, 0:T], ALU.mult, [kTt, crel], [ktl])
            yield
            if not isctx and not CHAIN:
                k.ACT(e1[:, :, 0:T], Ct[:, :, 0:T], AF.Exp, [Ct], [e1])
                k.TT("pool", qsg[:, :, 0:T], qTt[:, :, 0:T], e1[:, :, 0:T], ALU.mult, [qTt, e1], [qsg])
                k.ST(s_qseg.t[dr, :, :, tok0:tok0 + T].rearrange("h p t -> p h t"), qsg[:, :, 0:T], [qsg], [s_qseg])
            k.TT("dve", e14, C4, Cend.unsqueeze(3).to_broadcast([128, 4, nch, CH]), ALU.subtract, [Ct], [e1])
            k.ACT(e1[:, :, 0:T], e1[:, :, 0:T], AF.Exp, [e1], [e1], scale=-1.0)
            k.TT("pool", khl[:, :, 0:T], kTt[:, :, 0:T], e1[:, :, 0:T], ALU.mult, [kTt, e1], [khl])
            yield
            for h in range(4):
                pbT = k.ps()
                pT = pbT.t[:, :].bitcast(BF16)
                for n in range(nch):
                    k.TR(pT[0:CH, n * 128:(n + 1) * 128], khl[:, h, n * CH:(n + 1) * CH], ident[:], [khl, ident], [pbT])
                k.CP("act", khT[:, h, 0:nch, :], pT[0:CH, 0:nch * 128].rearrange("p (n k) -> p n k", k=128), [pbT], [khT])
                pbS = k.ps()
                for n in range(nch):
                    k.MM(pbS.t[0:CH, n * CH:(n + 1) * CH], ktl[:, h, n * CH:(n + 1) * CH], qtl[:, h, n * CH:(n + 1) * CH],
                         True, True, [ktl, qtl], [pbS])
                P.op("dve", (lambda sc_, m_, p_: (lambda e: e.copy_predicated(sc_, m_, p_)))(scT[:, h, 0:T], maski[:, dr, 0:T], pbS.t[0:CH, 0:T]),
                     [pbS, maski], [scT])
                if h % 2 == 1:
                    yield
            if dr == 1:
                k.LD(oft[:, 0:nch, :], s_of.t[tok0:tok0 + T, :].rearrange("(c p) n -> p c n", p=CH), [oft], R=[s_of])

        def hgB(dr, ui, tok0, T, isctx, sset):
            nch = T // CH
            vtt, qtl, khT, scT, BD = vtts[sset], qtls[sset], khTs[sset], scTs[dr][sset], BDs[sset]
            oft = ofts[sset]
            if ui == 0:
                for h in range(4):
                    k.MEMSET("dve", Sst[:, dr, h, :], 0.0, [SstV[dr][h]])
            chunks = list(range(nch)) if dr == 0 else list(range(nch))[::-1]
            prev = None

            def evac(po_, n_):
                if dr == 0:
                    k.CP("act", ot[:, n_, :], po_.t[0:CH, 0:512], [po_], [ot])
                else:
                    k.TT("dve", ot[:, n_, :], po_.t[0:CH, 0:512], oft[:, n_, :], ALU.add, [po_, oft], [ot])

            for ci, n in enumerate(chunks):
                sp = SpbV[ci % 2]
                for h in range(4):
                    k.ACT(sp[h][:], Sst[:, dr, h, :], AF.Copy, [SstV[dr][h], BD], [sp[h]], scale=BD[:, h, 0, n:n + 1])
                po = k.psb[6 + ci % 2]
                pks = [k.ps() for _ in range(4)]
                for h in range(4):
                    hs_ = slice(h * 128, (h + 1) * 128)
                    k.MM(po.t[0:CH, hs_], scT[:, h, n * CH:(n + 1) * CH], vtt[:, n, hs_], True, False, [scT, vtt], [po])
                    k.MM(po.t[0:CH, hs_], qtl[:, h, n * CH:(n + 1) * CH], sp[h][:], False, True, [qtl, sp[h]], [po])
                    k.MM(pks[h].t[:, 0:128], khT[:, h, n, :], vtt[:, n, hs_], True, True, [khT, vtt], [pks[h]])
                for h in range(4):
                    k.STT(Sst[:, dr, h, :], Sst[:, dr, h, :], BD[:, h, 1, n:n + 1], pks[h].t[:, 0:128], ALU.mult, ALU.add,
                          [SstV[dr][h], BD, pks[h]], [SstV[dr][h]])
                if prev is not None:
                    evac(*prev)
                prev = (po, n)
                yield
            evac(*prev)
            dst = s_of if dr == 0 else s_osum
            k.ST(dst.t[tok0:tok0 + T, :].rearrange("(c p) n -> p c n", p=CH), ot[:, 0:nch, :], [ot], [dst])
            if isctx and not CHAIN:
                for h in range(4):
                    k.CP("act", Sctx[:, dr, h, :], Sst[:, dr, h, :], [SstV[dr][h]], [Sctx])
                    k.MEMSET("dve", Sst[:, dr, h, :], 0.0, [SstV[dr][h]])

        k.nrot = 6
        seq = []
        for dr in range(2):
            order = [units[0]] + (units[1:] if dr == 0 else units[1:][::-1])
            for ui, (tok0, T, isctx) in enumerate(order):
                seq.append((dr, ui, tok0, T, isctx))
        drain(hgA(*seq[0], 0))
        for i, item in enumerate(seq):
            nxt = hgA(*seq[i + 1], (i + 1) % 2) if i + 1 < len(seq) else None
            if nxt is not None and seq[i + 1][0] != item[0]:
                if not CHAIN:
                    k.ACT(dtot[:, item[0], :], carC[:], AF.Exp, [carC], [dtot])
            interleave(nxt, hgB(*item, i % 2))
        if not CHAIN:
            k.ACT(dtot[:, 1, :], carC[:], AF.Exp, [carC], [dtot])
        P.end_phase()
        k.nrot = 8
        es3.close()
        if STAGE < 4:
            esm.close()
            break

        if not CHAIN:
            esx = ExitStack()
            xs = P.sbuf("xs", [128, XW], F32, esx)
            xg = P.sbuf("xg", [128, 8, XW], F32, esx)
            dm1 = P.sbuf("dm1", [128, 8, 16], F32, esx)
            tS = P.sbuf("tS", [128, 128], F32, esx)
            tH = P.sbuf("tH", [128, 4], F32, esx)
            k.CP("pool", xs[:, 0:1024], Sst[:].rearrange("p a b c -> p (a b c)"), [Sst], [xs])
            k.CP("pool", xs[:, 1024:1032], dtot[:].rearrange("p a b -> p (a b)"), [dtot], [xs])
            k.CP("pool", xs[:, 1032:1040], hfin[:].rearrange("p a b -> p (a b)"), [hfin], [xs])
            k.CP("pool", xs[:, 1040:1048], atot[:].rearrange("p a b -> p (a b)"), [atot], [xs])
            k.ST(s_xsrc.t.ap(), xs[:], [xs], [s_xsrc])
            if USE_CC:
                P.custom("pool", (lambda a_, b_: (lambda e: e.collective_compute("AllGather", ALU.bypass, replica_groups=[list(range(8))],
                                                                               ins=[a_], outs=[b_])))(s_xsrc.t.ap().opt(), s_xdst.t.ap().opt()),
                         reads=[s_xsrc], writes=[s_xdst], inc=1)
            else:
                for r in range(8):
                    k.ST(s_xdst.t.ap()[r * 128:(r + 1) * 128, :], s_xsrc.t.ap(), [s_xsrc], [s_xdst])
            k.LD(xg[:], s_xdst.t.ap().rearrange("(r p) c -> p r c", p=128), [xg], R=[s_xdst])
            k.TS("dve", dm1[:, :, 0:8], xg[:, :, 1024:1032], -1.0, None, ALU.add, None, [xg], [dm1])
            k.TS("dve", dm1[:, :, 8:16], xg[:, :, 1040:1048], -1.0, None, ALU.add, None, [xg], [dm1])
            k.CP("pool", Sin[:], Sctx[:], [Sctx], [Sin])
            k.CP("pool", hin[:], hctx[:], [hctx], [hin])
            for dr in range(2):
                ranks = list(range(0, 7)) if dr == 0 else list(range(7, 0, -1))
                for i in ranks:
                    fl = flags[:, dr * 8 + i:dr * 8 + i + 1]
                    for h in range(4):
                        c0 = (dr * 4 + h) * 128
                        k.STT(tS[:], Sin[:, dr, h, :], dm1[:, i, dr * 4 + h:dr * 4 + h + 1], xg[:, i, c0:c0 + 128], ALU.mult, ALU.add,
                              [Sin, dm1, xg], [tS])
                        k.STT(Sin[:, dr, h, :], tS[:], fl, Sin[:, dr, h, :], ALU.mult, ALU.add, [tS, flags, Sin], [Sin])
                    k.TT("dve", tH[:], hin[:, dr, :], dm1[:, i, 8 + dr * 4:8 + dr * 4 + 4], ALU.mult, [hin, dm1], [tH])
                    k.TT("dve", tH[:], tH[:], xg[:, i, 1032 + dr * 4:1032 + dr * 4 + 4], ALU.add, [tH, xg], [tH])
                    k.STT(hin[:, dr, :], tH[:], fl, hin[:, dr, :], ALU.mult, ALU.add, [tH, flags, hin], [hin])
            k.CP("dve", Sinb[:], Sin[:], [Sin], [Sinb])
            if DEBUG:
                k.ST(dbg["sst"].t.ap(), Sst[:].rearrange("p a b c -> p (a b c)"), [Sst], [dbg["sst"]])
                k.ST(dbg["sctx"].t.ap(), Sctx[:].rearrange("p a b c -> p (a b c)"), [Sctx], [dbg["sctx"]])
                k.ST(dbg["sin"].t.ap(), Sin[:].rearrange("p a b c -> p (a b c)"), [Sin], [dbg["sin"]])
            P.end_phase()
            esx.close()
        if STAGE < 5:
            esm.close()
            break

        es4 = ExitStack()
        wob = P.sbuf("wob", [128, KD, D], BF16, es4)
        wst = P.sbuf("wst", [128, D], F32, es4)
        for kc in range(KD):
            if kc < 4:
                k.LD(wst[:], w_out.ap()[L, kc * 128:(kc + 1) * 128, :], [wst])
                k.TS("dve", wob[:, kc, :], wst[:], gnv[:, 0:1], None, ALU.mult, None, [wst, gnv], [wob])
            else:
                k.LD(wob[:, kc, :], w_out.ap()[L, kc * 128:(kc + 1) * 128, :], [wob], q="pool")
        osm = P.sbuf("osm", [64, 8, HGW], F32, es4)
        gtt = P.sbuf("gtt", [64, 8, HGW], BF16, es4)
        qsf = P.sbuf("qsf", [128, 4, 512], BF16, es4)
        qsb = P.sbuf("qsb", [128, 4, 512], BF16, es4)
        hst = P.sbuf("hst", [128, 4, 512], F32, es4)
        acf = P.sbuf("acf", [128, 4, 512], F32, es4)
        acb = P.sbuf("acb", [128, 4, 512], F32, es4)
        gat = P.sbuf("gat", [128, 4, 512], BF16, es4)
        xt3 = P.sbuf("xt3", [128, 4, D], F32, es4)
        grow0 = [P.sbuf("grow0_%d" % i, [128, D], F32, es4) for i in range(2)]
        for i in range(2):
            k.LD(grow0[i][:], s_grow.t[i], [grow0[i]], R=[s_grow])
        ot3 = P.sbuf("ot3", [64, 8, HGW], F32, es4)
        mixb = P.sbuf("mixb", [64, 8, HGW], BF16, es4)
        mixT = P.sbuf("mixT", [128, KD, 512], BF16, es4)
        tA = P.sbuf("tA", [128, 512], F32, es4)
        tmp3 = P.sbuf("tmp3", [128, 512], F32, es4)
        junk3 = P.sbuf("junk3", [128, 512], BF16, es4)
        ssh = P.sbuf("ssh", [64, 32], F32, es4)
        ss2 = P.sbuf("ss2", [128, 4], F32, es4)
        for ui, (tok0, T, isctx) in enumerate(units):
            if isctx and last:
                continue
            nt = T // 128
            nch = T // CH
            j = 1 if isctx else 0
            k.LD(osm[:, 0:nch, :], s_osum.t[tok0:tok0 + T, :].rearrange("(c p) n -> p c n", p=CH), [osm], R=[s_osum])
            k.LD(gtt[:, 0:nch, :], s_g.t[tok0:tok0 + T, :].rearrange("(c p) n -> p c n", p=CH), [gtt], R=[s_g])
            k.LD(hst[:, :, 0:T], s_hsum.t[:, :, tok0:tok0 + T].rearrange("c p t -> p c t"), [hst], R=[s_hsum])
            k.LD(gat[:, :, 0:T], s_gate.t[:, :, tok0:tok0 + T].rearrange("c p t -> p c t"), [gat], R=[s_gate])
            fix = (not isctx) and (not CHAIN)
            if fix:
                k.LD(qsf[:, :, 0:T], s_qseg.t[0, :, :, tok0:tok0 + T].rearrange("h p t -> p h t"), [qsf], R=[s_qseg])
                k.LD(qsb[:, :, 0:T], s_qseg.t[1, :, :, tok0:tok0 + T].rearrange("h p t -> p h t"), [qsb], R=[s_qseg])
                k.LD(acf[:, :, 0:T], s_acum.t[0, :, :, tok0:tok0 + T].rearrange("c p t -> p c t"), [acf], R=[s_acum])
                k.LD(acb[:, :, 0:T], s_acum.t[1, :, :, tok0:tok0 + T].rearrange("c p t -> p c t"), [acb], R=[s_acum])
            src, srcbuf = x_src(L, tok0, T)
            k.LD(xt3[:, 0:nt, :], src.rearrange("(j p) d -> p j d", p=128), [xt3], R=[srcbuf])
            for n in range(nch):
                if fix:
                    pf = k.ps()
                    for h in range(4):
                        hs_ = slice(h * 128, (h + 1) * 128)
                        k.MM(pf.t[0:CH, hs_], qsf[:, h, n * CH:(n + 1) * CH], Sinb[:, 0, h, :], True, False, [qsf, Sinb], [pf])
                        k.MM(pf.t[0:CH, hs_], qsb[:, h, n * CH:(n + 1) * CH], Sinb[:, 1, h, :], False, True, [qsb, Sinb], [pf])
                    k.TT("dve", ot3[:, n, :], pf.t[0:CH, 0:512], osm[:, n, :], ALU.add, [pf, osm], [ot3])
                else:
                    k.CP("act", ot3[:, n, :], osm[:, n, :], [osm], [ot3])
            ov = ot3[:, 0:nch, :].rearrange("p n (h v) -> p (n h) v", v=128)
            sq = osm[:, 0:nch, :].rearrange("p n (h v) -> p (n h) v", v=128)
            k.TT("pool", sq, ov, ov, ALU.mult, [ot3], [osm])
            P.op("dve", (lambda o_, i_: (lambda e: e.tensor_reduce(out=o_, in_=i_, axis=AX.X, op=ALU.add)))(ssh[:, 0:nch * 4], sq), [osm], [ssh])
            k.TS("dve", ssh[:, 0:nch * 4], ssh[:, 0:nch * 4], 1.0 / 128, EPS, ALU.mult, ALU.add, [ssh], [ssh])
            k.ACT(ssh[:, 0:nch * 4], ssh[:, 0:nch * 4], AF.Sqrt, [ssh], [ssh])
            P.op("dve", (lambda o_: (lambda e: e.reciprocal(out=o_, in_=o_)))(ssh[:, 0:nch * 4]), [ssh], [ssh])
            k.TT("dve", ov, ov, ssh[:, 0:nch * 4].unsqueeze(2).to_broadcast([CH, nch * 4, 128]), ALU.mult, [ot3, ssh], [ot3])
            k.TT("pool", mixb[:, 0:nch, :], ot3[:, 0:nch, :], gtt[:, 0:nch, :], ALU.mult, [ot3, gtt], [mixb])
            for h in range(4):
                pbT = k.ps()
                pT = pbT.t[:, :].bitcast(BF16)
                for n in range(nch):
                    k.TR(pT[:, n * CH:(n + 1) * CH], mixb[:, n, h * 128:(h + 1) * 128], ident[0:CH, 0:CH], [mixb, ident], [pbT])
                k.CP("act", mixT[:, h, 0:T], pT[:, 0:T], [pbT], [mixT])
            for c in range(4):
                if fix:
                    k.STT(tA[:, 0:T], acf[:, c, 0:T], hin[:, 0, c:c + 1], hst[:, c, 0:T], ALU.mult, ALU.add, [acf, hin, hst], [tA])
                    k.STT(tA[:, 0:T], acb[:, c, 0:T], hin[:, 1, c:c + 1], tA[:, 0:T], ALU.mult, ALU.add, [acb, hin, tA], [tA])
                    k.TT("pool", mixT[:, 4 + c, 0:T], tA[:, 0:T], gat[:, c, 0:T], ALU.mult, [tA, gat], [mixT])
                else:
                    k.TT("pool", mixT[:, 4 + c, 0:T], hst[:, c, 0:T], gat[:, c, 0:T], ALU.mult, [hst, gat], [mixT])
            for jj in range(nt):
                pps = [k.ps(), k.ps()]
                for half in range(2):
                    for kc in range(KD):
                        k.MM(pps[half].t[:, 0:512], mixT[:, kc, jj * 128:(jj + 1) * 128], wob[:, kc, half * 512:(half + 1) * 512],
                             kc == 0, kc == KD - 1, [mixT, wob], [pps[half]])
                    k.ACT(junk3[:], pps[half].t[:, 0:512], AF.Square, [pps[half]], [junk3, ss2], accum=ss2[:, half:half + 1])
                k.TT("dve", ss2[:, 2:3], ss2[:, 0:1], ss2[:, 1:2], ALU.add, [ss2], [ss2])
                k.TS("dve", ss2[:, 2:3], ss2[:, 2:3], 1.0 / D, EPS, ALU.mult, ALU.add, [ss2], [ss2])
                k.ACT(ss2[:, 2:3], ss2[:, 2:3], AF.Sqrt, [ss2], [ss2])
                P.op("dve", (lambda o_: (lambda e: e.reciprocal(out=o_, in_=o_)))(ss2[:, 2:3]), [ss2], [ss2])
                for half in range(2):
                    hsl = slice(half * 512, (half + 1) * 512)
                    k.STT(tmp3[:], pps[half].t[:, 0:512], ss2[:, 2:3], grow0[j][:, hsl], ALU.mult, ALU.mult,
                          [pps[half], ss2, grow0[j]], [tmp3])
                    k.TT("dve", xt3[:, jj, hsl], xt3[:, jj, hsl], tmp3[:], ALU.add, [xt3, tmp3], [xt3])
            k.ST(s_xmid.t[tok0:tok0 + T, :].rearrange("(j p) d -> p j d", p=128), xt3[:, 0:nt, :], [xt3], [s_xmid])
        P.end_phase()
        es4.close()
        esm.close()
        if STAGE < 6:
            break

        es5 = ExitStack()
        wgb = P.sbuf("wgb", [128, KD, DFF], BF16, es5)
        wub = P.sbuf("wub", [128, KD, DFF], BF16, es5)
        wdb = P.sbuf("wdb", [128, NFF, D], BF16, es5)
        for kc in range(KD):
            k.LD(wgb[:, kc, :], w_gate.ap()[L, kc * 128:(kc + 1) * 128, :], [wgb], q="pool")
            k.LD(wub[:, kc, :], w_up.ap()[L, kc * 128:(kc + 1) * 128, :], [wub], q="pool")
        for jf in range(NFF):
            k.LD(wdb[:, jf, :], w_down.ap()[L, jf * 128:(jf + 1) * 128, :], [wdb], q="pool")
        xt5s = [P.sbuf("xt5_%d" % i, [128, 2, D], F32, es5) for i in range(2)]
        fT5s = [P.sbuf("fT5_%d" % i, [128, KD, 256], BF16, es5) for i in range(2)]
        grow1 = [P.sbuf("grow1_%d" % i, [128, D], F32, es5) for i in range(2)]
        for i in range(2):
            k.LD(grow1[i][:], s_grow.t[2 + i], [grow1[i]], R=[s_grow])
        hid = P.sbuf("hid", [128, NFF, 256], BF16, es5)
        sl5 = P.sbuf("sl5", [128, 256], F32, es5)
        tmp5 = P.sbuf("tmp5", [128, 512], F32, es5)
        junk5 = P.sbuf("junk5", [128, 512], BF16, es5)
        ss5 = P.sbuf("ss5", [128, 4], F32, es5)
        nb5 = norm_bufs(es5, "p5", 2)
        toks5 = [t_ for t_ in range(0, NT, 256) if not (t_ < NCTX and last)]

        def p5A(ui, tok0):
            j = 1 if tok0 < NCTX else 0
            xt5 = xt5s[ui % 2]
            k.LD(xt5[:], s_xmid.t[tok0:tok0 + 256, :].rearrange("(j p) d -> p j d", p=128), [xt5], R=[s_xmid])
            yield from norm_mod_g(nb5, xt5, 2, 1, j, fT5s[ui % 2])

        def p5B(ui, tok0):
            isctx = tok0 < NCTX
            j = 1 if isctx else 0
            T = 256
            xt5 = xt5s[ui % 2]
            fT5 = fT5s[ui % 2]
            for jf in range(NFF):
                pg = k.ps()
                pu = k.ps()
                for kc in range(KD):
                    k.MM(pg.t[:, 0:T], wgb[:, kc, jf * 128:(jf + 1) * 128], fT5[:, kc, 0:T], kc == 0, kc == KD - 1, [wgb, fT5], [pg])
                for kc in range(KD):
                    k.MM(pu.t[:, 0:T], wub[:, kc, jf * 128:(jf + 1) * 128], fT5[:, kc, 0:T], kc == 0, kc == KD - 1, [wub, fT5], [pu])
                k.ACT(sl5[:], pg.t[:, 0:T], AF.Silu, [pg], [sl5])
                k.TT("dve", hid[:, jf, :], sl5[:], pu.t[:, 0:T], ALU.mult, [sl5, pu], [hid])
                yield
            for jj in range(2):
                pps = [k.ps(), k.ps()]
                for half in range(2):
                    for jf in range(NFF):
                        k.MM(pps[half].t[:, 0:512], hid[:, jf, jj * 128:(jj + 1) * 128], wdb[:, jf, half * 512:(half + 1) * 512],
                             jf == 0, jf == NFF - 1, [hid, wdb], [pps[half]])
                    k.ACT(junk5[:], pps[half].t[:, 0:512], AF.Square, [pps[half]], [junk5, ss5], accum=ss5[:, half:half + 1])
                k.TT("dve", ss5[:, 2:3], ss5[:, 0:1], ss5[:, 1:2], ALU.add, [ss5], [ss5])
                k.TS("dve", ss5[:, 2:3], ss5[:, 2:3], 1.0 / D, EPS, ALU.mult, ALU.add, [ss5], [ss5])
                k.ACT(ss5[:, 2:3], ss5[:, 2:3], AF.Sqrt, [ss5], [ss5])
                P.op("dve", (lambda o_: (lambda e: e.reciprocal(out=o_, in_=o_)))(ss5[:, 2:3]), [ss5], [ss5])
                for half in range(2):
                    hsl = slice(half * 512, (half + 1) * 512)
                    k.STT(tmp5[:], pps[half].t[:, 0:512], ss5[:, 2:3], grow1[j][:, hsl], ALU.mult, ALU.mult,
                          [pps[half], ss5, grow1[j]], [tmp5])
                    k.TT("dve", xt5[:, jj, hsl], xt5[:, jj, hsl], tmp5[:], ALU.add, [xt5, tmp5], [xt5])
            if last:
                k.ST(out_t.t.ap()[tok0 - NCTX:tok0 - NCTX + T, :].rearrange("(j p) d -> p j d", p=128), xt5[:], [xt5], [out_t])
            else:
                k.ST(s_xres.t[tok0:tok0 + T, :].rearrange("(j p) d -> p j d", p=128), xt5[:], [xt5], [s_xres])
        pipeline(toks5, p5A, p5B)
        P.end_phase()
        es5.close()

    fin = []
    if DEBUG:
        pairs = [("qT", s_qT), ("fT", s_fT), ("v", s_v), ("g", s_g), ("u", s_u), ("gate", s_gate)]
        if STAGE >= 2:
            pairs += [("hsum", s_hsum), ("acum", s_acum)]
        if STAGE >= 3:
            pairs += [("osum", s_osum), ("qseg", s_qseg)]
        if STAGE >= 5:
            pairs += [("xdst", s_xdst)]
        if STAGE >= 6:
            pairs += [("xmid", s_xmid)]
        if STAGE >= 7:
            pairs += [("xres", s_xres)]
        P.barrier()
        for nm, sb in pairs:
            fin.append(k.ST(dbg[nm].t.ap(), sb.t.ap(), [sb], [dbg[nm]]))
        k.ST(dbg["misc"].t.ap()[:, 0:96], modfm[:].rearrange("p a b -> p (a b)"), [modfm], [dbg["misc"]])
        k.ST(dbg["misc"].t.ap()[:, 96:160], scsh[:].rearrange("p a b c -> p (a b c)"), [scsh], [dbg["misc"]])
        fin.append(k.ST(dbg["misc"].t.ap()[:, 160:168], c1v[:].rearrange("p a b -> p (a b)"), [c1v], [dbg["misc"]]))
        k.ST(dbg["misc"].t.ap()[:, 168:176], hctx[:].rearrange("p a b -> p (a b)"), [hctx], [dbg["misc"]])
        k.ST(dbg["misc"].t.ap()[:, 176:184], hfin[:].rearrange("p a b -> p (a b)"), [hfin], [dbg["misc"]])
        k.ST(dbg["misc"].t.ap()[:, 184:192], atot[:].rearrange("p a b -> p (a b)"), [atot], [dbg["misc"]])
        k.ST(dbg["misc"].t.ap()[:, 192:200], dtot[:].rearrange("p a b -> p (a b)"), [dtot], [dbg["misc"]])
        k.ST(dbg["misc"].t.ap()[:, 200:208], hin[:].rearrange("p a b -> p (a b)"), [hin], [dbg["misc"]])
    P.barrier()
    P.emit()
    return nc


def make_in_maps(inp):
    f = lambda a: np.ascontiguousarray(np.asarray(a, dtype=np.float32))
    x, c, ctx, c_ctx = f(inp["x"]), f(inp["c"]), f(inp["ctx"]), f(inp["c_ctx"])
    b_mod, norm_g = f(inp["b_mod"]), f(inp["norm_g"])
    common = {
        "w_mod": f(inp["w_mod"]),
        "bmod_fm": f(b_mod.reshape(DEPTH, 6, 8, 128).transpose(0, 3, 1, 2).reshape(DEPTH, 128, 48)),
        "bmod_row": f(b_mod.reshape(DEPTH, 1, 6 * D)),
        "normg_fm": f(norm_g.reshape(DEPTH, 4, 8, 128).transpose(0, 3, 1, 2).reshape(DEPTH, 128, 32)),
        "normg_row": norm_g,
        "w_in": f(inp["w_in"]),
        "lb_fm": f(f(inp["hg_lb_logits"]).reshape(DEPTH, 2, 4, 128).transpose(3, 0, 1, 2)),
        "gn_fm": f(f(inp["hg_gnorm"]).reshape(DEPTH, 128, 1)),
        "convw_fm": f(f(inp["rg_conv_w"]).reshape(DEPTH, 4, 4, 128).transpose(0, 3, 2, 1)),
        "convb_fm": f(f(inp["rg_conv_b"]).reshape(DEPTH, 4, 128).transpose(0, 2, 1)),
        "ba_fm": f(f(inp["rg_b_a"]).reshape(DEPTH, 2, 4, 128).transpose(0, 3, 1, 2)),
        "bx_fm": f(f(inp["rg_b_x"]).reshape(DEPTH, 2, 4, 128).transpose(0, 3, 1, 2)),
        "lam_fm": f(f(inp["rg_lambda"]).reshape(DEPTH, 2, 4, 128).transpose(0, 3, 1, 2)),
        "rg_w_a": f(inp["rg_w_a"]),
        "rg_w_x": f(inp["rg_w_x"]),
        "w_out": f(inp["w_out"]),
        "w_ffn_gate": f(inp["w_ffn_gate"]),
        "w_ffn_up": f(inp["w_ffn_up"]),
        "w_ffn_down": f(inp["w_ffn_down"]),
        "ident": np.eye(128, dtype=np.float32),
    }
    tri = np.triu(np.ones((64, 64), np.float32))
    common["masks"] = f(np.stack([np.tile(tri, (1, 8)), np.tile(tri.T, (1, 8))]))
    maps = []
    for core in range(8):
        if CHAIN:
            b, seg = core % 2, 0
        else:
            b, seg = core // 4, core % 4
        m = dict(common)
        m["x"] = f(x[b, seg * NLAT:(seg + 1) * NLAT])
        m["ctx"] = f(ctx[b])
        cv = np.stack([c[b].reshape(8, 128).T, c_ctx.reshape(8, 128).T], axis=-1)
        m["cvec"] = f(cv)
        fl = np.zeros((128, 16), np.float32)
        for r in range(8):
            same = (r // 4 == b) and not CHAIN
            fl[:, r] = 1.0 if (same and r % 4 < seg) else 0.0
            fl[:, 8 + r] = 1.0 if (same and r % 4 > seg) else 0.0
        m["flags"] = fl
        maps.append(m)
    return maps


_NC_CACHE = {}


def kernel(**inputs):
    if "nc" not in _NC_CACHE:
        _NC_CACHE["nc"] = build()
    nc = _NC_CACHE["nc"]
    maps = make_in_maps(inputs)
    res = run_bass_kernel_spmd(nc, maps, core_ids=list(range(8)))
    out = np.empty((2, 16384, D), np.float32)
    for core in range(8):
        if CHAIN:
            if core >= 2:
                continue
            b, seg = core, 0
        else:
            b, seg = core // 4, core % 4
        out[b, seg * NLAT:(seg + 1) * NLAT] = np.asarray(res.results[core]["out"], dtype=np.float32)
    return out
```

```python
import numpy as np
from contextlib import ExitStack
import concourse.bass as bass
import concourse.mybir as mybir
from concourse.bass_utils import run_bass_kernel_spmd

F32 = mybir.dt.float32
BF16 = mybir.dt.bfloat16
AF = mybir.ActivationFunctionType
ALU = mybir.AluOpType
AX = mybir.AxisListType

MODE = "whole2"
CHAIN = (MODE == "whole2")
D = 1024
KD = 8
NLAT = 16384 if CHAIN else 4096
NCTX = 256
NT = NLAT + NCTX
HGW = 512
RGW = 512
INC = 3584
DFF = 2816
NFF = 22
DEPTH = 2
EPS = 1e-6
CH = 64

DEBUG = False
STAGE = 99
USE_CC = True


class Buf:
    def __init__(self, prog, t, name):
        self.prog = prog
        self.t = t
        self.name = name
        self.last_w = None
        self.readers = []
        self.dma_sem = None
        self.dma_cnt = 0
        self.rd_sem = None
        self.rd_cnt = 0

    def __getitem__(self, idx):
        return self.t[idx]


class Prog:
    ENGS = ("pe", "act", "dve", "pool", "sp")

    def __init__(self, nc):
        self.nc = nc
        self.es = ExitStack()
        self.ops = {e: [] for e in self.ENGS}
        self.cnt = {e: 0 for e in self.ENGS}
        self.sems = {}
        for e in self.ENGS:
            self.sems[e] = self.es.enter_context(nc.semaphore("prog_" + e))
        self.known = {e: {} for e in self.ENGS}
        self.final_waits = []
        self._dma_sem_vals = {}
        self.sem_pool = []
        self.cur_bufs = []
        self.nsem = 0

    def sbuf(self, name, shape, dtype, es=None):
        self.nuniq = getattr(self, "nuniq", 0) + 1
        t = (es or self.es).enter_context(self.nc.sbuf_tensor("sb%d_%s" % (self.nuniq, name), list(shape), dtype))
        b = Buf(self, t, name)
        if es is not None:
            self.cur_bufs.append(b)
        return b

    def end_phase(self):
        self.barrier()
        for b in self.cur_bufs:
            if b.dma_sem is not None:
                self.sem_pool.append((b.dma_sem, b.dma_cnt))
                b.dma_sem = None
            if b.rd_sem is not None:
                self.sem_pool.append((b.rd_sem, b.rd_cnt))
                b.rd_sem = None
        self.cur_bufs = []

    def psum(self, name, shape, dtype):
        t = self.es.enter_context(self.nc.psum_tensor(name, list(shape), dtype))
        return Buf(self, t, name)

    def dram(self, name, shape, dtype):
        t = self.nc.dram_tensor(name, list(shape), dtype)
        return Buf(self, t, name)

    def _new_sem(self, name):
        if self.sem_pool:
            return self.sem_pool.pop()
        self.nsem += 1
        return (self.es.enter_context(self.nc.semaphore("s%d" % self.nsem)), 0)

    def _deps(self, eng, reads, writes):
        need = {}

        def add(ev):
            if ev is None:
                return
            key, val, src = ev
            if src == eng and eng == "pe":
                return
            if key not in need or need[key][0] < val:
                need[key] = (val, src)

        for b in reads:
            add(b.last_w)
        for b in writes:
            add(b.last_w)
            for r in b.readers:
                if r[2] == eng:
                    continue
                add(r)
        waits = []
        kn = self.known[eng]
        for key, (val, src) in need.items():
            if kn.get(key, 0) >= val:
                continue
            kn[key] = val
            waits.append((key, val))
        return waits

    def _semobj(self, key):
        if isinstance(key, str):
            return self.sems[key]
        return key

    def _mark(self, ev, reads, writes):
        for b in writes:
            b.last_w = ev
            b.readers = []
        for b in reads:
            if b in writes:
                continue
            b.readers.append(ev)
            if len(b.readers) > 48:
                latest = {}
                for r in b.readers:
                    k = r[0] if isinstance(r[0], str) else id(r[0])
                    if k not in latest or latest[k][1] < r[1]:
                        latest[k] = r
                b.readers = list(latest.values())

    def op(self, eng, fn, reads=(), writes=()):
        reads = [b for b in reads if b is not None]
        writes = [b for b in writes if b is not None]
        waits = self._deps(eng, reads, writes)
        self.cnt[eng] += 1
        ev = (eng, self.cnt[eng], eng)
        self.ops[eng].append(("op", waits, fn))
        self._mark(ev, reads, writes)
        return ev

    def dma(self, q, out_ap, in_ap, reads=(), writes=(), **kw):
        reads = [b for b in reads if b is not None]
        writes = [b for b in writes if b is not None]
        waits = self._deps(q, reads, writes)
        owner = writes[0] if writes else reads[0]
        if writes:
            if owner.dma_sem is None:
                owner.dma_sem, owner.dma_cnt = self._new_sem("dw_" + owner.name)
            owner.dma_cnt += 16
            sem, val = owner.dma_sem, owner.dma_cnt
        else:
            if owner.rd_sem is None:
                owner.rd_sem, owner.rd_cnt = self._new_sem("dr_" + owner.name)
            owner.rd_cnt += 16
            sem, val = owner.rd_sem, owner.rd_cnt
        ev = (sem, val, "dma")
        self._dma_sem_vals[sem] = val
        self.ops[q].append(("dma", waits, (out_ap, in_ap, sem, kw)))
        self._mark(ev, reads, writes)
        return ev

    def custom(self, q, fn, reads=(), writes=(), inc=16):
        reads = [b for b in reads if b is not None]
        writes = [b for b in writes if b is not None]
        waits = self._deps(q, reads, writes)
        owner = writes[0]
        if owner.dma_sem is None:
            owner.dma_sem, owner.dma_cnt = self._new_sem("dw_" + owner.name)
        owner.dma_cnt += inc
        sem, val = owner.dma_sem, owner.dma_cnt
        ev = (sem, val, "dma")
        self._dma_sem_vals[sem] = val
        self.ops[q].append(("custom", waits, (fn, sem, inc)))
        self._mark(ev, reads, writes)
        return ev

    def barrier(self):
        evs = [(e, self.cnt[e]) for e in self.ENGS if self.cnt[e] > 0]
        evs += list(self._dma_sem_vals.items())
        for e in self.ENGS:
            waits = []
            kn = self.known[e]
            for key, val in evs:
                if kn.get(key, 0) >= val:
                    continue
                kn[key] = val
                waits.append((key, val))
            if waits:
                self.ops[e].append(("wait", waits, None))

    def emit(self):
        nc = self.nc
        engmap = {"pe": "tensor", "act": "scalar", "dve": "vector", "pool": "gpsimd", "sp": "sync"}
        with nc.Block() as block:
            for e in self.ENGS:
                ops = self.ops[e]
                if not ops:
                    continue
                semself = self.sems[e]

                def body(engine, ops=ops, semself=semself):
                    for kind, waits, payload in ops:
                        for key, val in waits:
                            engine.wait_ge(self._semobj(key), val)
                        if kind == "op":
                            payload(engine).then_inc(semself, 1)
                        elif kind == "dma":
                            out_ap, in_ap, sem, kw = payload
                            engine.dma_start(out=out_ap, in_=in_ap, **kw).then_inc(sem, 16)
                        elif kind == "custom":
                            fn, sem, inc = payload
                            fn(engine).then_inc(sem, inc)

                getattr(block, engmap[e])(body)
        self.es.close()


class K:
    def __init__(self):
        nc = bass.Bass("TRN2", target_bir_lowering=False)
        self.nc = nc
        self.P = Prog(nc)
        self.ins = {}
        self.outs = {}
        self.psn = 0

    def inp(self, name, shape, dtype=F32):
        t = self.nc.dram_tensor(name, list(shape), dtype, kind="ExternalInput")
        self.ins[name] = t
        return t

    def outp(self, name, shape, dtype=F32):
        t = self.nc.dram_tensor(name, list(shape), dtype, kind="ExternalOutput")
        b = Buf(self.P, t, name)
        self.outs[name] = b
        return b

    def ACT(self, out, in_, func, R, W, bias=None, scale=None, accum=None):
        kw = {}
        if bias is not None:
            kw["bias"] = bias
        if scale is not None:
            kw["scale"] = scale
        if accum is not None:
            kw["accum_out"] = accum
        return self.P.op("act", lambda e: e.activation(out=out, in_=in_, func=func, **kw), R, W)

    def TS(self, eng, out, in0, s1, s2, op0, op1, R, W):
        if op1 is None:
            return self.P.op(eng, lambda e: e.tensor_scalar(out=out, in0=in0, scalar1=s1, scalar2=None, op0=op0), R, W)
        return self.P.op(eng, lambda e: e.tensor_scalar(out=out, in0=in0, scalar1=s1, scalar2=s2, op0=op0, op1=op1), R, W)

    def TT(self, eng, out, in0, in1, op, R, W):
        return self.P.op(eng, lambda e: e.tensor_tensor(out=out, in0=in0, in1=in1, op=op), R, W)

    def STT(self, out, in0, scalar, in1, op0, op1, R, W):
        return self.P.op("dve", lambda e: e.scalar_tensor_tensor(out=out, in0=in0, scalar=scalar, in1=in1, op0=op0, op1=op1), R, W)

    def MM(self, out, lhsT, rhs, start, stop, R, W):
        return self.P.op("pe", lambda e: e.matmul(out, lhsT=lhsT, rhs=rhs, start=start, stop=stop), R, W)

    def TR(self, out, in_, ident, R, W):
        return self.P.op("pe", lambda e: e.transpose(out, in_, ident), R, W)

    def CP(self, eng, out, in_, R, W):
        if eng == "act":
            return self.P.op("act", lambda e: e.copy(out=out, in_=in_), R, W)
        return self.P.op(eng, lambda e: e.tensor_copy(out=out, in_=in_), R, W)

    def SCAN(self, out, d0, d1, init, R, W):
        return self.P.op("dve", lambda e: e.tensor_tensor_scan(out=out, data0=d0, data1=d1, initial=init, op0=ALU.mult, op1=ALU.add), R, W)

    def MEMSET(self, eng, ap, val, W):
        return self.P.op(eng, lambda e: e.memset(ap, val), [], W)

    def LD(self, out, in_, W, R=(), q="sp"):
        return self.P.dma(q, out, in_, reads=list(R), writes=list(W))

    def ST(self, out, in_, R, W=(), q="sp"):
        return self.P.dma(q, out, in_, reads=list(R), writes=list(W))

    def ps(self):
        b = self.psb[self.psn % getattr(self, "nrot", 8)]
        self.psn += 1
        return b


def build(nlayers=DEPTH):
    k = K()
    nc, P = k.nc, k.P
    x_in = k.inp("x", [NLAT, D])
    ctx_in = k.inp("ctx", [NCTX, D])
    cvec = k.inp("cvec", [128, KD, 2])
    w_mod = k.inp("w_mod", [DEPTH, D, 6 * D])
    bmod_fm = k.inp("bmod_fm", [DEPTH, 128, 48])
    bmod_row = k.inp("bmod_row", [DEPTH, 1, 6 * D])
    normg_fm = k.inp("normg_fm", [DEPTH, 128, 32])
    normg_row = k.inp("normg_row", [DEPTH, 4, D])
    w_in = k.inp("w_in", [DEPTH, D, INC])
    lb_fm = k.inp("lb_fm", [128, DEPTH, 2, 4])
    gn_fm = k.inp("gn_fm", [DEPTH, 128, 1])
    convw_fm = k.inp("convw_fm", [DEPTH, 128, 4, 4])
    convb_fm = k.inp("convb_fm", [DEPTH, 128, 4])
    ba_fm = k.inp("ba_fm", [DEPTH, 128, 2, 4])
    bx_fm = k.inp("bx_fm", [DEPTH, 128, 2, 4])
    lam_fm = k.inp("lam_fm", [DEPTH, 128, 2, 4])
    rg_w_a = k.inp("rg_w_a", [DEPTH, 2, 8, 64, 64])
    rg_w_x = k.inp("rg_w_x", [DEPTH, 2, 8, 64, 64])
    w_out = k.inp("w_out", [DEPTH, D, D])
    w_gate = k.inp("w_ffn_gate", [DEPTH, D, DFF])
    w_up = k.inp("w_ffn_up", [DEPTH, D, DFF])
    w_down = k.inp("w_ffn_down", [DEPTH, DFF, D])
    flags_in = k.inp("flags", [128, 16])
    ident_in = k.inp("ident", [128, 128])
    masks_in = k.inp("masks", [2, 64, 512])
    out_t = k.outp("out", [NLAT, D])

    s_qT = P.dram("s_qT", [4, 128, NT], BF16)
    s_fT = P.dram("s_fT", [2, 4, 128, NT], F32)
    s_v = P.dram("s_v", [NT, HGW], BF16)
    s_g = P.dram("s_g", [NT, HGW], BF16)
    s_u = P.dram("s_u", [4, 128, NT], BF16)
    s_gate = P.dram("s_gate", [4, 128, NT], BF16)
    s_hsum = P.dram("s_hsum", [4, 128, NT], F32)
    s_hsumf = P.dram("s_hsumf", [4, 128, NT], F32)
    s_acum = P.dram("s_acum", [2, 4, 128, NT], F32)
    s_osum = P.dram("s_osum", [NT, HGW], F32)
    s_of = P.dram("s_of", [NT, HGW], F32)
    s_qseg = P.dram("s_qseg", [2, 4, 128, NT], BF16)
    s_xres = P.dram("s_xres", [NT, D], F32)
    s_xmid = P.dram("s_xmid", [NT, D], F32)
    s_grow = P.dram("s_grow", [4, 128, D], F32)
    XW = 1024 + 8 + 8 + 8
    s_xsrc = P.dram("s_xsrc", [128, XW], F32)
    s_xdst = P.dram("s_xdst", [8 * 128, XW], F32)

    dbg = {}
    if DEBUG:
        dbg["qT"] = k.outp("d_qT", [4, 128, NT], BF16)
        dbg["fT"] = k.outp("d_fT", [2, 4, 128, NT], F32)
        dbg["v"] = k.outp("d_v", [NT, HGW], BF16)
        dbg["g"] = k.outp("d_g", [NT, HGW], BF16)
        dbg["u"] = k.outp("d_u", [4, 128, NT], BF16)
        dbg["gate"] = k.outp("d_gate", [4, 128, NT], BF16)
        dbg["hsum"] = k.outp("d_hsum", [4, 128, NT], F32)
        dbg["acum"] = k.outp("d_acum", [2, 4, 128, NT], F32)
        dbg["osum"] = k.outp("d_osum", [NT, HGW], F32)
        dbg["qseg"] = k.outp("d_qseg", [2, 4, 128, NT], BF16)
        dbg["xmid"] = k.outp("d_xmid", [NT, D], F32)
        dbg["xres"] = k.outp("d_xres", [NT, D], F32)
        dbg["xdst"] = k.outp("d_xdst", [8 * 128, XW], F32)
        dbg["misc"] = k.outp("d_misc", [128, 256], F32)
        dbg["sst"] = k.outp("d_sst", [128, 1024], F32)
        dbg["sctx"] = k.outp("d_sctx", [128, 1024], F32)
        dbg["sin"] = k.outp("d_sin", [128, 1024], F32)

    k.psb = [P.psum("psb%d" % i, [128, 512], F32) for i in range(8)]

    ident = P.sbuf("ident", [128, 128], BF16)
    maski = P.sbuf("maski", [64, 2, 512], mybir.dt.int32)
    ones_row = P.sbuf("ones_row", [1, 128], F32)
    ones_t = P.sbuf("ones_t", [128, 512], F32)
    flags = P.sbuf("flags", [128, 16], F32)
    cv_f = P.sbuf("cv_f", [128, KD, 2], F32)
    scbf = P.sbuf("scbf", [128, KD, 2], BF16)
    modfm = P.sbuf("modfm", [128, 48, 2], F32)
    bmodfm = P.sbuf("bmodfm", [128, 48], F32)
    gfm = P.sbuf("gfm", [128, 32], F32)
    scsh = P.sbuf("scsh", [128, 4, KD, 2], F32)
    lbt = P.sbuf("lbt", [128, DEPTH, 2, 4], F32)
    lbv = P.sbuf("lbv", [128, 2, 4], F32)
    omlv = P.sbuf("omlv", [128, 2, 4], F32)
    gnv = P.sbuf("gnv", [128, 1], F32)
    cwv = P.sbuf("cwv", [128, 4, 4], F32)
    cbv = P.sbuf("cbv", [128, 4], F32)
    bav = P.sbuf("bav", [128, 2, 4], F32)
    bxv = P.sbuf("bxv", [128, 2, 4], F32)
    c1v = P.sbuf("c1v", [128, 2, 4], F32)
    Sst = P.sbuf("Sst", [128, 2, 4, 128], F32)
    dtot = P.sbuf("dtot", [128, 2, 4], F32)
    hctx = P.sbuf("hctx", [128, 2, 4], F32)
    hfin = P.sbuf("hfin", [128, 2, 4], F32)
    atot = P.sbuf("atot", [128, 2, 4], F32)
    hin = P.sbuf("hin", [128, 2, 4], F32)

    ess = ExitStack()
    ident_f = P.sbuf("ident_f", [128, 128], F32, ess)
    masks_f = P.sbuf("masks_f", [64, 2, 512], F32, ess)
    k.LD(ident_f[:], ident_in.ap(), [ident_f])
    k.CP("dve", ident[:], ident_f[:], [ident_f], [ident])
    k.LD(masks_f[:], masks_in.ap().rearrange("a s t -> s a t"), [masks_f])
    k.CP("dve", maski[:], masks_f[:], [masks_f], [maski])
    k.MEMSET("pool", ones_row[:], 1.0, [ones_row])
    k.MEMSET("pool", ones_t[:], 1.0, [ones_t])
    k.LD(flags[:], flags_in.ap(), [flags])
    k.LD(cv_f[:], cvec.ap(), [cv_f])
    k.ACT(scbf[:], cv_f[:], AF.Silu, [cv_f], [scbf])
    k.LD(lbt[:], lb_fm.ap(), [lbt])

    P.end_phase()
    ess.close()

    units = [(0, NCTX, True)] + [(NCTX + 512 * i, 512, False) for i in range(NLAT // 512)]

    def x_src(L, tok0, T):
        if L == 0:
            if tok0 < NCTX:
                return ctx_in.ap()[tok0:tok0 + T, :], None
            return x_in.ap()[tok0 - NCTX:tok0 - NCTX + T, :], None
        return s_xres.t[tok0:tok0 + T, :], s_xres

    def norm_bufs(es, tag, ntmax):
        return (P.sbuf("ssq_" + tag, [128, 4], F32, es), P.sbuf("rstd_" + tag, [128, 4], F32, es),
                P.sbuf("junk_" + tag, [128, D], BF16, es), P.sbuf("xn_" + tag, [128, ntmax, D], BF16, es))

    def norm_mod(nb, xt, nt, a, j, hT):
        drain(norm_mod_g(nb, xt, nt, a, j, hT))

    def norm_mod_g(nb, xt, nt, a, j, hT):
        T = nt * 128
        ssq, rstd, junk, xn = nb
        for jj in range(nt):
            k.ACT(junk[:], xt[:, jj, :], AF.Square, [xt], [junk, ssq], accum=ssq[:, jj:jj + 1])
        k.TS("dve", rstd[:, 0:nt], ssq[:, 0:nt], 1.0 / D, EPS, ALU.mult, ALU.add, [ssq], [rstd])
        k.ACT(rstd[:, 0:nt], rstd[:, 0:nt], AF.Sqrt, [rstd], [rstd])
        P.op("dve", lambda e: e.reciprocal(out=rstd[:, 0:nt], in_=rstd[:, 0:nt]), [rstd], [rstd])
        yield
        for jj in range(nt):
            k.ACT(xn[:, jj, :], xt[:, jj, :], AF.Copy, [xt, rstd], [xn], scale=rstd[:, jj:jj + 1])
        yield
        yield
        for kc in range(KD):
            if kc == 4:
                yield
            pb = k.ps()
            pst = pb.t[:, :].bitcast(BF16)
            for jj in range(nt):
                k.TR(pst[:, jj * 128:(jj + 1) * 128], xn[:, jj, kc * 128:(kc + 1) * 128], ident[:], [xn, ident], [pb])
            k.TS("dve", hT[:, kc, 0:T], pst[:, 0:T], scsh[:, 2 * a, kc, j:j + 1], scsh[:, 2 * a + 1, kc, j:j + 1],
                 ALU.mult, ALU.add, [pb, scsh], [hT])

    def drain(g):
        for _ in g:
            pass

    def interleave(g1, g2):
        a1, a2 = g1 is not None, g2 is not None
        while a1 or a2:
            if a1:
                try:
                    next(g1)
                except StopIteration:
                    a1 = False
            if a2:
                try:
                    next(g2)
                except StopIteration:
                    a2 = False

    def pipeline(items, genA, genB):
        if not items:
            return
        drain(genA(0, items[0]))
        for i, it_ in enumerate(items):
            nxt = genA(i + 1, items[i + 1]) if i + 1 < len(items) else None
            interleave(nxt, genB(i, it_))

    for L in range(nlayers):
        last = (L == DEPTH - 1)
        es0 = ExitStack()
        wblk = P.sbuf("wblk", [128, KD, D], BF16, es0)
        rowt = P.sbuf("rowt", [1, 512], F32, es0)
        growt = P.sbuf("growt", [128, D], F32, es0)
        brow = P.sbuf("brow", [1, 6 * D], F32, es0)
        g1row = P.sbuf("g1row", [1, D], F32, es0)
        g3row = P.sbuf("g3row", [1, D], F32, es0)
        lamt = P.sbuf("lamt", [128, 2, 4], F32, es0)
        k.LD(bmodfm[:], bmod_fm.ap()[L], [bmodfm])
        k.LD(gfm[:], normg_fm.ap()[L], [gfm])
        k.LD(brow[:], bmod_row.ap()[L], [brow])
        k.LD(g1row[:], normg_row.ap()[L, 1:2, :], [g1row])
        k.LD(g3row[:], normg_row.ap()[L, 3:4, :], [g3row])
        k.LD(gnv[:], gn_fm.ap()[L], [gnv])
        k.LD(cwv[:], convw_fm.ap()[L], [cwv])
        k.LD(cbv[:], convb_fm.ap()[L], [cbv])
        k.LD(bav[:], ba_fm.ap()[L], [bav])
        k.LD(bxv[:], bx_fm.ap()[L], [bxv])
        k.LD(lamt[:], lam_fm.ap()[L], [lamt])
        k.ACT(c1v[:], lamt[:], AF.Exp, [lamt], [c1v], scale=-1.0)
        k.ACT(c1v[:], c1v[:], AF.Ln, [c1v], [c1v], bias=1.0)
        k.TS("dve", c1v[:], c1v[:], -8.0, None, ALU.mult, None, [c1v], [c1v])
        if L == 0:
            k.MEMSET("pool", lbv[:], 0.0, [lbv])
        else:
            k.TT("dve", lbv[:], lbt[:, 1], lbt[:, 0], ALU.subtract, [lbt], [lbv])
            k.ACT(lbv[:], lbv[:], AF.Sigmoid, [lbv], [lbv])
        k.TS("dve", omlv[:], lbv[:], -1.0, 1.0, ALU.mult, ALU.add, [lbv], [omlv])
        for n in range(6):
            k.LD(wblk[:], w_mod.ap()[L, :, n * D:(n + 1) * D].rearrange("(kc p) c -> p kc c", p=128), [wblk], q="pool")
            pb = k.ps()
            for kd in range(KD):
                for kc in range(KD):
                    k.MM(pb.t[:, kd * 2:kd * 2 + 2], wblk[:, kc, kd * 128:(kd + 1) * 128], scbf[:, kc, :],
                         kc == 0, kc == KD - 1, [wblk, scbf], [pb])
            k.TT("dve", modfm[:, n * 8:(n + 1) * 8, :], pb.t[:, 0:16].rearrange("p (a b) -> p a b", b=2),
                 bmodfm[:, n * 8:(n + 1) * 8].unsqueeze(2).to_broadcast([128, 8, 2]), ALU.add, [pb, bmodfm], [modfm])
            if n in (2, 5):
                a = 0 if n == 2 else 1
                grow_g = g1row if n == 2 else g3row
                for j in range(2):
                    for half in range(2):
                        pb2 = k.ps()
                        for kc in range(KD):
                            k.MM(pb2.t[0:1, 0:512], scbf[:, kc, j:j + 1], wblk[:, kc, half * 512:(half + 1) * 512],
                                 kc == 0, kc == KD - 1, [wblk, scbf], [pb2])
                        k.TT("dve", rowt[:], pb2.t[0:1, 0:512], brow[0:1, n * D + half * 512:n * D + (half + 1) * 512],
                             ALU.add, [pb2, brow], [rowt])
                        k.TT("dve", rowt[:], rowt[:], grow_g[0:1, half * 512:(half + 1) * 512], ALU.mult,
                             [rowt, grow_g], [rowt])
                        pb3 = k.ps()
                        k.MM(pb3.t[:, 0:512], ones_row[0:1, :], rowt[0:1, :], True, True, [ones_row, rowt], [pb3])
                        k.CP("act", growt[:, half * 512:(half + 1) * 512], pb3.t[:, 0:512], [pb3], [growt])
                    k.ST(s_grow.t[2 * a + j], growt[:], [growt], [s_grow])
        for a, (nsc, nsh, gi) in enumerate(((1, 0, 0), (4, 3, 2))):
            k.TS("dve", scsh[:, 2 * a], modfm[:, nsc * 8:(nsc + 1) * 8, :], 1.0, None, ALU.add, None, [modfm], [scsh])
            k.TT("dve", scsh[:, 2 * a], scsh[:, 2 * a], gfm[:, gi * 8:(gi + 1) * 8].unsqueeze(2).to_broadcast([128, 8, 2]),
                 ALU.mult, [scsh, gfm], [scsh])
            k.CP("dve", scsh[:, 2 * a + 1], modfm[:, nsh * 8:(nsh + 1) * 8, :], [modfm], [scsh])
        P.end_phase()
        es0.close()
        if STAGE < 1:
            break

        es1 = ExitStack()
        winb = P.sbuf("winb", [128, KD, INC], BF16, es1)
        for kc in range(KD):
            k.LD(winb[:, kc, :], w_in.ap()[L, kc * 128:(kc + 1) * 128, :], [winb], q="pool")
        xts = [P.sbuf("xt%d" % i, [128, 4, D], F32, es1) for i in range(1)]
        hTs = [P.sbuf("hT%d" % i, [128, KD, 512], BF16, es1) for i in range(2)]
        qs = P.sbuf("qs", [128, 4, 512], BF16, es1)
        sg = P.sbuf("sg", [128, 512], F32, es1)
        ft = P.sbuf("ft", [128, 2, 4, 512], F32, es1)
        uf = P.sbuf("uf", [128, 512], F32, es1)
        uc = P.sbuf("uc", [128, 512], F32, es1)
        ub = P.sbuf("ub", [128, 4, 512], BF16, es1)
        gb = P.sbuf("gb", [128, 4, 512], BF16, es1)
        vt = P.sbuf("vt", [128, 4, HGW], BF16, es1)
        gt = P.sbuf("gt", [128, 4, HGW], BF16, es1)
        nb1 = norm_bufs(es1, "p1", 4)
        def p1A(ui, unit):
            tok0, T, isctx = unit
            nt = T // 128
            j = 1 if isctx else 0
            xt = xts[0]
            src, srcbuf = x_src(L, tok0, T)
            k.LD(xt[:, 0:nt, :], src.rearrange("(j p) d -> p j d", p=128), [xt], R=[srcbuf])
            yield from norm_mod_g(nb1, xt, nt, 0, j, hTs[ui % 2])

        def p1B(ui, unit):
            tok0, T, isctx = unit
            nt = T // 128
            nch = T // CH
            j = 1 if isctx else 0
            hT = hTs[ui % 2]
            for ct in range(20):
                if ct < 12:
                    c0 = ct * 128
                else:
                    c0 = 5 * HGW + (ct - 12) * 128
                pb = k.ps()
                for kc in range(KD):
                    k.MM(pb.t[:, 0:T], winb[:, kc, c0:c0 + 128], hT[:, kc, 0:T], kc == 0, kc == KD - 1, [winb, hT], [pb])
                if ct < 4:
                    k.ACT(qs[:, ct, 0:T], pb.t[:, 0:T], AF.Silu, [pb], [qs])
                elif ct < 12:
                    dr, h = (ct - 4) // 4, (ct - 4) % 4
                    k.ACT(sg[:, 0:T], pb.t[:, 0:T], AF.Sigmoid, [pb], [sg])
                    k.TS("dve", ft[:, dr, h, 0:T], sg[:, 0:T], omlv[:, dr, h:h + 1], lbv[:, dr, h:h + 1], ALU.mult, ALU.add,
                         [sg, omlv, lbv], [ft])
                elif ct < 16:
                    c = ct - 12
                    k.CP("act", uf[:, 0:T], pb.t[:, 0:T], [pb], [uf])
                    RW = T if isctx else 64
                    ufv = uf[:, 0:T].rearrange("p (r w) -> p r w", w=RW)
                    ucv = uc[:, 0:T].rearrange("p (r w) -> p r w", w=RW)
                    k.TS("dve", uc[:, 0:T], uf[:, 0:T], cwv[:, c, 2:3], cbv[:, c:c + 1], ALU.mult, ALU.add, [uf, cwv, cbv], [uc])
                    for tap in (0, 1, 3):
                        s = tap - 2
                        if s < 0:
                            o_sl = ucv[:, :, -s:RW]
                            i_sl = ufv[:, :, 0:RW + s]
                        else:
                            o_sl = ucv[:, :, 0:RW - s]
                            i_sl = ufv[:, :, s:RW]
                        k.STT(o_sl, i_sl, cwv[:, c, tap:tap + 1], o_sl, ALU.mult, ALU.add, [uf, uc, cwv], [uc])
                    k.CP("act", ub[:, c, 0:T], uc[:, 0:T], [uc], [ub])
                else:
                    c = ct - 16
                    k.ACT(gb[:, c, 0:T], pb.t[:, 0:T], AF.Gelu_apprx_tanh, [pb], [gb])
                if ct % 2 == 1:
                    yield
            for jj in range(nt):
                for which in range(2):
                    c0 = (3 + which) * HGW
                    pb = k.ps()
                    for kc in range(KD):
                        k.MM(pb.t[:, 0:512], hT[:, kc, jj * 128:(jj + 1) * 128], winb[:, kc, c0:c0 + 512],
                             kc == 0, kc == KD - 1, [winb, hT], [pb])
                    if which == 0:
                        k.CP("act", vt[:, jj, :], pb.t[:, 0:512], [pb], [vt])
                    else:
                        k.ACT(gt[:, jj, :], pb.t[:, 0:512], AF.Silu, [pb], [gt])
                yield
            k.ST(s_qT.t[:, :, tok0:tok0 + T].rearrange("h p t -> p h t"), qs[:, :, 0:T], [qs], [s_qT])
            for dr in range(2):
                k.ST(s_fT.t[dr, :, :, tok0:tok0 + T].rearrange("h p t -> p h t"), ft[:, dr, :, 0:T], [ft], [s_fT])
            k.ST(s_u.t[:, :, tok0:tok0 + T].rearrange("h p t -> p h t"), ub[:, :, 0:T], [ub], [s_u])
            k.ST(s_gate.t[:, :, tok0:tok0 + T].rearrange("h p t -> p h t"), gb[:, :, 0:T], [gb], [s_gate])
            k.ST(s_v.t[tok0:tok0 + T, :].rearrange("(j p) n -> p j n", p=128), vt[:, 0:nt, :], [vt], [s_v])
            k.ST(s_g.t[tok0:tok0 + T, :].rearrange("(j p) n -> p j n", p=128), gt[:, 0:nt, :], [gt], [s_g])
        pipeline(units, p1A, p1B)
        P.end_phase()
        es1.close()
        if STAGE < 2:
            break


        esm = ExitStack()
        Sst = P.sbuf("Sst", [128, 2, 4, 128], F32, esm)
        Sctx = P.sbuf("Sctx", [128, 2, 4, 128], F32, esm)
        Sin = P.sbuf("Sin", [128, 2, 4, 128], F32, esm)
        Sinb = P.sbuf("Sinb", [128, 2, 4, 128], BF16, esm)
        es2 = ExitStack()
        wbd = P.sbuf("wbd", [128, 2, 2, 4, 128], BF16, es2)
        wstage = P.sbuf("wstage", [128, 2, 2, 4, 128], F32, es2)
        zeros_t = P.sbuf("zeros_t", [128, 512], F32, es2)
        k.MEMSET("pool", zeros_t[:], 0.0, [zeros_t])
        k.MEMSET("pool", wstage[:], 0.0, [wstage])
        for gi, wsrc in enumerate((rg_w_a, rg_w_x)):
            for dr in range(2):
                for half in range(2):
                    src = wsrc.ap()[L, dr].rearrange("(ct h) i j -> h i ct j", h=2)[half]
                    k.LD(wstage[half * 64:(half + 1) * 64, gi, dr, :, half * 64:(half + 1) * 64], src, [wstage])
        k.CP("dve", wbd[:], wstage[:], [wstage], [wbd])
        uts = [P.sbuf("ut%d" % i, [128, 4, 512], BF16, es2) for i in range(2)]
        rt = P.sbuf("rt", [128, 4, 512], F32, es2)
        it = P.sbuf("it", [128, 4, 512], F32, es2)
        at = P.sbuf("at", [128, 4, 512], F32, es2)
        a2t = P.sbuf("a2t", [128, 4, 512], F32, es2)
        bxt = P.sbuf("bxt", [128, 4, 512], F32, es2)
        hl = P.sbuf("hl", [128, 4, 512], F32, es2)
        ac = P.sbuf("ac", [128, 4, 512], F32, es2)
        hss = [P.sbuf("hs%d" % i, [128, 4, 512], F32, es2) for i in range(2)]
        car_h = P.sbuf("car_h", [128, 4], F32, es2)
        car_a = P.sbuf("car_a", [128, 4], F32, es2)
        for dr in range(2):
            order = [units[0]] + (units[1:] if dr == 0 else units[1:][::-1])

            def rgA(ui, unit, dr=dr):
                tok0, T, isctx = unit
                k.LD(uts[ui % 2][:, :, 0:T], s_u.t[:, :, tok0:tok0 + T].rearrange("c p t -> p c t"), [uts[ui % 2]], R=[s_u])
                if dr == 1:
                    k.LD(hss[ui % 2][:, :, 0:T], s_hsumf.t[:, :, tok0:tok0 + T].rearrange("c p t -> p c t"), [hss[ui % 2]], R=[s_hsumf])
                yield

            def rgB(ui, unit, dr=dr):
                tok0, T, isctx = unit
                ut, hs = uts[ui % 2], hss[ui % 2]
                fresh = (ui == 0) if CHAIN else (ui <= 1)
                pbs = []
                for c in range(4):
                    for gi in range(2):
                        pb = k.ps()
                        k.MM(pb.t[:, 0:T], wbd[:, gi, dr, c, :], ut[:, c, 0:T], True, True, [wbd, ut], [pb])
                        pbs.append(pb)
                for c in range(4):
                    k.ACT(rt[:, c, 0:T], pbs[2 * c].t[:, 0:T], AF.Sigmoid, [pbs[2 * c], bav], [rt], bias=bav[:, dr, c:c + 1])
                    k.ACT(it[:, c, 0:T], pbs[2 * c + 1].t[:, 0:T], AF.Sigmoid, [pbs[2 * c + 1], bxv], [it], bias=bxv[:, dr, c:c + 1])
                for c in range(4):
                    k.ACT(at[:, c, 0:T], rt[:, c, 0:T], AF.Exp, [rt, c1v], [at], scale=c1v[:, dr, c:c + 1])
                k.ACT(a2t[:, :, 0:T], at[:, :, 0:T], AF.Square, [at], [a2t])
                k.ACT(a2t[:, :, 0:T], a2t[:, :, 0:T], AF.Sqrt, [a2t], [a2t], scale=-1.0, bias=1.0)
                k.TT("pool", it[:, :, 0:T], it[:, :, 0:T], ut[:, :, 0:T], ALU.mult, [it, ut], [it])
                k.TT("dve", bxt[:, :, 0:T], it[:, :, 0:T], a2t[:, :, 0:T], ALU.mult, [it, a2t], [bxt])
                for c in range(4):
                    ih = 0.0 if fresh else car_h[:, c:c + 1]
                    ia = 1.0 if fresh else car_a[:, c:c + 1]
                    if dr == 0:
                        k.SCAN(hl[:, c, 0:T], at[:, c, 0:T], bxt[:, c, 0:T], ih, [at, bxt, car_h], [hl])
                        if not CHAIN:
                            k.SCAN(ac[:, c, 0:T], at[:, c, 0:T], zeros_t[:, 0:T], ia, [at, zeros_t, car_a], [ac])
                        lastcol = slice(T - 1, T)
                    else:
                        k.SCAN(hl[:, c, T - 1::-1] if False else hl[:, c, 0:T][:, ::-1], at[:, c, 0:T][:, ::-1], bxt[:, c, 0:T][:, ::-1], ih,
                               [at, bxt, car_h], [hl])
                        if not CHAIN:
                            k.SCAN(ac[:, c, 0:T][:, ::-1], at[:, c, 0:T][:, ::-1], zeros_t[:, 0:T], ia, [at, zeros_t, car_a], [ac])
                        lastcol = slice(0, 1)
                    if isctx:
                        k.CP("act", hctx[:, dr, c:c + 1], hl[:, c, lastcol], [hl], [hctx])
                    if CHAIN or not isctx:
                        k.CP("act", car_h[:, c:c + 1], hl[:, c, lastcol], [hl], [car_h])
                        if not CHAIN:
                            k.CP("act", car_a[:, c:c + 1], ac[:, c, lastcol], [ac], [car_a])
                if dr == 0:
                    k.ST(s_hsumf.t[:, :, tok0:tok0 + T].rearrange("c p t -> p c t"), hl[:, :, 0:T], [hl], [s_hsumf])
                else:
                    k.TT("dve", hl[:, :, 0:T], hl[:, :, 0:T], hs[:, :, 0:T], ALU.add, [hl, hs], [hl])
                    k.ST(s_hsum.t[:, :, tok0:tok0 + T].rearrange("c p t -> p c t"), hl[:, :, 0:T], [hl], [s_hsum])
                if not CHAIN:
                    k.ST(s_acum.t[dr, :, :, tok0:tok0 + T].rearrange("c p t -> p c t"), ac[:, :, 0:T], [ac], [s_acum])
                yield

            pipeline(order, rgA, rgB)
            if not CHAIN:
                k.CP("pool", hfin[:, dr, :], car_h[:], [car_h], [hfin])
                k.CP("pool", atot[:, dr, :], car_a[:], [car_a], [atot])
        P.end_phase()
        es2.close()
        if STAGE < 3:
            esm.close()
            break

        es3 = ExitStack()
        fTt = P.sbuf("fTt", [128, 4, 512], F32, es3)
        qTt = P.sbuf("qTt", [128, 4, 512], BF16, es3)
        kTt = P.sbuf("kTt", [128, 4, 512], F32, es3)
        Ct = P.sbuf("Ct", [128, 4, 512], F32, es3)
        crel = P.sbuf("crel", [128, 4, 512], F32, es3)
        e1 = P.sbuf("e1", [128, 4, 512], F32, es3)
        ktl = P.sbuf("ktl", [128, 4, 512], BF16, es3)
        khl = P.sbuf("khl", [128, 4, 512], BF16, es3)
        qsg = None if CHAIN else P.sbuf("qsg", [128, 4, 512], BF16, es3)
        vtts = [P.sbuf("vtt%d" % i, [64, 8, HGW], BF16, es3) for i in range(2)]
        qtls = [P.sbuf("qtl%d" % i, [128, 4, 512], BF16, es3) for i in range(2)]
        khTs = [P.sbuf("khT%d" % i, [64, 4, 8, 128], BF16, es3) for i in range(2)]
        scTs = [[P.sbuf("scT%d%d" % (d_, i), [64, 4, 512], BF16, es3) for i in range(2)] for d_ in range(2)]
        BDs = [P.sbuf("BD%d" % i, [128, 4, 2, 8], F32, es3) for i in range(2)]
        for d_ in range(2):
            for i in range(2):
                k.MEMSET("pool", scTs[d_][i][:], 0.0, [scTs[d_][i]])
        ot = P.sbuf("ot", [64, 8, HGW], F32, es3)
        ofts = [P.sbuf("oft%d" % i, [64, 8, HGW], F32, es3) for i in range(2)]
        SpbV = [[P.sbuf("Spb%d_%d" % (i, h), [128, 128], BF16, es3) for h in range(4)] for i in range(2)]
        SstV = [[Buf(P, Sst.t, "SstV%d%d" % (d_, h)) for h in range(4)] for d_ in range(2)]
        carC = P.sbuf("carC", [128, 4], F32, es3)
        cprev = P.sbuf("cprev", [128, 4, 8], F32, es3)
        dif = P.sbuf("dif", [128, 4, 2, 8], F32, es3)

        def hgA(dr, ui, tok0, T, isctx, sset):
            fresh = (ui == 0) if CHAIN else (ui <= 1)
            nch = T // CH
            vtt, qtl, khT, scT, BD = vtts[sset], qtls[sset], khTs[sset], scTs[dr][sset], BDs[sset]
            oft = ofts[sset]
            k.LD(fTt[:, :, 0:T], s_fT.t[dr, :, :, tok0:tok0 + T].rearrange("h p t -> p h t"), [fTt], R=[s_fT])
            k.LD(qTt[:, :, 0:T], s_qT.t[:, :, tok0:tok0 + T].rearrange("h p t -> p h t"), [qTt], R=[s_qT])
            k.LD(vtt[:, 0:nch, :], s_v.t[tok0:tok0 + T, :].rearrange("(c p) n -> p c n", p=CH), [vtt], R=[s_v])
            k.ACT(kTt[:, :, 0:T], fTt[:, :, 0:T], AF.Copy, [fTt], [kTt], scale=-1.0, bias=1.0)
            k.ACT(fTt[:, :, 0:T], fTt[:, :, 0:T], AF.Ln, [fTt], [fTt])
            C4 = Ct[:, :, 0:T].rearrange("p h (n w) -> p h n w", w=CH)
            edge = 0 if dr == 0 else nch - 1
            if fresh:
                k.MEMSET("dve", cprev[:, :, edge:edge + 1], 0.0, [cprev])
            else:
                k.CP("act", cprev[:, :, edge:edge + 1], carC[:].unsqueeze(2), [carC], [cprev])
            for h in range(4):
                init = 0.0 if fresh else carC[:, h:h + 1]
                if dr == 0:
                    k.SCAN(Ct[:, h, 0:T], ones_t[:, 0:T], fTt[:, h, 0:T], init, [ones_t, fTt, carC], [Ct])
                else:
                    k.SCAN(Ct[:, h, 0:T][:, ::-1], ones_t[:, 0:T], fTt[:, h, 0:T][:, ::-1], init, [ones_t, fTt, carC], [Ct])
            if dr == 0:
                k.CP("act", carC[:].unsqueeze(2), Ct[:, :, T - 1:T], [Ct], [carC])
                Aanc = C4[:, :, :, 31]
                Cend = C4[:, :, :, 63]
                if nch > 1:
                    k.CP("act", cprev[:, :, 1:nch], C4[:, :, 0:nch - 1, 63], [Ct], [cprev])
            else:
                k.CP("act", carC[:].unsqueeze(2), Ct[:, :, 0:1], [Ct], [carC])
                Aanc = C4[:, :, :, 32]
                Cend = C4[:, :, :, 0]
                if nch > 1:
                    k.CP("act", cprev[:, :, 0:nch - 1], C4[:, :, 1:nch, 0], [Ct], [cprev])
            k.TT("dve", dif[:, :, 0, 0:nch], Aanc, cprev[:, :, 0:nch], ALU.subtract, [Ct, cprev], [dif])
            k.TT("dve", dif[:, :, 1, 0:nch], Cend, cprev[:, :, 0:nch], ALU.subtract, [Ct, cprev], [dif])
            k.ACT(BD[:, :, :, 0:nch], dif[:, :, :, 0:nch], AF.Exp, [dif], [BD])
            yield
            cr4 = crel[:, :, 0:T].rearrange("p h (n w) -> p h n w", w=CH)
            e14 = e1[:, :, 0:T].rearrange("p h (n w) -> p h n w", w=CH)
            k.TT("dve", cr4, C4, Aanc.unsqueeze(3).to_broadcast([128, 4, nch, CH]), ALU.subtract, [Ct], [crel])
            k.ACT(e1[:, :, 0:T], crel[:, :, 0:T], AF.Exp, [crel], [e1])
            k.ACT(crel[:, :, 0:T], crel[:, :, 0:T], AF.Exp, [crel], [crel], scale=-1.0)
            k.TT("pool", qtl[:, :, 0:T], qTt[:, :, 0:T], e1[:, :, 0:T], ALU.mult, [qTt, e1], [qtl])
            k.TT("dve", ktl[:, :, 0:T], kTt[:, :, 0:T], crel[:, :, 0:T], ALU.mult, [kTt, crel], [ktl])
            yield
            if not isctx and not CHAIN:
                k.ACT(e1[:, :, 0:T], Ct[:, :, 0:T], AF.Exp, [Ct], [e1])
                k.TT("pool", qsg[:, :, 0:T], qTt[:, :, 0:T], e1[:, :, 0:T], ALU.mult, [qTt, e1], [qsg])
                k.ST(s_qseg.t[dr, :, :, tok0:tok0 + T].rearrange("h p t -> p h t"), qsg[:, :, 0:T], [qsg], [s_qseg])
            k.TT("dve", e14, C4, Cend.unsqueeze(3).to_broadcast([128, 4, nch, CH]), ALU.subtract, [Ct], [e1])
            k.ACT(e1[:, :, 0:T], e1[:, :, 0:T], AF.Exp, [e1], [e1], scale=-1.0)
            k.TT("pool", khl[:, :, 0:T], kTt[:, :, 0:T], e1[:, :, 0:T], ALU.mult, [kTt, e1], [khl])
            yield
            for h in range(4):
                pbT = k.ps()
                pT = pbT.t[:, :].bitcast(BF16)
                for n in range(nch):
                    k.TR(pT[0:CH, n * 128:(n + 1) * 128], khl[:, h, n * CH:(n + 1) * CH], ident[:], [khl, ident], [pbT])
                k.CP("act", khT[:, h, 0:nch, :], pT[0:CH, 0:nch * 128].rearrange("p (n k) -> p n k", k=128), [pbT], [khT])
                pbS = k.ps()
                for n in range(nch):
                    k.MM(pbS.t[0:CH, n * CH:(n + 1) * CH], ktl[:, h, n * CH:(n + 1) * CH], qtl[:, h, n * CH:(n + 1) * CH],
                         True, True, [ktl, qtl], [pbS])
                P.op("dve", (lambda sc_, m_, p_: (lambda e: e.copy_predicated(sc_, m_, p_)))(scT[:, h, 0:T], maski[:, dr, 0:T], pbS.t[0:CH, 0:T]),
                     [pbS, maski], [scT])
                if h % 2 == 1:
                    yield
            if dr == 1:
                k.LD(oft[:, 0:nch, :], s_of.t[tok0:tok0 + T, :].rearrange("(c p) n -> p c n", p=CH), [oft], R=[s_of])

        def hgB(dr, ui, tok0, T, isctx, sset):
            nch = T // CH
            vtt, qtl, khT, scT, BD = vtts[sset], qtls[sset], khTs[sset], scTs[dr][sset], BDs[sset]
            oft = ofts[sset]
            if ui == 0:
                for h in range(4):
                    k.MEMSET("dve", Sst[:, dr, h, :], 0.0, [SstV[dr][h]])
            chunks = list(range(nch)) if dr == 0 else list(range(nch))[::-1]
            prev = None

            def evac(po_, n_):
                if dr == 0:
                    k.CP("act", ot[:, n_, :], po_.t[0:CH, 0:512], [po_], [ot])
                else:
                    k.TT("dve", ot[:, n_, :], po_.t[0:CH, 0:512], oft[:, n_, :], ALU.add, [po_, oft], [ot])

            for ci, n in enumerate(chunks):
                sp = SpbV[ci % 2]
                for h in range(4):
                    k.ACT(sp[h][:], Sst[:, dr, h, :], AF.Copy, [SstV[dr][h], BD], [sp[h]], scale=BD[:, h, 0, n:n + 1])
                po = k.psb[6 + ci % 2]
                pks = [k.ps() for _ in range(4)]
                for h in range(4):
                    hs_ = slice(h * 128, (h + 1) * 128)
                    k.MM(po.t[0:CH, hs_], scT[:, h, n * CH:(n + 1) * CH], vtt[:, n, hs_], True, False, [scT, vtt], [po])
                    k.MM(po.t[0:CH, hs_], qtl[:, h, n * CH:(n + 1) * CH], sp[h][:], False, True, [qtl, sp[h]], [po])
                    k.MM(pks[h].t[:, 0:128], khT[:, h, n, :], vtt[:, n, hs_], True, True, [khT, vtt], [pks[h]])
                for h in range(4):
                    k.STT(Sst[:, dr, h, :], Sst[:, dr, h, :], BD[:, h, 1, n:n + 1], pks[h].t[:, 0:128], ALU.mult, ALU.add,
                          [SstV[dr][h], BD, pks[h]], [SstV[dr][h]])
                if prev is not None:
                    evac(*prev)
                prev = (po, n)
                yield
            evac(*prev)
            dst = s_of if dr == 0 else s_osum
            k.ST(dst.t[tok0:tok0 + T, :].rearrange("(c p) n -> p c n", p=CH), ot[:, 0:nch, :], [ot], [dst])
            if isctx and not CHAIN:
                for h in range(4):
                    k.CP("act", Sctx[:, dr, h, :], Sst[:, dr, h, :], [SstV[dr][h]], [Sctx])
                    k.MEMSET("dve", Sst[:, dr, h, :], 0.0, [SstV[dr][h]])

        k.nrot = 6
        seq = []
        for dr in range(2):
            order = [units[0]] + (units[1:] if dr == 0 else units[1:][::-1])
            for ui, (tok0, T, isctx) in enumerate(order):
                seq.append((dr, ui, tok0, T, isctx))
        drain(hgA(*seq[0], 0))
        for i, item in enumerate(seq):
            nxt = hgA(*seq[i + 1], (i + 1) % 2) if i + 1 < len(seq) else None
            if nxt is not None and seq[i + 1][0] != item[0]:
                if not CHAIN:
                    k.ACT(dtot[:, item[0], :], carC[:], AF.Exp, [carC], [dtot])
            interleave(nxt, hgB(*item, i % 2))
        if not CHAIN:
            k.ACT(dtot[:, 1, :], carC[:], AF.Exp, [carC], [dtot])
        P.end_phase()
        k.nrot = 8
        es3.close()
        if STAGE < 4:
            esm.close()
            break

        if not CHAIN:
            esx = ExitStack()
            xs = P.sbuf("xs", [128, XW], F32, esx)
            xg = P.sbuf("xg", [128, 8, XW], F32, esx)
            dm1 = P.sbuf("dm1", [128, 8, 16], F32, esx)
            tS = P.sbuf("tS", [128, 128], F32, esx)
            tH = P.sbuf("tH", [128, 4], F32, esx)
            k.CP("pool", xs[:, 0:1024], Sst[:].rearrange("p a b c -> p (a b c)"), [Sst], [xs])
            k.CP("pool", xs[:, 1024:1032], dtot[:].rearrange("p a b -> p (a b)"), [dtot], [xs])
            k.CP("pool", xs[:, 1032:1040], hfin[:].rearrange("p a b -> p (a b)"), [hfin], [xs])
            k.CP("pool", xs[:, 1040:1048], atot[:].rearrange("p a b -> p (a b)"), [atot], [xs])
            k.ST(s_xsrc.t.ap(), xs[:], [xs], [s_xsrc])
            if USE_CC:
                P.custom("pool", (lambda a_, b_: (lambda e: e.collective_compute("AllGather", ALU.bypass, replica_groups=[list(range(8))],
                                                                               ins=[a_], outs=[b_])))(s_xsrc.t.ap().opt(), s_xdst.t.ap().opt()),
                         reads=[s_xsrc], writes=[s_xdst], inc=1)
            else:
                for r in range(8):
                    k.ST(s_xdst.t.ap()[r * 128:(r + 1) * 128, :], s_xsrc.t.ap(), [s_xsrc], [s_xdst])
            k.LD(xg[:], s_xdst.t.ap().rearrange("(r p) c -> p r c", p=128), [xg], R=[s_xdst])
            k.TS("dve", dm1[:, :, 0:8], xg[:, :, 1024:1032], -1.0, None, ALU.add, None, [xg], [dm1])
            k.TS("dve", dm1[:, :, 8:16], xg[:, :, 1040:1048], -1.0, None, ALU.add, None, [xg], [dm1])
            k.CP("pool", Sin[:], Sctx[:], [Sctx], [Sin])
            k.CP("pool", hin[:], hctx[:], [hctx], [hin])
            for dr in range(2):
                ranks = list(range(0, 7)) if dr == 0 else list(range(7, 0, -1))
                for i in ranks:
                    fl = flags[:, dr * 8 + i:dr * 8 + i + 1]
                    for h in range(4):
                        c0 = (dr * 4 + h) * 128
                        k.STT(tS[:], Sin[:, dr, h, :], dm1[:, i, dr * 4 + h:dr * 4 + h + 1], xg[:, i, c0:c0 + 128], ALU.mult, ALU.add,
                              [Sin, dm1, xg], [tS])
                        k.STT(Sin[:, dr, h, :], tS[:], fl, Sin[:, dr, h, :], ALU.mult, ALU.add, [tS, flags, Sin], [Sin])
                    k.TT("dve", tH[:], hin[:, dr, :], dm1[:, i, 8 + dr * 4:8 + dr * 4 + 4], ALU.mult, [hin, dm1], [tH])
                    k.TT("dve", tH[:], tH[:], xg[:, i, 1032 + dr * 4:1032 + dr * 4 + 4], ALU.add, [tH, xg], [tH])
                    k.STT(hin[:, dr, :], tH[:], fl, hin[:, dr, :], ALU.mult, ALU.add, [tH, flags, hin], [hin])
            k.CP("dve", Sinb[:], Sin[:], [Sin], [Sinb])
            if DEBUG:
                k.ST(dbg["sst"].t.ap(), Sst[:].rearrange("p a b c -> p (a b c)"), [Sst], [dbg["sst"]])
                k.ST(dbg["sctx"].t.ap(), Sctx[:].rearrange("p a b c -> p (a b c)"), [Sctx], [dbg["sctx"]])
                k.ST(dbg["sin"].t.ap(), Sin[:].rearrange("p a b c -> p (a b c)"), [Sin], [dbg["sin"]])
            P.end_phase()
            esx.close()
        if STAGE < 5:
            esm.close()
            break

        es4 = ExitStack()
        wob = P.sbuf("wob", [128, KD, D], BF16, es4)
        wst = P.sbuf("wst", [128, D], F32, es4)
        for kc in range(KD):
            if kc < 4:
                k.LD(wst[:], w_out.ap()[L, kc * 128:(kc + 1) * 128, :], [wst])
                k.TS("dve", wob[:, kc, :], wst[:], gnv[:, 0:1], None, ALU.mult, None, [wst, gnv], [wob])
            else:
                k.LD(wob[:, kc, :], w_out.ap()[L, kc * 128:(kc + 1) * 128, :], [wob], q="pool")
        osm = P.sbuf("osm", [64, 8, HGW], F32, es4)
        gtt = P.sbuf("gtt", [64, 8, HGW], BF16, es4)
        qsf = None if CHAIN else P.sbuf("qsf", [128, 4, 512], BF16, es4)
        qsb = None if CHAIN else P.sbuf("qsb", [128, 4, 512], BF16, es4)
        hst = P.sbuf("hst", [128, 4, 512], F32, es4)
        acf = None if CHAIN else P.sbuf("acf", [128, 4, 512], F32, es4)
        acb = None if CHAIN else P.sbuf("acb", [128, 4, 512], F32, es4)
        gat = P.sbuf("gat", [128, 4, 512], BF16, es4)
        xt3s = [P.sbuf("xt3_%d" % i, [128, 4, D], F32, es4) for i in range(2 if CHAIN else 1)]
        grow0 = [P.sbuf("grow0_%d" % i, [128, D], F32, es4) for i in range(2)]
        for i in range(2):
            k.LD(grow0[i][:], s_grow.t[i], [grow0[i]], R=[s_grow])
        ot3 = P.sbuf("ot3", [64, 8, HGW], F32, es4)
        mixb = P.sbuf("mixb", [64, 8, HGW], BF16, es4)
        mixT = P.sbuf("mixT", [128, KD, 512], BF16, es4)
        tA = P.sbuf("tA", [128, 512], F32, es4)
        tmp3 = P.sbuf("tmp3", [128, 512], F32, es4)
        junk3 = P.sbuf("junk3", [128, 512], BF16, es4)
        ssh = P.sbuf("ssh", [64, 32], F32, es4)
        ss2 = P.sbuf("ss2", [128, 4], F32, es4)
        units3 = [u_ for u_ in units if not (u_[2] and last)]

        def ld1(idx):
            tok0, T, isctx = units3[idx]
            nch = T // CH
            k.LD(osm[:, 0:nch, :], s_osum.t[tok0:tok0 + T, :].rearrange("(c p) n -> p c n", p=CH), [osm], R=[s_osum])
            k.LD(gtt[:, 0:nch, :], s_g.t[tok0:tok0 + T, :].rearrange("(c p) n -> p c n", p=CH), [gtt], R=[s_g])
            if (not isctx) and (not CHAIN):
                k.LD(qsf[:, :, 0:T], s_qseg.t[0, :, :, tok0:tok0 + T].rearrange("h p t -> p h t"), [qsf], R=[s_qseg])
                k.LD(qsb[:, :, 0:T], s_qseg.t[1, :, :, tok0:tok0 + T].rearrange("h p t -> p h t"), [qsb], R=[s_qseg])

        def ld2(idx):
            tok0, T, isctx = units3[idx]
            k.LD(hst[:, :, 0:T], s_hsum.t[:, :, tok0:tok0 + T].rearrange("c p t -> p c t"), [hst], R=[s_hsum])
            k.LD(gat[:, :, 0:T], s_gate.t[:, :, tok0:tok0 + T].rearrange("c p t -> p c t"), [gat], R=[s_gate])
            if (not isctx) and (not CHAIN):
                k.LD(acf[:, :, 0:T], s_acum.t[0, :, :, tok0:tok0 + T].rearrange("c p t -> p c t"), [acf], R=[s_acum])
                k.LD(acb[:, :, 0:T], s_acum.t[1, :, :, tok0:tok0 + T].rearrange("c p t -> p c t"), [acb], R=[s_acum])

        def ld3(idx):
            tok0, T, isctx = units3[idx]
            xb = xt3s[idx % len(xt3s)]
            src, srcbuf = x_src(L, tok0, T)
            k.LD(xb[:, 0:T // 128, :], src.rearrange("(j p) d -> p j d", p=128), [xb], R=[srcbuf])

        ld1(0)
        ld2(0)
        ld3(0)
        for ui, (tok0, T, isctx) in enumerate(units3):
            nt = T // 128
            nch = T // CH
            j = 1 if isctx else 0
            fix = (not isctx) and (not CHAIN)
            xt3 = xt3s[ui % len(xt3s)]
            more = ui + 1 < len(units3)
            if more and len(xt3s) == 2:
                ld3(ui + 1)
            for n in range(nch):
                if fix:
                    pf = k.ps()
                    for h in range(4):
                        hs_ = slice(h * 128, (h + 1) * 128)
                        k.MM(pf.t[0:CH, hs_], qsf[:, h, n * CH:(n + 1) * CH], Sinb[:, 0, h, :], True, False, [qsf, Sinb], [pf])
                        k.MM(pf.t[0:CH, hs_], qsb[:, h, n * CH:(n + 1) * CH], Sinb[:, 1, h, :], False, True, [qsb, Sinb], [pf])
                    k.TT("dve", ot3[:, n, :], pf.t[0:CH, 0:512], osm[:, n, :], ALU.add, [pf, osm], [ot3])
                else:
                    k.CP("act", ot3[:, n, :], osm[:, n, :], [osm], [ot3])
            ov = ot3[:, 0:nch, :].rearrange("p n (h v) -> p (n h) v", v=128)
            sq = osm[:, 0:nch, :].rearrange("p n (h v) -> p (n h) v", v=128)
            k.TT("pool", sq, ov, ov, ALU.mult, [ot3], [osm])
            P.op("dve", (lambda o_, i_: (lambda e: e.tensor_reduce(out=o_, in_=i_, axis=AX.X, op=ALU.add)))(ssh[:, 0:nch * 4], sq), [osm], [ssh])
            k.TS("dve", ssh[:, 0:nch * 4], ssh[:, 0:nch * 4], 1.0 / 128, EPS, ALU.mult, ALU.add, [ssh], [ssh])
            k.ACT(ssh[:, 0:nch * 4], ssh[:, 0:nch * 4], AF.Sqrt, [ssh], [ssh])
            P.op("dve", (lambda o_: (lambda e: e.reciprocal(out=o_, in_=o_)))(ssh[:, 0:nch * 4]), [ssh], [ssh])
            k.TT("dve", ov, ov, ssh[:, 0:nch * 4].unsqueeze(2).to_broadcast([CH, nch * 4, 128]), ALU.mult, [ot3, ssh], [ot3])
            k.TT("pool", mixb[:, 0:nch, :], ot3[:, 0:nch, :], gtt[:, 0:nch, :], ALU.mult, [ot3, gtt], [mixb])
            if more:
                ld1(ui + 1)
            for h in range(4):
                pbT = k.ps()
                pT = pbT.t[:, :].bitcast(BF16)
                for n in range(nch):
                    k.TR(pT[:, n * CH:(n + 1) * CH], mixb[:, n, h * 128:(h + 1) * 128], ident[0:CH, 0:CH], [mixb, ident], [pbT])
                k.CP("act", mixT[:, h, 0:T], pT[:, 0:T], [pbT], [mixT])
            for c in range(4):
                if fix:
                    k.STT(tA[:, 0:T], acf[:, c, 0:T], hin[:, 0, c:c + 1], hst[:, c, 0:T], ALU.mult, ALU.add, [acf, hin, hst], [tA])
                    k.STT(tA[:, 0:T], acb[:, c, 0:T], hin[:, 1, c:c + 1], tA[:, 0:T], ALU.mult, ALU.add, [acb, hin, tA], [tA])
                    k.TT("pool", mixT[:, 4 + c, 0:T], tA[:, 0:T], gat[:, c, 0:T], ALU.mult, [tA, gat], [mixT])
                else:
                    k.TT("pool", mixT[:, 4 + c, 0:T], hst[:, c, 0:T], gat[:, c, 0:T], ALU.mult, [hst, gat], [mixT])
            if more:
                ld2(ui + 1)
            for jj in range(nt):
                pps = [k.ps(), k.ps()]
                for half in range(2):
                    for kc in range(KD):
                        k.MM(pps[half].t[:, 0:512], mixT[:, kc, jj * 128:(jj + 1) * 128], wob[:, kc, half * 512:(half + 1) * 512],
                             kc == 0, kc == KD - 1, [mixT, wob], [pps[half]])
                    k.ACT(junk3[:], pps[half].t[:, 0:512], AF.Square, [pps[half]], [junk3, ss2], accum=ss2[:, half:half + 1])
                k.TT("dve", ss2[:, 2:3], ss2[:, 0:1], ss2[:, 1:2], ALU.add, [ss2], [ss2])
                k.TS("dve", ss2[:, 2:3], ss2[:, 2:3], 1.0 / D, EPS, ALU.mult, ALU.add, [ss2], [ss2])
                k.ACT(ss2[:, 2:3], ss2[:, 2:3], AF.Sqrt, [ss2], [ss2])
                P.op("dve", (lambda o_: (lambda e: e.reciprocal(out=o_, in_=o_)))(ss2[:, 2:3]), [ss2], [ss2])
                for half in range(2):
                    hsl = slice(half * 512, (half + 1) * 512)
                    k.STT(tmp3[:], pps[half].t[:, 0:512], ss2[:, 2:3], grow0[j][:, hsl], ALU.mult, ALU.mult,
                          [pps[half], ss2, grow0[j]], [tmp3])
                    k.TT("dve", xt3[:, jj, hsl], xt3[:, jj, hsl], tmp3[:], ALU.add, [xt3, tmp3], [xt3])
            k.ST(s_xmid.t[tok0:tok0 + T, :].rearrange("(j p) d -> p j d", p=128), xt3[:, 0:nt, :], [xt3], [s_xmid])
            if more and len(xt3s) == 1:
                ld3(ui + 1)
        P.end_phase()
        es4.close()
        esm.close()
        if STAGE < 6:
            break

        es5 = ExitStack()
        wgb = P.sbuf("wgb", [128, KD, DFF], BF16, es5)
        wub = P.sbuf("wub", [128, KD, DFF], BF16, es5)
        wdb = P.sbuf("wdb", [128, NFF, D], BF16, es5)
        for kc in range(KD):
            k.LD(wgb[:, kc, :], w_gate.ap()[L, kc * 128:(kc + 1) * 128, :], [wgb], q="pool")
            k.LD(wub[:, kc, :], w_up.ap()[L, kc * 128:(kc + 1) * 128, :], [wub], q="pool")
        for jf in range(NFF):
            k.LD(wdb[:, jf, :], w_down.ap()[L, jf * 128:(jf + 1) * 128, :], [wdb], q="pool")
        xt5s = [P.sbuf("xt5_%d" % i, [128, 2, D], F32, es5) for i in range(2)]
        fT5s = [P.sbuf("fT5_%d" % i, [128, KD, 256], BF16, es5) for i in range(2)]
        grow1 = [P.sbuf("grow1_%d" % i, [128, D], F32, es5) for i in range(2)]
        for i in range(2):
            k.LD(grow1[i][:], s_grow.t[2 + i], [grow1[i]], R=[s_grow])
        hid = P.sbuf("hid", [128, NFF, 256], BF16, es5)
        sl5 = P.sbuf("sl5", [128, 256], F32, es5)
        tmp5 = P.sbuf("tmp5", [128, 512], F32, es5)
        junk5 = P.sbuf("junk5", [128, 512], BF16, es5)
        ss5 = P.sbuf("ss5", [128, 4], F32, es5)
        nb5 = norm_bufs(es5, "p5", 2)
        toks5 = [t_ for t_ in range(0, NT, 256) if not (t_ < NCTX and last)]

        def p5A(ui, tok0):
            j = 1 if tok0 < NCTX else 0
            xt5 = xt5s[ui % 2]
            k.LD(xt5[:], s_xmid.t[tok0:tok0 + 256, :].rearrange("(j p) d -> p j d", p=128), [xt5], R=[s_xmid])
            yield from norm_mod_g(nb5, xt5, 2, 1, j, fT5s[ui % 2])

        def p5B(ui, tok0):
            isctx = tok0 < NCTX
            j = 1 if isctx else 0
            T = 256
            xt5 = xt5s[ui % 2]
            fT5 = fT5s[ui % 2]
            for jf in range(NFF):
                pg = k.ps()
                pu = k.ps()
                for kc in range(KD):
                    k.MM(pg.t[:, 0:T], wgb[:, kc, jf * 128:(jf + 1) * 128], fT5[:, kc, 0:T], kc == 0, kc == KD - 1, [wgb, fT5], [pg])
                for kc in range(KD):
                    k.MM(pu.t[:, 0:T], wub[:, kc, jf * 128:(jf + 1) * 128], fT5[:, kc, 0:T], kc == 0, kc == KD - 1, [wub, fT5], [pu])
                k.ACT(sl5[:], pg.t[:, 0:T], AF.Silu, [pg], [sl5])
                k.TT("dve", hid[:, jf, :], sl5[:], pu.t[:, 0:T], ALU.mult, [sl5, pu], [hid])
                yield
            for jj in range(2):
                pps = [k.ps(), k.ps()]
                for half in range(2):
                    for jf in range(NFF):
                        k.MM(pps[half].t[:, 0:512], hid[:, jf, jj * 128:(jj + 1) * 128], wdb[:, jf, half * 512:(half + 1) * 512],
                             jf == 0, jf == NFF - 1, [hid, wdb], [pps[half]])
                    k.ACT(junk5[:], pps[half].t[:, 0:512], AF.Square, [pps[half]], [junk5, ss5], accum=ss5[:, half:half + 1])
                k.TT("dve", ss5[:, 2:3], ss5[:, 0:1], ss5[:, 1:2], ALU.add, [ss5], [ss5])
                k.TS("dve", ss5[:, 2:3], ss5[:, 2:3], 1.0 / D, EPS, ALU.mult, ALU.add, [ss5], [ss5])
                k.ACT(ss5[:, 2:3], ss5[:, 2:3], AF.Sqrt, [ss5], [ss5])
                P.op("dve", (lambda o_: (lambda e: e.reciprocal(out=o_, in_=o_)))(ss5[:, 2:3]), [ss5], [ss5])
                for half in range(2):
                    hsl = slice(half * 512, (half + 1) * 512)
                    k.STT(tmp5[:], pps[half].t[:, 0:512], ss5[:, 2:3], grow1[j][:, hsl], ALU.mult, ALU.mult,
                          [pps[half], ss5, grow1[j]], [tmp5])
                    k.TT("dve", xt5[:, jj, hsl], xt5[:, jj, hsl], tmp5[:], ALU.add, [xt5, tmp5], [xt5])
            if last:
                k.ST(out_t.t.ap()[tok0 - NCTX:tok0 - NCTX + T, :].rearrange("(j p) d -> p j d", p=128), xt5[:], [xt5], [out_t])
            else:
                k.ST(s_xres.t[tok0:tok0 + T, :].rearrange("(j p) d -> p j d", p=128), xt5[:], [xt5], [s_xres])
        pipeline(toks5, p5A, p5B)
        P.end_phase()
        es5.close()

    fin = []
    if DEBUG:
        pairs = [("qT", s_qT), ("fT", s_fT), ("v", s_v), ("g", s_g), ("u", s_u), ("gate", s_gate)]
        if STAGE >= 2:
            pairs += [("hsum", s_hsum), ("acum", s_acum)]
        if STAGE >= 3:
            pairs += [("osum", s_osum), ("qseg", s_qseg)]
        if STAGE >= 5:
            pairs += [("xdst", s_xdst)]
        if STAGE >= 6:
            pairs += [("xmid", s_xmid)]
        if STAGE >= 7:
            pairs += [("xres", s_xres)]
        P.barrier()
        for nm, sb in pairs:
            fin.append(k.ST(dbg[nm].t.ap(), sb.t.ap(), [sb], [dbg[nm]]))
        k.ST(dbg["misc"].t.ap()[:, 0:96], modfm[:].rearrange("p a b -> p (a b)"), [modfm], [dbg["misc"]])
        k.ST(dbg["misc"].t.ap()[:, 96:160], scsh[:].rearrange("p a b c -> p (a b c)"), [scsh], [dbg["misc"]])
        fin.append(k.ST(dbg["misc"].t.ap()[:, 160:168], c1v[:].rearrange("p a b -> p (a b)"), [c1v], [dbg["misc"]]))
        k.ST(dbg["misc"].t.ap()[:, 168:176], hctx[:].rearrange("p a b -> p (a b)"), [hctx], [dbg["misc"]])
        k.ST(dbg["misc"].t.ap()[:, 176:184], hfin[:].rearrange("p a b -> p (a b)"), [hfin], [dbg["misc"]])
        k.ST(dbg["misc"].t.ap()[:, 184:192], atot[:].rearrange("p a b -> p (a b)"), [atot], [dbg["misc"]])
        k.ST(dbg["misc"].t.ap()[:, 192:200], dtot[:].rearrange("p a b -> p (a b)"), [dtot], [dbg["misc"]])
        k.ST(dbg["misc"].t.ap()[:, 200:208], hin[:].rearrange("p a b -> p (a b)"), [hin], [dbg["misc"]])
    P.barrier()
    P.emit()
    return nc


def make_in_maps(inp):
    f = lambda a: np.ascontiguousarray(np.asarray(a, dtype=np.float32))
    x, c, ctx, c_ctx = f(inp["x"]), f(inp["c"]), f(inp["ctx"]), f(inp["c_ctx"])
    b_mod, norm_g = f(inp["b_mod"]), f(inp["norm_g"])
    common = {
        "w_mod": f(inp["w_mod"]),
        "bmod_fm": f(b_mod.reshape(DEPTH, 6, 8, 128).transpose(0, 3, 1, 2).reshape(DEPTH, 128, 48)),
        "bmod_row": f(b_mod.reshape(DEPTH, 1, 6 * D)),
        "normg_fm": f(norm_g.reshape(DEPTH, 4, 8, 128).transpose(0, 3, 1, 2).reshape(DEPTH, 128, 32)),
        "normg_row": norm_g,
        "w_in": f(inp["w_in"]),
        "lb_fm": f(f(inp["hg_lb_logits"]).reshape(DEPTH, 2, 4, 128).transpose(3, 0, 1, 2)),
        "gn_fm": f(f(inp["hg_gnorm"]).reshape(DEPTH, 128, 1)),
        "convw_fm": f(f(inp["rg_conv_w"]).reshape(DEPTH, 4, 4, 128).transpose(0, 3, 2, 1)),
        "convb_fm": f(f(inp["rg_conv_b"]).reshape(DEPTH, 4, 128).transpose(0, 2, 1)),
        "ba_fm": f(f(inp["rg_b_a"]).reshape(DEPTH, 2, 4, 128).transpose(0, 3, 1, 2)),
        "bx_fm": f(f(inp["rg_b_x"]).reshape(DEPTH, 2, 4, 128).transpose(0, 3, 1, 2)),
        "lam_fm": f(f(inp["rg_lambda"]).reshape(DEPTH, 2, 4, 128).transpose(0, 3, 1, 2)),
        "rg_w_a": f(inp["rg_w_a"]),
        "rg_w_x": f(inp["rg_w_x"]),
        "w_out": f(inp["w_out"]),
        "w_ffn_gate": f(inp["w_ffn_gate"]),
        "w_ffn_up": f(inp["w_ffn_up"]),
        "w_ffn_down": f(inp["w_ffn_down"]),
        "ident": np.eye(128, dtype=np.float32),
    }
    tri = np.triu(np.ones((64, 64), np.float32))
    common["masks"] = f(np.stack([np.tile(tri, (1, 8)), np.tile(tri.T, (1, 8))]))
    maps = []
    for core in range(8):
        if CHAIN:
            b, seg = core % 2, 0
        else:
            b, seg = core // 4, core % 4
        m = dict(common)
        m["x"] = f(x[b, seg * NLAT:(seg + 1) * NLAT])
        m["ctx"] = f(ctx[b])
        cv = np.stack([c[b].reshape(8, 128).T, c_ctx.reshape(8, 128).T], axis=-1)
        m["cvec"] = f(cv)
        fl = np.zeros((128, 16), np.float32)
        for r in range(8):
            same = (r // 4 == b) and not CHAIN
            fl[:, r] = 1.0 if (same and r % 4 < seg) else 0.0
            fl[:, 8 + r] = 1.0 if (same and r % 4 > seg) else 0.0
        m["flags"] = fl
        maps.append(m)
    return maps


_NC_CACHE = {}


def kernel(**inputs):
    if "nc" not in _NC_CACHE:
        _NC_CACHE["nc"] = build()
    nc = _NC_CACHE["nc"]
    maps = make_in_maps(inputs)
    res = run_bass_kernel_spmd(nc, maps, core_ids=list(range(8)))
    out = np.empty((2, 16384, D), np.float32)
    for core in range(8):
        if CHAIN:
            if core >= 2:
                continue
            b, seg = core, 0
        else:
            b, seg = core // 4, core % 4
        out[b, seg * NLAT:(seg + 1) * NLAT] = np.asarray(res.results[core]["out"], dtype=np.float32)
    return out
```

```python
import numpy as np
from contextlib import ExitStack
import concourse.bass as bass
import concourse.mybir as mybir
from concourse.bass_utils import run_bass_kernel_spmd

F32 = mybir.dt.float32
BF16 = mybir.dt.bfloat16
AF = mybir.ActivationFunctionType
ALU = mybir.AluOpType
AX = mybir.AxisListType

MODE = "cc8"
CHAIN = (MODE == "whole2")
D = 1024
KD = 8
NLAT = 16384 if CHAIN else 4096
NCTX = 256
NT = NLAT + NCTX
HGW = 512
RGW = 512
INC = 3584
DFF = 2816
NFF = 22
DEPTH = 2
EPS = 1e-6
CH = 64

DEBUG = False
STAGE = 99
USE_CC = True


class Buf:
    def __init__(self, prog, t, name):
        self.prog = prog
        self.t = t
        self.name = name
        self.last_w = None
        self.readers = []
        self.dma_sem = None
        self.dma_cnt = 0
        self.rd_sem = None
        self.rd_cnt = 0

    def __getitem__(self, idx):
        return self.t[idx]


class Prog:
    ENGS = ("pe", "act", "dve", "pool", "sp")

    def __init__(self, nc):
        self.nc = nc
        self.es = ExitStack()
        self.ops = {e: [] for e in self.ENGS}
        self.cnt = {e: 0 for e in self.ENGS}
        self.sems = {}
        for e in self.ENGS:
            self.sems[e] = self.es.enter_context(nc.semaphore("prog_" + e))
        self.known = {e: {} for e in self.ENGS}
        self.final_waits = []
        self._dma_sem_vals = {}
        self.sem_pool = []
        self.cur_bufs = []
        self.nsem = 0

    def sbuf(self, name, shape, dtype, es=None):
        self.nuniq = getattr(self, "nuniq", 0) + 1
        t = (es or self.es).enter_context(self.nc.sbuf_tensor("sb%d_%s" % (self.nuniq, name), list(shape), dtype))
        b = Buf(self, t, name)
        if es is not None:
            self.cur_bufs.append(b)
        return b

    def end_phase(self):
        self.barrier()
        for b in self.cur_bufs:
            if b.dma_sem is not None:
                self.sem_pool.append((b.dma_sem, b.dma_cnt))
                b.dma_sem = None
            if b.rd_sem is not None:
                self.sem_pool.append((b.rd_sem, b.rd_cnt))
                b.rd_sem = None
        self.cur_bufs = []

    def psum(self, name, shape, dtype):
        t = self.es.enter_context(self.nc.psum_tensor(name, list(shape), dtype))
        return Buf(self, t, name)

    def dram(self, name, shape, dtype):
        t = self.nc.dram_tensor(name, list(shape), dtype)
        return Buf(self, t, name)

    def _new_sem(self, name):
        if self.sem_pool:
            return self.sem_pool.pop()
        self.nsem += 1
        return (self.es.enter_context(self.nc.semaphore("s%d" % self.nsem)), 0)

    def _deps(self, eng, reads, writes):
        need = {}

        def add(ev):
            if ev is None:
                return
            key, val, src = ev
            if src == eng and eng == "pe":
                return
            if key not in need or need[key][0] < val:
                need[key] = (val, src)

        for b in reads:
            add(b.last_w)
        for b in writes:
            add(b.last_w)
            for r in b.readers:
                if r[2] == eng:
                    continue
                add(r)
        waits = []
        kn = self.known[eng]
        for key, (val, src) in need.items():
            if kn.get(key, 0) >= val:
                continue
            kn[key] = val
            waits.append((key, val))
        return waits

    def _semobj(self, key):
        if isinstance(key, str):
            return self.sems[key]
        return key

    def _mark(self, ev, reads, writes):
        for b in writes:
            b.last_w = ev
            b.readers = []
        for b in reads:
            if b in writes:
                continue
            b.readers.append(ev)
            if len(b.readers) > 48:
                latest = {}
                for r in b.readers:
                    k = r[0] if isinstance(r[0], str) else id(r[0])
                    if k not in latest or latest[k][1] < r[1]:
                        latest[k] = r
                b.readers = list(latest.values())

    def op(self, eng, fn, reads=(), writes=()):
        reads = [b for b in reads if b is not None]
        writes = [b for b in writes if b is not None]
        waits = self._deps(eng, reads, writes)
        self.cnt[eng] += 1
        ev = (eng, self.cnt[eng], eng)
        self.ops[eng].append(("op", waits, fn))
        self._mark(ev, reads, writes)
        return ev

    def dma(self, q, out_ap, in_ap, reads=(), writes=(), **kw):
        reads = [b for b in reads if b is not None]
        writes = [b for b in writes if b is not None]
        waits = self._deps(q, reads, writes)
        owner = writes[0] if writes else reads[0]
        if writes:
            if owner.dma_sem is None:
                owner.dma_sem, owner.dma_cnt = self._new_sem("dw_" + owner.name)
            owner.dma_cnt += 16
            sem, val = owner.dma_sem, owner.dma_cnt
        else:
            if owner.rd_sem is None:
                owner.rd_sem, owner.rd_cnt = self._new_sem("dr_" + owner.name)
            owner.rd_cnt += 16
            sem, val = owner.rd_sem, owner.rd_cnt
        ev = (sem, val, "dma")
        self._dma_sem_vals[sem] = val
        self.ops[q].append(("dma", waits, (out_ap, in_ap, sem, kw)))
        self._mark(ev, reads, writes)
        return ev

    def custom(self, q, fn, reads=(), writes=(), inc=16):
        reads = [b for b in reads if b is not None]
        writes = [b for b in writes if b is not None]
        waits = self._deps(q, reads, writes)
        owner = writes[0]
        if owner.dma_sem is None:
            owner.dma_sem, owner.dma_cnt = self._new_sem("dw_" + owner.name)
        owner.dma_cnt += inc
        sem, val = owner.dma_sem, owner.dma_cnt
        ev = (sem, val, "dma")
        self._dma_sem_vals[sem] = val
        self.ops[q].append(("custom", waits, (fn, sem, inc)))
        self._mark(ev, reads, writes)
        return ev

    def barrier(self):
        evs = [(e, self.cnt[e]) for e in self.ENGS if self.cnt[e] > 0]
        evs += list(self._dma_sem_vals.items())
        for e in self.ENGS:
            waits = []
            kn = self.known[e]
            for key, val in evs:
                if kn.get(key, 0) >= val:
                    continue
                kn[key] = val
                waits.append((key, val))
            if waits:
                self.ops[e].append(("wait", waits, None))

    def emit(self):
        nc = self.nc
        engmap = {"pe": "tensor", "act": "scalar", "dve": "vector", "pool": "gpsimd", "sp": "sync"}
        with nc.Block() as block:
            for e in self.ENGS:
                ops = self.ops[e]
                if not ops:
                    continue
                semself = self.sems[e]

                def body(engine, ops=ops, semself=semself):
                    for kind, waits, payload in ops:
                        for key, val in waits:
                            engine.wait_ge(self._semobj(key), val)
                        if kind == "op":
                            payload(engine).then_inc(semself, 1)
                        elif kind == "dma":
                            out_ap, in_ap, sem, kw = payload
                            engine.dma_start(out=out_ap, in_=in_ap, **kw).then_inc(sem, 16)
                        elif kind == "custom":
                            fn, sem, inc = payload
                            fn(engine).then_inc(sem, inc)

                getattr(block, engmap[e])(body)
        self.es.close()


class K:
    def __init__(self):
        nc = bass.Bass("TRN2", target_bir_lowering=False)
        self.nc = nc
        self.P = Prog(nc)
        self.ins = {}
        self.outs = {}
        self.psn = 0

    def inp(self, name, shape, dtype=F32):
        t = self.nc.dram_tensor(name, list(shape), dtype, kind="ExternalInput")
        self.ins[name] = t
        return t

    def outp(self, name, shape, dtype=F32):
        t = self.nc.dram_tensor(name, list(shape), dtype, kind="ExternalOutput")
        b = Buf(self.P, t, name)
        self.outs[name] = b
        return b

    def ACT(self, out, in_, func, R, W, bias=None, scale=None, accum=None):
        kw = {}
        if bias is not None:
            kw["bias"] = bias
        if scale is not None:
            kw["scale"] = scale
        if accum is not None:
            kw["accum_out"] = accum
        return self.P.op("act", lambda e: e.activation(out=out, in_=in_, func=func, **kw), R, W)

    def TS(self, eng, out, in0, s1, s2, op0, op1, R, W):
        if op1 is None:
            return self.P.op(eng, lambda e: e.tensor_scalar(out=out, in0=in0, scalar1=s1, scalar2=None, op0=op0), R, W)
        return self.P.op(eng, lambda e: e.tensor_scalar(out=out, in0=in0, scalar1=s1, scalar2=s2, op0=op0, op1=op1), R, W)

    def TT(self, eng, out, in0, in1, op, R, W):
        return self.P.op(eng, lambda e: e.tensor_tensor(out=out, in0=in0, in1=in1, op=op), R, W)

    def STT(self, out, in0, scalar, in1, op0, op1, R, W):
        return self.P.op("dve", lambda e: e.scalar_tensor_tensor(out=out, in0=in0, scalar=scalar, in1=in1, op0=op0, op1=op1), R, W)

    def MM(self, out, lhsT, rhs, start, stop, R, W):
        return self.P.op("pe", lambda e: e.matmul(out, lhsT=lhsT, rhs=rhs, start=start, stop=stop), R, W)

    def TR(self, out, in_, ident, R, W):
        return self.P.op("pe", lambda e: e.transpose(out, in_, ident), R, W)

    def CP(self, eng, out, in_, R, W):
        if eng == "act":
            return self.P.op("act", lambda e: e.copy(out=out, in_=in_), R, W)
        return self.P.op(eng, lambda e: e.tensor_copy(out=out, in_=in_), R, W)

    def SCAN(self, out, d0, d1, init, R, W):
        return self.P.op("dve", lambda e: e.tensor_tensor_scan(out=out, data0=d0, data1=d1, initial=init, op0=ALU.mult, op1=ALU.add), R, W)

    def MEMSET(self, eng, ap, val, W):
        return self.P.op(eng, lambda e: e.memset(ap, val), [], W)

    def LD(self, out, in_, W, R=(), q="sp"):
        return self.P.dma(q, out, in_, reads=list(R), writes=list(W))

    def ST(self, out, in_, R, W=(), q="sp"):
        return self.P.dma(q, out, in_, reads=list(R), writes=list(W))

    def ps(self):
        b = self.psb[self.psn % getattr(self, "nrot", 8)]
        self.psn += 1
        return b


def build(nlayers=DEPTH):
    k = K()
    nc, P = k.nc, k.P
    x_in = k.inp("x", [NLAT, D])
    ctx_in = k.inp("ctx", [NCTX, D])
    cvec = k.inp("cvec", [128, KD, 2])
    w_mod = k.inp("w_mod", [DEPTH, D, 6 * D])
    bmod_fm = k.inp("bmod_fm", [DEPTH, 128, 48])
    bmod_row = k.inp("bmod_row", [DEPTH, 1, 6 * D])
    normg_fm = k.inp("normg_fm", [DEPTH, 128, 32])
    normg_row = k.inp("normg_row", [DEPTH, 4, D])
    w_in = k.inp("w_in", [DEPTH, D, INC])
    lb_fm = k.inp("lb_fm", [128, DEPTH, 2, 4])
    gn_fm = k.inp("gn_fm", [DEPTH, 128, 1])
    convw_fm = k.inp("convw_fm", [DEPTH, 128, 4, 4])
    convb_fm = k.inp("convb_fm", [DEPTH, 128, 4])
    ba_fm = k.inp("ba_fm", [DEPTH, 128, 2, 4])
    bx_fm = k.inp("bx_fm", [DEPTH, 128, 2, 4])
    lam_fm = k.inp("lam_fm", [DEPTH, 128, 2, 4])
    rg_w_a = k.inp("rg_w_a", [DEPTH, 2, 8, 64, 64])
    rg_w_x = k.inp("rg_w_x", [DEPTH, 2, 8, 64, 64])
    w_out = k.inp("w_out", [DEPTH, D, D])
    w_gate = k.inp("w_ffn_gate", [DEPTH, D, DFF])
    w_up = k.inp("w_ffn_up", [DEPTH, D, DFF])
    w_down = k.inp("w_ffn_down", [DEPTH, DFF, D])
    flags_in = k.inp("flags", [128, 16])
    ident_in = k.inp("ident", [128, 128])
    masks_in = k.inp("masks", [2, 64, 512])
    out_t = k.outp("out", [NLAT, D])

    s_qT = P.dram("s_qT", [4, 128, NT], BF16)
    s_fT = P.dram("s_fT", [2, 4, 128, NT], F32)
    s_v = P.dram("s_v", [NT, HGW], BF16)
    s_g = P.dram("s_g", [NT, HGW], BF16)
    s_u = P.dram("s_u", [4, 128, NT], BF16)
    s_gate = P.dram("s_gate", [4, 128, NT], BF16)
    s_hsum = P.dram("s_hsum", [4, 128, NT], F32)
    s_hsumf = P.dram("s_hsumf", [4, 128, NT], F32)
    s_acum = P.dram("s_acum", [2, 4, 128, NT], F32)
    s_osum = P.dram("s_osum", [NT, HGW], F32)
    s_of = P.dram("s_of", [NT, HGW], F32)
    s_qseg = P.dram("s_qseg", [2, 4, 128, NT], BF16)
    s_xres = P.dram("s_xres", [NT, D], F32)
    s_xmid = P.dram("s_xmid", [NT, D], F32)
    s_grow = P.dram("s_grow", [4, 128, D], F32)
    XW = 1024 + 8 + 8 + 8
    s_xsrc = P.dram("s_xsrc", [128, XW], F32)
    s_xdst = P.dram("s_xdst", [8 * 128, XW], F32)

    dbg = {}
    if DEBUG:
        dbg["qT"] = k.outp("d_qT", [4, 128, NT], BF16)
        dbg["fT"] = k.outp("d_fT", [2, 4, 128, NT], F32)
        dbg["v"] = k.outp("d_v", [NT, HGW], BF16)
        dbg["g"] = k.outp("d_g", [NT, HGW], BF16)
        dbg["u"] = k.outp("d_u", [4, 128, NT], BF16)
        dbg["gate"] = k.outp("d_gate", [4, 128, NT], BF16)
        dbg["hsum"] = k.outp("d_hsum", [4, 128, NT], F32)
        dbg["acum"] = k.outp("d_acum", [2, 4, 128, NT], F32)
        dbg["osum"] = k.outp("d_osum", [NT, HGW], F32)
        dbg["qseg"] = k.outp("d_qseg", [2, 4, 128, NT], BF16)
        dbg["xmid"] = k.outp("d_xmid", [NT, D], F32)
        dbg["xres"] = k.outp("d_xres", [NT, D], F32)
        dbg["xdst"] = k.outp("d_xdst", [8 * 128, XW], F32)
        dbg["misc"] = k.outp("d_misc", [128, 256], F32)
        dbg["sst"] = k.outp("d_sst", [128, 1024], F32)
        dbg["sctx"] = k.outp("d_sctx", [128, 1024], F32)
        dbg["sin"] = k.outp("d_sin", [128, 1024], F32)

    k.psb = [P.psum("psb%d" % i, [128, 512], F32) for i in range(8)]

    ident = P.sbuf("ident", [128, 128], BF16)
    maski = P.sbuf("maski", [64, 2, 512], mybir.dt.int32)
    ones_row = P.sbuf("ones_row", [1, 128], F32)
    ones_t = P.sbuf("ones_t", [128, 512], F32)
    flags = P.sbuf("flags", [128, 16], F32)
    cv_f = P.sbuf("cv_f", [128, KD, 2], F32)
    scbf = P.sbuf("scbf", [128, KD, 2], BF16)
    modfm = P.sbuf("modfm", [128, 48, 2], F32)
    bmodfm = P.sbuf("bmodfm", [128, 48], F32)
    gfm = P.sbuf("gfm", [128, 32], F32)
    scsh = P.sbuf("scsh", [128, 4, KD, 2], F32)
    lbt = P.sbuf("lbt", [128, DEPTH, 2, 4], F32)
    lbv = P.sbuf("lbv", [128, 2, 4], F32)
    omlv = P.sbuf("omlv", [128, 2, 4], F32)
    gnv = P.sbuf("gnv", [128, 1], F32)
    cwv = P.sbuf("cwv", [128, 4, 4], F32)
    cbv = P.sbuf("cbv", [128, 4], F32)
    bav = P.sbuf("bav", [128, 2, 4], F32)
    bxv = P.sbuf("bxv", [128, 2, 4], F32)
    c1v = P.sbuf("c1v", [128, 2, 4], F32)
    Sst = P.sbuf("Sst", [128, 2, 4, 128], F32)
    dtot = P.sbuf("dtot", [128, 2, 4], F32)
    hctx = P.sbuf("hctx", [128, 2, 4], F32)
    hfin = P.sbuf("hfin", [128, 2, 4], F32)
    atot = P.sbuf("atot", [128, 2, 4], F32)
    hin = P.sbuf("hin", [128, 2, 4], F32)

    ess = ExitStack()
    ident_f = P.sbuf("ident_f", [128, 128], F32, ess)
    masks_f = P.sbuf("masks_f", [64, 2, 512], F32, ess)
    k.LD(ident_f[:], ident_in.ap(), [ident_f])
    k.CP("dve", ident[:], ident_f[:], [ident_f], [ident])
    k.LD(masks_f[:], masks_in.ap().rearrange("a s t -> s a t"), [masks_f])
    k.CP("dve", maski[:], masks_f[:], [masks_f], [maski])
    k.MEMSET("pool", ones_row[:], 1.0, [ones_row])
    k.MEMSET("pool", ones_t[:], 1.0, [ones_t])
    k.LD(flags[:], flags_in.ap(), [flags])
    k.LD(cv_f[:], cvec.ap(), [cv_f])
    k.ACT(scbf[:], cv_f[:], AF.Silu, [cv_f], [scbf])
    k.LD(lbt[:], lb_fm.ap(), [lbt])

    P.end_phase()
    ess.close()

    units = [(0, NCTX, True)] + [(NCTX + 512 * i, 512, False) for i in range(NLAT // 512)]

    def x_src(L, tok0, T):
        if L == 0:
            if tok0 < NCTX:
                return ctx_in.ap()[tok0:tok0 + T, :], None
            return x_in.ap()[tok0 - NCTX:tok0 - NCTX + T, :], None
        return s_xres.t[tok0:tok0 + T, :], s_xres

    def norm_bufs(es, tag, ntmax):
        return (P.sbuf("ssq_" + tag, [128, 4], F32, es), P.sbuf("rstd_" + tag, [128, 4], F32, es),
                P.sbuf("junk_" + tag, [128, D], BF16, es), P.sbuf("xn_" + tag, [128, ntmax, D], BF16, es))

    def norm_mod(nb, xt, nt, a, j, hT):
        drain(norm_mod_g(nb, xt, nt, a, j, hT))

    def norm_mod_g(nb, xt, nt, a, j, hT):
        T = nt * 128
        ssq, rstd, junk, xn = nb
        for jj in range(nt):
            k.ACT(junk[:], xt[:, jj, :], AF.Square, [xt], [junk, ssq], accum=ssq[:, jj:jj + 1])
        k.TS("dve", rstd[:, 0:nt], ssq[:, 0:nt], 1.0 / D, EPS, ALU.mult, ALU.add, [ssq], [rstd])
        k.ACT(rstd[:, 0:nt], rstd[:, 0:nt], AF.Sqrt, [rstd], [rstd])
        P.op("dve", lambda e: e.reciprocal(out=rstd[:, 0:nt], in_=rstd[:, 0:nt]), [rstd], [rstd])
        yield
        for jj in range(nt):
            k.ACT(xn[:, jj, :], xt[:, jj, :], AF.Copy, [xt, rstd], [xn], scale=rstd[:, jj:jj + 1])
        yield
        yield
        for kc in range(KD):
            if kc == 4:
                yield
            pb = k.ps()
            pst = pb.t[:, :].bitcast(BF16)
            for jj in range(nt):
                k.TR(pst[:, jj * 128:(jj + 1) * 128], xn[:, jj, kc * 128:(kc + 1) * 128], ident[:], [xn, ident], [pb])
            k.TS("dve", hT[:, kc, 0:T], pst[:, 0:T], scsh[:, 2 * a, kc, j:j + 1], scsh[:, 2 * a + 1, kc, j:j + 1],
                 ALU.mult, ALU.add, [pb, scsh], [hT])

    def drain(g):
        for _ in g:
            pass

    def interleave(g1, g2):
        a1, a2 = g1 is not None, g2 is not None
        while a1 or a2:
            if a1:
                try:
                    next(g1)
                except StopIteration:
                    a1 = False
            if a2:
                try:
                    next(g2)
                except StopIteration:
                    a2 = False

    def pipeline(items, genA, genB):
        if not items:
            return
        drain(genA(0, items[0]))
        for i, it_ in enumerate(items):
            nxt = genA(i + 1, items[i + 1]) if i + 1 < len(items) else None
            interleave(nxt, genB(i, it_))

    for L in range(nlayers):
        last = (L == DEPTH - 1)
        es0 = ExitStack()
        wblk = P.sbuf("wblk", [128, KD, D], BF16, es0)
        rowt = P.sbuf("rowt", [1, 512], F32, es0)
        growt = P.sbuf("growt", [128, D], F32, es0)
        brow = P.sbuf("brow", [1, 6 * D], F32, es0)
        g1row = P.sbuf("g1row", [1, D], F32, es0)
        g3row = P.sbuf("g3row", [1, D], F32, es0)
        lamt = P.sbuf("lamt", [128, 2, 4], F32, es0)
        k.LD(bmodfm[:], bmod_fm.ap()[L], [bmodfm])
        k.LD(gfm[:], normg_fm.ap()[L], [gfm])
        k.LD(brow[:], bmod_row.ap()[L], [brow])
        k.LD(g1row[:], normg_row.ap()[L, 1:2, :], [g1row])
        k.LD(g3row[:], normg_row.ap()[L, 3:4, :], [g3row])
        k.LD(gnv[:], gn_fm.ap()[L], [gnv])
        k.LD(cwv[:], convw_fm.ap()[L], [cwv])
        k.LD(cbv[:], convb_fm.ap()[L], [cbv])
        k.LD(bav[:], ba_fm.ap()[L], [bav])
        k.LD(bxv[:], bx_fm.ap()[L], [bxv])
        k.LD(lamt[:], lam_fm.ap()[L], [lamt])
        k.ACT(c1v[:], lamt[:], AF.Exp, [lamt], [c1v], scale=-1.0)
        k.ACT(c1v[:], c1v[:], AF.Ln, [c1v], [c1v], bias=1.0)
        k.TS("dve", c1v[:], c1v[:], -8.0, None, ALU.mult, None, [c1v], [c1v])
        if L == 0:
            k.MEMSET("pool", lbv[:], 0.0, [lbv])
        else:
            k.TT("dve", lbv[:], lbt[:, 1], lbt[:, 0], ALU.subtract, [lbt], [lbv])
            k.ACT(lbv[:], lbv[:], AF.Sigmoid, [lbv], [lbv])
        k.TS("dve", omlv[:], lbv[:], -1.0, 1.0, ALU.mult, ALU.add, [lbv], [omlv])
        for n in range(6):
            k.LD(wblk[:], w_mod.ap()[L, :, n * D:(n + 1) * D].rearrange("(kc p) c -> p kc c", p=128), [wblk], q="pool")
            pb = k.ps()
            for kd in range(KD):
                for kc in range(KD):
                    k.MM(pb.t[:, kd * 2:kd * 2 + 2], wblk[:, kc, kd * 128:(kd + 1) * 128], scbf[:, kc, :],
                         kc == 0, kc == KD - 1, [wblk, scbf], [pb])
            k.TT("dve", modfm[:, n * 8:(n + 1) * 8, :], pb.t[:, 0:16].rearrange("p (a b) -> p a b", b=2),
                 bmodfm[:, n * 8:(n + 1) * 8].unsqueeze(2).to_broadcast([128, 8, 2]), ALU.add, [pb, bmodfm], [modfm])
            if n in (2, 5):
                a = 0 if n == 2 else 1
                grow_g = g1row if n == 2 else g3row
                for j in range(2):
                    for half in range(2):
                        pb2 = k.ps()
                        for kc in range(KD):
                            k.MM(pb2.t[0:1, 0:512], scbf[:, kc, j:j + 1], wblk[:, kc, half * 512:(half + 1) * 512],
                                 kc == 0, kc == KD - 1, [wblk, scbf], [pb2])
                        k.TT("dve", rowt[:], pb2.t[0:1, 0:512], brow[0:1, n * D + half * 512:n * D + (half + 1) * 512],
                             ALU.add, [pb2, brow], [rowt])
                        k.TT("dve", rowt[:], rowt[:], grow_g[0:1, half * 512:(half + 1) * 512], ALU.mult,
                             [rowt, grow_g], [rowt])
                        pb3 = k.ps()
                        k.MM(pb3.t[:, 0:512], ones_row[0:1, :], rowt[0:1, :], True, True, [ones_row, rowt], [pb3])
                        k.CP("act", growt[:, half * 512:(half + 1) * 512], pb3.t[:, 0:512], [pb3], [growt])
                    k.ST(s_grow.t[2 * a + j], growt[:], [growt], [s_grow])
        for a, (nsc, nsh, gi) in enumerate(((1, 0, 0), (4, 3, 2))):
            k.TS("dve", scsh[:, 2 * a], modfm[:, nsc * 8:(nsc + 1) * 8, :], 1.0, None, ALU.add, None, [modfm], [scsh])
            k.TT("dve", scsh[:, 2 * a], scsh[:, 2 * a], gfm[:, gi * 8:(gi + 1) * 8].unsqueeze(2).to_broadcast([128, 8, 2]),
                 ALU.mult, [scsh, gfm], [scsh])
            k.CP("dve", scsh[:, 2 * a + 1], modfm[:, nsh * 8:(nsh + 1) * 8, :], [modfm], [scsh])
        P.end_phase()
        es0.close()
        if STAGE < 1:
            break

        es1 = ExitStack()
        winb = P.sbuf("winb", [128, KD, INC], BF16, es1)
        for kc in range(KD):
            k.LD(winb[:, kc, :], w_in.ap()[L, kc * 128:(kc + 1) * 128, :], [winb], q="pool")
        xts = [P.sbuf("xt%d" % i, [128, 4, D], F32, es1) for i in range(1)]
        hTs = [P.sbuf("hT%d" % i, [128, KD, 512], BF16, es1) for i in range(2)]
        qs = P.sbuf("qs", [128, 4, 512], BF16, es1)
        sg = P.sbuf("sg", [128, 512], F32, es1)
        ft = P.sbuf("ft", [128, 2, 4, 512], F32, es1)
        uf = P.sbuf("uf", [128, 512], F32, es1)
        uc = P.sbuf("uc", [128, 512], F32, es1)
        ub = P.sbuf("ub", [128, 4, 512], BF16, es1)
        gb = P.sbuf("gb", [128, 4, 512], BF16, es1)
        vt = P.sbuf("vt", [128, 4, HGW], BF16, es1)
        gt = P.sbuf("gt", [128, 4, HGW], BF16, es1)
        nb1 = norm_bufs(es1, "p1", 4)
        def p1A(ui, unit):
            tok0, T, isctx = unit
            nt = T // 128
            j = 1 if isctx else 0
            xt = xts[0]
            src, srcbuf = x_src(L, tok0, T)
            k.LD(xt[:, 0:nt, :], src.rearrange("(j p) d -> p j d", p=128), [xt], R=[srcbuf])
            yield from norm_mod_g(nb1, xt, nt, 0, j, hTs[ui % 2])

        def p1B(ui, unit):
            tok0, T, isctx = unit
            nt = T // 128
            nch = T // CH
            j = 1 if isctx else 0
            hT = hTs[ui % 2]
            for ct in range(20):
                if ct < 12:
                    c0 = ct * 128
                else:
                    c0 = 5 * HGW + (ct - 12) * 128
                pb = k.ps()
                for kc in range(KD):
                    k.MM(pb.t[:, 0:T], winb[:, kc, c0:c0 + 128], hT[:, kc, 0:T], kc == 0, kc == KD - 1, [winb, hT], [pb])
                if ct < 4:
                    k.ACT(qs[:, ct, 0:T], pb.t[:, 0:T], AF.Silu, [pb], [qs])
                elif ct < 12:
                    dr, h = (ct - 4) // 4, (ct - 4) % 4
                    k.ACT(sg[:, 0:T], pb.t[:, 0:T], AF.Sigmoid, [pb], [sg])
                    k.TS("dve", ft[:, dr, h, 0:T], sg[:, 0:T], omlv[:, dr, h:h + 1], lbv[:, dr, h:h + 1], ALU.mult, ALU.add,
                         [sg, omlv, lbv], [ft])
                elif ct < 16:
                    c = ct - 12
                    k.CP("act", uf[:, 0:T], pb.t[:, 0:T], [pb], [uf])
                    RW = T if isctx else 64
                    ufv = uf[:, 0:T].rearrange("p (r w) -> p r w", w=RW)
                    ucv = uc[:, 0:T].rearrange("p (r w) -> p r w", w=RW)
                    k.TS("dve", uc[:, 0:T], uf[:, 0:T], cwv[:, c, 2:3], cbv[:, c:c + 1], ALU.mult, ALU.add, [uf, cwv, cbv], [uc])
                    for tap in (0, 1, 3):
                        s = tap - 2
                        if s < 0:
                            o_sl = ucv[:, :, -s:RW]
                            i_sl = ufv[:, :, 0:RW + s]
                        else:
                            o_sl = ucv[:, :, 0:RW - s]
                            i_sl = ufv[:, :, s:RW]
                        k.STT(o_sl, i_sl, cwv[:, c, tap:tap + 1], o_sl, ALU.mult, ALU.add, [uf, uc, cwv], [uc])
                    k.CP("act", ub[:, c, 0:T], uc[:, 0:T], [uc], [ub])
                else:
                    c = ct - 16
                    k.ACT(gb[:, c, 0:T], pb.t[:, 0:T], AF.Gelu_apprx_tanh, [pb], [gb])
                if ct % 2 == 1:
                    yield
            for jj in range(nt):
                for which in range(2):
                    c0 = (3 + which) * HGW
                    pb = k.ps()
                    for kc in range(KD):
                        k.MM(pb.t[:, 0:512], hT[:, kc, jj * 128:(jj + 1) * 128], winb[:, kc, c0:c0 + 512],
                             kc == 0, kc == KD - 1, [winb, hT], [pb])
                    if which == 0:
                        k.CP("act", vt[:, jj, :], pb.t[:, 0:512], [pb], [vt])
                    else:
                        k.ACT(gt[:, jj, :], pb.t[:, 0:512], AF.Silu, [pb], [gt])
                yield
            k.ST(s_qT.t[:, :, tok0:tok0 + T].rearrange("h p t -> p h t"), qs[:, :, 0:T], [qs], [s_qT])
            for dr in range(2):
                k.ST(s_fT.t[dr, :, :, tok0:tok0 + T].rearrange("h p t -> p h t"), ft[:, dr, :, 0:T], [ft], [s_fT])
            k.ST(s_u.t[:, :, tok0:tok0 + T].rearrange("h p t -> p h t"), ub[:, :, 0:T], [ub], [s_u])
            k.ST(s_gate.t[:, :, tok0:tok0 + T].rearrange("h p t -> p h t"), gb[:, :, 0:T], [gb], [s_gate])
            k.ST(s_v.t[tok0:tok0 + T, :].rearrange("(j p) n -> p j n", p=128), vt[:, 0:nt, :], [vt], [s_v])
            k.ST(s_g.t[tok0:tok0 + T, :].rearrange("(j p) n -> p j n", p=128), gt[:, 0:nt, :], [gt], [s_g])
        pipeline(units, p1A, p1B)
        P.end_phase()
        es1.close()
        if STAGE < 2:
            break


        esm = ExitStack()
        Sst = P.sbuf("Sst", [128, 2, 4, 128], F32, esm)
        Sctx = P.sbuf("Sctx", [128, 2, 4, 128], F32, esm)
        Sin = P.sbuf("Sin", [128, 2, 4, 128], F32, esm)
        Sinb = P.sbuf("Sinb", [128, 2, 4, 128], BF16, esm)
        es2 = ExitStack()
        wbd = P.sbuf("wbd", [128, 2, 2, 4, 128], BF16, es2)
        wstage = P.sbuf("wstage", [128, 2, 2, 4, 128], F32, es2)
        zeros_t = P.sbuf("zeros_t", [128, 512], F32, es2)
        k.MEMSET("pool", zeros_t[:], 0.0, [zeros_t])
        k.MEMSET("pool", wstage[:], 0.0, [wstage])
        for gi, wsrc in enumerate((rg_w_a, rg_w_x)):
            for dr in range(2):
                for half in range(2):
                    src = wsrc.ap()[L, dr].rearrange("(ct h) i j -> h i ct j", h=2)[half]
                    k.LD(wstage[half * 64:(half + 1) * 64, gi, dr, :, half * 64:(half + 1) * 64], src, [wstage])
        k.CP("dve", wbd[:], wstage[:], [wstage], [wbd])
        uts = [P.sbuf("ut%d" % i, [128, 4, 512], BF16, es2) for i in range(2)]
        rt = P.sbuf("rt", [128, 4, 512], F32, es2)
        it = P.sbuf("it", [128, 4, 512], F32, es2)
        at = P.sbuf("at", [128, 4, 512], F32, es2)
        a2t = P.sbuf("a2t", [128, 4, 512], F32, es2)
        bxt = P.sbuf("bxt", [128, 4, 512], F32, es2)
        hl = P.sbuf("hl", [128, 4, 512], F32, es2)
        ac = P.sbuf("ac", [128, 4, 512], F32, es2)
        hss = [P.sbuf("hs%d" % i, [128, 4, 512], F32, es2) for i in range(2)]
        car_h = P.sbuf("car_h", [128, 4], F32, es2)
        car_a = P.sbuf("car_a", [128, 4], F32, es2)
        for dr in range(2):
            order = [units[0]] + (units[1:] if dr == 0 else units[1:][::-1])

            def rgA(ui, unit, dr=dr):
                tok0, T, isctx = unit
                k.LD(uts[ui % 2][:, :, 0:T], s_u.t[:, :, tok0:tok0 + T].rearrange("c p t -> p c t"), [uts[ui % 2]], R=[s_u])
                if dr == 1:
                    k.LD(hss[ui % 2][:, :, 0:T], s_hsumf.t[:, :, tok0:tok0 + T].rearrange("c p t -> p c t"), [hss[ui % 2]], R=[s_hsumf])
                yield

            def rgB(ui, unit, dr=dr):
                tok0, T, isctx = unit
                ut, hs = uts[ui % 2], hss[ui % 2]
                fresh = (ui == 0) if CHAIN else (ui <= 1)
                pbs = []
                for c in range(4):
                    for gi in range(2):
                        pb = k.ps()
                        k.MM(pb.t[:, 0:T], wbd[:, gi, dr, c, :], ut[:, c, 0:T], True, True, [wbd, ut], [pb])
                        pbs.append(pb)
                for c in range(4):
                    k.ACT(rt[:, c, 0:T], pbs[2 * c].t[:, 0:T], AF.Sigmoid, [pbs[2 * c], bav], [rt], bias=bav[:, dr, c:c + 1])
                    k.ACT(it[:, c, 0:T], pbs[2 * c + 1].t[:, 0:T], AF.Sigmoid, [pbs[2 * c + 1], bxv], [it], bias=bxv[:, dr, c:c + 1])
                for c in range(4):
                    k.ACT(at[:, c, 0:T], rt[:, c, 0:T], AF.Exp, [rt, c1v], [at], scale=c1v[:, dr, c:c + 1])
                k.ACT(a2t[:, :, 0:T], at[:, :, 0:T], AF.Square, [at], [a2t])
                k.ACT(a2t[:, :, 0:T], a2t[:, :, 0:T], AF.Sqrt, [a2t], [a2t], scale=-1.0, bias=1.0)
                k.TT("pool", it[:, :, 0:T], it[:, :, 0:T], ut[:, :, 0:T], ALU.mult, [it, ut], [it])
                k.TT("dve", bxt[:, :, 0:T], it[:, :, 0:T], a2t[:, :, 0:T], ALU.mult, [it, a2t], [bxt])
                for c in range(4):
                    ih = 0.0 if fresh else car_h[:, c:c + 1]
                    ia = 1.0 if fresh else car_a[:, c:c + 1]
                    if dr == 0:
                        k.SCAN(hl[:, c, 0:T], at[:, c, 0:T], bxt[:, c, 0:T], ih, [at, bxt, car_h], [hl])
                        if not CHAIN:
                            k.SCAN(ac[:, c, 0:T], at[:, c, 0:T], zeros_t[:, 0:T], ia, [at, zeros_t, car_a], [ac])
                        lastcol = slice(T - 1, T)
                    else:
                        k.SCAN(hl[:, c, T - 1::-1] if False else hl[:, c, 0:T][:, ::-1], at[:, c, 0:T][:, ::-1], bxt[:, c, 0:T][:, ::-1], ih,
                               [at, bxt, car_h], [hl])
                        if not CHAIN:
                            k.SCAN(ac[:, c, 0:T][:, ::-1], at[:, c, 0:T][:, ::-1], zeros_t[:, 0:T], ia, [at, zeros_t, car_a], [ac])
                        lastcol = slice(0, 1)
                    if isctx:
                        k.CP("act", hctx[:, dr, c:c + 1], hl[:, c, lastcol], [hl], [hctx])
                    if CHAIN or not isctx:
                        k.CP("act", car_h[:, c:c + 1], hl[:, c, lastcol], [hl], [car_h])
                        if not CHAIN:
                            k.CP("act", car_a[:, c:c + 1], ac[:, c, lastcol], [ac], [car_a])
                if dr == 0:
                    k.ST(s_hsumf.t[:, :, tok0:tok0 + T].rearrange("c p t -> p c t"), hl[:, :, 0:T], [hl], [s_hsumf])
                else:
                    k.TT("dve", hl[:, :, 0:T], hl[:, :, 0:T], hs[:, :, 0:T], ALU.add, [hl, hs], [hl])
                    k.ST(s_hsum.t[:, :, tok0:tok0 + T].rearrange("c p t -> p c t"), hl[:, :, 0:T], [hl], [s_hsum])
                if not CHAIN:
                    k.ST(s_acum.t[dr, :, :, tok0:tok0 + T].rearrange("c p t -> p c t"), ac[:, :, 0:T], [ac], [s_acum])
                yield

            pipeline(order, rgA, rgB)
            if not CHAIN:
                k.CP("pool", hfin[:, dr, :], car_h[:], [car_h], [hfin])
                k.CP("pool", atot[:, dr, :], car_a[:], [car_a], [atot])
        P.end_phase()
        es2.close()
        if STAGE < 3:
            esm.close()
            break

        es3 = ExitStack()
        fTt = P.sbuf("fTt", [128, 4, 512], F32, es3)
        qTt = P.sbuf("qTt", [128, 4, 512], BF16, es3)
        kTt = P.sbuf("kTt", [128, 4, 512], F32, es3)
        Ct = P.sbuf("Ct", [128, 4, 512], F32, es3)
        crel = P.sbuf("crel", [128, 4, 512], F32, es3)
        e1 = P.sbuf("e1", [128, 4, 512], F32, es3)
        ktl = P.sbuf("ktl", [128, 4, 512], BF16, es3)
        khl = P.sbuf("khl", [128, 4, 512], BF16, es3)
        qsg = None if CHAIN else P.sbuf("qsg", [128, 4, 512], BF16, es3)
        vtts = [P.sbuf("vtt%d" % i, [64, 8, HGW], BF16, es3) for i in range(2)]
        qtls = [P.sbuf("qtl%d" % i, [128, 4, 512], BF16, es3) for i in range(2)]
        khTs = [P.sbuf("khT%d" % i, [64, 4, 8, 128], BF16, es3) for i in range(2)]
        scTs = [[P.sbuf("scT%d%d" % (d_, i), [64, 4, 512], BF16, es3) for i in range(2)] for d_ in range(2)]
        BDs = [P.sbuf("BD%d" % i, [128, 4, 2, 8], F32, es3) for i in range(2)]
        for d_ in range(2):
            for i in range(2):
                k.MEMSET("pool", scTs[d_][i][:], 0.0, [scTs[d_][i]])
        ot = P.sbuf("ot", [64, 8, HGW], F32, es3)
        ofts = [P.sbuf("oft%d" % i, [64, 8, HGW], F32, es3) for i in range(2)]
        SpbV = [[P.sbuf("Spb%d_%d" % (i, h), [128, 128], BF16, es3) for h in range(4)] for i in range(2)]
        SstV = [[Buf(P, Sst.t, "SstV%d%d" % (d_, h)) for h in range(4)] for d_ in range(2)]
        carC = P.sbuf("carC", [128, 4], F32, es3)
        cprev = P.sbuf("cprev", [128, 4, 8], F32, es3)
        dif = P.sbuf("dif", [128, 4, 2, 8], F32, es3)

        def hgA(dr, ui, tok0, T, isctx, sset):
            fresh = (ui == 0) if CHAIN else (ui <= 1)
            nch = T // CH
            vtt, qtl, khT, scT, BD = vtts[sset], qtls[sset], khTs[sset], scTs[dr][sset], BDs[sset]
            oft = ofts[sset]
            k.LD(fTt[:, :, 0:T], s_fT.t[dr, :, :, tok0:tok0 + T].rearrange("h p t -> p h t"), [fTt], R=[s_fT])
            k.LD(qTt[:, :, 0:T], s_qT.t[:, :, tok0:tok0 + T].rearrange("h p t -> p h t"), [qTt], R=[s_qT])
            k.LD(vtt[:, 0:nch, :], s_v.t[tok0:tok0 + T, :].rearrange("(c p) n -> p c n", p=CH), [vtt], R=[s_v])
            k.ACT(kTt[:, :, 0:T], fTt[:, :, 0:T], AF.Copy, [fTt], [kTt], scale=-1.0, bias=1.0)
            k.ACT(fTt[:, :, 0:T], fTt[:, :, 0:T], AF.Ln, [fTt], [fTt])
            C4 = Ct[:, :, 0:T].rearrange("p h (n w) -> p h n w", w=CH)
            edge = 0 if dr == 0 else nch - 1
            if fresh:
                k.MEMSET("dve", cprev[:, :, edge:edge + 1], 0.0, [cprev])
            else:
                k.CP("act", cprev[:, :, edge:edge + 1], carC[:].unsqueeze(2), [carC], [cprev])
            for h in range(4):
                init = 0.0 if fresh else carC[:, h:h + 1]
                if dr == 0:
                    k.SCAN(Ct[:, h, 0:T], ones_t[:, 0:T], fTt[:, h, 0:T], init, [ones_t, fTt, carC], [Ct])
                else:
                    k.SCAN(Ct[:, h, 0:T][:, ::-1], ones_t[:, 0:T], fTt[:, h, 0:T][:, ::-1], init, [ones_t, fTt, carC], [Ct])
            if dr == 0:
                k.CP("act", carC[:].unsqueeze(2), Ct[:, :, T - 1:T], [Ct], [carC])
                Aanc = C4[:, :, :, 31]
                Cend = C4[:, :, :, 63]
                if nch > 1:
                    k.CP("act", cprev[:, :, 1:nch], C4[:, :, 0:nch - 1, 63], [Ct], [cprev])
            else:
                k.CP("act", carC[:].unsqueeze(2), Ct[:, :, 0:1], [Ct], [carC])
                Aanc = C4[:, :, :, 32]
                Cend = C4[:, :, :, 0]
                if nch > 1:
                    k.CP("act", cprev[:, :, 0:nch - 1], C4[:, :, 1:nch, 0], [Ct], [cprev])
            k.TT("dve", dif[:, :, 0, 0:nch], Aanc, cprev[:, :, 0:nch], ALU.subtract, [Ct, cprev], [dif])
            k.TT("dve", dif[:, :, 1, 0:nch], Cend, cprev[:, :, 0:nch], ALU.subtract, [Ct, cprev], [dif])
            k.ACT(BD[:, :, :, 0:nch], dif[:, :, :, 0:nch], AF.Exp, [dif], [BD])
            yield
            cr4 = crel[:, :, 0:T].rearrange("p h (n w) -> p h n w", w=CH)
            e14 = e1[:, :, 0:T].rearrange("p h (n w) -> p h n w", w=CH)
            k.TT("dve", cr4, C4, Aanc.unsqueeze(3).to_broadcast([128, 4, nch, CH]), ALU.subtract, [Ct], [crel])
            k.ACT(e1[:, :, 0:T], crel[:, :, 0:T], AF.Exp, [crel], [e1])
            k.ACT(crel[:, :, 0:T], crel[:, :, 0:T], AF.Exp, [crel], [crel], scale=-1.0)
            k.TT("pool", qtl[:, :, 0:T], qTt[:, :, 0:T], e1[:, :, 0:T], ALU.mult, [qTt, e1], [qtl])
            k.TT("dve", ktl[:, :, 0:T], kTt[:, :, 0:T], crel[:, :, 0:T], ALU.mult, [kTt, crel], [ktl])
            yield
            if not isctx and not CHAIN:
                k.ACT(e1[:, :, 0:T], Ct[:, :, 0:T], AF.Exp, [Ct], [e1])
                k.TT("pool", qsg[:, :, 0:T], qTt[:, :, 0:T], e1[:, :, 0:T], ALU.mult, [qTt, e1], [qsg])
                k.ST(s_qseg.t[dr, :, :, tok0:tok0 + T].rearrange("h p t -> p h t"), qsg[:, :, 0:T], [qsg], [s_qseg])
            k.TT("dve", e14, C4, Cend.unsqueeze(3).to_broadcast([128, 4, nch, CH]), ALU.subtract, [Ct], [e1])
            k.ACT(e1[:, :, 0:T], e1[:, :, 0:T], AF.Exp, [e1], [e1], scale=-1.0)
            k.TT("pool", khl[:, :, 0:T], kTt[:, :, 0:T], e1[:, :, 0:T], ALU.mult, [kTt, e1], [khl])
            yield
            for h in range(4):
                pbT = k.ps()
                pT = pbT.t[:, :].bitcast(BF16)
                for n in range(nch):
                    k.TR(pT[0:CH, n * 128:(n + 1) * 128], khl[:, h, n * CH:(n + 1) * CH], ident[:], [khl, ident], [pbT])
                k.CP("act", khT[:, h, 0:nch, :], pT[0:CH, 0:nch * 128].rearrange("p (n k) -> p n k", k=128), [pbT], [khT])
                pbS = k.ps()
                for n in range(nch):
                    k.MM(pbS.t[0:CH, n * CH:(n + 1) * CH], ktl[:, h, n * CH:(n + 1) * CH], qtl[:, h, n * CH:(n + 1) * CH],
                         True, True, [ktl, qtl], [pbS])
                P.op("dve", (lambda sc_, m_, p_: (lambda e: e.copy_predicated(sc_, m_, p_)))(scT[:, h, 0:T], maski[:, dr, 0:T], pbS.t[0:CH, 0:T]),
                     [pbS, maski], [scT])
                if h % 2 == 1:
                    yield
            if dr == 1:
                k.LD(oft[:, 0:nch, :], s_of.t[tok0:tok0 + T, :].rearrange("(c p) n -> p c n", p=CH), [oft], R=[s_of])

        def hgB(dr, ui, tok0, T, isctx, sset):
            nch = T // CH
            vtt, qtl, khT, scT, BD = vtts[sset], qtls[sset], khTs[sset], scTs[dr][sset], BDs[sset]
            oft = ofts[sset]
            if ui == 0:
                for h in range(4):
                    k.MEMSET("dve", Sst[:, dr, h, :], 0.0, [SstV[dr][h]])
            chunks = list(range(nch)) if dr == 0 else list(range(nch))[::-1]
            prev = None

            def evac(po_, n_):
                if dr == 0:
                    k.CP("act", ot[:, n_, :], po_.t[0:CH, 0:512], [po_], [ot])
                else:
                    k.TT("dve", ot[:, n_, :], po_.t[0:CH, 0:512], oft[:, n_, :], ALU.add, [po_, oft], [ot])

            for ci, n in enumerate(chunks):
                sp = SpbV[ci % 2]
                for h in range(4):
                    k.ACT(sp[h][:], Sst[:, dr, h, :], AF.Copy, [SstV[dr][h], BD], [sp[h]], scale=BD[:, h, 0, n:n + 1])
                po = k.psb[6 + ci % 2]
                pks = [k.ps() for _ in range(4)]
                for h in range(4):
                    hs_ = slice(h * 128, (h + 1) * 128)
                    k.MM(po.t[0:CH, hs_], scT[:, h, n * CH:(n + 1) * CH], vtt[:, n, hs_], True, False, [scT, vtt], [po])
                    k.MM(po.t[0:CH, hs_], qtl[:, h, n * CH:(n + 1) * CH], sp[h][:], False, True, [qtl, sp[h]], [po])
                    k.MM(pks[h].t[:, 0:128], khT[:, h, n, :], vtt[:, n, hs_], True, True, [khT, vtt], [pks[h]])
                for h in range(4):
                    k.STT(Sst[:, dr, h, :], Sst[:, dr, h, :], BD[:, h, 1, n:n + 1], pks[h].t[:, 0:128], ALU.mult, ALU.add,
                          [SstV[dr][h], BD, pks[h]], [SstV[dr][h]])
                if prev is not None:
                    evac(*prev)
                prev = (po, n)
                yield
            evac(*prev)
            dst = s_of if dr == 0 else s_osum
            k.ST(dst.t[tok0:tok0 + T, :].rearrange("(c p) n -> p c n", p=CH), ot[:, 0:nch, :], [ot], [dst])
            if isctx and not CHAIN:
                for h in range(4):
                    k.CP("act", Sctx[:, dr, h, :], Sst[:, dr, h, :], [SstV[dr][h]], [Sctx])
                    k.MEMSET("dve", Sst[:, dr, h, :], 0.0, [SstV[dr][h]])

        k.nrot = 6
        seq = []
        for dr in range(2):
            order = [units[0]] + (units[1:] if dr == 0 else units[1:][::-1])
            for ui, (tok0, T, isctx) in enumerate(order):
                seq.append((dr, ui, tok0, T, isctx))
        drain(hgA(*seq[0], 0))
        for i, item in enumerate(seq):
            nxt = hgA(*seq[i + 1], (i + 1) % 2) if i + 1 < len(seq) else None
            if nxt is not None and seq[i + 1][0] != item[0]:
                if not CHAIN:
                    k.ACT(dtot[:, item[0], :], carC[:], AF.Exp, [carC], [dtot])
            interleave(nxt, hgB(*item, i % 2))
        if not CHAIN:
            k.ACT(dtot[:, 1, :], carC[:], AF.Exp, [carC], [dtot])
        P.end_phase()
        k.nrot = 8
        es3.close()
        if STAGE < 4:
            esm.close()
            break

        if not CHAIN:
            esx = ExitStack()
            xs = P.sbuf("xs", [128, XW], F32, esx)
            xg = P.sbuf("xg", [128, 8, XW], F32, esx)
            dm1 = P.sbuf("dm1", [128, 8, 16], F32, esx)
            tS = P.sbuf("tS", [128, 128], F32, esx)
            tH = P.sbuf("tH", [128, 4], F32, esx)
            k.CP("pool", xs[:, 0:1024], Sst[:].rearrange("p a b c -> p (a b c)"), [Sst], [xs])
            k.CP("pool", xs[:, 1024:1032], dtot[:].rearrange("p a b -> p (a b)"), [dtot], [xs])
            k.CP("pool", xs[:, 1032:1040], hfin[:].rearrange("p a b -> p (a b)"), [hfin], [xs])
            k.CP("pool", xs[:, 1040:1048], atot[:].rearrange("p a b -> p (a b)"), [atot], [xs])
            k.ST(s_xsrc.t.ap(), xs[:], [xs], [s_xsrc])
            if USE_CC:
                P.custom("pool", (lambda a_, b_: (lambda e: e.collective_compute("AllGather", ALU.bypass, replica_groups=[list(range(8))],
                                                                               ins=[a_], outs=[b_])))(s_xsrc.t.ap().opt(), s_xdst.t.ap().opt()),
                         reads=[s_xsrc], writes=[s_xdst], inc=1)
            else:
                for r in range(8):
                    k.ST(s_xdst.t.ap()[r * 128:(r + 1) * 128, :], s_xsrc.t.ap(), [s_xsrc], [s_xdst])
            k.LD(xg[:], s_xdst.t.ap().rearrange("(r p) c -> p r c", p=128), [xg], R=[s_xdst])
            k.TS("dve", dm1[:, :, 0:8], xg[:, :, 1024:1032], -1.0, None, ALU.add, None, [xg], [dm1])
            k.TS("dve", dm1[:, :, 8:16], xg[:, :, 1040:1048], -1.0, None, ALU.add, None, [xg], [dm1])
            k.CP("pool", Sin[:], Sctx[:], [Sctx], [Sin])
            k.CP("pool", hin[:], hctx[:], [hctx], [hin])
            for dr in range(2):
                ranks = list(range(0, 7)) if dr == 0 else list(range(7, 0, -1))
                for i in ranks:
                    fl = flags[:, dr * 8 + i:dr * 8 + i + 1]
                    for h in range(4):
                        c0 = (dr * 4 + h) * 128
                        k.STT(tS[:], Sin[:, dr, h, :], dm1[:, i, dr * 4 + h:dr * 4 + h + 1], xg[:, i, c0:c0 + 128], ALU.mult, ALU.add,
                              [Sin, dm1, xg], [tS])
                        k.STT(Sin[:, dr, h, :], tS[:], fl, Sin[:, dr, h, :], ALU.mult, ALU.add, [tS, flags, Sin], [Sin])
                    k.TT("dve", tH[:], hin[:, dr, :], dm1[:, i, 8 + dr * 4:8 + dr * 4 + 4], ALU.mult, [hin, dm1], [tH])
                    k.TT("dve", tH[:], tH[:], xg[:, i, 1032 + dr * 4:1032 + dr * 4 + 4], ALU.add, [tH, xg], [tH])
                    k.STT(hin[:, dr, :], tH[:], fl, hin[:, dr, :], ALU.mult, ALU.add, [tH, flags, hin], [hin])
            k.CP("dve", Sinb[:], Sin[:], [Sin], [Sinb])
            if DEBUG:
                k.ST(dbg["sst"].t.ap(), Sst[:].rearrange("p a b c -> p (a b c)"), [Sst], [dbg["sst"]])
                k.ST(dbg["sctx"].t.ap(), Sctx[:].rearrange("p a b c -> p (a b c)"), [Sctx], [dbg["sctx"]])
                k.ST(dbg["sin"].t.ap(), Sin[:].rearrange("p a b c -> p (a b c)"), [Sin], [dbg["sin"]])
            P.end_phase()
            esx.close()
        if STAGE < 5:
            esm.close()
            break

        es4 = ExitStack()
        wob = P.sbuf("wob", [128, KD, D], BF16, es4)
        wst = P.sbuf("wst", [128, D], F32, es4)
        for kc in range(KD):
            if kc < 4:
                k.LD(wst[:], w_out.ap()[L, kc * 128:(kc + 1) * 128, :], [wst])
                k.TS("dve", wob[:, kc, :], wst[:], gnv[:, 0:1], None, ALU.mult, None, [wst, gnv], [wob])
            else:
                k.LD(wob[:, kc, :], w_out.ap()[L, kc * 128:(kc + 1) * 128, :], [wob], q="pool")
        osm = P.sbuf("osm", [64, 8, HGW], F32, es4)
        gtt = P.sbuf("gtt", [64, 8, HGW], BF16, es4)
        qsf = None if CHAIN else P.sbuf("qsf", [128, 4, 512], BF16, es4)
        qsb = None if CHAIN else P.sbuf("qsb", [128, 4, 512], BF16, es4)
        hst = P.sbuf("hst", [128, 4, 512], F32, es4)
        acf = None if CHAIN else P.sbuf("acf", [128, 4, 512], F32, es4)
        acb = None if CHAIN else P.sbuf("acb", [128, 4, 512], F32, es4)
        gat = P.sbuf("gat", [128, 4, 512], BF16, es4)
        xt3s = [P.sbuf("xt3_%d" % i, [128, 4, D], F32, es4) for i in range(2 if CHAIN else 1)]
        grow0 = [P.sbuf("grow0_%d" % i, [128, D], F32, es4) for i in range(2)]
        for i in range(2):
            k.LD(grow0[i][:], s_grow.t[i], [grow0[i]], R=[s_grow])
        ot3 = P.sbuf("ot3", [64, 8, HGW], F32, es4)
        mixb = P.sbuf("mixb", [64, 8, HGW], BF16, es4)
        mixT = P.sbuf("mixT", [128, KD, 512], BF16, es4)
        tA = P.sbuf("tA", [128, 512], F32, es4)
        tmp3 = P.sbuf("tmp3", [128, 512], F32, es4)
        junk3 = P.sbuf("junk3", [128, 512], BF16, es4)
        ssh = P.sbuf("ssh", [64, 32], F32, es4)
        ss2 = P.sbuf("ss2", [128, 4], F32, es4)
        units3 = [u_ for u_ in units if not (u_[2] and last)]

        def ld1(idx):
            tok0, T, isctx = units3[idx]
            nch = T // CH
            k.LD(osm[:, 0:nch, :], s_osum.t[tok0:tok0 + T, :].rearrange("(c p) n -> p c n", p=CH), [osm], R=[s_osum])
            k.LD(gtt[:, 0:nch, :], s_g.t[tok0:tok0 + T, :].rearrange("(c p) n -> p c n", p=CH), [gtt], R=[s_g])
            if (not isctx) and (not CHAIN):
                k.LD(qsf[:, :, 0:T], s_qseg.t[0, :, :, tok0:tok0 + T].rearrange("h p t -> p h t"), [qsf], R=[s_qseg])
                k.LD(qsb[:, :, 0:T], s_qseg.t[1, :, :, tok0:tok0 + T].rearrange("h p t -> p h t"), [qsb], R=[s_qseg])

        def ld2(idx):
            tok0, T, isctx = units3[idx]
            k.LD(hst[:, :, 0:T], s_hsum.t[:, :, tok0:tok0 + T].rearrange("c p t -> p c t"), [hst], R=[s_hsum])
            k.LD(gat[:, :, 0:T], s_gate.t[:, :, tok0:tok0 + T].rearrange("c p t -> p c t"), [gat], R=[s_gate])
            if (not isctx) and (not CHAIN):
                k.LD(acf[:, :, 0:T], s_acum.t[0, :, :, tok0:tok0 + T].rearrange("c p t -> p c t"), [acf], R=[s_acum])
                k.LD(acb[:, :, 0:T], s_acum.t[1, :, :, tok0:tok0 + T].rearrange("c p t -> p c t"), [acb], R=[s_acum])

        def ld3(idx):
            tok0, T, isctx = units3[idx]
            xb = xt3s[idx % len(xt3s)]
            src, srcbuf = x_src(L, tok0, T)
            k.LD(xb[:, 0:T // 128, :], src.rearrange("(j p) d -> p j d", p=128), [xb], R=[srcbuf])

        ld1(0)
        ld2(0)
        ld3(0)
        for ui, (tok0, T, isctx) in enumerate(units3):
            nt = T // 128
            nch = T // CH
            j = 1 if isctx else 0
            fix = (not isctx) and (not CHAIN)
            xt3 = xt3s[ui % len(xt3s)]
            more = ui + 1 < len(units3)
            if more and len(xt3s) == 2:
                ld3(ui + 1)
            for n in range(nch):
                if fix:
                    pf = k.ps()
                    for h in range(4):
                        hs_ = slice(h * 128, (h + 1) * 128)
                        k.MM(pf.t[0:CH, hs_], qsf[:, h, n * CH:(n + 1) * CH], Sinb[:, 0, h, :], True, False, [qsf, Sinb], [pf])
                        k.MM(pf.t[0:CH, hs_], qsb[:, h, n * CH:(n + 1) * CH], Sinb[:, 1, h, :], False, True, [qsb, Sinb], [pf])
                    k.TT("dve", ot3[:, n, :], pf.t[0:CH, 0:512], osm[:, n, :], ALU.add, [pf, osm], [ot3])
                else:
                    k.CP("act", ot3[:, n, :], osm[:, n, :], [osm], [ot3])
            ov = ot3[:, 0:nch, :].rearrange("p n (h v) -> p (n h) v", v=128)
            sq = osm[:, 0:nch, :].rearrange("p n (h v) -> p (n h) v", v=128)
            k.TT("pool", sq, ov, ov, ALU.mult, [ot3], [osm])
            P.op("dve", (lambda o_, i_: (lambda e: e.tensor_reduce(out=o_, in_=i_, axis=AX.X, op=ALU.add)))(ssh[:, 0:nch * 4], sq), [osm], [ssh])
            k.TS("dve", ssh[:, 0:nch * 4], ssh[:, 0:nch * 4], 1.0 / 128, EPS, ALU.mult, ALU.add, [ssh], [ssh])
            k.ACT(ssh[:, 0:nch * 4], ssh[:, 0:nch * 4], AF.Sqrt, [ssh], [ssh])
            P.op("dve", (lambda o_: (lambda e: e.reciprocal(out=o_, in_=o_)))(ssh[:, 0:nch * 4]), [ssh], [ssh])
            k.TT("dve", ov, ov, ssh[:, 0:nch * 4].unsqueeze(2).to_broadcast([CH, nch * 4, 128]), ALU.mult, [ot3, ssh], [ot3])
            k.TT("pool", mixb[:, 0:nch, :], ot3[:, 0:nch, :], gtt[:, 0:nch, :], ALU.mult, [ot3, gtt], [mixb])
            if more:
                ld1(ui + 1)
            for h in range(4):
                pbT = k.ps()
                pT = pbT.t[:, :].bitcast(BF16)
                for n in range(nch):
                    k.TR(pT[:, n * CH:(n + 1) * CH], mixb[:, n, h * 128:(h + 1) * 128], ident[0:CH, 0:CH], [mixb, ident], [pbT])
                k.CP("act", mixT[:, h, 0:T], pT[:, 0:T], [pbT], [mixT])
            for c in range(4):
                if fix:
                    k.STT(tA[:, 0:T], acf[:, c, 0:T], hin[:, 0, c:c + 1], hst[:, c, 0:T], ALU.mult, ALU.add, [acf, hin, hst], [tA])
                    k.STT(tA[:, 0:T], acb[:, c, 0:T], hin[:, 1, c:c + 1], tA[:, 0:T], ALU.mult, ALU.add, [acb, hin, tA], [tA])
                    k.TT("pool", mixT[:, 4 + c, 0:T], tA[:, 0:T], gat[:, c, 0:T], ALU.mult, [tA, gat], [mixT])
                else:
                    k.TT("pool", mixT[:, 4 + c, 0:T], hst[:, c, 0:T], gat[:, c, 0:T], ALU.mult, [hst, gat], [mixT])
            if more:
                ld2(ui + 1)
            for jj in range(nt):
                pps = [k.ps(), k.ps()]
                for half in range(2):
                    for kc in range(KD):
                        k.MM(pps[half].t[:, 0:512], mixT[:, kc, jj * 128:(jj + 1) * 128], wob[:, kc, half * 512:(half + 1) * 512],
                             kc == 0, kc == KD - 1, [mixT, wob], [pps[half]])
                    k.ACT(junk3[:], pps[half].t[:, 0:512], AF.Square, [pps[half]], [junk3, ss2], accum=ss2[:, half:half + 1])
                k.TT("dve", ss2[:, 2:3], ss2[:, 0:1], ss2[:, 1:2], ALU.add, [ss2], [ss2])
                k.TS("dve", ss2[:, 2:3], ss2[:, 2:3], 1.0 / D, EPS, ALU.mult, ALU.add, [ss2], [ss2])
                k.ACT(ss2[:, 2:3], ss2[:, 2:3], AF.Sqrt, [ss2], [ss2])
                P.op("dve", (lambda o_: (lambda e: e.reciprocal(out=o_, in_=o_)))(ss2[:, 2:3]), [ss2], [ss2])
                for half in range(2):
                    hsl = slice(half * 512, (half + 1) * 512)
                    k.STT(tmp3[:], pps[half].t[:, 0:512], ss2[:, 2:3], grow0[j][:, hsl], ALU.mult, ALU.mult,
                          [pps[half], ss2, grow0[j]], [tmp3])
                    k.TT("dve", xt3[:, jj, hsl], xt3[:, jj, hsl], tmp3[:], ALU.add, [xt3, tmp3], [xt3])
            k.ST(s_xmid.t[tok0:tok0 + T, :].rearrange("(j p) d -> p j d", p=128), xt3[:, 0:nt, :], [xt3], [s_xmid])
            if more and len(xt3s) == 1:
                ld3(ui + 1)
        P.end_phase()
        es4.close()
        esm.close()
        if STAGE < 6:
            break

        es5 = ExitStack()
        wgb = P.sbuf("wgb", [128, KD, DFF], BF16, es5)
        wub = P.sbuf("wub", [128, KD, DFF], BF16, es5)
        wdb = P.sbuf("wdb", [128, NFF, D], BF16, es5)
        for kc in range(KD):
            k.LD(wgb[:, kc, :], w_gate.ap()[L, kc * 128:(kc + 1) * 128, :], [wgb], q="pool")
            k.LD(wub[:, kc, :], w_up.ap()[L, kc * 128:(kc + 1) * 128, :], [wub], q="pool")
        for jf in range(NFF):
            k.LD(wdb[:, jf, :], w_down.ap()[L, jf * 128:(jf + 1) * 128, :], [wdb], q="pool")
        xt5s = [P.sbuf("xt5_%d" % i, [128, 2, D], F32, es5) for i in range(2)]
        fT5s = [P.sbuf("fT5_%d" % i, [128, KD, 256], BF16, es5) for i in range(2)]
        grow1 = [P.sbuf("grow1_%d" % i, [128, D], F32, es5) for i in range(2)]
        for i in range(2):
            k.LD(grow1[i][:], s_grow.t[2 + i], [grow1[i]], R=[s_grow])
        hid = P.sbuf("hid", [128, NFF, 256], BF16, es5)
        sl5 = P.sbuf("sl5", [128, 256], F32, es5)
        tmp5 = P.sbuf("tmp5", [128, 512], F32, es5)
        junk5 = P.sbuf("junk5", [128, 512], BF16, es5)
        ss5 = P.sbuf("ss5", [128, 4], F32, es5)
        nb5 = norm_bufs(es5, "p5", 2)
        toks5 = [t_ for t_ in range(0, NT, 256) if not (t_ < NCTX and last)]

        def p5A(ui, tok0):
            j = 1 if tok0 < NCTX else 0
            xt5 = xt5s[ui % 2]
            k.LD(xt5[:], s_xmid.t[tok0:tok0 + 256, :].rearrange("(j p) d -> p j d", p=128), [xt5], R=[s_xmid])
            yield from norm_mod_g(nb5, xt5, 2, 1, j, fT5s[ui % 2])

        def p5B(ui, tok0):
            isctx = tok0 < NCTX
            j = 1 if isctx else 0
            T = 256
            xt5 = xt5s[ui % 2]
            fT5 = fT5s[ui % 2]
            for jf in range(NFF):
                pg = k.ps()
                pu = k.ps()
                for kc in range(KD):
                    k.MM(pg.t[:, 0:T], wgb[:, kc, jf * 128:(jf + 1) * 128], fT5[:, kc, 0:T], kc == 0, kc == KD - 1, [wgb, fT5], [pg])
                for kc in range(KD):
                    k.MM(pu.t[:, 0:T], wub[:, kc, jf * 128:(jf + 1) * 128], fT5[:, kc, 0:T], kc == 0, kc == KD - 1, [wub, fT5], [pu])
                k.ACT(sl5[:], pg.t[:, 0:T], AF.Silu, [pg], [sl5])
                k.TT("dve", hid[:, jf, :], sl5[:], pu.t[:, 0:T], ALU.mult, [sl5, pu], [hid])
                yield
            for jj in range(2):
                pps = [k.ps(), k.ps()]
                for half in range(2):
                    for jf in range(NFF):
                        k.MM(pps[half].t[:, 0:512], hid[:, jf, jj * 128:(jj + 1) * 128], wdb[:, jf, half * 512:(half + 1) * 512],
                             jf == 0, jf == NFF - 1, [hid, wdb], [pps[half]])
                    k.ACT(junk5[:], pps[half].t[:, 0:512], AF.Square, [pps[half]], [junk5, ss5], accum=ss5[:, half:half + 1])
                k.TT("dve", ss5[:, 2:3], ss5[:, 0:1], ss5[:, 1:2], ALU.add, [ss5], [ss5])
                k.TS("dve", ss5[:, 2:3], ss5[:, 2:3], 1.0 / D, EPS, ALU.mult, ALU.add, [ss5], [ss5])
                k.ACT(ss5[:, 2:3], ss5[:, 2:3], AF.Sqrt, [ss5], [ss5])
                P.op("dve", (lambda o_: (lambda e: e.reciprocal(out=o_, in_=o_)))(ss5[:, 2:3]), [ss5], [ss5])
                for half in range(2):
                    hsl = slice(half * 512, (half + 1) * 512)
                    k.STT(tmp5[:], pps[half].t[:, 0:512], ss5[:, 2:3], grow1[j][:, hsl], ALU.mult, ALU.mult,
                          [pps[half], ss5, grow1[j]], [tmp5])
                    k.TT("dve", xt5[:, jj, hsl], xt5[:, jj, hsl], tmp5[:], ALU.add, [xt5, tmp5], [xt5])
            if last:
                k.ST(out_t.t.ap()[tok0 - NCTX:tok0 - NCTX + T, :].rearrange("(j p) d -> p j d", p=128), xt5[:], [xt5], [out_t])
            else:
                k.ST(s_xres.t[tok0:tok0 + T, :].rearrange("(j p) d -> p j d", p=128), xt5[:], [xt5], [s_xres])
        pipeline(toks5, p5A, p5B)
        P.end_phase()
        es5.close()

    fin = []
    if DEBUG:
        pairs = [("qT", s_qT), ("fT", s_fT), ("v", s_v), ("g", s_g), ("u", s_u), ("gate", s_gate)]
        if STAGE >= 2:
            pairs += [("hsum", s_hsum), ("acum", s_acum)]
        if STAGE >= 3:
            pairs += [("osum", s_osum), ("qseg", s_qseg)]
        if STAGE >= 5:
            pairs += [("xdst", s_xdst)]
        if STAGE >= 6:
            pairs += [("xmid", s_xmid)]
        if STAGE >= 7:
            pairs += [("xres", s_xres)]
        P.barrier()
        for nm, sb in pairs:
            fin.append(k.ST(dbg[nm].t.ap(), sb.t.ap(), [sb], [dbg[nm]]))
        k.ST(dbg["misc"].t.ap()[:, 0:96], modfm[:].rearrange("p a b -> p (a b)"), [modfm], [dbg["misc"]])
        k.ST(dbg["misc"].t.ap()[:, 96:160], scsh[:].rearrange("p a b c -> p (a b c)"), [scsh], [dbg["misc"]])
        fin.append(k.ST(dbg["misc"].t.ap()[:, 160:168], c1v[:].rearrange("p a b -> p (a b)"), [c1v], [dbg["misc"]]))
        k.ST(dbg["misc"].t.ap()[:, 168:176], hctx[:].rearrange("p a b -> p (a b)"), [hctx], [dbg["misc"]])
        k.ST(dbg["misc"].t.ap()[:, 176:184], hfin[:].rearrange("p a b -> p (a b)"), [hfin], [dbg["misc"]])
        k.ST(dbg["misc"].t.ap()[:, 184:192], atot[:].rearrange("p a b -> p (a b)"), [atot], [dbg["misc"]])
        k.ST(dbg["misc"].t.ap()[:, 192:200], dtot[:].rearrange("p a b -> p (a b)"), [dtot], [dbg["misc"]])
        k.ST(dbg["misc"].t.ap()[:, 200:208], hin[:].rearrange("p a b -> p (a b)"), [hin], [dbg["misc"]])
    P.barrier()
    P.emit()
    return nc


def make_in_maps(inp):
    f = lambda a: np.ascontiguousarray(np.asarray(a, dtype=np.float32))
    x, c, ctx, c_ctx = f(inp["x"]), f(inp["c"]), f(inp["ctx"]), f(inp["c_ctx"])
    b_mod, norm_g = f(inp["b_mod"]), f(inp["norm_g"])
    common = {
        "w_mod": f(inp["w_mod"]),
        "bmod_fm": f(b_mod.reshape(DEPTH, 6, 8, 128).transpose(0, 3, 1, 2).reshape(DEPTH, 128, 48)),
        "bmod_row": f(b_mod.reshape(DEPTH, 1, 6 * D)),
        "normg_fm": f(norm_g.reshape(DEPTH, 4, 8, 128).transpose(0, 3, 1, 2).reshape(DEPTH, 128, 32)),
        "normg_row": norm_g,
        "w_in": f(inp["w_in"]),
        "lb_fm": f(f(inp["hg_lb_logits"]).reshape(DEPTH, 2, 4, 128).transpose(3, 0, 1, 2)),
        "gn_fm": f(f(inp["hg_gnorm"]).reshape(DEPTH, 128, 1)),
        "convw_fm": f(f(inp["rg_conv_w"]).reshape(DEPTH, 4, 4, 128).transpose(0, 3, 2, 1)),
        "convb_fm": f(f(inp["rg_conv_b"]).reshape(DEPTH, 4, 128).transpose(0, 2, 1)),
        "ba_fm": f(f(inp["rg_b_a"]).reshape(DEPTH, 2, 4, 128).transpose(0, 3, 1, 2)),
        "bx_fm": f(f(inp["rg_b_x"]).reshape(DEPTH, 2, 4, 128).transpose(0, 3, 1, 2)),
        "lam_fm": f(f(inp["rg_lambda"]).reshape(DEPTH, 2, 4, 128).transpose(0, 3, 1, 2)),
        "rg_w_a": f(inp["rg_w_a"]),
        "rg_w_x": f(inp["rg_w_x"]),
        "w_out": f(inp["w_out"]),
        "w_ffn_gate": f(inp["w_ffn_gate"]),
        "w_ffn_up": f(inp["w_ffn_up"]),
        "w_ffn_down": f(inp["w_ffn_down"]),
        "ident": np.eye(128, dtype=np.float32),
    }
    tri = np.triu(np.ones((64, 64), np.float32))
    common["masks"] = f(np.stack([np.tile(tri, (1, 8)), np.tile(tri.T, (1, 8))]))
    maps = []
    for core in range(8):
        if CHAIN:
            b, seg = core % 2, 0
        else:
            b, seg = core // 4, core % 4
        m = dict(common)
        m["x"] = f(x[b, seg * NLAT:(seg + 1) * NLAT])
        m["ctx"] = f(ctx[b])
        cv = np.stack([c[b].reshape(8, 128).T, c_ctx.reshape(8, 128).T], axis=-1)
        m["cvec"] = f(cv)
        fl = np.zeros((128, 16), np.float32)
        for r in range(8):
            same = (r // 4 == b) and not CHAIN
            fl[:, r] = 1.0 if (same and r % 4 < seg) else 0.0
            fl[:, 8 + r] = 1.0 if (same and r % 4 > seg) else 0.0
        m["flags"] = fl
        maps.append(m)
    return maps


_NC_CACHE = {}


def kernel(**inputs):
    if "nc" not in _NC_CACHE:
        _NC_CACHE["nc"] = build()
    nc = _NC_CACHE["nc"]
    maps = make_in_maps(inputs)
    res = run_bass_kernel_spmd(nc, maps, core_ids=list(range(8)))
    out = np.empty((2, 16384, D), np.float32)
    for core in range(8):
        if CHAIN:
            if core >= 2:
                continue
            b, seg = core, 0
        else:
            b, seg = core // 4, core % 4
        out[b, seg * NLAT:(seg + 1) * NLAT] = np.asarray(res.results[core]["out"], dtype=np.float32)
    return out
```
